# Optimizing a Trainium2 kernel written in Bass

```python
import jax, jax.numpy as jnp
from jax import lax
import numpy as np

D_MODEL = 1024
BATCH = 8
SEQ = 4096
DEPTH = 4

D_MIX = 2 * D_MODEL
W_POOL = D_MIX // 4
W_FFT = D_MIX // 4
W_SSD = D_MIX // 4
W_ATTN = D_MIX - W_POOL - W_FFT - W_SSD
POOL_WINDOWS = (2, 4, 8, 16)
POOL_GC = W_POOL // len(POOL_WINDOWS)
SSD_HEAD_DIM = 64
SSD_HEADS = W_SSD // SSD_HEAD_DIM
SSD_GROUPS = 2
SSD_STATE = 128
CONV_K = 4
CONV_LEFT = 2
CONV_CH = W_SSD + 2 * SSD_GROUPS * SSD_STATE
CHUNK = 128
HEAD_DIM = 64
N_Q_HEADS = W_ATTN // HEAD_DIM
N_KV_HEADS = 2
Q_PER_KV = N_Q_HEADS // N_KV_HEADS
KV_W = N_KV_HEADS * HEAD_DIM
ROPE_AXIS_DIM = HEAD_DIM // 2
ROPE_THETA = 10000.0
GRID_W = 64
BLOCK_Q = CHUNK
N_META = 16
META_PAD = (-N_META) % CHUNK
NORM_EPS = 1e-6
SPLIT_SIZES = (W_POOL, W_POOL, W_FFT, W_FFT, CONV_CH, W_SSD, 2 * SSD_HEADS, W_ATTN, KV_W, KV_W, W_ATTN)
IN_COLS = sum(SPLIT_SIZES)

kernel_name = "hybrid_pool_fourier_ssd_gqa_encoder"


def rmsnorm(x, w):
    xf = x.astype(jnp.float32)
    y = xf * lax.rsqrt(jnp.mean(xf * xf, axis=-1, keepdims=True) + NORM_EPS)
    return (y * w.astype(jnp.float32)).astype(x.dtype)


def multiscale_pool(u, pool_w, pool_scale):
    n = u.shape[1]
    uf = u.astype(jnp.float32)
    csum = jnp.pad(jnp.cumsum(uf, axis=1), ((0, 0), (1, 0), (0, 0)))
    t = np.arange(n)
    diffs = []
    for g, w in enumerate(POOL_WINDOWS):
        lo = np.clip(t - w // 2, 0, n)
        hi = np.clip(t - w // 2 + w, 0, n)
        sl = slice(g * POOL_GC, (g + 1) * POOL_GC)
        cg = csum[..., sl]
        cnt = jnp.asarray((hi - lo).astype(np.float32))[None, :, None]
        mean = (jnp.take(cg, hi, axis=1) - jnp.take(cg, lo, axis=1)) / cnt
        diffs.append(mean - uf[..., sl])
    d = jnp.stack(diffs, axis=2).astype(u.dtype)
    y = jnp.einsum("blgc,gce->blge", d, pool_w).reshape(u.shape)
    return y * pool_scale


def fourier_mix(u, w):
    mixed = jnp.fft.fft2(u.astype(jnp.float32), axes=(1, 2), norm="ortho").real
    return jnp.einsum("blc,cd->bld", mixed.astype(u.dtype), w)


def ssd_direction(xh, dt, a, bm, cm):
    b, lp, nh, p = xh.shape
    nc = lp // CHUNK
    r = nh // SSD_GROUPS
    xdt = (xh * dt[..., None]).reshape(b, nc, CHUNK, SSD_GROUPS, r, p)
    da = (dt * a).reshape(b, nc, CHUNK, SSD_GROUPS, r).transpose(0, 1, 3, 4, 2)
    bc = bm.reshape(b, nc, CHUNK, SSD_GROUPS, SSD_STATE)
    cc = cm.reshape(b, nc, CHUNK, SSD_GROUPS, SSD_STATE)
    cs = jnp.cumsum(da, axis=-1)
    lower = np.tril(np.ones((CHUNK, CHUNK), dtype=bool))
    seg = cs[..., :, None] - cs[..., None, :]
    lmat = jnp.exp(jnp.where(lower, seg, -jnp.inf))
    cb = jnp.einsum("bclgn,bcsgn->bcgls", cc, bc)
    y_diag = jnp.einsum("bcgrls,bcsgrp->bclgrp", cb[:, :, :, None] * lmat, xdt)
    decay_to_end = jnp.exp(cs[..., -1:] - cs)
    chunk_states = jnp.einsum("bclgn,bcgrl,bclgrp->bcgrpn", bc, decay_to_end, xdt)
    chunk_decay = jnp.exp(cs[..., -1])

    def step(h, inp):
        st, dec = inp
        return h * dec[..., None, None] + st, h

    h0 = jnp.zeros((b, SSD_GROUPS, r, p, SSD_STATE), jnp.float32)
    _, prev = lax.scan(step, h0, (jnp.moveaxis(chunk_states, 1, 0), jnp.moveaxis(chunk_decay, 1, 0)))
    prev = jnp.moveaxis(prev, 0, 1)
    y_off = jnp.einsum("bclgn,bcgrpn,bcgrl->bclgrp", cc, prev, jnp.exp(cs))
    return (y_diag + y_off).reshape(b, lp, nh, p)


def ssd_mixer(xbc, z, dt_raw, conv_w, conv_b, dt_bias, a_log, d_skip, norm_w):
    out_dtype = z.dtype
    b, n, _ = xbc.shape
    xbc = lax.conv_general_dilated(
        xbc, conv_w[:, None, :], window_strides=(1,),
        padding=[(CONV_LEFT, CONV_K - 1 - CONV_LEFT)],
        dimension_numbers=("NWC", "WIO", "NWC"), feature_group_count=CONV_CH)
    xbc = jax.nn.silu(xbc + conv_b).astype(jnp.float32)
    xs, bm, cm = jnp.split(xbc, [W_SSD, W_SSD + SSD_GROUPS * SSD_STATE], axis=-1)
    xh = xs.reshape(b, n, SSD_HEADS, SSD_HEAD_DIM)
    bm = bm.reshape(b, n, SSD_GROUPS, SSD_STATE)
    cm = cm.reshape(b, n, SSD_GROUPS, SSD_STATE)
    dt = jax.nn.softplus(dt_raw.astype(jnp.float32).reshape(b, n, 2, SSD_HEADS) + dt_bias.astype(jnp.float32))
    a = -jnp.exp(a_log.astype(jnp.float32))
    pad4 = ((0, 0), (META_PAD, 0), (0, 0), (0, 0))
    xp, bp, cp, dtp = (jnp.pad(t, pad4) for t in (xh, bm, cm, dt))
    y_fwd = ssd_direction(xp, dtp[:, :, 0], a[0], bp, cp)
    fl = lambda t: jnp.flip(t, axis=1)
    y_bwd = fl(ssd_direction(fl(xp), fl(dtp[:, :, 1]), a[1], fl(bp), fl(cp)))
    y = (y_fwd + y_bwd)[:, META_PAD:] + d_skip.astype(jnp.float32)[:, None] * xh
    y = y.reshape(b, n, W_SSD) * jax.nn.silu(z.astype(jnp.float32))
    return rmsnorm(y, norm_w).astype(out_dtype)


def axial_rope_tables(n_tok):
    rows = n_tok // GRID_W
    row_ids = jnp.repeat(jnp.arange(rows, dtype=jnp.float32), GRID_W)
    col_ids = jnp.broadcast_to(jnp.arange(GRID_W, dtype=jnp.float32)[None], (rows, GRID_W)).reshape(-1)
    zeros = jnp.zeros((N_META,), jnp.float32)
    row_ids = jnp.concatenate([zeros, row_ids])
    col_ids = jnp.concatenate([zeros, col_ids])
    freqs = ROPE_THETA ** (-jnp.arange(0, ROPE_AXIS_DIM, 2, dtype=jnp.float32) / ROPE_AXIS_DIM)
    ang = jnp.concatenate([row_ids[:, None] * freqs, col_ids[:, None] * freqs], axis=-1)
    return jnp.cos(ang), jnp.sin(ang)


def rope2d(x, cos, sin):
    xf = x.astype(jnp.float32).reshape(*x.shape[:-1], HEAD_DIM // 2, 2)
    x0, x1 = xf[..., 0], xf[..., 1]
    c = cos[None, :, None, :]
    s = sin[None, :, None, :]
    out = jnp.stack([x0 * c - x1 * s, x0 * s + x1 * c], axis=-1).reshape(x.shape)
    return out.astype(x.dtype)


def gqa_axial(q, k, v, q_norm_w, k_norm_w, cos, sin):
    b, n, _ = q.shape
    q = rope2d(rmsnorm(q.reshape(b, n, N_Q_HEADS, HEAD_DIM), q_norm_w), cos, sin)
    k = rope2d(rmsnorm(k.reshape(b, n, N_KV_HEADS, HEAD_DIM), k_norm_w), cos, sin)
    kf = k.astype(jnp.float32)
    vf = v.reshape(b, n, N_KV_HEADS, HEAD_DIM).astype(jnp.float32)
    qp = jnp.pad(q, ((0, 0), (META_PAD, 0), (0, 0), (0, 0)))
    nb = qp.shape[1] // BLOCK_Q
    qb = qp.reshape(b, nb, BLOCK_Q, N_KV_HEADS, Q_PER_KV, HEAD_DIM).transpose(1, 0, 2, 3, 4, 5)
    scale = HEAD_DIM ** -0.5

    def block(qblk):
        s = jnp.einsum("bqgrd,bkgd->bgrqk", qblk.astype(jnp.float32), kf) * scale
        p = jax.nn.softmax(s, axis=-1)
        return jnp.einsum("bgrqk,bkgd->bqgrd", p, vf)

    o = lax.map(block, qb)
    o = o.transpose(1, 0, 2, 3, 4, 5).reshape(b, nb * BLOCK_Q, W_ATTN)[:, META_PAD:]
    return o.astype(v.dtype)


def setup_inputs(seed: int = 0) -> dict:
    key = jax.random.key(seed)
    ks = jax.random.split(key, 16)
    f32 = jnp.float32
    nrm = lambda k, shape, s: jax.random.normal(k, shape, f32) * s
    x = nrm(ks[0], (BATCH, SEQ, D_MODEL), 1.0)
    meta_tokens = nrm(ks[1], (N_META, D_MODEL), 1.0)
    norm_w = 1.0 + nrm(ks[2], (DEPTH, D_MODEL), 0.02)
    w_in = nrm(ks[3], (DEPTH, D_MODEL, IN_COLS), D_MODEL ** -0.5)
    w_out = nrm(ks[4], (DEPTH, D_MIX, D_MODEL), D_MIX ** -0.5)
    pool_w = nrm(ks[5], (DEPTH, len(POOL_WINDOWS), POOL_GC, POOL_GC), POOL_GC ** -0.5)
    pool_scale = 1.0 + nrm(ks[6], (DEPTH, W_POOL), 0.02)
    fourier_w = nrm(ks[7], (DEPTH, W_FFT, W_FFT), W_FFT ** -0.5)
    conv_w = nrm(ks[8], (DEPTH, CONV_K, CONV_CH), CONV_K ** -0.5)
    conv_b = nrm(ks[9], (DEPTH, CONV_CH), 0.02)
    dt0 = jnp.exp(jax.random.uniform(ks[10], (DEPTH, 2, SSD_HEADS), f32, np.log(1e-3), np.log(1e-1)))
    dt_bias = dt0 + jnp.log(-jnp.expm1(-dt0))
    a_log = jnp.log(jax.random.uniform(ks[11], (DEPTH, 2, SSD_HEADS), f32, 1.0, 16.0))
    d_skip = 1.0 + nrm(ks[12], (DEPTH, SSD_HEADS), 0.1)
    ssd_norm_w = 1.0 + nrm(ks[13], (DEPTH, W_SSD), 0.02)
    q_norm_w = 1.0 + nrm(ks[14], (DEPTH, HEAD_DIM), 0.02)
    k_norm_w = 1.0 + nrm(ks[15], (DEPTH, HEAD_DIM), 0.02)
    return {"x": x, "meta_tokens": meta_tokens, "norm_w": norm_w, "w_in": w_in, "w_out": w_out,
            "pool_w": pool_w, "pool_scale": pool_scale, "fourier_w": fourier_w,
            "conv_w": conv_w, "conv_b": conv_b, "dt_bias": dt_bias, "a_log": a_log,
            "d_skip": d_skip, "ssd_norm_w": ssd_norm_w, "q_norm_w": q_norm_w, "k_norm_w": k_norm_w}


def reference(x, meta_tokens, norm_w, w_in, w_out, pool_w, pool_scale, fourier_w,
              conv_w, conv_b, dt_bias, a_log, d_skip, ssd_norm_w, q_norm_w, k_norm_w):
    b, n_tok, _ = x.shape
    meta = jnp.broadcast_to(meta_tokens.astype(x.dtype)[None], (b, N_META, D_MODEL))
    h = jnp.concatenate([meta, x], axis=1)
    cos, sin = axial_rope_tables(n_tok)
    points = np.cumsum(SPLIT_SIZES)[:-1].tolist()
    for i in range(DEPTH):
        hn = rmsnorm(h, norm_w[i])
        proj = jnp.einsum("bld,de->ble", hn, w_in[i])
        (u_pool, g_pool, u_fft, g_fft, xbc, z, dt_raw, q, k, v, g_attn) = jnp.split(proj, points, axis=-1)
        y_pool = multiscale_pool(u_pool, pool_w[i], pool_scale[i]) * jax.nn.silu(g_pool)
        y_fft = fourier_mix(u_fft, fourier_w[i]) * jax.nn.silu(g_fft)
        y_ssd = ssd_mixer(xbc, z, dt_raw, conv_w[i], conv_b[i], dt_bias[i], a_log[i], d_skip[i], ssd_norm_w[i])
        y_att = gqa_axial(q, k, v, q_norm_w[i], k_norm_w[i], cos, sin) * jax.nn.silu(g_attn)
        mixed = jnp.concatenate([y_pool, y_fft, y_ssd, y_att], axis=-1)
        h = h + jnp.einsum("ble,ed->bld", mixed, w_out[i])
    return h[:, N_META:]
```

```python
import contextlib
import numpy as np
import ml_dtypes
import concourse.bass as bass
import concourse.mybir as mybir
from concourse.bass_utils import run_bass_kernel_spmd

F32 = mybir.dt.float32
BF16 = mybir.dt.bfloat16
AF = mybir.ActivationFunctionType
ALU = mybir.AluOpType
AX = mybir.AxisListType

D = 1024
NTOK = 4096
NMETA = 16
L = NTOK + NMETA
NPAD = 112
TP = 4224
NCH = 33
NT = 11
TS = 384
TW = TP + 16
DEPTH = 4
INC = 4880
EPS = 1e-6
NPC = 3
PCC = 11

ENGS = ("tensor", "vector", "scalar", "gpsimd", "sync")
N_DMA_SEMS = 24
PSUM_KEYS = frozenset(["pT", "pf0", "pf1", "pz", "pq", "pkv", "ptq", "py", "acc", "pcv0", "pcv1", "ptr", "pseg", "pyd",
                       "pyo", "psn", "Sp0", "Sp1", "Oe0", "Oe1", "Oo0", "Oo1", "po"])


class Op:
    __slots__ = ("eng", "fn", "deps", "is_dma", "needed", "sem", "val", "nobar", "skipped")

    def __init__(self, eng, fn, deps, is_dma):
        self.eng = eng
        self.fn = fn
        self.deps = deps
        self.is_dma = is_dma
        self.needed = False
        self.sem = None
        self.val = 0
        self.nobar = False
        self.skipped = False


class Sched:
    def __init__(self):
        self.ops = []
        self.last_w = {}
        self.readers = {}
        self.last_eng = {}
        self.dmas = []
        self.limit = None

    def _add(self, eng, fn, reads, writes, is_dma, extra=()):
        ex = tuple(k for k in reads if (k[0] if isinstance(k, tuple) else k) in PSUM_KEYS)
        if ex:
            writes = tuple(writes) + tuple(k for k in ex if k not in writes)
        deps = list(extra)
        for k in reads:
            w = self.last_w.get(k)
            if w is not None:
                deps.append(w)
        for k in writes:
            w = self.last_w.get(k)
            if w is not None:
                deps.append(w)
            deps.extend(self.readers.get(k, ()))
        op = Op(eng, fn, deps, is_dma)
        if self.limit is not None and len(self.ops) >= self.limit:
            op.skipped = True
            return op
        self.ops.append(op)
        for k in writes:
            self.last_w[k] = op
            self.readers[k] = []
        for k in reads:
            if k in writes:
                continue
            lst = self.readers.setdefault(k, [])
            if not is_dma:
                lst[:] = [o for o in lst if o.is_dma or o.eng != eng]
            lst.append(op)
        if is_dma:
            self.dmas.append(op)
        else:
            self.last_eng[eng] = op
        return op

    def op(self, eng, fn, reads=(), writes=()):
        return self._add(eng, fn, tuple(reads), tuple(writes), False)

    def dma(self, eng, fn, reads=(), writes=(), nobar=False):
        o = self._add(eng, fn, tuple(reads), tuple(writes), True)
        o.nobar = nobar
        return o

    def barrier(self):
        deps = [o for o in self.last_eng.values()] + [o for o in self.dmas if not o.nobar]
        self.dmas = [o for o in self.dmas if o.nobar]
        for e in ENGS:
            self._add(e, lambda eng: eng.nop(), (), (), False, extra=deps)

    def emit(self, nc, final_wait_ops=()):
        ops = self.ops
        for o in ops:
            for d in o.deps:
                if d.eng == "tensor" and o.eng == "tensor" and not d.is_dma and not o.is_dma:
                    continue
                d.needed = True
        for o in final_wait_ops:
            o.needed = True
        with contextlib.ExitStack() as st:
            esem = {e: st.enter_context(nc.semaphore("s_" + e)) for e in ENGS}
            dsem = [st.enter_context(nc.semaphore("d_%d" % i)) for i in range(N_DMA_SEMS)]
            ecnt = {e: 0 for e in ENGS}
            dcnt = [0] * N_DMA_SEMS
            dnext = 0
            prev_on_dsem = [None] * N_DMA_SEMS
            for o in ops:
                if o.is_dma:
                    i = dnext
                    dnext = (dnext + 1) % N_DMA_SEMS
                    dcnt[i] += 16
                    o.sem = ("d", i)
                    o.val = dcnt[i]
                    if prev_on_dsem[i] is not None:
                        o.deps.append(prev_on_dsem[i])
                    prev_on_dsem[i] = o
                elif o.needed:
                    ecnt[o.eng] += 1
                    o.sem = ("e", o.eng)
                    o.val = ecnt[o.eng]
            per = {e: [o for o in ops if o.eng == e] for e in ENGS}
            final = list(final_wait_ops)
            blk = st.enter_context(nc.Block())

            def semh(s):
                return esem[s[1]] if s[0] == "e" else dsem[s[1]]

            def run(e, eng):
                seen = {}
                for o in per[e]:
                    need = {}
                    for d in o.deps:
                        if d.sem is None:
                            continue
                        if (not d.is_dma) and (not o.is_dma) and d.eng == "tensor" and e == "tensor":
                            continue
                        if need.get(d.sem, 0) < d.val:
                            need[d.sem] = d.val
                    for s, v in need.items():
                        if seen.get(s, 0) < v:
                            eng.wait_ge(semh(s), v)
                            seen[s] = v
                    ins = o.fn(eng)
                    if o.is_dma:
                        ins.then_inc(semh(o.sem), 16)
                    elif o.sem is not None:
                        ins.then_inc(semh(o.sem), 1)
                if e == "sync":
                    for o in final:
                        if seen.get(o.sem, 0) < o.val:
                            eng.wait_ge(semh(o.sem), o.val)
                            seen[o.sem] = o.val

            @blk.tensor
            def _(eng):
                run("tensor", eng)

            @blk.vector
            def _(eng):
                run("vector", eng)

            @blk.scalar
            def _(eng):
                run("scalar", eng)

            @blk.gpsimd
            def _(eng):
                run("gpsimd", eng)

            @blk.sync
            def _(eng):
                run("sync", eng)
        return ecnt, dcnt


_CONSTS = None


def host_consts():
    global _CONSTS
    if _CONSTS is not None:
        return _CONSTS
    bf = ml_dtypes.bfloat16
    c = {}
    c["c_ident"] = np.eye(128, dtype=np.float32).astype(bf)
    tphys = np.arange(TP)
    tlog = tphys - NPAD
    row = np.where(tlog >= NMETA, (tlog - NMETA) // 64, 0).astype(np.float64)
    col = np.where(tlog >= NMETA, (tlog - NMETA) % 64, 0).astype(np.float64)
    freqs = (10000.0 ** (-np.arange(0, 32, 2, dtype=np.float32) / 32.0)).astype(np.float32)
    ang = np.concatenate([row[:, None].astype(np.float32) * freqs[None], col[:, None].astype(np.float32) * freqs[None]],
                         axis=-1).astype(np.float32)
    cos2 = np.repeat(np.cos(ang), 2, axis=-1).astype(np.float32)
    sin2 = np.repeat(np.sin(ang), 2, axis=-1).astype(np.float32)
    rope = np.stack([cos2, sin2], 0).reshape(2, NCH, 128, 64).transpose(2, 0, 1, 3)
    c["c_rope"] = np.ascontiguousarray(rope, dtype=np.float32)
    k = np.arange(128)[:, None]
    l_ = np.arange(128)[None, :]
    trif = (k <= l_).astype(np.float32)
    trib = (k >= l_).astype(np.float32)
    ones = np.ones((128, 128), np.float32)
    c["c_ssd_f32"] = np.ascontiguousarray(np.stack([trif, trib, ones], 1), dtype=np.float32)
    t1f = (k > l_).astype(np.float32)
    t1b = (k < l_).astype(np.float32)
    c["c_ssd_bf"] = np.ascontiguousarray(np.stack([trif, trib, t1f, t1b, trif, trib], 1)).astype(bf)
    m = np.arange(512)
    a = 2.0 * np.pi * ((m[:, None] * m[None, :]) % 512) / 512.0
    cc = np.cos(a).reshape(4, 128, 512).transpose(1, 0, 2)
    sc = np.sin(a).reshape(4, 128, 512).transpose(1, 0, 2)
    c["c_dft_c"] = np.ascontiguousarray(np.stack([cc, sc], 1)).astype(bf)
    scale = 1.0 / np.sqrt(float(L) * 512.0)
    tl = (np.arange(TP) - NPAD).astype(np.int64)
    valid = (tl >= 0)
    prod = (np.maximum(tl, 0)[:, None] * np.maximum(tl, 0)[None, :]) % L
    angp = (2.0 * np.pi / L) * prod.astype(np.float64)
    vm = (valid[:, None] & valid[None, :])
    tabs = []
    for fn_, sgn in ((np.cos, 1.0), (np.sin, -1.0)):
        t_ = (fn_(angp) * (sgn * scale) * vm).astype(np.float32).astype(bf)
        t_ = t_.reshape(NPC, PCC, 128, NT, TS)
        t_ = t_.transpose(3, 0, 2, 1, 4)
        tabs.append(np.ascontiguousarray(t_).reshape(NT, NPC, 128, PCC * TS))
    c["c_tab"] = np.ascontiguousarray(np.stack(tabs, 0))
    pinv = np.zeros((128, 4, 2, 8), np.float32)
    for g, w in enumerate((2, 4, 8, 16)):
        for side in range(2):
            for q in range(8):
                t = q if side == 0 else L - 8 + q
                lo = min(max(t - w // 2, 0), L)
                hi = min(max(t - w // 2 + w, 0), L)
                pinv[:, g, side, q] = 1.0 / float(hi - lo)
    c["c_pool"] = pinv
    vmask = np.ones((128, 64), np.float32)
    vmask[:NPAD] = 0.0
    c["c_vmask"] = vmask.astype(bf)
    _CONSTS = c
    return c


def build(n_layers=DEPTH, dbg=False, phases="012345", limit=None):
    nc = bass.Bass("TRN2", target_bir_lowering=False)
    S = Sched()
    S.limit = limit
    uid = [0]

    def din(name, shape, dt):
        return nc.dram_tensor(name, list(shape), dt, kind="ExternalInput")

    x_d = din("x", [NTOK, D], F32)
    meta_d = din("meta_tokens", [NMETA, D], F32)
    normw_d = din("norm_w", [DEPTH, D], F32)
    win_d = din("w_in", [DEPTH, D, INC], F32)
    wout_d = din("w_out", [DEPTH, 2048, D], F32)
    poolw_d = din("pool_w", [DEPTH, 4, 128, 128], F32)
    pools_d = din("pool_scale", [DEPTH, 512], F32)
    fw_d = din("fourier_w", [DEPTH, 512, 512], F32)
    convw_d = din("conv_w", [DEPTH, 4, 1024], F32)
    convb_d = din("conv_b", [DEPTH, 1024], F32)
    dtb_d = din("dt_bias", [DEPTH, 2, 8], F32)
    alog_d = din("a_log", [DEPTH, 2, 8], F32)
    dskip_d = din("d_skip", [DEPTH, 8], F32)
    ssdnw_d = din("ssd_norm_w", [DEPTH, 512], F32)
    qnw_d = din("q_norm_w", [DEPTH, 64], F32)
    knw_d = din("k_norm_w", [DEPTH, 64], F32)
    c_ident_d = din("c_ident", [128, 128], BF16)
    c_rope_d = din("c_rope", [128, 2, NCH, 64], F32)
    c_ssdf_d = din("c_ssd_f32", [128, 3, 128], F32)
    c_ssdb_d = din("c_ssd_bf", [128, 6, 128], BF16)
    c_dftc_d = din("c_dft_c", [128, 2, 4, 512], BF16)
    c_tab_d = din("c_tab", [2, NT, NPC, 128, PCC * TS], BF16)
    c_pool_d = din("c_pool", [128, 4, 2, 8], F32)
    c_vmask_d = din("c_vmask", [128, 64], BF16)
    out_d = nc.dram_tensor("out", [NTOK, D], F32, kind="ExternalOutput")
    skind = "ExternalOutput" if dbg else "Internal"

    def dscr(name, shape, dt):
        return nc.dram_tensor(name, list(shape), dt, kind=skind)

    hbuf_d = dscr("hbuf", [TP, D], F32)
    fm_d = dscr("fm", [3584, TW], BF16)
    qT_d = dscr("qT", [512, TP], BF16)
    kT_d = dscr("kT", [128, TP], BF16)
    vtm_d = dscr("vtm", [TP, 128], BF16)
    sz_d = dscr("sz", [TP, 512], BF16)
    dtr_d = dscr("dtr", [TP, 16], F32)
    yf_d = dscr("yf", [TP, 512], F32)
    mixT_d = dscr("mixT", [2048, TP], BF16)

    def DA(h, off, dims):
        return bass.AP(h, off, [list(d) for d in dims])

    def mm(out, lhsT, rhs, start, stop, r, w):
        return S.op("tensor", lambda e: e.matmul(out, lhsT=lhsT, rhs=rhs, start=start, stop=stop), r, w)

    def act(out, in_, func, r, w, bias=None, scale=None, accum_out=None, eng="scalar"):
        kw = {}
        if bias is not None:
            kw["bias"] = bias
        if scale is not None:
            kw["scale"] = scale
        if accum_out is not None:
            kw["accum_out"] = accum_out
        return S.op("scalar", lambda e: e.activation(out=out, in_=in_, func=func, **kw), r, w)

    def tt(eng, out, in0, in1, op, r, w):
        return S.op(eng, lambda e: e.tensor_tensor(out=out, in0=in0, in1=in1, op=op), r, w)

    def tsc(eng, out, in0, s1, s2, op0, op1, r, w):
        if op1 is None:
            return S.op(eng, lambda e: e.tensor_scalar(out=out, in0=in0, scalar1=s1, scalar2=None, op0=op0), r, w)
        return S.op(eng, lambda e: e.tensor_scalar(out=out, in0=in0, scalar1=s1, scalar2=s2, op0=op0, op1=op1), r, w)

    def stt(out, in0, scalar, in1, op0, op1, r, w):
        return S.op("vector", lambda e: e.scalar_tensor_tensor(out=out, in0=in0, scalar=scalar, in1=in1, op0=op0, op1=op1), r, w)

    def cp(eng, out, in_, r, w):
        if eng == "scalar":
            return S.op("scalar", lambda e: e.activation(out=out, in_=in_, func=AF.Copy), r, w)
        return S.op(eng, lambda e: e.tensor_copy(out=out, in_=in_), r, w)

    def recip(out, in_, r, w):
        return S.op("vector", lambda e: e.reciprocal(out=out, in_=in_), r, w)

    def memset(eng, ap, val, w):
        return S.op(eng, lambda e: e.memset(ap, val), (), w)

    def dma(eng, out, in_, r, w, nobar=False, slow=False):
        if slow:
            return S.dma(eng, lambda e: e.dma_start(out=out, in_=in_, allow_slow_non_contiguous=True), r, w, nobar=nobar)
        return S.dma(eng, lambda e: e.dma_start(out=out, in_=in_), r, w, nobar=nobar)

    final_ops = []

    with contextlib.ExitStack() as top:
        def alloc(st, name, shape, dt):
            uid[0] += 1
            return st.enter_context(nc.sbuf_tensor("%s_%d" % (name, uid[0]), list(shape), dt))

        def palloc(st, name, shape, dt):
            uid[0] += 1
            return st.enter_context(nc.psum_tensor("%s_%d" % (name, uid[0]), list(shape), dt))

        ident = alloc(top, "ident", [128, 128], BF16)
        zero = alloc(top, "zero", [128, 16], BF16)
        dma("sync", ident[:], c_ident_d.ap(), (), ["ident"])
        memset("gpsimd", zero[:], 0.0, ["zero"])
        for r0, nj in ((0, 4), (2048, 8)):
            for c0 in (0, 8 + TP):
                dma("sync", DA(fm_d, r0 * TW + c0, [[TW, 128], [128 * TW, nj], [1, 8]]),
                    zero[:, 0:8].unsqueeze(1).broadcast_to([128, nj, 8]), ["zero"], ["fm_margin"])
        zf = alloc(top, "zf", [128, D], F32)
        memset("gpsimd", zf[:], 0.0, ["zf"])
        dma("sync", hbuf_d.ap()[0:NPAD, :], zf[0:NPAD, :], ["zf"], ["hbuf_pad"])
        S.barrier()

        def identr(o, i_, r, w):
            return S.op("tensor", lambda e: e.transpose(o, i_, ident[:]), list(r) + ["ident"], w)

        def phase_p0(l):
            with contextlib.ExitStack() as st:
                W = alloc(st, "W", [128, 8, INC], BF16)
                pieces = ((0, 1024, 0), (1024, 1024, 1024), (2048, 1024, 2048), (4368, 512, 3072),
                          (3072, 512, 3584), (3600, 512, 4096), (4112, 256, 4608), (3584, 16, 4864))
                for (s0, n, d0) in pieces:
                    dma("gpsimd", W[:, :, d0:d0 + n],
                        DA(win_d, l * D * INC + s0, [[INC, 128], [128 * INC, 8], [1, n]]), (), [("W", d0)])
                wkeys = [("W", p[2]) for p in pieces]
                hc = [alloc(st, "hc", [128, D], F32) for _ in range(2)]
                junk = alloc(st, "junk", [128, D], BF16)
                nwb = alloc(st, "nwb", [128, D], F32)
                hnb = [alloc(st, "hnb", [128, D], BF16) for _ in range(2)]
                hnT = [alloc(st, "hnT", [128, 8, TS], BF16) for _ in range(2)]
                stt_t = [alloc(st, "stat", [128, 48], F32) for _ in range(2)]
                fst = [alloc(st, "fst", [128, 4, TS], BF16) for _ in range(3)]
                zst = [alloc(st, "zst", [128, 3, 512], BF16) for _ in range(2)]
                qTst = [alloc(st, "qTst", [128, 4, TS], BF16) for _ in range(2)]
                kTst = [alloc(st, "kTst", [128, TS], BF16) for _ in range(2)]
                vst = [alloc(st, "vst", [128, 3, 128], BF16) for _ in range(2)]
                dtst = [alloc(st, "dtst", [128, 3, 16], F32) for _ in range(2)]
                sq = alloc(st, "sq", [128, 640], F32)
                qn = alloc(st, "qn", [128, 640], F32)
                qw = alloc(st, "qw", [128, 640], F32)
                m1 = alloc(st, "m1", [128, 640], F32)
                m2 = alloc(st, "m2", [128, 640], F32)
                qr = [alloc(st, "qr", [128, 640], BF16) for _ in range(2)]
                wqk = alloc(st, "wqk", [128, 640], F32)
                rope = alloc(st, "rope", [128, 2, NCH, 64], F32)
                pT = palloc(st, "pT", [128, 1024], BF16)
                pf = [palloc(st, "pf", [128, 512], F32) for _ in range(2)]
                pz = palloc(st, "pz", [128, 512], F32)
                pq = palloc(st, "pq", [128, 512], F32)
                pkv = palloc(st, "pkv", [128, 512], F32)
                ptq = palloc(st, "ptq", [128, 1024], BF16)

                dma("sync", rope[:], c_rope_d.ap(), (), ["rope"])
                dma("sync", nwb[:], DA(normw_d, l * D, [[0, 128], [1, D]]), (), ["nwb"])
                dma("sync", wqk[:, 0:512].rearrange("p (h d) -> p h d", h=8),
                    DA(qnw_d, l * 64, [[0, 128], [0, 8], [1, 64]]), (), ["wqk"])
                dma("sync", wqk[:, 512:640].rearrange("p (h d) -> p h d", h=2),
                    DA(knw_d, l * 64, [[0, 128], [0, 2], [1, 64]]), (), ["wqk"])
                tsc("gpsimd", wqk[:, 0:512], wqk[:, 0:512], 0.125, None, ALU.mult, None, ["wqk"], ["wqk"])

                nfm = 0
                for i in range(NT):
                    t0 = i * TS
                    hT = hnT[i % 2]
                    kh = "hnT%d" % (i % 2)
                    for cl in range(3):
                        c = 3 * i + cl
                        hcb = hc[c % 2]
                        khc = "hc%d" % (c % 2)
                        sta = stt_t[c % 2]
                        ks = "stat%d" % (c % 2)
                        if l == 0 and c == 0:
                            memset("gpsimd", hcb[:], 0.0, [khc])
                            dma("sync", hcb[NPAD:128, :], meta_d.ap(), (), [khc])
                        elif l == 0:
                            dma("sync", hcb[:], x_d.ap()[(c - 1) * 128:c * 128, :], (), [khc])
                        else:
                            dma("sync", hcb[:], hbuf_d.ap()[c * 128:(c + 1) * 128, :], ["hbuf", "hbuf_pad"], [khc])
                        act(junk[:], hcb[:], AF.Square, [khc], ["junk", (ks, 0)], accum_out=sta[:, 0:1])
                        tsc("vector", sta[:, 1:2], sta[:, 0:1], 1.0 / D, EPS, ALU.mult, ALU.add, [(ks, 0)], [(ks, 1)])
                        act(sta[:, 2:3], sta[:, 1:2], AF.Sqrt, [(ks, 1)], [(ks, 2)])
                        recip(sta[:, 3:4], sta[:, 2:3], [(ks, 2)], [(ks, 3)])
                        hb = hnb[c % 2]
                        khb = "hnb%d" % (c % 2)
                        stt(hb[:], hcb[:], sta[:, 3:4], nwb[:], ALU.mult, ALU.mult, [khc, (ks, 3), "nwb"], [khb])
                        for j in range(8):
                            identr(pT[:, j * 128:(j + 1) * 128], hb[:, j * 128:(j + 1) * 128], [khb], ["pT"])
                        cp("scalar", hT[:, :, cl * 128:(cl + 1) * 128], pT[:].rearrange("p (j t) -> p j t", j=8),
                           ["pT"], [(kh, cl)])
                    hkeys = [(kh, 0), (kh, 1), (kh, 2)]
                    for seg in range(7):
                        fs = fst[seg % 3]
                        kfs = "fst%d" % (seg % 3)
                        gate = seg in (1, 3, 6)
                        for j in range(4):
                            p = pf[nfm % 2]
                            kp = "pf%d" % (nfm % 2)
                            col0 = seg * 512 + j * 128
                            for dk in range(8):
                                mm(p[:, 0:TS], W[:, dk, col0:col0 + 128], hT[:, dk, :], dk == 0, dk == 7,
                                   wkeys + hkeys, [kp])
                            if gate:
                                act(fs[:, j, :], p[:, 0:TS], AF.Silu, [kp], [kfs])
                            else:
                                cp("vector", fs[:, j, :], p[:, 0:TS], [kp], [kfs])
                            nfm += 1
                        dma("sync", DA(fm_d, seg * 512 * TW + 8 + t0, [[TW, 128], [128 * TW, 4], [1, TS]]), fs[:],
                            [kfs, "fm_margin"], [("fm", seg)])
                    zs = zst[i % 2]
                    kzs = "zst%d" % (i % 2)
                    qTs = qTst[i % 2]
                    kqs = "qTst%d" % (i % 2)
                    kTs = kTst[i % 2]
                    kks = "kTst%d" % (i % 2)
                    vs = vst[i % 2]
                    kvs = "vst%d" % (i % 2)
                    dts = dtst[i % 2]
                    kds = "dtst%d" % (i % 2)
                    for cl in range(3):
                        c = 3 * i + cl
                        lt = slice(cl * 128, (cl + 1) * 128)
                        for dk in range(8):
                            mm(pz[:, :], hT[:, dk, lt], W[:, dk, 3584:4096], dk == 0, dk == 7, wkeys + hkeys, ["pz"])
                        act(zs[:, cl, :], pz[:, :], AF.Silu, ["pz"], [kzs])
                        for dk in range(8):
                            mm(pq[:, :], hT[:, dk, lt], W[:, dk, 4096:4608], dk == 0, dk == 7, wkeys + hkeys, ["pq"])
                        for dk in range(8):
                            mm(pkv[:, 0:272], hT[:, dk, lt], W[:, dk, 4608:4880], dk == 0, dk == 7, wkeys + hkeys, ["pkv"])
                        cp("vector", vs[:, cl, :], pkv[:, 128:256], ["pkv"], [kvs])
                        cp("vector", dts[:, cl, :], pkv[:, 256:272], ["pkv"], [kds])
                        sta = stt_t[c % 2]
                        ks = "stat%d" % (c % 2)
                        act(sq[:, 0:512], pq[:, :], AF.Square, ["pq"], [("sq", 0)])
                        act(sq[:, 512:640], pkv[:, 0:128], AF.Square, ["pkv"], [("sq", 1)])
                        S.op("vector", lambda e, o=sta[:, 8:18], a=sq[:].rearrange("p (h d) -> p h d", h=10):
                             e.tensor_reduce(out=o, in_=a, axis=AX.X, op=ALU.add), [("sq", 0), ("sq", 1)], [(ks, 8)])
                        tsc("vector", sta[:, 18:28], sta[:, 8:18], 1.0 / 64, EPS, ALU.mult, ALU.add, [(ks, 8)], [(ks, 18)])
                        act(sta[:, 28:38], sta[:, 18:28], AF.Sqrt, [(ks, 18)], [(ks, 28)])
                        recip(sta[:, 38:48], sta[:, 28:38], [(ks, 28)], [(ks, 38)])
                        tt("vector", qn[:, 0:512].rearrange("p (h d) -> p h d", h=8),
                           pq[:, :].rearrange("p (h d) -> p h d", h=8),
                           sta[:, 38:46].unsqueeze(2).broadcast_to([128, 8, 64]), ALU.mult, ["pq", (ks, 38)], [("qn", 0)])
                        tt("vector", qn[:, 512:640].rearrange("p (h d) -> p h d", h=2),
                           pkv[:, 0:128].rearrange("p (h d) -> p h d", h=2),
                           sta[:, 46:48].unsqueeze(2).broadcast_to([128, 2, 64]), ALU.mult, ["pkv", (ks, 38)], [("qn", 1)])
                        tt("gpsimd", qw[:], qn[:], wqk[:], ALU.mult, [("qn", 0), ("qn", 1), "wqk"], ["qw"])
                        qw3 = qw[:].rearrange("p (h d) -> p h d", h=10)
                        tt("gpsimd", m1[:].rearrange("p (h d) -> p h d", h=10), qw3,
                           rope[:, 0, c, :].unsqueeze(1).broadcast_to([128, 10, 64]), ALU.mult, ["qw", "rope"], ["m1"])
                        tt("gpsimd", m2[:].rearrange("p (h d) -> p h d", h=10), qw3,
                           rope[:, 1, c, :].unsqueeze(1).broadcast_to([128, 10, 64]), ALU.mult, ["qw", "rope"], ["m2"])
                        qrb = qr[c % 2]
                        kqr = "qr%d" % (c % 2)

                        def pv(tile_, off):
                            return bass.AP(tile_, off, [[640, 128], [2, 320]])
                        tt("vector", pv(qrb, 0), pv(m1, 0), pv(m2, 1), ALU.subtract, ["m1", "m2"], [(kqr, 0)])
                        tt("gpsimd", pv(qrb, 1), pv(m2, 0), pv(m1, 1), ALU.add, ["m1", "m2"], [(kqr, 1)])
                        for j in range(5):
                            identr(ptq[:, j * 128:(j + 1) * 128], qrb[:, j * 128:(j + 1) * 128],
                                   [(kqr, 0), (kqr, 1)], ["ptq"])
                        cp("scalar", qTs[:, :, lt], ptq[:, 0:512].rearrange("p (j t) -> p j t", j=4), ["ptq"], [kqs])
                        cp("vector", kTs[:, lt], ptq[:, 512:640], ["ptq"], [kks])
                    dma("sync", DA(sz_d, t0 * 512, [[512, 128], [128 * 512, 3], [1, 512]]), zs[:], [kzs], ["sz"])
                    dma("sync", DA(qT_d, t0, [[TP, 128], [128 * TP, 4], [1, TS]]), qTs[:], [kqs], ["qT"])
                    dma("sync", DA(kT_d, t0, [[TP, 128], [1, TS]]), kTs[:], [kks], ["kT"])
                    dma("sync", DA(vtm_d, t0 * 128, [[128, 128], [128 * 128, 3], [1, 128]]), vs[:], [kvs], ["vtm"])
                    dma("sync", DA(dtr_d, t0 * 16, [[16, 128], [128 * 16, 3], [1, 16]]), dts[:], [kds], ["dtr"])
            S.barrier()

        def phase_p1(l):
            with contextlib.ExitStack() as st:
                pw = alloc(st, "pw", [128, 4, 128], BF16)
                psc = alloc(st, "psc", [128, 4], F32)
                cinv = alloc(st, "cinv", [128, 4, 2, 8], F32)
                ub = [alloc(st, "ub", [128, 4, 400], BF16) for _ in range(2)]
                sg = [alloc(st, "sg", [128, 4, TS], BF16) for _ in range(2)]
                T2 = alloc(st, "T2", [128, 4, 400], F32)
                T4 = alloc(st, "T4", [128, 4, 400], F32)
                T8 = alloc(st, "T8", [128, 4, 400], F32)
                T16 = alloc(st, "T16", [128, 4, 400], F32)
                tmp = alloc(st, "ptmp", [128, 4, 8], F32)
                dT = [alloc(st, "dT", [128, 4, TS], BF16) for _ in range(2)]
                ost = [alloc(st, "ost", [128, 4, TS], BF16) for _ in range(2)]
                py = [palloc(st, "py", [128, 512], F32) for _ in range(4)]
                dma("gpsimd", pw[:], DA(poolw_d, l * 4 * 128 * 128, [[128, 128], [128 * 128, 4], [1, 128]]), (), ["pw"])
                dma("sync", psc[:].unsqueeze(2), DA(pools_d, l * 512, [[1, 128], [128, 4], [1, 1]]), (), ["psc"], slow=True)
                dma("sync", cinv[:], c_pool_d.ap(), (), ["cinv"])
                Ts = (T2, T4, T8, T16)
                for i in range(NT):
                    t0 = i * TS
                    u = ub[i % 2]
                    ku = "ub%d" % (i % 2)
                    s_ = sg[i % 2]
                    ksg = "sg%d" % (i % 2)
                    d_ = dT[i % 2]
                    kd = "dT%d" % (i % 2)
                    o_ = ost[i % 2]
                    ko = "ost%d" % (i % 2)
                    dma("sync", u[:], DA(fm_d, t0, [[TW, 128], [128 * TW, 4], [1, 400]]), [("fm", 0), "fm_margin"], [ku])
                    dma("sync", s_[:], DA(fm_d, 512 * TW + 8 + t0, [[TW, 128], [128 * TW, 4], [1, TS]]), [("fm", 1)], [ksg])
                    tt("vector", T2[:, :, 0:399], u[:, :, 0:399], u[:, :, 1:400], ALU.add, [ku], ["T2"])
                    tt("gpsimd", T4[:, 1:4, 0:397], T2[:, 1:4, 0:397], T2[:, 1:4, 2:399], ALU.add, ["T2"], ["T4"])
                    tt("vector", T8[:, 2:4, 0:393], T4[:, 2:4, 0:393], T4[:, 2:4, 4:397], ALU.add, ["T4"], ["T8"])
                    tt("gpsimd", T16[:, 3, 0:385], T8[:, 3, 0:385], T8[:, 3, 8:393], ALU.add, ["T8"], ["T16"])
                    for g, w_ in enumerate((2, 4, 8, 16)):
                        half = w_ // 2
                        Pg = Ts[g]
                        kP = ("T2", "T4", "T8", "T16")[g]
                        o8 = 8 - half
                        stt(d_[:, g, :], Pg[:, g, o8:o8 + TS], 1.0 / w_, u[:, g, 8:8 + TS], ALU.mult, ALU.subtract,
                            [kP, ku], [(kd, g)])
                        for side, tile_i, m0 in ((0, 0, NPAD), (1, NT - 1, TS - 8)):
                            if i != tile_i:
                                continue
                            tt("vector", tmp[:, g, :], Pg[:, g, o8 + m0:o8 + m0 + 8], cinv[:, g, side, :], ALU.mult,
                               [kP, "cinv"], [("ptmp", g)])
                            tt("vector", d_[:, g, m0:m0 + 8], tmp[:, g, :], u[:, g, 8 + m0:8 + m0 + 8], ALU.subtract,
                               [("ptmp", g), ku], [(kd, g)])
                        mm(py[g][:, 0:TS], pw[:, g, :], d_[:, g, :], True, True, ["pw", (kd, g)], [("py", g)])
                        stt(o_[:, g, :], py[g][:, 0:TS], psc[:, g:g + 1], s_[:, g, :], ALU.mult, ALU.mult,
                            [("py", g), "psc", ksg], [(ko, g)])
                    dma("sync", DA(mixT_d, t0, [[TP, 128], [128 * TP, 4], [1, TS]]), o_[:],
                        [(ko, g) for g in range(4)], [("mixT", 0)])
            S.barrier()

        def phase_p2(l):
            with contextlib.ExitStack() as st:
                Ap = alloc(st, "Ap", [128, NCH, 512], BF16)
                Bp = alloc(st, "Bp", [128, NCH, 512], BF16)
                ring = [alloc(st, "ring", [128, PCC, TS], BF16) for _ in range(4)]
                cs = alloc(st, "dftc", [128, 2, 4, 512], BF16)
                wf = alloc(st, "wf", [128, 4, 512], BF16)
                cw = alloc(st, "cw", [128, 2, 4, 512], BF16)
                uT = [alloc(st, "uT", [128, 4, TS], BF16) for _ in range(2)]
                sg = [alloc(st, "sgf", [128, 4, TS], BF16) for _ in range(2)]
                ost = [alloc(st, "ostf", [128, 4, TS], BF16) for _ in range(2)]
                acc = [[palloc(st, "acc", [128, 512], F32) for _ in range(4)] for _ in range(2)]
                dma("sync", cs[:], c_dftc_d.ap(), (), ["dftc"])
                dma("gpsimd", wf[:], DA(fw_d, l * 512 * 512, [[512, 128], [128 * 512, 4], [1, 512]]), (), ["wf"])
                n = 0
                for tb in range(2):
                    for ci in range(4):
                        p = acc[n % 2][n // 2 % 4]
                        kp = ("acc", n % 2, n // 2 % 4)
                        for mk in range(4):
                            mm(p[:, :], cs[:, tb, mk, ci * 128:(ci + 1) * 128], wf[:, mk, :], mk == 0, mk == 3,
                               ["dftc", "wf"], [kp])
                        cp("scalar" if n % 2 else "vector", cw[:, tb, ci, :], p[:, :], [kp], [("cw", tb, ci)])
                        n += 1
                cwk = [("cw", tb, ci) for tb in range(2) for ci in range(4)]
                for i in range(NT):
                    t0 = i * TS
                    u = uT[i % 2]
                    ku = "uT%d" % (i % 2)
                    dma("sync", u[:], DA(fm_d, 1024 * TW + 8 + t0, [[TW, 128], [128 * TW, 4], [1, TS]]), [("fm", 2)], [ku])
                    for cl in range(3):
                        c = 3 * i + cl
                        for tb, dst in ((0, Ap), (1, Bp)):
                            p = acc[tb][cl]
                            kp = ("acc", tb, cl)
                            for ck in range(4):
                                mm(p[:, :], u[:, ck, cl * 128:(cl + 1) * 128], cw[:, tb, ck, :], ck == 0, ck == 3,
                                   [ku] + cwk, [kp])
                            cp("scalar" if tb else "vector", dst[:, c, :], p[:, :], [kp], [("ApBp", c)])
                apk = [("ApBp", c) for c in range(NCH)]
                npiece = 0
                for kt in range(NT):
                    t0 = kt * TS
                    a4 = acc[kt % 2]
                    s_ = sg[kt % 2]
                    ksg = "sgf%d" % (kt % 2)
                    o_ = ost[kt % 2]
                    ko = "ostf%d" % (kt % 2)
                    dma("sync", s_[:], DA(fm_d, 1536 * TW + 8 + t0, [[TW, 128], [128 * TW, 4], [1, TS]]), [("fm", 3)], [ksg])
                    for pc in range(NPC):
                        for tb in range(2):
                            rb = ring[npiece % 4]
                            kr = "ring%d" % (npiece % 4)
                            npiece += 1
                            off = (((tb * NT + kt) * NPC + pc) * 128) * (PCC * TS)
                            dma("sync", rb[:], DA(c_tab_d, off, [[PCC * TS, 128], [TS, PCC], [1, TS]]), (), [kr])
                            src = Bp if tb else Ap
                            for tci in range(PCC):
                                tc_ = pc * PCC + tci
                                first = (pc == 0 and tb == 0 and tci == 0)
                                last = (pc == NPC - 1 and tb == 1 and tci == PCC - 1)
                                for dc in range(4):
                                    mm(a4[dc][:, 0:TS], src[:, tc_, dc * 128:(dc + 1) * 128], rb[:, tci, :], first, last,
                                       apk + [kr], [("acc", kt % 2, dc)])
                    for dc in range(4):
                        tt("vector", o_[:, dc, :], a4[dc][:, 0:TS], s_[:, dc, :], ALU.mult,
                           [("acc", kt % 2, dc), ksg], [(ko, dc)])
                    dma("sync", DA(mixT_d, 512 * TP + t0, [[TP, 128], [128 * TP, 4], [1, TS]]), o_[:],
                        [(ko, dc) for dc in range(4)], [("mixT", 1)])
            S.barrier()

        def phase_p3(l):
            with contextlib.ExitStack() as st:
                BT = alloc(st, "BT", [128, 2, TP], BF16)
                CT = alloc(st, "CT", [128, 2, TP], BF16)
                xtm = alloc(st, "xtm", [128, NCH, 512], BF16)
                btm = alloc(st, "btm", [128, NCH, 256], BF16)
                dtr = alloc(st, "dtr", [128, NCH, 16], F32)
                dtt = alloc(st, "dtt", [128, NCH, 16], F32)
                da = alloc(st, "da", [128, NCH, 16], F32)
                tA = alloc(st, "tA", [128, NCH, 16], F32)
                tB = alloc(st, "tB", [128, NCH, 16], F32)
                dtb = alloc(st, "dtb", [128, 16], F32)
                alg = alloc(st, "alg", [128, 16], F32)
                dsk = alloc(st, "dsk", [128, 8], F32)
                snw = alloc(st, "snw", [128, 512], F32)
                cwt = alloc(st, "cwt", [128, 8, 4], F32)
                cbt = alloc(st, "cbt", [128, 8], F32)
                dg = alloc(st, "dg", [128, 8, 4, 128], BF16)
                cf = alloc(st, "cf", [128, 3, 128], F32)
                cb = alloc(st, "cb", [128, 6, 128], BF16)
                xin = [alloc(st, "xin", [128, 8, TS + 3], BF16) for _ in range(2)]
                xTt = [alloc(st, "xTt", [128, 4, TS], BF16) for _ in range(2)]
                H = alloc(st, "H", [128, 512], F32)
                Hb = alloc(st, "Hb", [128, 512], BF16)
                Ht = alloc(st, "Ht", [128, 512], F32)
                cst = alloc(st, "cst", [128, 24], F32)
                E = alloc(st, "E", [128, 24], F32)
                R = alloc(st, "R", [128, 8, 128], BF16)
                cbm = alloc(st, "cbm", [128, 2, 128], BF16)
                eseg = alloc(st, "eseg", [128, 8, 128], BF16)
                MT = alloc(st, "MT", [128, 8, 128], BF16)
                xdt = alloc(st, "xdt", [128, 512], BF16)
                xw = alloc(st, "xw", [128, 512], BF16)
                ytmp = alloc(st, "ytmp", [128, 512], F32)
                yc = [alloc(st, "yc", [128, 512], F32) for _ in range(2)]
                yfl = [alloc(st, "yfl", [128, 512], F32) for _ in range(2)]
                szl = [alloc(st, "szl", [128, 512], BF16) for _ in range(2)]
                yg = alloc(st, "yg", [128, 512], F32)
                yjunk = alloc(st, "yjunk", [128, 512], BF16)
                ynb = alloc(st, "ynb", [128, 512], BF16)
                ystat = alloc(st, "ystat", [128, 4], F32)
                yst = [alloc(st, "yst", [128, 4, TS], BF16) for _ in range(2)]
                pcv = [palloc(st, "pcv", [128, 512], F32) for _ in range(2)]
                ptr = palloc(st, "ptr", [128, 1024], BF16)
                pcb = pcv[0]
                pcs = pcv[1]
                pseg = [palloc(st, "pseg", [128, 512], F32) for _ in range(2)]
                pyd = palloc(st, "pyd", [128, 512], F32)
                pyo = palloc(st, "pyo", [128, 512], F32)
                psn = palloc(st, "psn", [128, 512], F32)

                dma("sync", cf[:], c_ssdf_d.ap(), (), ["cf"])
                dma("sync", cb[:], c_ssdb_d.ap(), (), ["cb"])
                dma("sync", dtb[:], DA(dtb_d, l * 16, [[0, 128], [1, 16]]), (), ["dtb"])
                dma("sync", alg[:], DA(alog_d, l * 16, [[0, 128], [1, 16]]), (), ["alg"])
                dma("sync", dsk[:], DA(dskip_d, l * 8, [[0, 128], [1, 8]]), (), ["dsk"])
                dma("sync", snw[:], DA(ssdnw_d, l * 512, [[0, 128], [1, 512]]), (), ["snw"])
                for j in range(4):
                    dma("sync", cwt[:, :, j:j + 1], DA(convw_d, (l * 4 + j) * 1024, [[1, 128], [128, 8], [1, 1]]), (), ["cwt"], slow=True)
                dma("sync", cbt[:].unsqueeze(2), DA(convb_d, l * 1024, [[1, 128], [128, 8], [1, 1]]), (), ["cbt"], slow=True)
                for ch in range(8):
                    for j in range(4):
                        tsc("gpsimd", dg[:, ch, j, :], ident[:], cwt[:, ch, j:j + 1], None, ALU.mult, None,
                            ["ident", "cwt"], [("dg", ch)])
                for c3 in range(3):
                    dma("sync", dtr[:, c3 * 11:(c3 + 1) * 11, :],
                        DA(dtr_d, c3 * 11 * 128 * 16, [[16, 128], [128 * 16, 11], [1, 16]]), ["dtr"], ["dtrl"])
                bcd = dtb[:].unsqueeze(1).broadcast_to([128, NCH, 16])
                tt("vector", tA[:], dtr[:], bcd, ALU.add, ["dtrl", "dtb"], ["tA"])
                act(tB[:], tA[:], AF.Abs, ["tA"], ["tB"])
                act(tB[:], tB[:], AF.Exp, ["tB"], ["tB"], scale=-1.0)
                tsc("vector", tB[:], tB[:], 1.0, None, ALU.add, None, ["tB"], ["tB"])
                act(tB[:], tB[:], AF.Ln, ["tB"], ["tB"])
                tsc("vector", tA[:], tA[:], 0.0, None, ALU.max, None, ["tA"], ["tA"])
                tt("vector", dtt[:], tA[:], tB[:], ALU.add, ["tA", "tB"], ["dtt"])
                act(alg[:], alg[:], AF.Exp, ["alg"], ["alg"])
                tsc("vector", alg[:], alg[:], -1.0, None, ALU.mult, None, ["alg"], ["alg"])
                tt("vector", da[:], dtt[:], alg[:].unsqueeze(1).broadcast_to([128, NCH, 16]), ALU.mult, ["dtt", "alg"], ["da"])

                for i in range(NT):
                    t0 = i * TS
                    xi = xin[i % 2]
                    kx = "xin%d" % (i % 2)
                    xt_ = xTt[i % 2]
                    kxt = "xTt%d" % (i % 2)
                    dma("sync", xi[:], DA(fm_d, 2048 * TW + 8 + t0 - 2, [[TW, 128], [128 * TW, 8], [1, TS + 3]]),
                        [("fm", 4), ("fm", 5), "fm_margin"], [kx])
                    for ch in range(8):
                        p = pcv[ch % 2]
                        kp = "pcv%d" % (ch % 2)
                        for j in range(4):
                            mm(p[:, 0:TS], dg[:, ch, j, :], xi[:, ch, j:j + TS], j == 0, j == 3, [("dg", ch), kx], [kp])
                        if ch < 4:
                            dst, kdst = xt_[:, ch, :], (kxt, ch)
                        elif ch < 6:
                            dst, kdst = BT[:, ch - 4, t0:t0 + TS], ("BT", i)
                        else:
                            dst, kdst = CT[:, ch - 6, t0:t0 + TS], ("CT", i)
                        act(dst, p[:, 0:TS], AF.Silu, [kp, "cbt"], [kdst], bias=cbt[:, ch:ch + 1])
                    if i == 0:
                        memset("gpsimd", xt_[:, :, 0:NPAD], 0.0, [(kxt, ch) for ch in range(4)])
                    for cl in range(3):
                        c = 3 * i + cl
                        lt = slice(cl * 128, (cl + 1) * 128)
                        for ch in range(4):
                            identr(ptr[:, ch * 128:(ch + 1) * 128], xt_[:, ch, lt], [(kxt, q) for q in range(4)], ["ptr"])
                        for g in range(2):
                            identr(ptr[:, 512 + g * 128:512 + (g + 1) * 128], BT[:, g, t0 + cl * 128:t0 + (cl + 1) * 128],
                                   [("BT", i)], ["ptr"])
                        cp("vector", xtm[:, c, :], ptr[:, 0:512], ["ptr"], [("xtm", c)])
                        cp("scalar", btm[:, c, :], ptr[:, 512:768], ["ptr"], [("btm", c)])

                for d_ in range(2):
                    memset("gpsimd", H[:], 0.0, ["H"])
                    memset("gpsimd", Hb[:], 0.0, ["Hb"])
                    order = range(NCH) if d_ == 0 else range(NCH - 1, -1, -1)
                    for n_, c in enumerate(order):
                        i = c // 3
                        cols = slice(c * 128, (c + 1) * 128)
                        dac = da[:, c, d_ * 8:(d_ + 1) * 8]
                        for g in range(2):
                            mm(pcb[:, g * 128:(g + 1) * 128], BT[:, g, cols], CT[:, g, cols], True, True,
                               [("BT", i), ("CT", i)], ["pcv0"])
                        tt("vector", cbm[:], pcb[:, 0:256].rearrange("p (g s) -> p g s", g=2),
                           cb[:, 4 + d_, :].unsqueeze(1).broadcast_to([128, 2, 128]), ALU.mult, ["pcv0", "cb"], ["cbm"])
                        mm(pcs[:, 0:8], cf[:, d_, :], dac, True, True, ["cf", "da"], ["pcv1"])
                        mm(pcs[:, 8:16], cf[:, 2, :], dac, True, True, ["cf", "da"], ["pcv1"])
                        cp("vector", cst[:, 0:16], pcs[:, 0:16], ["pcv1"], [("cst", 0)])
                        tt("vector", cst[:, 16:24], cst[:, 8:16], cst[:, 0:8], ALU.subtract, [("cst", 0)], [("cst", 1)])
                        act(E[:], cst[:], AF.Exp, [("cst", 0), ("cst", 1)], ["E"])
                        tt("gpsimd", R[:], cb[:, d_, :].unsqueeze(1).broadcast_to([128, 8, 128]),
                           dac.unsqueeze(2).broadcast_to([128, 8, 128]), ALU.mult, ["cb", "da"], ["R"])
                        for hh in range(2):
                            mm(pseg[hh][:, :], cb[:, 2 + d_, :], R[:, hh * 4:(hh + 1) * 4, :], True, True, ["cb", "R"],
                               [("pseg", hh)])
                            act(eseg[:, hh * 4:(hh + 1) * 4, :], pseg[hh][:, :].rearrange("p (h l) -> p h l", h=4), AF.Exp,
                                [("pseg", hh)], [("eseg", hh)])
                            tt("vector" if hh else "gpsimd", MT[:, hh * 4:(hh + 1) * 4, :], eseg[:, hh * 4:(hh + 1) * 4, :],
                               cbm[:, hh, :].unsqueeze(1).broadcast_to([128, 4, 128]), ALU.mult,
                               [("eseg", hh), "cbm"], [("MT", hh)])
                        tt("gpsimd", xdt[:].rearrange("p (h d) -> p h d", h=8), xtm[:, c, :].rearrange("p (h d) -> p h d", h=8),
                           dtt[:, c, d_ * 8:(d_ + 1) * 8].unsqueeze(2).broadcast_to([128, 8, 64]), ALU.mult,
                           [("xtm", c), "dtt"], ["xdt"])
                        tt("gpsimd", xw[:].rearrange("p (h d) -> p h d", h=8), xdt[:].rearrange("p (h d) -> p h d", h=8),
                           E[:, 16:24].unsqueeze(2).broadcast_to([128, 8, 64]), ALU.mult, ["xdt", "E"], ["xw"])
                        for h in range(8):
                            mm(pyd[:, h * 64:(h + 1) * 64], MT[:, h, :], xdt[:, h * 64:(h + 1) * 64], True, True,
                               [("MT", h // 4), "xdt"], ["pyd"])
                        for g in range(2):
                            mm(pyo[:, g * 256:(g + 1) * 256], CT[:, g, cols], Hb[:, g * 256:(g + 1) * 256], True, True,
                               [("CT", i), "Hb"], ["pyo"])
                        tt("vector", ytmp[:].rearrange("p (h d) -> p h d", h=8), pyo[:, :].rearrange("p (h d) -> p h d", h=8),
                           E[:, 0:8].unsqueeze(2).broadcast_to([128, 8, 64]), ALU.mult, ["pyo", "E"], ["ytmp"])
                        ycb = yc[n_ % 2]
                        kyc = "yc%d" % (n_ % 2)
                        tt("vector", ycb[:], pyd[:, :], ytmp[:], ALU.add, ["pyd", "ytmp"], [kyc])
                        for g in range(2):
                            mm(psn[:, g * 256:(g + 1) * 256], btm[:, c, g * 128:(g + 1) * 128], xw[:, g * 256:(g + 1) * 256],
                               True, True, [("btm", c), "xw"], ["psn"])
                        tt("gpsimd", Ht[:].rearrange("p (h d) -> p h d", h=8), H[:].rearrange("p (h d) -> p h d", h=8),
                           E[:, 8:16].unsqueeze(2).broadcast_to([128, 8, 64]), ALU.mult, ["H", "E"], ["Ht"])
                        tt("vector", H[:], psn[:, :], Ht[:], ALU.add, ["psn", "Ht"], ["H"])
                        cp("scalar", Hb[:], H[:], ["H"], ["Hb"])
                        if d_ == 0:
                            dma("sync", yf_d.ap()[c * 128:(c + 1) * 128, :], ycb[:], [kyc], ["yf"])
                        else:
                            yfb = yfl[n_ % 2]
                            kyf = "yfl%d" % (n_ % 2)
                            szb = szl[n_ % 2]
                            ksz = "szl%d" % (n_ % 2)
                            dma("sync", yfb[:], yf_d.ap()[c * 128:(c + 1) * 128, :], ["yf"], [kyf])
                            dma("sync", szb[:], sz_d.ap()[c * 128:(c + 1) * 128, :], ["sz"], [ksz])
                            tt("gpsimd", ytmp[:].rearrange("p (h d) -> p h d", h=8),
                               xtm[:, c, :].rearrange("p (h d) -> p h d", h=8),
                               dsk[:].unsqueeze(2).broadcast_to([128, 8, 64]), ALU.mult, [("xtm", c), "dsk"], ["ytmp"])
                            tt("vector", yg[:], ycb[:], yfb[:], ALU.add, [kyc, kyf], ["yg"])
                            tt("vector", yg[:], yg[:], ytmp[:], ALU.add, ["yg", "ytmp"], ["yg"])
                            tt("vector", yg[:], yg[:], szb[:], ALU.mult, ["yg", ksz], ["yg"])
                            act(yjunk[:], yg[:], AF.Square, ["yg"], ["yjunk", ("ystat", 0)], accum_out=ystat[:, 0:1])
                            tsc("vector", ystat[:, 1:2], ystat[:, 0:1], 1.0 / 512, EPS, ALU.mult, ALU.add, [("ystat", 0)], [("ystat", 1)])
                            act(ystat[:, 2:3], ystat[:, 1:2], AF.Sqrt, [("ystat", 1)], [("ystat", 2)])
                            recip(ystat[:, 3:4], ystat[:, 2:3], [("ystat", 2)], [("ystat", 3)])
                            stt(ynb[:], yg[:], ystat[:, 3:4], snw[:], ALU.mult, ALU.mult, ["yg", ("ystat", 3), "snw"], ["ynb"])
                            ti = c // 3
                            cl = c % 3
                            ys = yst[ti % 2]
                            kys = "yst%d" % (ti % 2)
                            for ch in range(4):
                                identr(ptr[:, ch * 128:(ch + 1) * 128], ynb[:, ch * 128:(ch + 1) * 128], ["ynb"], ["ptr"])
                            cp("scalar", ys[:, :, cl * 128:(cl + 1) * 128], ptr[:, 0:512].rearrange("p (j t) -> p j t", j=4),
                               ["ptr"], [(kys, cl)])
                            if cl == 0:
                                dma("sync", DA(mixT_d, 1024 * TP + ti * TS, [[TP, 128], [128 * TP, 4], [1, TS]]), ys[:],
                                    [(kys, q) for q in range(3)], [("mixT", 2)])
            S.barrier()

        def phase_p4(l):
            with contextlib.ExitStack() as st:
                KT2 = alloc(st, "KT2", [128, 2, TP], BF16)
                Vp = alloc(st, "Vp", [128, NCH, 2, 192], BF16)
                vmask = alloc(st, "vmask", [128, 64], BF16)
                QT = [alloc(st, "QT", [128, 4, TS], BF16) for _ in range(2)]
                sg = [alloc(st, "sga", [128, 4, TS], BF16) for _ in range(2)]
                P = [alloc(st, "P", [128, 2, TS], BF16) for _ in range(3)]
                rec = alloc(st, "rec", [128, TS], F32)
                yt = alloc(st, "yt", [128, TS], F32)
                ost = [alloc(st, "osta", [128, 4, TS], BF16) for _ in range(2)]
                Sp = [[palloc(st, "Sp", [128, 512], F32) for _ in range(2)] for _ in range(2)]
                Oe = [palloc(st, "Oe", [128, 512], F32) for _ in range(2)]
                Oo = [palloc(st, "Oo", [128, 512], F32) for _ in range(2)]
                dma("sync", vmask[:], c_vmask_d.ap(), (), ["vmask"])
                for g in range(2):
                    for hs in range(2):
                        dma("sync", KT2[hs * 64:(hs + 1) * 64, g, :], DA(kT_d, g * 64 * TP, [[TP, 64], [1, TP]]), ["kT"], ["KT2"])
                memset("gpsimd", Vp[:], 1.0, ["Vp"])
                for c3 in range(3):
                    for g in range(2):
                        dma("sync", Vp[:, c3 * 11:(c3 + 1) * 11, g, 64:128],
                            DA(vtm_d, c3 * 11 * 128 * 128 + g * 64, [[128, 128], [128 * 128, 11], [1, 64]]), ["vtm"], ["Vp"])
                for g in range(2):
                    for o0 in (0, 128):
                        cp("gpsimd", Vp[:, 0, g, o0:o0 + 64], vmask[:], ["vmask"], ["Vp"])
                npair = 0
                nS = 0
                for i in range(NT):
                    t0 = i * TS
                    q_ = QT[i % 2]
                    kq = "QT%d" % (i % 2)
                    s_ = sg[i % 2]
                    ksg = "sga%d" % (i % 2)
                    o_ = ost[i % 2]
                    ko = "osta%d" % (i % 2)
                    dma("sync", q_[:], DA(qT_d, t0, [[TP, 128], [128 * TP, 4], [1, TS]]), ["qT"], [kq])
                    dma("sync", s_[:], DA(fm_d, 3072 * TW + 8 + t0, [[TW, 128], [128 * TW, 4], [1, TS]]), [("fm", 6)], [ksg])
                    for j in range(4):
                        g = j // 2
                        oe = Oe[npair % 2]
                        oo = Oo[npair % 2]
                        koe = "Oe%d" % (npair % 2)
                        koo = "Oo%d" % (npair % 2)
                        npair += 1
                        for kc in range(NCH):
                            sp = Sp[nS % 2]
                            ksp = "Sp%d" % (nS % 2)
                            pb = P[nS % 3]
                            kpb = "P%d" % (nS % 3)
                            nS += 1
                            kcols = slice(kc * 128, (kc + 1) * 128)
                            mm(sp[0][:, 0:TS], KT2[0:64, g, kcols], q_[0:64, j, :], True, True, ["KT2", kq], [(ksp, 0)])
                            mm(sp[1][:, 0:TS], KT2[64:128, g, kcols], q_[64:128, j, :], True, True, ["KT2", kq], [(ksp, 1)])
                            act(pb[:, 0, :], sp[0][:, 0:TS], AF.Exp, [(ksp, 0)], [(kpb, 0)])
                            act(pb[:, 1, :], sp[1][:, 0:TS], AF.Exp, [(ksp, 1)], [(kpb, 1)])
                            mm(oe[:, 0:TS], Vp[:, kc, g, 64:192], pb[:, 0, :], kc == 0, kc == NCH - 1, ["Vp", (kpb, 0)], [koe])
                            mm(oo[:, 0:TS], Vp[:, kc, g, 0:128], pb[:, 1, :], kc == 0, kc == NCH - 1, ["Vp", (kpb, 1)], [koo])
                        recip(rec[0:64, :], oe[64:128, 0:TS], [koe], [("rec", 0)])
                        recip(rec[64:128, :], oo[0:64, 0:TS], [koo], [("rec", 1)])
                        tt("vector", yt[0:64, :], oe[0:64, 0:TS], rec[0:64, :], ALU.mult, [koe, ("rec", 0)], [("yt", 0)])
                        tt("vector", yt[64:128, :], oo[64:128, 0:TS], rec[64:128, :], ALU.mult, [koo, ("rec", 1)], [("yt", 1)])
                        tt("gpsimd", o_[:, j, :], yt[:], s_[:, j, :], ALU.mult, [("yt", 0), ("yt", 1), ksg], [(ko, j)])
                    dma("sync", DA(mixT_d, 1536 * TP + t0, [[TP, 128], [128 * TP, 4], [1, TS]]), o_[:],
                        [(ko, j) for j in range(4)], [("mixT", 3)])
            S.barrier()

        def phase_p5(l, last):
            with contextlib.ExitStack() as st:
                wo = alloc(st, "wo", [128, 16, D], BF16)
                mx = [alloc(st, "mx", [128, 16, TS], BF16) for _ in range(2)]
                hc = [alloc(st, "hc5", [128, D], F32) for _ in range(2)]
                ho = [alloc(st, "ho5", [128, D], F32) for _ in range(2)]
                po = [[palloc(st, "po", [128, 512], F32) for _ in range(2)] for _ in range(2)]
                for hh in range(2):
                    dma("gpsimd", wo[:, :, hh * 512:(hh + 1) * 512],
                        DA(wout_d, l * 2048 * D + hh * 512, [[D, 128], [128 * D, 16], [1, 512]]), (), [("wo", hh)])
                for i in range(NT):
                    t0 = i * TS
                    m_ = mx[i % 2]
                    km = "mx%d" % (i % 2)
                    for q4 in range(4):
                        dma("sync", m_[:, q4 * 4:(q4 + 1) * 4, :], DA(mixT_d, q4 * 512 * TP + t0, [[TP, 128], [128 * TP, 4], [1, TS]]),
                            [("mixT", q4)], [km])
                    for cl in range(3):
                        c = 3 * i + cl
                        if last and c == 0:
                            continue
                        hcb = hc[c % 2]
                        khc = "hc5%d" % (c % 2)
                        hob = ho[c % 2]
                        kho = "ho5%d" % (c % 2)
                        if l == 0 and c == 0:
                            memset("gpsimd", hcb[:], 0.0, [khc])
                            dma("sync", hcb[NPAD:128, :], meta_d.ap(), (), [khc])
                        elif l == 0:
                            dma("sync", hcb[:], x_d.ap()[(c - 1) * 128:c * 128, :], (), [khc])
                        else:
                            dma("sync", hcb[:], hbuf_d.ap()[c * 128:(c + 1) * 128, :], ["hbuf", "hbuf_pad"], [khc])
                        for hh in range(2):
                            p = po[c % 2][hh]
                            kp = ("po", c % 2, hh)
                            for ec in range(16):
                                mm(p[:, :], m_[:, ec, cl * 128:(cl + 1) * 128], wo[:, ec, hh * 512:(hh + 1) * 512],
                                   ec == 0, ec == 15, [km, ("wo", hh)], [kp])
                            tt("vector", hob[:, hh * 512:(hh + 1) * 512], p[:, :], hcb[:, hh * 512:(hh + 1) * 512], ALU.add,
                               [kp, khc], [(kho, hh)])
                        if last:
                            o = dma("sync", out_d.ap()[(c - 1) * 128:c * 128, :], hob[:], [(kho, 0), (kho, 1)], ["out"])
                            final_ops.append(o)
                        elif c == 0:
                            dma("sync", hbuf_d.ap()[NPAD:128, :], hob[NPAD:128, :], [(kho, 0), (kho, 1)], ["hbuf"])
                        else:
                            dma("sync", hbuf_d.ap()[c * 128:(c + 1) * 128, :], hob[:], [(kho, 0), (kho, 1)], ["hbuf"])
            S.barrier()

        for l in range(n_layers):
            if "0" in phases:
                phase_p0(l)
            if "1" in phases:
                phase_p1(l)
            if "2" in phases:
                phase_p2(l)
            if "3" in phases:
                phase_p3(l)
            if "4" in phases:
                phase_p4(l)
            if "5" in phases:
                phase_p5(l, l == n_layers - 1)
        final_ops[:] = [o for o in final_ops if not o.skipped]
        S.limit = None
        print("n_ops", len(S.ops))
        if not final_ops:
            zo = dma("sync", out_d.ap()[0:128, :], zf[:], ["zf"], ["out"])
            final_ops.append(zo)
        S.emit(nc, final_ops)
    return nc


_PARAM_NAMES = ("meta_tokens", "norm_w", "w_in", "w_out", "pool_w", "pool_scale", "fourier_w", "conv_w", "conv_b",
                "dt_bias", "a_log", "d_skip", "ssd_norm_w", "q_norm_w", "k_norm_w")


def make_in_maps(inputs, cores):
    consts = host_consts()
    shared = {k: np.ascontiguousarray(np.asarray(inputs[k], dtype=np.float32)) for k in _PARAM_NAMES}
    shared.update(consts)
    x = np.asarray(inputs["x"], dtype=np.float32)
    maps = []
    for b in cores:
        m = dict(shared)
        m["x"] = np.ascontiguousarray(x[b])
        maps.append(m)
    return maps


def kernel(**inputs):
    nc = build()
    in_maps = make_in_maps(inputs, list(range(8)))
    res = run_bass_kernel_spmd(nc, in_maps, core_ids=list(range(8)))
    return np.stack([np.asarray(r["out"], dtype=np.float32) for r in res.results], axis=0)
```

```python
import contextlib
import numpy as np
import ml_dtypes
import concourse.bass as bass
import concourse.mybir as mybir
from concourse.bass_utils import run_bass_kernel_spmd

F32 = mybir.dt.float32
BF16 = mybir.dt.bfloat16
AF = mybir.ActivationFunctionType
ALU = mybir.AluOpType
AX = mybir.AxisListType

D = 1024
NTOK = 4096
NMETA = 16
L = NTOK + NMETA
NPAD = 112
TP = 4224
NCH = 33
NT = 11
TS = 384
TW = TP + 16
DEPTH = 4
INC = 4880
EPS = 1e-6
NPC = 3
PCC = 11

ENGS = ("tensor", "vector", "scalar", "gpsimd", "sync")
N_DMA_SEMS = 24
PSUM_KEYS = frozenset(["pT", "pf0", "pf1", "pz", "pq", "pkv", "ptq", "py", "acc", "pcv0", "pcv1", "ptr", "pseg", "pyd",
                       "pyo", "psn", "Sp0", "Sp1", "Oe0", "Oe1", "Oo0", "Oo1", "po"])


class Op:
    __slots__ = ("eng", "fn", "deps", "is_dma", "needed", "sem", "val", "nobar", "skipped")

    def __init__(self, eng, fn, deps, is_dma):
        self.eng = eng
        self.fn = fn
        self.deps = deps
        self.is_dma = is_dma
        self.needed = False
        self.sem = None
        self.val = 0
        self.nobar = False
        self.skipped = False


class Sched:
    def __init__(self):
        self.ops = []
        self.last_w = {}
        self.readers = {}
        self.last_eng = {}
        self.dmas = []
        self.limit = None

    def _add(self, eng, fn, reads, writes, is_dma, extra=()):
        ex = tuple(k for k in reads if (k[0] if isinstance(k, tuple) else k) in PSUM_KEYS)
        if ex:
            writes = tuple(writes) + tuple(k for k in ex if k not in writes)
        deps = list(extra)
        for k in reads:
            w = self.last_w.get(k)
            if w is not None:
                deps.append(w)
        for k in writes:
            w = self.last_w.get(k)
            if w is not None:
                deps.append(w)
            deps.extend(self.readers.get(k, ()))
        op = Op(eng, fn, deps, is_dma)
        if self.limit is not None and len(self.ops) >= self.limit:
            op.skipped = True
            return op
        self.ops.append(op)
        for k in writes:
            self.last_w[k] = op
            self.readers[k] = []
        for k in reads:
            if k in writes:
                continue
            lst = self.readers.setdefault(k, [])
            if not is_dma:
                lst[:] = [o for o in lst if o.is_dma or o.eng != eng]
            lst.append(op)
        if is_dma:
            self.dmas.append(op)
        else:
            self.last_eng[eng] = op
        return op

    def op(self, eng, fn, reads=(), writes=()):
        return self._add(eng, fn, tuple(reads), tuple(writes), False)

    def dma(self, eng, fn, reads=(), writes=(), nobar=False):
        o = self._add(eng, fn, tuple(reads), tuple(writes), True)
        o.nobar = nobar
        return o

    def barrier(self):
        deps = [o for o in self.last_eng.values()] + [o for o in self.dmas if not o.nobar]
        self.dmas = [o for o in self.dmas if o.nobar]
        for e in ENGS:
            self._add(e, lambda eng: eng.nop(), (), (), False, extra=deps)

    def emit(self, nc, final_wait_ops=()):
        ops = self.ops
        for o in ops:
            for d in o.deps:
                if d.eng == "tensor" and o.eng == "tensor" and not d.is_dma and not o.is_dma:
                    continue
                d.needed = True
        for o in final_wait_ops:
            o.needed = True
        with contextlib.ExitStack() as st:
            esem = {e: st.enter_context(nc.semaphore("s_" + e)) for e in ENGS}
            dsem = [st.enter_context(nc.semaphore("d_%d" % i)) for i in range(N_DMA_SEMS)]
            ecnt = {e: 0 for e in ENGS}
            dcnt = [0] * N_DMA_SEMS
            dnext = 0
            prev_on_dsem = [None] * N_DMA_SEMS
            for o in ops:
                if o.is_dma:
                    i = dnext
                    dnext = (dnext + 1) % N_DMA_SEMS
                    dcnt[i] += 16
                    o.sem = ("d", i)
                    o.val = dcnt[i]
                    if prev_on_dsem[i] is not None:
                        o.deps.append(prev_on_dsem[i])
                    prev_on_dsem[i] = o
                elif o.needed:
                    ecnt[o.eng] += 1
                    o.sem = ("e", o.eng)
                    o.val = ecnt[o.eng]
            per = {e: [o for o in ops if o.eng == e] for e in ENGS}
            final = list(final_wait_ops)
            blk = st.enter_context(nc.Block())

            def semh(s):
                return esem[s[1]] if s[0] == "e" else dsem[s[1]]

            def run(e, eng):
                seen = {}
                for o in per[e]:
                    need = {}
                    for d in o.deps:
                        if d.sem is None:
                            continue
                        if (not d.is_dma) and (not o.is_dma) and d.eng == "tensor" and e == "tensor":
                            continue
                        if need.get(d.sem, 0) < d.val:
                            need[d.sem] = d.val
                    for s, v in need.items():
                        if seen.get(s, 0) < v:
                            eng.wait_ge(semh(s), v)
                            seen[s] = v
                    ins = o.fn(eng)
                    if o.is_dma:
                        ins.then_inc(semh(o.sem), 16)
                    elif o.sem is not None:
                        ins.then_inc(semh(o.sem), 1)
                if e == "sync":
                    for o in final:
                        if seen.get(o.sem, 0) < o.val:
                            eng.wait_ge(semh(o.sem), o.val)
                            seen[o.sem] = o.val

            @blk.tensor
            def _(eng):
                run("tensor", eng)

            @blk.vector
            def _(eng):
                run("vector", eng)

            @blk.scalar
            def _(eng):
                run("scalar", eng)

            @blk.gpsimd
            def _(eng):
                run("gpsimd", eng)

            @blk.sync
            def _(eng):
                run("sync", eng)
        return ecnt, dcnt


_CONSTS = None


def host_consts():
    global _CONSTS
    if _CONSTS is not None:
        return _CONSTS
    bf = ml_dtypes.bfloat16
    c = {}
    c["c_ident"] = np.eye(128, dtype=np.float32).astype(bf)
    tphys = np.arange(TP)
    tlog = tphys - NPAD
    row = np.where(tlog >= NMETA, (tlog - NMETA) // 64, 0).astype(np.float64)
    col = np.where(tlog >= NMETA, (tlog - NMETA) % 64, 0).astype(np.float64)
    freqs = (10000.0 ** (-np.arange(0, 32, 2, dtype=np.float32) / 32.0)).astype(np.float32)
    ang = np.concatenate([row[:, None].astype(np.float32) * freqs[None], col[:, None].astype(np.float32) * freqs[None]],
                         axis=-1).astype(np.float32)
    cos2 = np.repeat(np.cos(ang), 2, axis=-1).astype(np.float32)
    sin2 = np.repeat(np.sin(ang), 2, axis=-1).astype(np.float32)
    rope = np.stack([cos2, sin2], 0).reshape(2, NCH, 128, 64).transpose(2, 0, 1, 3)
    c["c_rope"] = np.ascontiguousarray(rope, dtype=np.float32)
    k = np.arange(128)[:, None]
    l_ = np.arange(128)[None, :]
    trif = (k <= l_).astype(np.float32)
    trib = (k >= l_).astype(np.float32)
    ones = np.ones((128, 128), np.float32)
    c["c_ssd_f32"] = np.ascontiguousarray(np.stack([trif, trib, ones], 1), dtype=np.float32)
    t1f = (k > l_).astype(np.float32)
    t1b = (k < l_).astype(np.float32)
    c["c_ssd_bf"] = np.ascontiguousarray(np.stack([trif, trib, t1f, t1b, trif, trib], 1)).astype(bf)
    m = np.arange(512)
    a = 2.0 * np.pi * ((m[:, None] * m[None, :]) % 512) / 512.0
    cc = np.cos(a).reshape(4, 128, 512).transpose(1, 0, 2)
    sc = np.sin(a).reshape(4, 128, 512).transpose(1, 0, 2)
    c["c_dft_c"] = np.ascontiguousarray(np.stack([cc, sc], 1)).astype(bf)
    scale = 1.0 / np.sqrt(float(L) * 512.0)
    tl = (np.arange(TP) - NPAD).astype(np.int64)
    valid = (tl >= 0)
    prod = (np.maximum(tl, 0)[:, None] * np.maximum(tl, 0)[None, :]) % L
    angp = (2.0 * np.pi / L) * prod.astype(np.float64)
    vm = (valid[:, None] & valid[None, :])
    tabs = []
    for fn_, sgn in ((np.cos, 1.0), (np.sin, -1.0)):
        t_ = (fn_(angp) * (sgn * scale) * vm).astype(np.float32).astype(bf)
        t_ = t_.reshape(NPC, PCC, 128, NT, TS)
        t_ = t_.transpose(3, 0, 2, 1, 4)
        tabs.append(np.ascontiguousarray(t_).reshape(NT, NPC, 128, PCC * TS))
    c["c_tab"] = np.ascontiguousarray(np.stack(tabs, 0))
    pinv = np.zeros((128, 4, 2, 8), np.float32)
    for g, w in enumerate((2, 4, 8, 16)):
        for side in range(2):
            for q in range(8):
                t = q if side == 0 else L - 8 + q
                lo = min(max(t - w // 2, 0), L)
                hi = min(max(t - w // 2 + w, 0), L)
                pinv[:, g, side, q] = 1.0 / float(hi - lo)
    c["c_pool"] = pinv
    vmask = np.ones((128, 64), np.float32)
    vmask[:NPAD] = 0.0
    c["c_vmask"] = vmask.astype(bf)
    _CONSTS = c
    return c


def build(n_layers=DEPTH, dbg=False, phases="012345", limit=None):
    nc = bass.Bass("TRN2", target_bir_lowering=False)
    S = Sched()
    S.limit = limit
    uid = [0]

    def din(name, shape, dt):
        return nc.dram_tensor(name, list(shape), dt, kind="ExternalInput")

    x_d = din("x", [NTOK, D], F32)
    meta_d = din("meta_tokens", [NMETA, D], F32)
    normw_d = din("norm_w", [DEPTH, D], F32)
    win_d = din("w_in", [DEPTH, D, INC], F32)
    wout_d = din("w_out", [DEPTH, 2048, D], F32)
    poolw_d = din("pool_w", [DEPTH, 4, 128, 128], F32)
    pools_d = din("pool_scale", [DEPTH, 512], F32)
    fw_d = din("fourier_w", [DEPTH, 512, 512], F32)
    convw_d = din("conv_w", [DEPTH, 4, 1024], F32)
    convb_d = din("conv_b", [DEPTH, 1024], F32)
    dtb_d = din("dt_bias", [DEPTH, 2, 8], F32)
    alog_d = din("a_log", [DEPTH, 2, 8], F32)
    dskip_d = din("d_skip", [DEPTH, 8], F32)
    ssdnw_d = din("ssd_norm_w", [DEPTH, 512], F32)
    qnw_d = din("q_norm_w", [DEPTH, 64], F32)
    knw_d = din("k_norm_w", [DEPTH, 64], F32)
    c_ident_d = din("c_ident", [128, 128], BF16)
    c_rope_d = din("c_rope", [128, 2, NCH, 64], F32)
    c_ssdf_d = din("c_ssd_f32", [128, 3, 128], F32)
    c_ssdb_d = din("c_ssd_bf", [128, 6, 128], BF16)
    c_dftc_d = din("c_dft_c", [128, 2, 4, 512], BF16)
    c_tab_d = din("c_tab", [2, NT, NPC, 128, PCC * TS], BF16)
    c_pool_d = din("c_pool", [128, 4, 2, 8], F32)
    c_vmask_d = din("c_vmask", [128, 64], BF16)
    out_d = nc.dram_tensor("out", [NTOK, D], F32, kind="ExternalOutput")
    skind = "ExternalOutput" if dbg else "Internal"

    def dscr(name, shape, dt):
        return nc.dram_tensor(name, list(shape), dt, kind=skind)

    hbuf_d = dscr("hbuf", [TP, D], F32)
    fm_d = dscr("fm", [3584, TW], BF16)
    qT_d = dscr("qT", [512, TP], BF16)
    kT_d = dscr("kT", [128, TP], BF16)
    vtm_d = dscr("vtm", [TP, 128], BF16)
    sz_d = dscr("sz", [TP, 512], BF16)
    dtr_d = dscr("dtr", [TP, 16], F32)
    yf_d = dscr("yf", [TP, 512], F32)
    mixT_d = dscr("mixT", [2048, TP], BF16)

    def DA(h, off, dims):
        return bass.AP(h, off, [list(d) for d in dims])

    def mm(out, lhsT, rhs, start, stop, r, w):
        return S.op("tensor", lambda e: e.matmul(out, lhsT=lhsT, rhs=rhs, start=start, stop=stop), r, w)

    def act(out, in_, func, r, w, bias=None, scale=None, accum_out=None, eng="scalar"):
        kw = {}
        if bias is not None:
            kw["bias"] = bias
        if scale is not None:
            kw["scale"] = scale
        if accum_out is not None:
            kw["accum_out"] = accum_out
        return S.op("scalar", lambda e: e.activation(out=out, in_=in_, func=func, **kw), r, w)

    def tt(eng, out, in0, in1, op, r, w):
        return S.op(eng, lambda e: e.tensor_tensor(out=out, in0=in0, in1=in1, op=op), r, w)

    def tsc(eng, out, in0, s1, s2, op0, op1, r, w):
        if op1 is None:
            return S.op(eng, lambda e: e.tensor_scalar(out=out, in0=in0, scalar1=s1, scalar2=None, op0=op0), r, w)
        return S.op(eng, lambda e: e.tensor_scalar(out=out, in0=in0, scalar1=s1, scalar2=s2, op0=op0, op1=op1), r, w)

    def stt(out, in0, scalar, in1, op0, op1, r, w):
        return S.op("vector", lambda e: e.scalar_tensor_tensor(out=out, in0=in0, scalar=scalar, in1=in1, op0=op0, op1=op1), r, w)

    def cp(eng, out, in_, r, w):
        if eng == "scalar":
            return S.op("scalar", lambda e: e.activation(out=out, in_=in_, func=AF.Copy), r, w)
        return S.op(eng, lambda e: e.tensor_copy(out=out, in_=in_), r, w)

    def recip(out, in_, r, w):
        return S.op("vector", lambda e: e.reciprocal(out=out, in_=in_), r, w)

    def memset(eng, ap, val, w):
        return S.op(eng, lambda e: e.memset(ap, val), (), w)

    def dma(eng, out, in_, r, w, nobar=False, slow=False):
        if slow:
            return S.dma(eng, lambda e: e.dma_start(out=out, in_=in_, allow_slow_non_contiguous=True), r, w, nobar=nobar)
        return S.dma(eng, lambda e: e.dma_start(out=out, in_=in_), r, w, nobar=nobar)

    final_ops = []

    with contextlib.ExitStack() as top:
        def alloc(st, name, shape, dt):
            uid[0] += 1
            return st.enter_context(nc.sbuf_tensor("%s_%d" % (name, uid[0]), list(shape), dt))

        def palloc(st, name, shape, dt):
            uid[0] += 1
            return st.enter_context(nc.psum_tensor("%s_%d" % (name, uid[0]), list(shape), dt))

        ident = alloc(top, "ident", [128, 128], BF16)
        zero = alloc(top, "zero", [128, 16], BF16)
        dma("sync", ident[:], c_ident_d.ap(), (), ["ident"])
        memset("gpsimd", zero[:], 0.0, ["zero"])
        for r0, nj in ((0, 4), (2048, 8)):
            for c0 in (0, 8 + TP):
                dma("sync", DA(fm_d, r0 * TW + c0, [[TW, 128], [128 * TW, nj], [1, 8]]),
                    zero[:, 0:8].unsqueeze(1).broadcast_to([128, nj, 8]), ["zero"], ["fm_margin"])
        mhalf = alloc(top, "mhalf", [128, 16], F32)
        memset("gpsimd", mhalf[:], -0.5, ["mhalf"])
        zf = alloc(top, "zf", [128, D], F32)
        memset("gpsimd", zf[:], 0.0, ["zf"])
        dma("sync", hbuf_d.ap()[0:NPAD, :], zf[0:NPAD, :], ["zf"], ["hbuf_pad"])
        S.barrier()

        def identr(o, i_, r, w):
            return S.op("tensor", lambda e: e.transpose(o, i_, ident[:]), list(r) + ["ident"], w)

        def phase_p0(l):
            with contextlib.ExitStack() as st:
                W = alloc(st, "W", [128, 8, INC], BF16)
                pieces = ((0, 1024, 0), (1024, 1024, 1024), (2048, 1024, 2048), (4368, 512, 3072),
                          (3072, 512, 3584), (3600, 512, 4096), (4112, 256, 4608), (3584, 16, 4864))
                for (s0, n, d0) in pieces:
                    dma("gpsimd", W[:, :, d0:d0 + n],
                        DA(win_d, l * D * INC + s0, [[INC, 128], [128 * INC, 8], [1, n]]), (), [("W", d0)])
                wkeys = [("W", p[2]) for p in pieces]
                hc = [alloc(st, "hc", [128, D], F32) for _ in range(3)]
                junk = alloc(st, "junk", [128, D], BF16)
                nwb = alloc(st, "nwb", [128, D], F32)
                hnb = [alloc(st, "hnb", [128, D], BF16) for _ in range(3)]
                hnT = [alloc(st, "hnT", [128, 8, TS], BF16) for _ in range(2)]
                stt_t = [alloc(st, "stat", [128, 48], F32) for _ in range(2)]
                fst = [alloc(st, "fst", [128, 4, TS], BF16) for _ in range(3)]
                zst = [alloc(st, "zst", [128, 3, 512], BF16) for _ in range(2)]
                qTst = [alloc(st, "qTst", [128, 4, TS], BF16) for _ in range(2)]
                kTst = [alloc(st, "kTst", [128, TS], BF16) for _ in range(2)]
                vst = [alloc(st, "vst", [128, 3, 128], BF16) for _ in range(2)]
                dtst = [alloc(st, "dtst", [128, 3, 16], F32) for _ in range(2)]
                sq = alloc(st, "sq", [128, 640], F32)
                qn = alloc(st, "qn", [128, 640], F32)
                qw = alloc(st, "qw", [128, 640], F32)
                m1 = alloc(st, "m1", [128, 640], F32)
                m2 = alloc(st, "m2", [128, 640], F32)
                qr = [alloc(st, "qr", [128, 640], BF16) for _ in range(2)]
                wqk = alloc(st, "wqk", [128, 640], F32)
                rope = alloc(st, "rope", [128, 2, NCH, 64], F32)
                pT = palloc(st, "pT", [128, 1024], BF16)
                pf = [palloc(st, "pf", [128, 512], F32) for _ in range(2)]
                pz = palloc(st, "pz", [128, 512], F32)
                pq = palloc(st, "pq", [128, 512], F32)
                pkv = palloc(st, "pkv", [128, 512], F32)
                ptq = palloc(st, "ptq", [128, 1024], BF16)

                dma("sync", rope[:], c_rope_d.ap(), (), ["rope"])
                dma("sync", nwb[:], DA(normw_d, l * D, [[0, 128], [1, D]]), (), ["nwb"])
                dma("sync", wqk[:, 0:512].rearrange("p (h d) -> p h d", h=8),
                    DA(qnw_d, l * 64, [[0, 128], [0, 8], [1, 64]]), (), ["wqk"])
                dma("sync", wqk[:, 512:640].rearrange("p (h d) -> p h d", h=2),
                    DA(knw_d, l * 64, [[0, 128], [0, 2], [1, 64]]), (), ["wqk"])
                tsc("gpsimd", wqk[:, 0:512], wqk[:, 0:512], 0.125, None, ALU.mult, None, ["wqk"], ["wqk"])

                nfm = [0]

                def norm_load(c):
                    hcb = hc[c % 3]
                    khc = "hc%d" % (c % 3)
                    if l == 0 and c == 0:
                        memset("gpsimd", hcb[:], 0.0, [khc])
                        dma("sync", hcb[NPAD:128, :], meta_d.ap(), (), [khc])
                    elif l == 0:
                        dma("sync", hcb[:], x_d.ap()[(c - 1) * 128:c * 128, :], (), [khc])
                    else:
                        dma("sync", hcb[:], hbuf_d.ap()[c * 128:(c + 1) * 128, :], ["hbuf", "hbuf_pad"], [khc])

                def norm_front(c):
                    hcb = hc[c % 3]
                    khc = "hc%d" % (c % 3)
                    sta = stt_t[c % 2]
                    ks = "stat%d" % (c % 2)
                    act(junk[:], hcb[:], AF.Square, [khc], ["junk", (ks, 0)], accum_out=sta[:, 0:1])
                    tsc("vector", sta[:, 1:2], sta[:, 0:1], 1.0 / D, EPS, ALU.mult, ALU.add, [(ks, 0)], [(ks, 1)])
                    tt("gpsimd", sta[:, 3:4], sta[:, 1:2], mhalf[:, 0:1], ALU.pow, [(ks, 1), "mhalf"], [(ks, 3)])
                    stt(hnb[c % 3][:], hcb[:], sta[:, 3:4], nwb[:], ALU.mult, ALU.mult, [khc, (ks, 3), "nwb"], ["hnb%d" % (c % 3)])

                def norm_back(c):
                    i_, cl = c // 3, c % 3
                    hT = hnT[i_ % 2]
                    kh = "hnT%d" % (i_ % 2)
                    hb = hnb[c % 3]
                    khb = "hnb%d" % (c % 3)
                    for j in range(8):
                        identr(pT[:, j * 128:(j + 1) * 128], hb[:, j * 128:(j + 1) * 128], [khb], ["pT"])
                    cp("scalar", hT[:, :, cl * 128:(cl + 1) * 128], pT[:].rearrange("p (j t) -> p j t", j=8),
                       ["pT"], [(kh, cl)])

                for c in range(3):
                    norm_load(c)
                for c in range(3):
                    norm_front(c)
                for c in range(3):
                    norm_back(c)
                for i in range(NT):
                    t0 = i * TS
                    hT = hnT[i % 2]
                    kh = "hnT%d" % (i % 2)
                    hkeys = [(kh, 0), (kh, 1), (kh, 2)]
                    if i + 1 < NT:
                        for cl in range(3):
                            norm_load(3 * (i + 1) + cl)
                    def fm_seg(seg, i=i, t0=t0, hT=hT, hkeys=hkeys):
                        fs = fst[seg % 3]
                        kfs = "fst%d" % (seg % 3)
                        gate = seg in (1, 3, 6)
                        for j in range(4):
                            p = pf[nfm[0] % 2]
                            kp = "pf%d" % (nfm[0] % 2)
                            col0 = seg * 512 + j * 128
                            for dk in range(8):
                                mm(p[:, 0:TS], W[:, dk, col0:col0 + 128], hT[:, dk, :], dk == 0, dk == 7,
                                   wkeys + hkeys, [kp])
                            if gate:
                                act(fs[:, j, :], p[:, 0:TS], AF.Silu, [kp], [kfs])
                            else:
                                cp("vector", fs[:, j, :], p[:, 0:TS], [kp], [kfs])
                            nfm[0] += 1
                        dma("sync", DA(fm_d, seg * 512 * TW + 8 + t0, [[TW, 128], [128 * TW, 4], [1, TS]]), fs[:],
                            [kfs, "fm_margin"], [("fm", seg)])
                        if seg in (0, 2, 4) and i + 1 < NT:
                            norm_front(3 * (i + 1) + seg // 2)
                    zs = zst[i % 2]
                    kzs = "zst%d" % (i % 2)
                    qTs = qTst[i % 2]
                    kqs = "qTst%d" % (i % 2)
                    kTs = kTst[i % 2]
                    kks = "kTst%d" % (i % 2)
                    vs = vst[i % 2]
                    kvs = "vst%d" % (i % 2)
                    dts = dtst[i % 2]
                    kds = "dtst%d" % (i % 2)

                    def qtrans(cl, i=i, qTs=qTs, kqs=kqs, kTs=kTs, kks=kks):
                        c = 3 * i + cl
                        lt = slice(cl * 128, (cl + 1) * 128)
                        qrb = qr[c % 2]
                        kqr = "qr%d" % (c % 2)
                        for j in range(5):
                            identr(ptq[:, j * 128:(j + 1) * 128], qrb[:, j * 128:(j + 1) * 128],
                                   [(kqr, 0), (kqr, 1)], ["ptq"])
                        cp("scalar", qTs[:, :, lt], ptq[:, 0:512].rearrange("p (j t) -> p j t", j=4), ["ptq"], [kqs])
                        cp("vector", kTs[:, lt], ptq[:, 512:640], ["ptq"], [kks])

                    for cl in range(3):
                        c = 3 * i + cl
                        lt = slice(cl * 128, (cl + 1) * 128)
                        if cl >= 1:
                            fm_seg(2 * cl - 2)
                            fm_seg(2 * cl - 1)
                        for dk in range(8):
                            mm(pz[:, :], hT[:, dk, lt], W[:, dk, 3584:4096], dk == 0, dk == 7, wkeys + hkeys, ["pz"])
                        act(zs[:, cl, :], pz[:, :], AF.Silu, ["pz"], [kzs])
                        for dk in range(8):
                            mm(pq[:, :], hT[:, dk, lt], W[:, dk, 4096:4608], dk == 0, dk == 7, wkeys + hkeys, ["pq"])
                        for dk in range(8):
                            mm(pkv[:, 0:272], hT[:, dk, lt], W[:, dk, 4608:4880], dk == 0, dk == 7, wkeys + hkeys, ["pkv"])
                        if cl >= 1:
                            qtrans(cl - 1)
                        cp("vector", vs[:, cl, :], pkv[:, 128:256], ["pkv"], [kvs])
                        cp("vector", dts[:, cl, :], pkv[:, 256:272], ["pkv"], [kds])
                        sta = stt_t[c % 2]
                        ks = "stat%d" % (c % 2)
                        act(sq[:, 0:512], pq[:, :], AF.Square, ["pq"], [("sq", 0)])
                        act(sq[:, 512:640], pkv[:, 0:128], AF.Square, ["pkv"], [("sq", 1)])
                        S.op("vector", lambda e, o=sta[:, 8:18], a=sq[:].rearrange("p (h d) -> p h d", h=10):
                             e.tensor_reduce(out=o, in_=a, axis=AX.X, op=ALU.add), [("sq", 0), ("sq", 1)], [(ks, 8)])
                        tsc("vector", sta[:, 18:28], sta[:, 8:18], 1.0 / 64, EPS, ALU.mult, ALU.add, [(ks, 8)], [(ks, 18)])
                        tt("gpsimd", sta[:, 38:48], sta[:, 18:28], mhalf[:, 0:10], ALU.pow, [(ks, 18), "mhalf"], [(ks, 38)])
                        tt("vector", qn[:, 0:512].rearrange("p (h d) -> p h d", h=8),
                           pq[:, :].rearrange("p (h d) -> p h d", h=8),
                           sta[:, 38:46].unsqueeze(2).broadcast_to([128, 8, 64]), ALU.mult, ["pq", (ks, 38)], [("qn", 0)])
                        tt("vector", qn[:, 512:640].rearrange("p (h d) -> p h d", h=2),
                           pkv[:, 0:128].rearrange("p (h d) -> p h d", h=2),
                           sta[:, 46:48].unsqueeze(2).broadcast_to([128, 2, 64]), ALU.mult, ["pkv", (ks, 38)], [("qn", 1)])
                        tt("gpsimd", qw[:], qn[:], wqk[:], ALU.mult, [("qn", 0), ("qn", 1), "wqk"], ["qw"])
                        qw3 = qw[:].rearrange("p (h d) -> p h d", h=10)
                        tt("gpsimd", m1[:].rearrange("p (h d) -> p h d", h=10), qw3,
                           rope[:, 0, c, :].unsqueeze(1).broadcast_to([128, 10, 64]), ALU.mult, ["qw", "rope"], ["m1"])
                        tt("gpsimd", m2[:].rearrange("p (h d) -> p h d", h=10), qw3,
                           rope[:, 1, c, :].unsqueeze(1).broadcast_to([128, 10, 64]), ALU.mult, ["qw", "rope"], ["m2"])
                        qrb = qr[c % 2]
                        kqr = "qr%d" % (c % 2)

                        def pv(tile_, off):
                            return bass.AP(tile_, off, [[640, 128], [2, 320]])
                        tt("vector", pv(qrb, 0), pv(m1, 0), pv(m2, 1), ALU.subtract, ["m1", "m2"], [(kqr, 0)])
                        tt("gpsimd", pv(qrb, 1), pv(m2, 0), pv(m1, 1), ALU.add, ["m1", "m2"], [(kqr, 1)])
                    for seg in (4, 5, 6):
                        fm_seg(seg)
                    if i + 1 < NT:
                        for cl in range(3):
                            norm_back(3 * (i + 1) + cl)
                    qtrans(2)
                    dma("sync", DA(sz_d, t0 * 512, [[512, 128], [128 * 512, 3], [1, 512]]), zs[:], [kzs], ["sz"])
                    dma("sync", DA(qT_d, t0, [[TP, 128], [128 * TP, 4], [1, TS]]), qTs[:], [kqs], ["qT"])
                    dma("sync", DA(kT_d, t0, [[TP, 128], [1, TS]]), kTs[:], [kks], ["kT"])
                    dma("sync", DA(vtm_d, t0 * 128, [[128, 128], [128 * 128, 3], [1, 128]]), vs[:], [kvs], ["vtm"])
                    dma("sync", DA(dtr_d, t0 * 16, [[16, 128], [128 * 16, 3], [1, 16]]), dts[:], [kds], ["dtr"])
            S.barrier()

        def phase_p1(l):
            with contextlib.ExitStack() as st:
                pw = alloc(st, "pw", [128, 4, 128], BF16)
                psc = alloc(st, "psc", [128, 4], F32)
                cinv = alloc(st, "cinv", [128, 4, 2, 8], F32)
                ub = [alloc(st, "ub", [128, 4, 400], BF16) for _ in range(2)]
                sg = [alloc(st, "sg", [128, 4, TS], BF16) for _ in range(2)]
                T2 = alloc(st, "T2", [128, 4, 400], F32)
                T4 = alloc(st, "T4", [128, 4, 400], F32)
                T8 = alloc(st, "T8", [128, 4, 400], F32)
                T16 = alloc(st, "T16", [128, 4, 400], F32)
                tmp = alloc(st, "ptmp", [128, 4, 8], F32)
                dT = [alloc(st, "dT", [128, 4, TS], BF16) for _ in range(2)]
                ost = [alloc(st, "ost", [128, 4, TS], BF16) for _ in range(2)]
                py = [palloc(st, "py", [128, 512], F32) for _ in range(4)]
                dma("gpsimd", pw[:], DA(poolw_d, l * 4 * 128 * 128, [[128, 128], [128 * 128, 4], [1, 128]]), (), ["pw"])
                dma("sync", psc[:].unsqueeze(2), DA(pools_d, l * 512, [[1, 128], [128, 4], [1, 1]]), (), ["psc"], slow=True)
                dma("sync", cinv[:], c_pool_d.ap(), (), ["cinv"])
                Ts = (T2, T4, T8, T16)
                for i in range(NT):
                    t0 = i * TS
                    u = ub[i % 2]
                    ku = "ub%d" % (i % 2)
                    s_ = sg[i % 2]
                    ksg = "sg%d" % (i % 2)
                    d_ = dT[i % 2]
                    kd = "dT%d" % (i % 2)
                    o_ = ost[i % 2]
                    ko = "ost%d" % (i % 2)
                    dma("sync", u[:], DA(fm_d, t0, [[TW, 128], [128 * TW, 4], [1, 400]]), [("fm", 0), "fm_margin"], [ku])
                    dma("sync", s_[:], DA(fm_d, 512 * TW + 8 + t0, [[TW, 128], [128 * TW, 4], [1, TS]]), [("fm", 1)], [ksg])
                    tt("vector", T2[:, :, 0:399], u[:, :, 0:399], u[:, :, 1:400], ALU.add, [ku], ["T2"])
                    tt("gpsimd", T4[:, 1:4, 0:397], T2[:, 1:4, 0:397], T2[:, 1:4, 2:399], ALU.add, ["T2"], ["T4"])
                    tt("vector", T8[:, 2:4, 0:393], T4[:, 2:4, 0:393], T4[:, 2:4, 4:397], ALU.add, ["T4"], ["T8"])
                    tt("gpsimd", T16[:, 3, 0:385], T8[:, 3, 0:385], T8[:, 3, 8:393], ALU.add, ["T8"], ["T16"])
                    for g, w_ in enumerate((2, 4, 8, 16)):
                        half = w_ // 2
                        Pg = Ts[g]
                        kP = ("T2", "T4", "T8", "T16")[g]
                        o8 = 8 - half
                        stt(d_[:, g, :], Pg[:, g, o8:o8 + TS], 1.0 / w_, u[:, g, 8:8 + TS], ALU.mult, ALU.subtract,
                            [kP, ku], [(kd, g)])
                        for side, tile_i, m0 in ((0, 0, NPAD), (1, NT - 1, TS - 8)):
                            if i != tile_i:
                                continue
                            tt("vector", tmp[:, g, :], Pg[:, g, o8 + m0:o8 + m0 + 8], cinv[:, g, side, :], ALU.mult,
                               [kP, "cinv"], [("ptmp", g)])
                            tt("vector", d_[:, g, m0:m0 + 8], tmp[:, g, :], u[:, g, 8 + m0:8 + m0 + 8], ALU.subtract,
                               [("ptmp", g), ku], [(kd, g)])
                        mm(py[g][:, 0:TS], pw[:, g, :], d_[:, g, :], True, True, ["pw", (kd, g)], [("py", g)])
                        stt(o_[:, g, :], py[g][:, 0:TS], psc[:, g:g + 1], s_[:, g, :], ALU.mult, ALU.mult,
                            [("py", g), "psc", ksg], [(ko, g)])
                    dma("sync", DA(mixT_d, t0, [[TP, 128], [128 * TP, 4], [1, TS]]), o_[:],
                        [(ko, g) for g in range(4)], [("mixT", 0)])
            S.barrier()

        def phase_p2(l):
            with contextlib.ExitStack() as st:
                Ap = alloc(st, "Ap", [128, NCH, 512], BF16)
                Bp = alloc(st, "Bp", [128, NCH, 512], BF16)
                ring = [alloc(st, "ring", [128, PCC, TS], BF16) for _ in range(4)]
                cs = alloc(st, "dftc", [128, 2, 4, 512], BF16)
                wf = alloc(st, "wf", [128, 4, 512], BF16)
                cw = alloc(st, "cw", [128, 2, 4, 512], BF16)
                uT = [alloc(st, "uT", [128, 4, TS], BF16) for _ in range(2)]
                sg = [alloc(st, "sgf", [128, 4, TS], BF16) for _ in range(2)]
                ost = [alloc(st, "ostf", [128, 4, TS], BF16) for _ in range(2)]
                acc = [[palloc(st, "acc", [128, 512], F32) for _ in range(4)] for _ in range(2)]
                dma("sync", cs[:], c_dftc_d.ap(), (), ["dftc"])
                dma("gpsimd", wf[:], DA(fw_d, l * 512 * 512, [[512, 128], [128 * 512, 4], [1, 512]]), (), ["wf"])
                n = 0
                for tb in range(2):
                    for ci in range(4):
                        p = acc[n % 2][n // 2 % 4]
                        kp = ("acc", n % 2, n // 2 % 4)
                        for mk in range(4):
                            mm(p[:, :], cs[:, tb, mk, ci * 128:(ci + 1) * 128], wf[:, mk, :], mk == 0, mk == 3,
                               ["dftc", "wf"], [kp])
                        cp("scalar" if n % 2 else "vector", cw[:, tb, ci, :], p[:, :], [kp], [("cw", tb, ci)])
                        n += 1
                cwk = [("cw", tb, ci) for tb in range(2) for ci in range(4)]
                for i in range(NT):
                    t0 = i * TS
                    u = uT[i % 2]
                    ku = "uT%d" % (i % 2)
                    dma("sync", u[:], DA(fm_d, 1024 * TW + 8 + t0, [[TW, 128], [128 * TW, 4], [1, TS]]), [("fm", 2)], [ku])
                    for cl in range(3):
                        c = 3 * i + cl
                        for tb, dst in ((0, Ap), (1, Bp)):
                            p = acc[tb][cl]
                            kp = ("acc", tb, cl)
                            for ck in range(4):
                                mm(p[:, :], u[:, ck, cl * 128:(cl + 1) * 128], cw[:, tb, ck, :], ck == 0, ck == 3,
                                   [ku] + cwk, [kp])
                            cp("scalar" if tb else "vector", dst[:, c, :], p[:, :], [kp], [("ApBp", c)])
                apk = [("ApBp", c) for c in range(NCH)]
                npiece = 0
                for kt in range(NT):
                    t0 = kt * TS
                    a4 = acc[kt % 2]
                    s_ = sg[kt % 2]
                    ksg = "sgf%d" % (kt % 2)
                    o_ = ost[kt % 2]
                    ko = "ostf%d" % (kt % 2)
                    dma("sync", s_[:], DA(fm_d, 1536 * TW + 8 + t0, [[TW, 128], [128 * TW, 4], [1, TS]]), [("fm", 3)], [ksg])
                    for pc in range(NPC):
                        for tb in range(2):
                            rb = ring[npiece % 4]
                            kr = "ring%d" % (npiece % 4)
                            npiece += 1
                            off = (((tb * NT + kt) * NPC + pc) * 128) * (PCC * TS)
                            dma("sync", rb[:], DA(c_tab_d, off, [[PCC * TS, 128], [TS, PCC], [1, TS]]), (), [kr])
                            src = Bp if tb else Ap
                            for tci in range(PCC):
                                tc_ = pc * PCC + tci
                                first = (pc == 0 and tb == 0 and tci == 0)
                                last = (pc == NPC - 1 and tb == 1 and tci == PCC - 1)
                                for dc in range(4):
                                    mm(a4[dc][:, 0:TS], src[:, tc_, dc * 128:(dc + 1) * 128], rb[:, tci, :], first, last,
                                       apk + [kr], [("acc", kt % 2, dc)])
                    for dc in range(4):
                        tt("vector", o_[:, dc, :], a4[dc][:, 0:TS], s_[:, dc, :], ALU.mult,
                           [("acc", kt % 2, dc), ksg], [(ko, dc)])
                    dma("sync", DA(mixT_d, 512 * TP + t0, [[TP, 128], [128 * TP, 4], [1, TS]]), o_[:],
                        [(ko, dc) for dc in range(4)], [("mixT", 1)])
            S.barrier()

        def phase_p3(l):
            with contextlib.ExitStack() as st:
                BT = alloc(st, "BT", [128, 2, TP], BF16)
                CT = alloc(st, "CT", [128, 2, TP], BF16)
                xtm = alloc(st, "xtm", [128, NCH, 512], BF16)
                btm = alloc(st, "btm", [128, NCH, 256], BF16)
                dtr = alloc(st, "dtr", [128, NCH, 16], F32)
                dtt = alloc(st, "dtt", [128, NCH, 16], F32)
                da = alloc(st, "da", [128, NCH, 16], F32)
                tA = alloc(st, "tA", [128, NCH, 16], F32)
                tB = alloc(st, "tB", [128, NCH, 16], F32)
                dtb = alloc(st, "dtb", [128, 16], F32)
                alg = alloc(st, "alg", [128, 16], F32)
                dsk = alloc(st, "dsk", [128, 8], F32)
                snw = alloc(st, "snw", [128, 512], F32)
                cwt = alloc(st, "cwt", [128, 8, 4], F32)
                cbt = alloc(st, "cbt", [128, 8], F32)
                dg = alloc(st, "dg", [128, 8, 4, 128], BF16)
                cf = alloc(st, "cf", [128, 3, 128], F32)
                cb = alloc(st, "cb", [128, 6, 128], BF16)
                xin = [alloc(st, "xin", [128, 8, TS + 3], BF16) for _ in range(2)]
                xTt = [alloc(st, "xTt", [128, 4, TS], BF16) for _ in range(2)]
                H = alloc(st, "H", [128, 512], F32)
                Hb = alloc(st, "Hb", [128, 512], BF16)
                Ht = alloc(st, "Ht", [128, 512], F32)
                cst = alloc(st, "cst", [128, 24], F32)
                E = alloc(st, "E", [128, 24], F32)
                R = alloc(st, "R", [128, 8, 128], BF16)
                cbm = alloc(st, "cbm", [128, 2, 128], BF16)
                eseg = alloc(st, "eseg", [128, 8, 128], BF16)
                MT = alloc(st, "MT", [128, 8, 128], BF16)
                xdt = alloc(st, "xdt", [128, 512], BF16)
                xw = alloc(st, "xw", [128, 512], BF16)
                ytmp = alloc(st, "ytmp", [128, 512], F32)
                yc = [alloc(st, "yc", [128, 512], F32) for _ in range(2)]
                yfl = [alloc(st, "yfl", [128, 512], F32) for _ in range(2)]
                szl = [alloc(st, "szl", [128, 512], BF16) for _ in range(2)]
                yg = alloc(st, "yg", [128, 512], F32)
                yjunk = alloc(st, "yjunk", [128, 512], BF16)
                ynb = alloc(st, "ynb", [128, 512], BF16)
                ystat = alloc(st, "ystat", [128, 4], F32)
                yst = [alloc(st, "yst", [128, 4, TS], BF16) for _ in range(2)]
                pcv = [palloc(st, "pcv", [128, 512], F32) for _ in range(2)]
                ptr = palloc(st, "ptr", [128, 1024], BF16)
                pcb = pcv[0]
                pcs = pcv[1]
                pseg = [palloc(st, "pseg", [128, 512], F32) for _ in range(2)]
                pyd = palloc(st, "pyd", [128, 512], F32)
                pyo = palloc(st, "pyo", [128, 512], F32)
                psn = palloc(st, "psn", [128, 512], F32)

                dma("sync", cf[:], c_ssdf_d.ap(), (), ["cf"])
                dma("sync", cb[:], c_ssdb_d.ap(), (), ["cb"])
                dma("sync", dtb[:], DA(dtb_d, l * 16, [[0, 128], [1, 16]]), (), ["dtb"])
                dma("sync", alg[:], DA(alog_d, l * 16, [[0, 128], [1, 16]]), (), ["alg"])
                dma("sync", dsk[:], DA(dskip_d, l * 8, [[0, 128], [1, 8]]), (), ["dsk"])
                dma("sync", snw[:], DA(ssdnw_d, l * 512, [[0, 128], [1, 512]]), (), ["snw"])
                for j in range(4):
                    dma("sync", cwt[:, :, j:j + 1], DA(convw_d, (l * 4 + j) * 1024, [[1, 128], [128, 8], [1, 1]]), (), ["cwt"], slow=True)
                dma("sync", cbt[:].unsqueeze(2), DA(convb_d, l * 1024, [[1, 128], [128, 8], [1, 1]]), (), ["cbt"], slow=True)
                for ch in range(8):
                    for j in range(4):
                        tsc("gpsimd", dg[:, ch, j, :], ident[:], cwt[:, ch, j:j + 1], None, ALU.mult, None,
                            ["ident", "cwt"], [("dg", ch)])
                for c3 in range(3):
                    dma("sync", dtr[:, c3 * 11:(c3 + 1) * 11, :],
                        DA(dtr_d, c3 * 11 * 128 * 16, [[16, 128], [128 * 16, 11], [1, 16]]), ["dtr"], ["dtrl"])
                bcd = dtb[:].unsqueeze(1).broadcast_to([128, NCH, 16])
                tt("vector", tA[:], dtr[:], bcd, ALU.add, ["dtrl", "dtb"], ["tA"])
                act(tB[:], tA[:], AF.Abs, ["tA"], ["tB"])
                act(tB[:], tB[:], AF.Exp, ["tB"], ["tB"], scale=-1.0)
                tsc("vector", tB[:], tB[:], 1.0, None, ALU.add, None, ["tB"], ["tB"])
                act(tB[:], tB[:], AF.Ln, ["tB"], ["tB"])
                tsc("vector", tA[:], tA[:], 0.0, None, ALU.max, None, ["tA"], ["tA"])
                tt("vector", dtt[:], tA[:], tB[:], ALU.add, ["tA", "tB"], ["dtt"])
                act(alg[:], alg[:], AF.Exp, ["alg"], ["alg"])
                tsc("vector", alg[:], alg[:], -1.0, None, ALU.mult, None, ["alg"], ["alg"])
                tt("vector", da[:], dtt[:], alg[:].unsqueeze(1).broadcast_to([128, NCH, 16]), ALU.mult, ["dtt", "alg"], ["da"])

                for i in range(NT):
                    t0 = i * TS
                    xi = xin[i % 2]
                    kx = "xin%d" % (i % 2)
                    xt_ = xTt[i % 2]
                    kxt = "xTt%d" % (i % 2)
                    dma("sync", xi[:], DA(fm_d, 2048 * TW + 8 + t0 - 2, [[TW, 128], [128 * TW, 8], [1, TS + 3]]),
                        [("fm", 4), ("fm", 5), "fm_margin"], [kx])
                    for ch in range(8):
                        p = pcv[ch % 2]
                        kp = "pcv%d" % (ch % 2)
                        for j in range(4):
                            mm(p[:, 0:TS], dg[:, ch, j, :], xi[:, ch, j:j + TS], j == 0, j == 3, [("dg", ch), kx], [kp])
                        if ch < 4:
                            dst, kdst = xt_[:, ch, :], (kxt, ch)
                        elif ch < 6:
                            dst, kdst = BT[:, ch - 4, t0:t0 + TS], ("BT", i)
                        else:
                            dst, kdst = CT[:, ch - 6, t0:t0 + TS], ("CT", i)
                        act(dst, p[:, 0:TS], AF.Silu, [kp, "cbt"], [kdst], bias=cbt[:, ch:ch + 1])
                    if i == 0:
                        memset("gpsimd", xt_[:, :, 0:NPAD], 0.0, [(kxt, ch) for ch in range(4)])
                    for cl in range(3):
                        c = 3 * i + cl
                        lt = slice(cl * 128, (cl + 1) * 128)
                        for ch in range(4):
                            identr(ptr[:, ch * 128:(ch + 1) * 128], xt_[:, ch, lt], [(kxt, q) for q in range(4)], ["ptr"])
                        for g in range(2):
                            identr(ptr[:, 512 + g * 128:512 + (g + 1) * 128], BT[:, g, t0 + cl * 128:t0 + (cl + 1) * 128],
                                   [("BT", i)], ["ptr"])
                        cp("vector", xtm[:, c, :], ptr[:, 0:512], ["ptr"], [("xtm", c)])
                        cp("scalar", btm[:, c, :], ptr[:, 512:768], ["ptr"], [("btm", c)])

                for d_ in range(2):
                    memset("gpsimd", H[:], 0.0, ["H"])
                    memset("gpsimd", Hb[:], 0.0, ["Hb"])
                    order = range(NCH) if d_ == 0 else range(NCH - 1, -1, -1)
                    for n_, c in enumerate(order):
                        i = c // 3
                        cols = slice(c * 128, (c + 1) * 128)
                        dac = da[:, c, d_ * 8:(d_ + 1) * 8]
                        for g in range(2):
                            mm(pcb[:, g * 128:(g + 1) * 128], BT[:, g, cols], CT[:, g, cols], True, True,
                               [("BT", i), ("CT", i)], ["pcv0"])
                        tt("vector", cbm[:], pcb[:, 0:256].rearrange("p (g s) -> p g s", g=2),
                           cb[:, 4 + d_, :].unsqueeze(1).broadcast_to([128, 2, 128]), ALU.mult, ["pcv0", "cb"], ["cbm"])
                        mm(pcs[:, 0:8], cf[:, d_, :], dac, True, True, ["cf", "da"], ["pcv1"])
                        mm(pcs[:, 8:16], cf[:, 2, :], dac, True, True, ["cf", "da"], ["pcv1"])
                        cp("vector", cst[:, 0:16], pcs[:, 0:16], ["pcv1"], [("cst", 0)])
                        tt("vector", cst[:, 16:24], cst[:, 8:16], cst[:, 0:8], ALU.subtract, [("cst", 0)], [("cst", 1)])
                        act(E[:], cst[:], AF.Exp, [("cst", 0), ("cst", 1)], ["E"])
                        tt("gpsimd", R[:], cb[:, d_, :].unsqueeze(1).broadcast_to([128, 8, 128]),
                           dac.unsqueeze(2).broadcast_to([128, 8, 128]), ALU.mult, ["cb", "da"], ["R"])
                        for hh in range(2):
                            mm(pseg[hh][:, :], cb[:, 2 + d_, :], R[:, hh * 4:(hh + 1) * 4, :], True, True, ["cb", "R"],
                               [("pseg", hh)])
                            act(eseg[:, hh * 4:(hh + 1) * 4, :], pseg[hh][:, :].rearrange("p (h l) -> p h l", h=4), AF.Exp,
                                [("pseg", hh)], [("eseg", hh)])
                            tt("vector" if hh else "gpsimd", MT[:, hh * 4:(hh + 1) * 4, :], eseg[:, hh * 4:(hh + 1) * 4, :],
                               cbm[:, hh, :].unsqueeze(1).broadcast_to([128, 4, 128]), ALU.mult,
                               [("eseg", hh), "cbm"], [("MT", hh)])
                        tt("gpsimd", xdt[:].rearrange("p (h d) -> p h d", h=8), xtm[:, c, :].rearrange("p (h d) -> p h d", h=8),
                           dtt[:, c, d_ * 8:(d_ + 1) * 8].unsqueeze(2).broadcast_to([128, 8, 64]), ALU.mult,
                           [("xtm", c), "dtt"], ["xdt"])
                        tt("gpsimd", xw[:].rearrange("p (h d) -> p h d", h=8), xdt[:].rearrange("p (h d) -> p h d", h=8),
                           E[:, 16:24].unsqueeze(2).broadcast_to([128, 8, 64]), ALU.mult, ["xdt", "E"], ["xw"])
                        for h in range(8):
                            mm(pyd[:, h * 64:(h + 1) * 64], MT[:, h, :], xdt[:, h * 64:(h + 1) * 64], True, True,
                               [("MT", h // 4), "xdt"], ["pyd"])
                        for g in range(2):
                            mm(pyo[:, g * 256:(g + 1) * 256], CT[:, g, cols], Hb[:, g * 256:(g + 1) * 256], True, True,
                               [("CT", i), "Hb"], ["pyo"])
                        tt("vector", ytmp[:].rearrange("p (h d) -> p h d", h=8), pyo[:, :].rearrange("p (h d) -> p h d", h=8),
                           E[:, 0:8].unsqueeze(2).broadcast_to([128, 8, 64]), ALU.mult, ["pyo", "E"], ["ytmp"])
                        ycb = yc[n_ % 2]
                        kyc = "yc%d" % (n_ % 2)
                        tt("vector", ycb[:], pyd[:, :], ytmp[:], ALU.add, ["pyd", "ytmp"], [kyc])
                        for g in range(2):
                            mm(psn[:, g * 256:(g + 1) * 256], btm[:, c, g * 128:(g + 1) * 128], xw[:, g * 256:(g + 1) * 256],
                               True, True, [("btm", c), "xw"], ["psn"])
                        tt("gpsimd", Ht[:].rearrange("p (h d) -> p h d", h=8), H[:].rearrange("p (h d) -> p h d", h=8),
                           E[:, 8:16].unsqueeze(2).broadcast_to([128, 8, 64]), ALU.mult, ["H", "E"], ["Ht"])
                        tt("vector", H[:], psn[:, :], Ht[:], ALU.add, ["psn", "Ht"], ["H"])
                        cp("scalar", Hb[:], H[:], ["H"], ["Hb"])
                        if d_ == 0:
                            dma("sync", yf_d.ap()[c * 128:(c + 1) * 128, :], ycb[:], [kyc], ["yf"])
                        else:
                            yfb = yfl[n_ % 2]
                            kyf = "yfl%d" % (n_ % 2)
                            szb = szl[n_ % 2]
                            ksz = "szl%d" % (n_ % 2)
                            dma("sync", yfb[:], yf_d.ap()[c * 128:(c + 1) * 128, :], ["yf"], [kyf])
                            dma("sync", szb[:], sz_d.ap()[c * 128:(c + 1) * 128, :], ["sz"], [ksz])
                            tt("gpsimd", ytmp[:].rearrange("p (h d) -> p h d", h=8),
                               xtm[:, c, :].rearrange("p (h d) -> p h d", h=8),
                               dsk[:].unsqueeze(2).broadcast_to([128, 8, 64]), ALU.mult, [("xtm", c), "dsk"], ["ytmp"])
                            tt("vector", yg[:], ycb[:], yfb[:], ALU.add, [kyc, kyf], ["yg"])
                            tt("vector", yg[:], yg[:], ytmp[:], ALU.add, ["yg", "ytmp"], ["yg"])
                            tt("vector", yg[:], yg[:], szb[:], ALU.mult, ["yg", ksz], ["yg"])
                            act(yjunk[:], yg[:], AF.Square, ["yg"], ["yjunk", ("ystat", 0)], accum_out=ystat[:, 0:1])
                            tsc("vector", ystat[:, 1:2], ystat[:, 0:1], 1.0 / 512, EPS, ALU.mult, ALU.add, [("ystat", 0)], [("ystat", 1)])
                            tt("gpsimd", ystat[:, 3:4], ystat[:, 1:2], mhalf[:, 0:1], ALU.pow, [("ystat", 1), "mhalf"], [("ystat", 3)])
                            stt(ynb[:], yg[:], ystat[:, 3:4], snw[:], ALU.mult, ALU.mult, ["yg", ("ystat", 3), "snw"], ["ynb"])
                            ti = c // 3
                            cl = c % 3
                            ys = yst[ti % 2]
                            kys = "yst%d" % (ti % 2)
                            for ch in range(4):
                                identr(ptr[:, ch * 128:(ch + 1) * 128], ynb[:, ch * 128:(ch + 1) * 128], ["ynb"], ["ptr"])
                            cp("scalar", ys[:, :, cl * 128:(cl + 1) * 128], ptr[:, 0:512].rearrange("p (j t) -> p j t", j=4),
                               ["ptr"], [(kys, cl)])
                            if cl == 0:
                                dma("sync", DA(mixT_d, 1024 * TP + ti * TS, [[TP, 128], [128 * TP, 4], [1, TS]]), ys[:],
                                    [(kys, q) for q in range(3)], [("mixT", 2)])
            S.barrier()

        def phase_p4(l):
            with contextlib.ExitStack() as st:
                KT2 = alloc(st, "KT2", [128, 2, TP], BF16)
                Vp = alloc(st, "Vp", [128, NCH, 2, 192], BF16)
                vmask = alloc(st, "vmask", [128, 64], BF16)
                QT = [alloc(st, "QT", [128, 4, TS], BF16) for _ in range(2)]
                sg = [alloc(st, "sga", [128, 4, TS], BF16) for _ in range(2)]
                P = [alloc(st, "P", [128, 2, TS], BF16) for _ in range(3)]
                rec = alloc(st, "rec", [128, TS], F32)
                yt = alloc(st, "yt", [128, TS], F32)
                ost = [alloc(st, "osta", [128, 4, TS], BF16) for _ in range(2)]
                Sp = [palloc(st, "Sp", [128, 2, 512], F32) for _ in range(2)]
                Oe = [palloc(st, "Oe", [128, 512], F32) for _ in range(2)]
                Oo = [palloc(st, "Oo", [128, 512], F32) for _ in range(2)]
                dma("sync", vmask[:], c_vmask_d.ap(), (), ["vmask"])
                for g in range(2):
                    for hs in range(2):
                        dma("sync", KT2[hs * 64:(hs + 1) * 64, g, :], DA(kT_d, g * 64 * TP, [[TP, 64], [1, TP]]), ["kT"], ["KT2"])
                memset("gpsimd", Vp[:], 1.0, ["Vp"])
                for c3 in range(3):
                    for g in range(2):
                        dma("sync", Vp[:, c3 * 11:(c3 + 1) * 11, g, 64:128],
                            DA(vtm_d, c3 * 11 * 128 * 128 + g * 64, [[128, 128], [128 * 128, 11], [1, 64]]), ["vtm"], ["Vp"])
                for g in range(2):
                    for o0 in (0, 128):
                        cp("gpsimd", Vp[:, 0, g, o0:o0 + 64], vmask[:], ["vmask"], ["Vp"])
                npair = 0
                nS = [0]

                def p4_load(i):
                    dma("sync", QT[i % 2][:], DA(qT_d, i * TS, [[TP, 128], [128 * TP, 4], [1, TS]]), ["qT"], ["QT%d" % (i % 2)])
                    dma("sync", sg[i % 2][:], DA(fm_d, 3072 * TW + 8 + i * TS, [[TW, 128], [128 * TW, 4], [1, TS]]),
                        [("fm", 6)], ["sga%d" % (i % 2)])

                for i in range(NT):
                    t0 = i * TS
                    q_ = QT[i % 2]
                    kq = "QT%d" % (i % 2)
                    s_ = sg[i % 2]
                    ksg = "sga%d" % (i % 2)
                    o_ = ost[i % 2]
                    ko = "osta%d" % (i % 2)
                    if i == 0:
                        p4_load(0)
                    if i + 1 < NT:
                        p4_load(i + 1)
                    for j in range(4):
                        g = j // 2
                        oe = Oe[npair % 2]
                        oo = Oo[npair % 2]
                        koe = "Oe%d" % (npair % 2)
                        koo = "Oo%d" % (npair % 2)
                        npair += 1
                        pend = {}

                        def qk(kc, g=g, j=j, q_=q_, kq=kq, pend=pend):
                            n_ = nS[0]
                            nS[0] += 1
                            sp = Sp[n_ % 2]
                            ksp = "Sp%d" % (n_ % 2)
                            pb = P[n_ % 3]
                            kpb = "P%d" % (n_ % 3)
                            kcols = slice(kc * 128, (kc + 1) * 128)
                            mm(sp[:, 0, 0:TS], KT2[0:64, g, kcols], q_[0:64, j, :], True, True, ["KT2", kq], [ksp])
                            mm(sp[:, 1, 0:TS], KT2[64:128, g, kcols], q_[64:128, j, :], True, True, ["KT2", kq], [ksp])
                            act(pb[:, :, :], sp[:, :, 0:TS], AF.Exp, [ksp], [kpb])
                            pend[kc] = (pb, kpb)

                        def pvm(kc, g=g, oe=oe, oo=oo, koe=koe, koo=koo, pend=pend):
                            pb, kpb = pend.pop(kc)
                            mm(oe[:, 0:TS], Vp[:, kc, g, 64:192], pb[:, 0, :], kc == 0, kc == NCH - 1, ["Vp", kpb], [koe])
                            mm(oo[:, 0:TS], Vp[:, kc, g, 0:128], pb[:, 1, :], kc == 0, kc == NCH - 1, ["Vp", kpb], [koo])

                        qk(0)
                        for kc in range(NCH):
                            if kc + 1 < NCH:
                                qk(kc + 1)
                            pvm(kc)
                        recip(rec[0:64, :], oe[64:128, 0:TS], [koe], [("rec", 0)])
                        recip(rec[64:128, :], oo[0:64, 0:TS], [koo], [("rec", 1)])
                        tt("vector", yt[0:64, :], oe[0:64, 0:TS], rec[0:64, :], ALU.mult, [koe, ("rec", 0)], [("yt", 0)])
                        tt("vector", yt[64:128, :], oo[64:128, 0:TS], rec[64:128, :], ALU.mult, [koo, ("rec", 1)], [("yt", 1)])
                        tt("gpsimd", o_[:, j, :], yt[:], s_[:, j, :], ALU.mult, [("yt", 0), ("yt", 1), ksg], [(ko, j)])
                    dma("sync", DA(mixT_d, 1536 * TP + t0, [[TP, 128], [128 * TP, 4], [1, TS]]), o_[:],
                        [(ko, j) for j in range(4)], [("mixT", 3)])
            S.barrier()

        def phase_p5(l, last):
            with contextlib.ExitStack() as st:
                wo = alloc(st, "wo", [128, 16, D], BF16)
                mx = [alloc(st, "mx", [128, 16, TS], BF16) for _ in range(2)]
                hc = [alloc(st, "hc5", [128, D], F32) for _ in range(4)]
                ho = [alloc(st, "ho5", [128, D], F32) for _ in range(2)]
                po = [[palloc(st, "po", [128, 512], F32) for _ in range(2)] for _ in range(2)]
                for hh in range(2):
                    dma("gpsimd", wo[:, :, hh * 512:(hh + 1) * 512],
                        DA(wout_d, l * 2048 * D + hh * 512, [[D, 128], [128 * D, 16], [1, 512]]), (), [("wo", hh)])
                def p5_load(i):
                    for q4 in range(4):
                        dma("sync", mx[i % 2][:, q4 * 4:(q4 + 1) * 4, :],
                            DA(mixT_d, q4 * 512 * TP + i * TS, [[TP, 128], [128 * TP, 4], [1, TS]]), [("mixT", q4)], ["mx%d" % (i % 2)])

                def p5_hload(c):
                    if c >= NCH or (last and c == 0):
                        return
                    hcb = hc[c % 4]
                    khc = "hc5%d" % (c % 4)
                    if l == 0 and c == 0:
                        memset("gpsimd", hcb[:], 0.0, [khc])
                        dma("sync", hcb[NPAD:128, :], meta_d.ap(), (), [khc])
                    elif l == 0:
                        dma("sync", hcb[:], x_d.ap()[(c - 1) * 128:c * 128, :], (), [khc])
                    else:
                        dma("sync", hcb[:], hbuf_d.ap()[c * 128:(c + 1) * 128, :], ["hbuf", "hbuf_pad"], [khc])

                p5_load(0)
                p5_hload(0)
                p5_hload(1)
                for i in range(NT):
                    t0 = i * TS
                    m_ = mx[i % 2]
                    km = "mx%d" % (i % 2)
                    if i + 1 < NT:
                        p5_load(i + 1)
                    for cl in range(3):
                        c = 3 * i + cl
                        p5_hload(c + 2)
                        if last and c == 0:
                            continue
                        hcb = hc[c % 4]
                        khc = "hc5%d" % (c % 4)
                        hob = ho[c % 2]
                        kho = "ho5%d" % (c % 2)
                        for hh in range(2):
                            p = po[c % 2][hh]
                            kp = ("po", c % 2, hh)
                            for ec in range(16):
                                mm(p[:, :], m_[:, ec, cl * 128:(cl + 1) * 128], wo[:, ec, hh * 512:(hh + 1) * 512],
                                   ec == 0, ec == 15, [km, ("wo", hh)], [kp])
                            tt("vector", hob[:, hh * 512:(hh + 1) * 512], p[:, :], hcb[:, hh * 512:(hh + 1) * 512], ALU.add,
                               [kp, khc], [(kho, hh)])
                        if last:
                            o = dma("sync", out_d.ap()[(c - 1) * 128:c * 128, :], hob[:], [(kho, 0), (kho, 1)], ["out"])
                            final_ops.append(o)
                        elif c == 0:
                            dma("sync", hbuf_d.ap()[NPAD:128, :], hob[NPAD:128, :], [(kho, 0), (kho, 1)], ["hbuf"])
                        else:
                            dma("sync", hbuf_d.ap()[c * 128:(c + 1) * 128, :], hob[:], [(kho, 0), (kho, 1)], ["hbuf"])
            S.barrier()

        for l in range(n_layers):
            if "0" in phases:
                phase_p0(l)
            if "1" in phases:
                phase_p1(l)
            if "2" in phases:
                phase_p2(l)
            if "3" in phases:
                phase_p3(l)
            if "4" in phases:
                phase_p4(l)
            if "5" in phases:
                phase_p5(l, l == n_layers - 1)
        final_ops[:] = [o for o in final_ops if not o.skipped]
        S.limit = None
        print("n_ops", len(S.ops))
        if not final_ops:
            zo = dma("sync", out_d.ap()[0:128, :], zf[:], ["zf"], ["out"])
            final_ops.append(zo)
        S.emit(nc, final_ops)
    return nc


_PARAM_NAMES = ("meta_tokens", "norm_w", "w_in", "w_out", "pool_w", "pool_scale", "fourier_w", "conv_w", "conv_b",
                "dt_bias", "a_log", "d_skip", "ssd_norm_w", "q_norm_w", "k_norm_w")


def make_in_maps(inputs, cores):
    consts = host_consts()
    shared = {k: np.ascontiguousarray(np.asarray(inputs[k], dtype=np.float32)) for k in _PARAM_NAMES}
    shared.update(consts)
    x = np.asarray(inputs["x"], dtype=np.float32)
    maps = []
    for b in cores:
        m = dict(shared)
        m["x"] = np.ascontiguousarray(x[b])
        maps.append(m)
    return maps


def kernel(**inputs):
    nc = build()
    in_maps = make_in_maps(inputs, list(range(8)))
    res = run_bass_kernel_spmd(nc, in_maps, core_ids=list(range(8)))
    return np.stack([np.asarray(r["out"], dtype=np.float32) for r in res.results], axis=0)
```

```python
import contextlib
import numpy as np
import ml_dtypes
import concourse.bass as bass
import concourse.mybir as mybir
from concourse.bass_utils import run_bass_kernel_spmd

F32 = mybir.dt.float32
BF16 = mybir.dt.bfloat16
AF = mybir.ActivationFunctionType
ALU = mybir.AluOpType
AX = mybir.AxisListType

D = 1024
NTOK = 4096
NMETA = 16
L = NTOK + NMETA
NPAD = 112
TP = 4224
NCH = 33
NT = 11
TS = 384
TW = TP + 16
DEPTH = 4
INC = 4880
EPS = 1e-6
NPC = 3
PCC = 11

ENGS = ("tensor", "vector", "scalar", "gpsimd", "sync")
N_DMA_SEMS = 24
PSUM_KEYS = frozenset(["pT", "pf0", "pf1", "pz", "pq", "pkv", "ptq", "py", "acc", "pcv0", "pcv1", "ptr", "pseg", "pyd",
                       "pyo", "psn", "Sp0", "Sp1", "Oe0", "Oe1", "Oo0", "Oo1", "po"])


class Op:
    __slots__ = ("eng", "fn", "deps", "is_dma", "needed", "sem", "val", "nobar", "skipped")

    def __init__(self, eng, fn, deps, is_dma):
        self.eng = eng
        self.fn = fn
        self.deps = deps
        self.is_dma = is_dma
        self.needed = False
        self.sem = None
        self.val = 0
        self.nobar = False
        self.skipped = False


class Sched:
    def __init__(self):
        self.ops = []
        self.last_w = {}
        self.readers = {}
        self.last_eng = {}
        self.dmas = []
        self.limit = None

    def _add(self, eng, fn, reads, writes, is_dma, extra=()):
        ex = tuple(k for k in reads if (k[0] if isinstance(k, tuple) else k) in PSUM_KEYS)
        if ex:
            writes = tuple(writes) + tuple(k for k in ex if k not in writes)
        deps = list(extra)
        for k in reads:
            w = self.last_w.get(k)
            if w is not None:
                deps.append(w)
        for k in writes:
            w = self.last_w.get(k)
            if w is not None:
                deps.append(w)
            deps.extend(self.readers.get(k, ()))
        op = Op(eng, fn, deps, is_dma)
        if self.limit is not None and len(self.ops) >= self.limit:
            op.skipped = True
            return op
        self.ops.append(op)
        for k in writes:
            self.last_w[k] = op
            self.readers[k] = []
        for k in reads:
            if k in writes:
                continue
            lst = self.readers.setdefault(k, [])
            if not is_dma:
                lst[:] = [o for o in lst if o.is_dma or o.eng != eng]
            lst.append(op)
        if is_dma:
            self.dmas.append(op)
        else:
            self.last_eng[eng] = op
        return op

    def op(self, eng, fn, reads=(), writes=()):
        return self._add(eng, fn, tuple(reads), tuple(writes), False)

    def dma(self, eng, fn, reads=(), writes=(), nobar=False):
        o = self._add(eng, fn, tuple(reads), tuple(writes), True)
        o.nobar = nobar
        return o

    def barrier(self):
        deps = [o for o in self.last_eng.values()] + [o for o in self.dmas if not o.nobar]
        self.dmas = [o for o in self.dmas if o.nobar]
        for e in ENGS:
            self._add(e, lambda eng: eng.nop(), (), (), False, extra=deps)

    def emit(self, nc, final_wait_ops=()):
        ops = self.ops
        for o in ops:
            for d in o.deps:
                if d.eng == "tensor" and o.eng == "tensor" and not d.is_dma and not o.is_dma:
                    continue
                d.needed = True
        for o in final_wait_ops:
            o.needed = True
        with contextlib.ExitStack() as st:
            esem = {e: st.enter_context(nc.semaphore("s_" + e)) for e in ENGS}
            dsem = [st.enter_context(nc.semaphore("d_%d" % i)) for i in range(N_DMA_SEMS)]
            ecnt = {e: 0 for e in ENGS}
            dcnt = [0] * N_DMA_SEMS
            dnext = 0
            prev_on_dsem = [None] * N_DMA_SEMS
            for o in ops:
                if o.is_dma:
                    i = dnext
                    dnext = (dnext + 1) % N_DMA_SEMS
                    dcnt[i] += 16
                    o.sem = ("d", i)
                    o.val = dcnt[i]
                    if prev_on_dsem[i] is not None:
                        o.deps.append(prev_on_dsem[i])
                    prev_on_dsem[i] = o
                elif o.needed:
                    ecnt[o.eng] += 1
                    o.sem = ("e", o.eng)
                    o.val = ecnt[o.eng]
            per = {e: [o for o in ops if o.eng == e] for e in ENGS}
            final = list(final_wait_ops)
            blk = st.enter_context(nc.Block())

            def semh(s):
                return esem[s[1]] if s[0] == "e" else dsem[s[1]]

            def run(e, eng):
                seen = {}
                for o in per[e]:
                    need = {}
                    for d in o.deps:
                        if d.sem is None:
                            continue
                        if (not d.is_dma) and (not o.is_dma) and d.eng == "tensor" and e == "tensor":
                            continue
                        if need.get(d.sem, 0) < d.val:
                            need[d.sem] = d.val
                    for s, v in need.items():
                        if seen.get(s, 0) < v:
                            eng.wait_ge(semh(s), v)
                            seen[s] = v
                    ins = o.fn(eng)
                    if o.is_dma:
                        ins.then_inc(semh(o.sem), 16)
                    elif o.sem is not None:
                        ins.then_inc(semh(o.sem), 1)
                if e == "sync":
                    for o in final:
                        if seen.get(o.sem, 0) < o.val:
                            eng.wait_ge(semh(o.sem), o.val)
                            seen[o.sem] = o.val

            @blk.tensor
            def _(eng):
                run("tensor", eng)

            @blk.vector
            def _(eng):
                run("vector", eng)

            @blk.scalar
            def _(eng):
                run("scalar", eng)

            @blk.gpsimd
            def _(eng):
                run("gpsimd", eng)

            @blk.sync
            def _(eng):
                run("sync", eng)
        return ecnt, dcnt


_CONSTS = None


def host_consts():
    global _CONSTS
    if _CONSTS is not None:
        return _CONSTS
    bf = ml_dtypes.bfloat16
    c = {}
    c["c_ident"] = np.eye(128, dtype=np.float32).astype(bf)
    tphys = np.arange(TP)
    tlog = tphys - NPAD
    row = np.where(tlog >= NMETA, (tlog - NMETA) // 64, 0).astype(np.float64)
    col = np.where(tlog >= NMETA, (tlog - NMETA) % 64, 0).astype(np.float64)
    freqs = (10000.0 ** (-np.arange(0, 32, 2, dtype=np.float32) / 32.0)).astype(np.float32)
    ang = np.concatenate([row[:, None].astype(np.float32) * freqs[None], col[:, None].astype(np.float32) * freqs[None]],
                         axis=-1).astype(np.float32)
    cos2 = np.repeat(np.cos(ang), 2, axis=-1).astype(np.float32)
    sin2 = np.repeat(np.sin(ang), 2, axis=-1).astype(np.float32)
    rope = np.stack([cos2, sin2], 0).reshape(2, NCH, 128, 64).transpose(2, 0, 1, 3)
    c["c_rope"] = np.ascontiguousarray(rope, dtype=np.float32)
    k = np.arange(128)[:, None]
    l_ = np.arange(128)[None, :]
    trif = (k <= l_).astype(np.float32)
    trib = (k >= l_).astype(np.float32)
    ones = np.ones((128, 128), np.float32)
    c["c_ssd_f32"] = np.ascontiguousarray(np.stack([trif, trib, ones], 1), dtype=np.float32)
    t1f = (k > l_).astype(np.float32)
    t1b = (k < l_).astype(np.float32)
    c["c_ssd_bf"] = np.ascontiguousarray(np.stack([trif, trib, t1f, t1b, trif, trib], 1)).astype(bf)
    m = np.arange(512)
    a = 2.0 * np.pi * ((m[:, None] * m[None, :]) % 512) / 512.0
    cc = np.cos(a).reshape(4, 128, 512).transpose(1, 0, 2)
    sc = np.sin(a).reshape(4, 128, 512).transpose(1, 0, 2)
    c["c_dft_c"] = np.ascontiguousarray(np.stack([cc, sc], 1)).astype(bf)
    scale = 1.0 / np.sqrt(float(L) * 512.0)
    tl = (np.arange(TP) - NPAD).astype(np.int64)
    valid = (tl >= 0)
    prod = (np.maximum(tl, 0)[:, None] * np.maximum(tl, 0)[None, :]) % L
    angp = (2.0 * np.pi / L) * prod.astype(np.float64)
    vm = (valid[:, None] & valid[None, :])
    tabs = []
    for fn_, sgn in ((np.cos, 1.0), (np.sin, -1.0)):
        t_ = (fn_(angp) * (sgn * scale) * vm).astype(np.float32).astype(bf)
        t_ = t_.reshape(NPC, PCC, 128, NT, TS)
        t_ = t_.transpose(3, 0, 2, 1, 4)
        tabs.append(np.ascontiguousarray(t_).reshape(NT, NPC, 128, PCC * TS))
    c["c_tab"] = np.ascontiguousarray(np.stack(tabs, 0))
    pinv = np.zeros((128, 4, 2, 8), np.float32)
    for g, w in enumerate((2, 4, 8, 16)):
        for side in range(2):
            for q in range(8):
                t = q if side == 0 else L - 8 + q
                lo = min(max(t - w // 2, 0), L)
                hi = min(max(t - w // 2 + w, 0), L)
                pinv[:, g, side, q] = 1.0 / float(hi - lo)
    c["c_pool"] = pinv
    vmask = np.ones((128, 64), np.float32)
    vmask[:NPAD] = 0.0
    c["c_vmask"] = vmask.astype(bf)
    _CONSTS = c
    return c


def build(n_layers=DEPTH, dbg=False, phases="012345", limit=None):
    nc = bass.Bass("TRN2", target_bir_lowering=False)
    S = Sched()
    S.limit = limit
    uid = [0]

    def din(name, shape, dt):
        return nc.dram_tensor(name, list(shape), dt, kind="ExternalInput")

    x_d = din("x", [NTOK, D], F32)
    meta_d = din("meta_tokens", [NMETA, D], F32)
    normw_d = din("norm_w", [DEPTH, D], F32)
    win_d = din("w_in", [DEPTH, D, INC], F32)
    wout_d = din("w_out", [DEPTH, 2048, D], F32)
    poolw_d = din("pool_w", [DEPTH, 4, 128, 128], F32)
    pools_d = din("pool_scale", [DEPTH, 512], F32)
    fw_d = din("fourier_w", [DEPTH, 512, 512], F32)
    convw_d = din("conv_w", [DEPTH, 4, 1024], F32)
    convb_d = din("conv_b", [DEPTH, 1024], F32)
    dtb_d = din("dt_bias", [DEPTH, 2, 8], F32)
    alog_d = din("a_log", [DEPTH, 2, 8], F32)
    dskip_d = din("d_skip", [DEPTH, 8], F32)
    ssdnw_d = din("ssd_norm_w", [DEPTH, 512], F32)
    qnw_d = din("q_norm_w", [DEPTH, 64], F32)
    knw_d = din("k_norm_w", [DEPTH, 64], F32)
    c_ident_d = din("c_ident", [128, 128], BF16)
    c_rope_d = din("c_rope", [128, 2, NCH, 64], F32)
    c_ssdf_d = din("c_ssd_f32", [128, 3, 128], F32)
    c_ssdb_d = din("c_ssd_bf", [128, 6, 128], BF16)
    c_dftc_d = din("c_dft_c", [128, 2, 4, 512], BF16)
    c_tab_d = din("c_tab", [2, NT, NPC, 128, PCC * TS], BF16)
    c_pool_d = din("c_pool", [128, 4, 2, 8], F32)
    c_vmask_d = din("c_vmask", [128, 64], BF16)
    out_d = nc.dram_tensor("out", [NTOK, D], F32, kind="ExternalOutput")
    skind = "ExternalOutput" if dbg else "Internal"

    def dscr(name, shape, dt):
        return nc.dram_tensor(name, list(shape), dt, kind=skind)

    hbuf_d = dscr("hbuf", [TP, D], F32)
    fm_d = dscr("fm", [3584, TW], BF16)
    qT_d = dscr("qT", [512, TP], BF16)
    kT_d = dscr("kT", [128, TP], BF16)
    vtm_d = dscr("vtm", [TP, 128], BF16)
    sz_d = dscr("sz", [TP, 512], BF16)
    dtr_d = dscr("dtr", [TP, 16], F32)
    yf_d = dscr("yf", [TP, 512], F32)
    mixT_d = dscr("mixT", [2048, TP], BF16)

    def DA(h, off, dims):
        return bass.AP(h, off, [list(d) for d in dims])

    def mm(out, lhsT, rhs, start, stop, r, w):
        return S.op("tensor", lambda e: e.matmul(out, lhsT=lhsT, rhs=rhs, start=start, stop=stop), r, w)

    def act(out, in_, func, r, w, bias=None, scale=None, accum_out=None, eng="scalar"):
        kw = {}
        if bias is not None:
            kw["bias"] = bias
        if scale is not None:
            kw["scale"] = scale
        if accum_out is not None:
            kw["accum_out"] = accum_out
        return S.op("scalar", lambda e: e.activation(out=out, in_=in_, func=func, **kw), r, w)

    def tt(eng, out, in0, in1, op, r, w):
        return S.op(eng, lambda e: e.tensor_tensor(out=out, in0=in0, in1=in1, op=op), r, w)

    def tsc(eng, out, in0, s1, s2, op0, op1, r, w):
        if op1 is None:
            return S.op(eng, lambda e: e.tensor_scalar(out=out, in0=in0, scalar1=s1, scalar2=None, op0=op0), r, w)
        return S.op(eng, lambda e: e.tensor_scalar(out=out, in0=in0, scalar1=s1, scalar2=s2, op0=op0, op1=op1), r, w)

    def stt(out, in0, scalar, in1, op0, op1, r, w):
        return S.op("vector", lambda e: e.scalar_tensor_tensor(out=out, in0=in0, scalar=scalar, in1=in1, op0=op0, op1=op1), r, w)

    def cp(eng, out, in_, r, w):
        if eng == "scalar":
            return S.op("scalar", lambda e: e.activation(out=out, in_=in_, func=AF.Copy), r, w)
        return S.op(eng, lambda e: e.tensor_copy(out=out, in_=in_), r, w)

    def recip(out, in_, r, w):
        return S.op("vector", lambda e: e.reciprocal(out=out, in_=in_), r, w)

    def memset(eng, ap, val, w):
        return S.op(eng, lambda e: e.memset(ap, val), (), w)

    def dma(eng, out, in_, r, w, nobar=False, slow=False):
        if slow:
            return S.dma(eng, lambda e: e.dma_start(out=out, in_=in_, allow_slow_non_contiguous=True), r, w, nobar=nobar)
        return S.dma(eng, lambda e: e.dma_start(out=out, in_=in_), r, w, nobar=nobar)

    final_ops = []

    with contextlib.ExitStack() as top:
        def alloc(st, name, shape, dt):
            uid[0] += 1
            return st.enter_context(nc.sbuf_tensor("%s_%d" % (name, uid[0]), list(shape), dt))

        def palloc(st, name, shape, dt):
            uid[0] += 1
            return st.enter_context(nc.psum_tensor("%s_%d" % (name, uid[0]), list(shape), dt))

        ident = alloc(top, "ident", [128, 128], BF16)
        zero = alloc(top, "zero", [128, 16], BF16)
        dma("sync", ident[:], c_ident_d.ap(), (), ["ident"])
        memset("gpsimd", zero[:], 0.0, ["zero"])
        for r0, nj in ((0, 4), (2048, 8)):
            for c0 in (0, 8 + TP):
                dma("sync", DA(fm_d, r0 * TW + c0, [[TW, 128], [128 * TW, nj], [1, 8]]),
                    zero[:, 0:8].unsqueeze(1).broadcast_to([128, nj, 8]), ["zero"], ["fm_margin"])
        mhalf = alloc(top, "mhalf", [128, 16], F32)
        memset("gpsimd", mhalf[:], -0.5, ["mhalf"])
        zf = alloc(top, "zf", [128, D], F32)
        memset("gpsimd", zf[:], 0.0, ["zf"])
        dma("sync", hbuf_d.ap()[0:NPAD, :], zf[0:NPAD, :], ["zf"], ["hbuf_pad"])
        S.barrier()

        def identr(o, i_, r, w):
            return S.op("tensor", lambda e: e.transpose(o, i_, ident[:]), list(r) + ["ident"], w)

        def phase_p0(l):
            with contextlib.ExitStack() as st:
                W = alloc(st, "W", [128, 8, INC], BF16)
                pieces = ((0, 1024, 0), (1024, 1024, 1024), (2048, 1024, 2048), (4368, 512, 3072),
                          (3072, 512, 3584), (3600, 512, 4096), (4112, 256, 4608), (3584, 16, 4864))
                for (s0, n, d0) in pieces:
                    dma("gpsimd", W[:, :, d0:d0 + n],
                        DA(win_d, l * D * INC + s0, [[INC, 128], [128 * INC, 8], [1, n]]), (), [("W", d0)])
                wkeys = [("W", p[2]) for p in pieces]
                hc = [alloc(st, "hc", [128, D], F32) for _ in range(3)]
                junk = alloc(st, "junk", [128, D], BF16)
                nwb = alloc(st, "nwb", [128, D], F32)
                hnb = [alloc(st, "hnb", [128, D], BF16) for _ in range(3)]
                hnT = [alloc(st, "hnT", [128, 8, TS], BF16) for _ in range(2)]
                stt_t = [alloc(st, "stat", [128, 48], F32) for _ in range(2)]
                fst = [alloc(st, "fst", [128, 4, TS], BF16) for _ in range(3)]
                zst = [alloc(st, "zst", [128, 3, 512], BF16) for _ in range(2)]
                qTst = [alloc(st, "qTst", [128, 4, TS], BF16) for _ in range(2)]
                kTst = [alloc(st, "kTst", [128, TS], BF16) for _ in range(2)]
                vst = [alloc(st, "vst", [128, 3, 128], BF16) for _ in range(2)]
                dtst = [alloc(st, "dtst", [128, 3, 16], F32) for _ in range(2)]
                sq = alloc(st, "sq", [128, 640], F32)
                qn = alloc(st, "qn", [128, 640], F32)
                qw = alloc(st, "qw", [128, 640], F32)
                m1 = alloc(st, "m1", [128, 640], F32)
                m2 = alloc(st, "m2", [128, 640], F32)
                qr = [alloc(st, "qr", [128, 640], BF16) for _ in range(2)]
                wqk = alloc(st, "wqk", [128, 640], F32)
                rope = alloc(st, "rope", [128, 2, NCH, 64], F32)
                pT = palloc(st, "pT", [128, 1024], BF16)
                pf = [palloc(st, "pf", [128, 512], F32) for _ in range(2)]
                pz = palloc(st, "pz", [128, 512], F32)
                pq = palloc(st, "pq", [128, 512], F32)
                pkv = palloc(st, "pkv", [128, 512], F32)
                ptq = palloc(st, "ptq", [128, 1024], BF16)

                dma("sync", rope[:], c_rope_d.ap(), (), ["rope"])
                dma("sync", nwb[:], DA(normw_d, l * D, [[0, 128], [1, D]]), (), ["nwb"])
                dma("sync", wqk[:, 0:512].rearrange("p (h d) -> p h d", h=8),
                    DA(qnw_d, l * 64, [[0, 128], [0, 8], [1, 64]]), (), ["wqk"])
                dma("sync", wqk[:, 512:640].rearrange("p (h d) -> p h d", h=2),
                    DA(knw_d, l * 64, [[0, 128], [0, 2], [1, 64]]), (), ["wqk"])
                tsc("gpsimd", wqk[:, 0:512], wqk[:, 0:512], 0.125, None, ALU.mult, None, ["wqk"], ["wqk"])

                nfm = [0]

                def norm_load(c):
                    hcb = hc[c % 3]
                    khc = "hc%d" % (c % 3)
                    if l == 0 and c == 0:
                        memset("gpsimd", hcb[:], 0.0, [khc])
                        dma("sync", hcb[NPAD:128, :], meta_d.ap(), (), [khc])
                    elif l == 0:
                        dma("sync", hcb[:], x_d.ap()[(c - 1) * 128:c * 128, :], (), [khc])
                    else:
                        dma("sync", hcb[:], hbuf_d.ap()[c * 128:(c + 1) * 128, :], ["hbuf", "hbuf_pad"], [khc])

                def norm_front(c):
                    hcb = hc[c % 3]
                    khc = "hc%d" % (c % 3)
                    sta = stt_t[c % 2]
                    ks = "stat%d" % (c % 2)
                    act(junk[:], hcb[:], AF.Square, [khc], ["junk", (ks, 0)], accum_out=sta[:, 0:1])
                    tsc("vector", sta[:, 1:2], sta[:, 0:1], 1.0 / D, EPS, ALU.mult, ALU.add, [(ks, 0)], [(ks, 1)])
                    tt("gpsimd", sta[:, 3:4], sta[:, 1:2], mhalf[:, 0:1], ALU.pow, [(ks, 1), "mhalf"], [(ks, 3)])
                    stt(hnb[c % 3][:], hcb[:], sta[:, 3:4], nwb[:], ALU.mult, ALU.mult, [khc, (ks, 3), "nwb"], ["hnb%d" % (c % 3)])

                def norm_back(c):
                    i_, cl = c // 3, c % 3
                    hT = hnT[i_ % 2]
                    kh = "hnT%d" % (i_ % 2)
                    hb = hnb[c % 3]
                    khb = "hnb%d" % (c % 3)
                    for j in range(8):
                        identr(pT[:, j * 128:(j + 1) * 128], hb[:, j * 128:(j + 1) * 128], [khb], ["pT"])
                    cp("scalar", hT[:, :, cl * 128:(cl + 1) * 128], pT[:].rearrange("p (j t) -> p j t", j=8),
                       ["pT"], [(kh, cl)])

                for c in range(3):
                    norm_load(c)
                for c in range(3):
                    norm_front(c)
                for c in range(3):
                    norm_back(c)
                for i in range(NT):
                    t0 = i * TS
                    hT = hnT[i % 2]
                    kh = "hnT%d" % (i % 2)
                    hkeys = [(kh, 0), (kh, 1), (kh, 2)]
                    if i + 1 < NT:
                        for cl in range(3):
                            norm_load(3 * (i + 1) + cl)
                    def fm_seg(seg, i=i, t0=t0, hT=hT, hkeys=hkeys):
                        fs = fst[seg % 3]
                        kfs = "fst%d" % (seg % 3)
                        gate = seg in (1, 3, 6)
                        for j in range(4):
                            p = pf[nfm[0] % 2]
                            kp = "pf%d" % (nfm[0] % 2)
                            col0 = seg * 512 + j * 128
                            for dk in range(8):
                                mm(p[:, 0:TS], W[:, dk, col0:col0 + 128], hT[:, dk, :], dk == 0, dk == 7,
                                   wkeys + hkeys, [kp])
                            if gate:
                                act(fs[:, j, :], p[:, 0:TS], AF.Silu, [kp], [kfs])
                            else:
                                cp("scalar", fs[:, j, :], p[:, 0:TS], [kp], [kfs])
                            nfm[0] += 1
                        dma("sync", DA(fm_d, seg * 512 * TW + 8 + t0, [[TW, 128], [128 * TW, 4], [1, TS]]), fs[:],
                            [kfs, "fm_margin"], [("fm", seg)])
                        if seg in (0, 2, 4) and i + 1 < NT:
                            norm_front(3 * (i + 1) + seg // 2)
                    zs = zst[i % 2]
                    kzs = "zst%d" % (i % 2)
                    qTs = qTst[i % 2]
                    kqs = "qTst%d" % (i % 2)
                    kTs = kTst[i % 2]
                    kks = "kTst%d" % (i % 2)
                    vs = vst[i % 2]
                    kvs = "vst%d" % (i % 2)
                    dts = dtst[i % 2]
                    kds = "dtst%d" % (i % 2)

                    def qtrans(cl, i=i, qTs=qTs, kqs=kqs, kTs=kTs, kks=kks):
                        c = 3 * i + cl
                        lt = slice(cl * 128, (cl + 1) * 128)
                        qrb = qr[c % 2]
                        kqr = "qr%d" % (c % 2)
                        for j in range(5):
                            identr(ptq[:, j * 128:(j + 1) * 128], qrb[:, j * 128:(j + 1) * 128],
                                   [(kqr, 0), (kqr, 1)], ["ptq"])
                        cp("scalar", qTs[:, :, lt], ptq[:, 0:512].rearrange("p (j t) -> p j t", j=4), ["ptq"], [kqs])
                        cp("vector", kTs[:, lt], ptq[:, 512:640], ["ptq"], [kks])

                    for cl in range(3):
                        c = 3 * i + cl
                        lt = slice(cl * 128, (cl + 1) * 128)
                        if cl >= 1:
                            fm_seg(2 * cl - 2)
                            fm_seg(2 * cl - 1)
                        for dk in range(8):
                            mm(pz[:, :], hT[:, dk, lt], W[:, dk, 3584:4096], dk == 0, dk == 7, wkeys + hkeys, ["pz"])
                        act(zs[:, cl, :], pz[:, :], AF.Silu, ["pz"], [kzs])
                        for dk in range(8):
                            mm(pq[:, :], hT[:, dk, lt], W[:, dk, 4096:4608], dk == 0, dk == 7, wkeys + hkeys, ["pq"])
                        for dk in range(8):
                            mm(pkv[:, 0:272], hT[:, dk, lt], W[:, dk, 4608:4880], dk == 0, dk == 7, wkeys + hkeys, ["pkv"])
                        if cl >= 1:
                            qtrans(cl - 1)
                        cp("vector", vs[:, cl, :], pkv[:, 128:256], ["pkv"], [kvs])
                        cp("vector", dts[:, cl, :], pkv[:, 256:272], ["pkv"], [kds])
                        sta = stt_t[c % 2]
                        ks = "stat%d" % (c % 2)
                        act(sq[:, 0:512], pq[:, :], AF.Square, ["pq"], [("sq", 0)])
                        act(sq[:, 512:640], pkv[:, 0:128], AF.Square, ["pkv"], [("sq", 1)])
                        S.op("vector", lambda e, o=sta[:, 8:18], a=sq[:].rearrange("p (h d) -> p h d", h=10):
                             e.tensor_reduce(out=o, in_=a, axis=AX.X, op=ALU.add), [("sq", 0), ("sq", 1)], [(ks, 8)])
                        tsc("vector", sta[:, 18:28], sta[:, 8:18], 1.0 / 64, EPS, ALU.mult, ALU.add, [(ks, 8)], [(ks, 18)])
                        tt("gpsimd", sta[:, 38:48], sta[:, 18:28], mhalf[:, 0:10], ALU.pow, [(ks, 18), "mhalf"], [(ks, 38)])
                        tt("vector", qn[:, 0:512].rearrange("p (h d) -> p h d", h=8),
                           pq[:, :].rearrange("p (h d) -> p h d", h=8),
                           sta[:, 38:46].unsqueeze(2).broadcast_to([128, 8, 64]), ALU.mult, ["pq", (ks, 38)], [("qn", 0)])
                        tt("vector", qn[:, 512:640].rearrange("p (h d) -> p h d", h=2),
                           pkv[:, 0:128].rearrange("p (h d) -> p h d", h=2),
                           sta[:, 46:48].unsqueeze(2).broadcast_to([128, 2, 64]), ALU.mult, ["pkv", (ks, 38)], [("qn", 1)])
                        tt("gpsimd", qw[:], qn[:], wqk[:], ALU.mult, [("qn", 0), ("qn", 1), "wqk"], ["qw"])
                        qw3 = qw[:].rearrange("p (h d) -> p h d", h=10)
                        tt("gpsimd", m1[:].rearrange("p (h d) -> p h d", h=10), qw3,
                           rope[:, 0, c, :].unsqueeze(1).broadcast_to([128, 10, 64]), ALU.mult, ["qw", "rope"], ["m1"])
                        tt("gpsimd", m2[:].rearrange("p (h d) -> p h d", h=10), qw3,
                           rope[:, 1, c, :].unsqueeze(1).broadcast_to([128, 10, 64]), ALU.mult, ["qw", "rope"], ["m2"])
                        qrb = qr[c % 2]
                        kqr = "qr%d" % (c % 2)

                        def pv(tile_, off):
                            return bass.AP(tile_, off, [[640, 128], [2, 320]])
                        tt("vector", pv(qrb, 0), pv(m1, 0), pv(m2, 1), ALU.subtract, ["m1", "m2"], [(kqr, 0)])
                        tt("gpsimd", pv(qrb, 1), pv(m2, 0), pv(m1, 1), ALU.add, ["m1", "m2"], [(kqr, 1)])
                    for seg in (4, 5, 6):
                        fm_seg(seg)
                    if i + 1 < NT:
                        for cl in range(3):
                            norm_back(3 * (i + 1) + cl)
                    qtrans(2)
                    dma("sync", DA(sz_d, t0 * 512, [[512, 128], [128 * 512, 3], [1, 512]]), zs[:], [kzs], ["sz"])
                    dma("sync", DA(qT_d, t0, [[TP, 128], [128 * TP, 4], [1, TS]]), qTs[:], [kqs], ["qT"])
                    dma("sync", DA(kT_d, t0, [[TP, 128], [1, TS]]), kTs[:], [kks], ["kT"])
                    dma("sync", DA(vtm_d, t0 * 128, [[128, 128], [128 * 128, 3], [1, 128]]), vs[:], [kvs], ["vtm"])
                    dma("sync", DA(dtr_d, t0 * 16, [[16, 128], [128 * 16, 3], [1, 16]]), dts[:], [kds], ["dtr"])
            S.barrier()

        def phase_p1(l):
            with contextlib.ExitStack() as st:
                pw = alloc(st, "pw", [128, 4, 128], BF16)
                psc = alloc(st, "psc", [128, 4], F32)
                cinv = alloc(st, "cinv", [128, 4, 2, 8], F32)
                ub = [alloc(st, "ub", [128, 4, 400], BF16) for _ in range(2)]
                sg = [alloc(st, "sg", [128, 4, TS], BF16) for _ in range(2)]
                T2 = alloc(st, "T2", [128, 4, 400], F32)
                T4 = alloc(st, "T4", [128, 4, 400], F32)
                T8 = alloc(st, "T8", [128, 4, 400], F32)
                T16 = alloc(st, "T16", [128, 4, 400], F32)
                tmp = alloc(st, "ptmp", [128, 4, 8], F32)
                dT = [alloc(st, "dT", [128, 4, TS], BF16) for _ in range(2)]
                ost = [alloc(st, "ost", [128, 4, TS], BF16) for _ in range(2)]
                py = [palloc(st, "py", [128, 512], F32) for _ in range(4)]
                dma("gpsimd", pw[:], DA(poolw_d, l * 4 * 128 * 128, [[128, 128], [128 * 128, 4], [1, 128]]), (), ["pw"])
                dma("sync", psc[:].unsqueeze(2), DA(pools_d, l * 512, [[1, 128], [128, 4], [1, 1]]), (), ["psc"], slow=True)
                dma("sync", cinv[:], c_pool_d.ap(), (), ["cinv"])
                Ts = (T2, T4, T8, T16)
                for i in range(NT):
                    t0 = i * TS
                    u = ub[i % 2]
                    ku = "ub%d" % (i % 2)
                    s_ = sg[i % 2]
                    ksg = "sg%d" % (i % 2)
                    d_ = dT[i % 2]
                    kd = "dT%d" % (i % 2)
                    o_ = ost[i % 2]
                    ko = "ost%d" % (i % 2)
                    dma("sync", u[:], DA(fm_d, t0, [[TW, 128], [128 * TW, 4], [1, 400]]), [("fm", 0), "fm_margin"], [ku])
                    dma("sync", s_[:], DA(fm_d, 512 * TW + 8 + t0, [[TW, 128], [128 * TW, 4], [1, TS]]), [("fm", 1)], [ksg])
                    tt("vector", T2[:, :, 0:399], u[:, :, 0:399], u[:, :, 1:400], ALU.add, [ku], ["T2"])
                    tt("gpsimd", T4[:, 1:4, 0:397], T2[:, 1:4, 0:397], T2[:, 1:4, 2:399], ALU.add, ["T2"], ["T4"])
                    tt("vector", T8[:, 2:4, 0:393], T4[:, 2:4, 0:393], T4[:, 2:4, 4:397], ALU.add, ["T4"], ["T8"])
                    tt("gpsimd", T16[:, 3, 0:385], T8[:, 3, 0:385], T8[:, 3, 8:393], ALU.add, ["T8"], ["T16"])
                    for g, w_ in enumerate((2, 4, 8, 16)):
                        half = w_ // 2
                        Pg = Ts[g]
                        kP = ("T2", "T4", "T8", "T16")[g]
                        o8 = 8 - half
                        stt(d_[:, g, :], Pg[:, g, o8:o8 + TS], 1.0 / w_, u[:, g, 8:8 + TS], ALU.mult, ALU.subtract,
                            [kP, ku], [(kd, g)])
                        for side, tile_i, m0 in ((0, 0, NPAD), (1, NT - 1, TS - 8)):
                            if i != tile_i:
                                continue
                            tt("vector", tmp[:, g, :], Pg[:, g, o8 + m0:o8 + m0 + 8], cinv[:, g, side, :], ALU.mult,
                               [kP, "cinv"], [("ptmp", g)])
                            tt("vector", d_[:, g, m0:m0 + 8], tmp[:, g, :], u[:, g, 8 + m0:8 + m0 + 8], ALU.subtract,
                               [("ptmp", g), ku], [(kd, g)])
                        mm(py[g][:, 0:TS], pw[:, g, :], d_[:, g, :], True, True, ["pw", (kd, g)], [("py", g)])
                        stt(o_[:, g, :], py[g][:, 0:TS], psc[:, g:g + 1], s_[:, g, :], ALU.mult, ALU.mult,
                            [("py", g), "psc", ksg], [(ko, g)])
                    dma("sync", DA(mixT_d, t0, [[TP, 128], [128 * TP, 4], [1, TS]]), o_[:],
                        [(ko, g) for g in range(4)], [("mixT", 0)])
            S.barrier()

        def phase_p2(l):
            with contextlib.ExitStack() as st:
                Ap = alloc(st, "Ap", [128, NCH, 512], BF16)
                Bp = alloc(st, "Bp", [128, NCH, 512], BF16)
                ring = [alloc(st, "ring", [128, PCC, TS], BF16) for _ in range(4)]
                cs = alloc(st, "dftc", [128, 2, 4, 512], BF16)
                wf = alloc(st, "wf", [128, 4, 512], BF16)
                cw = alloc(st, "cw", [128, 2, 4, 512], BF16)
                uT = [alloc(st, "uT", [128, 4, TS], BF16) for _ in range(2)]
                sg = [alloc(st, "sgf", [128, 4, TS], BF16) for _ in range(2)]
                ost = [alloc(st, "ostf", [128, 4, TS], BF16) for _ in range(2)]
                acc = [[palloc(st, "acc", [128, 512], F32) for _ in range(4)] for _ in range(2)]
                dma("sync", cs[:], c_dftc_d.ap(), (), ["dftc"])
                dma("gpsimd", wf[:], DA(fw_d, l * 512 * 512, [[512, 128], [128 * 512, 4], [1, 512]]), (), ["wf"])
                n = 0
                for tb in range(2):
                    for ci in range(4):
                        p = acc[n % 2][n // 2 % 4]
                        kp = ("acc", n % 2, n // 2 % 4)
                        for mk in range(4):
                            mm(p[:, :], cs[:, tb, mk, ci * 128:(ci + 1) * 128], wf[:, mk, :], mk == 0, mk == 3,
                               ["dftc", "wf"], [kp])
                        cp("scalar" if n % 2 else "vector", cw[:, tb, ci, :], p[:, :], [kp], [("cw", tb, ci)])
                        n += 1
                cwk = [("cw", tb, ci) for tb in range(2) for ci in range(4)]
                for i in range(NT):
                    t0 = i * TS
                    u = uT[i % 2]
                    ku = "uT%d" % (i % 2)
                    dma("sync", u[:], DA(fm_d, 1024 * TW + 8 + t0, [[TW, 128], [128 * TW, 4], [1, TS]]), [("fm", 2)], [ku])
                    for cl in range(3):
                        c = 3 * i + cl
                        for tb, dst in ((0, Ap), (1, Bp)):
                            p = acc[tb][cl]
                            kp = ("acc", tb, cl)
                            for ck in range(4):
                                mm(p[:, :], u[:, ck, cl * 128:(cl + 1) * 128], cw[:, tb, ck, :], ck == 0, ck == 3,
                                   [ku] + cwk, [kp])
                            cp("scalar" if tb else "vector", dst[:, c, :], p[:, :], [kp], [("ApBp", c)])
                apk = [("ApBp", c) for c in range(NCH)]
                npiece = 0
                for kt in range(NT):
                    t0 = kt * TS
                    a4 = acc[kt % 2]
                    s_ = sg[kt % 2]
                    ksg = "sgf%d" % (kt % 2)
                    o_ = ost[kt % 2]
                    ko = "ostf%d" % (kt % 2)
                    dma("sync", s_[:], DA(fm_d, 1536 * TW + 8 + t0, [[TW, 128], [128 * TW, 4], [1, TS]]), [("fm", 3)], [ksg])
                    for pc in range(NPC):
                        for tb in range(2):
                            rb = ring[npiece % 4]
                            kr = "ring%d" % (npiece % 4)
                            npiece += 1
                            off = (((tb * NT + kt) * NPC + pc) * 128) * (PCC * TS)
                            dma("sync", rb[:], DA(c_tab_d, off, [[PCC * TS, 128], [TS, PCC], [1, TS]]), (), [kr])
                            src = Bp if tb else Ap
                            for tci in range(PCC):
                                tc_ = pc * PCC + tci
                                first = (pc == 0 and tb == 0 and tci == 0)
                                last = (pc == NPC - 1 and tb == 1 and tci == PCC - 1)
                                for dc in range(4):
                                    mm(a4[dc][:, 0:TS], src[:, tc_, dc * 128:(dc + 1) * 128], rb[:, tci, :], first, last,
                                       apk + [kr], [("acc", kt % 2, dc)])
                    for dc in range(4):
                        tt("vector", o_[:, dc, :], a4[dc][:, 0:TS], s_[:, dc, :], ALU.mult,
                           [("acc", kt % 2, dc), ksg], [(ko, dc)])
                    dma("sync", DA(mixT_d, 512 * TP + t0, [[TP, 128], [128 * TP, 4], [1, TS]]), o_[:],
                        [(ko, dc) for dc in range(4)], [("mixT", 1)])
            S.barrier()

        def phase_p3(l):
            with contextlib.ExitStack() as st:
                BT = alloc(st, "BT", [128, 2, TP], BF16)
                CT = alloc(st, "CT", [128, 2, TP], BF16)
                xtm = alloc(st, "xtm", [128, NCH, 512], BF16)
                btm = alloc(st, "btm", [128, NCH, 256], BF16)
                dtr = alloc(st, "dtr", [128, NCH, 16], F32)
                dtt = alloc(st, "dtt", [128, NCH, 16], F32)
                da = alloc(st, "da", [128, NCH, 16], F32)
                tA = alloc(st, "tA", [128, NCH, 16], F32)
                tB = alloc(st, "tB", [128, NCH, 16], F32)
                dtb = alloc(st, "dtb", [128, 16], F32)
                alg = alloc(st, "alg", [128, 16], F32)
                dsk = alloc(st, "dsk", [128, 8], F32)
                snw = alloc(st, "snw", [128, 512], F32)
                cwt = alloc(st, "cwt", [128, 8, 4], F32)
                cbt = alloc(st, "cbt", [128, 8], F32)
                dg = alloc(st, "dg", [128, 8, 4, 128], BF16)
                cf = alloc(st, "cf", [128, 3, 128], F32)
                cb = alloc(st, "cb", [128, 6, 128], BF16)
                xin = [alloc(st, "xin", [128, 8, TS + 3], BF16) for _ in range(2)]
                xTt = [alloc(st, "xTt", [128, 4, TS], BF16) for _ in range(2)]
                H = alloc(st, "H", [128, 512], F32)
                Hb = alloc(st, "Hb", [128, 512], BF16)
                Ht = alloc(st, "Ht", [128, 512], F32)
                cst = alloc(st, "cst", [128, 24], F32)
                E = alloc(st, "E", [128, 24], F32)
                R = alloc(st, "R", [128, 8, 128], BF16)
                cbm = alloc(st, "cbm", [128, 2, 128], BF16)
                eseg = alloc(st, "eseg", [128, 8, 128], BF16)
                MT = alloc(st, "MT", [128, 8, 128], BF16)
                xdt = alloc(st, "xdt", [128, 512], BF16)
                xw = alloc(st, "xw", [128, 512], BF16)
                ytmp = alloc(st, "ytmp", [128, 512], F32)
                yc = [alloc(st, "yc", [128, 512], F32) for _ in range(2)]
                yfl = [alloc(st, "yfl", [128, 512], F32) for _ in range(2)]
                szl = [alloc(st, "szl", [128, 512], BF16) for _ in range(2)]
                yg = alloc(st, "yg", [128, 512], F32)
                yjunk = alloc(st, "yjunk", [128, 512], BF16)
                ynb = alloc(st, "ynb", [128, 512], BF16)
                ystat = alloc(st, "ystat", [128, 4], F32)
                yst = [alloc(st, "yst", [128, 4, TS], BF16) for _ in range(2)]
                pcv = [palloc(st, "pcv", [128, 512], F32) for _ in range(2)]
                ptr = palloc(st, "ptr", [128, 1024], BF16)
                pcb = pcv[0]
                pseg = [palloc(st, "pseg", [128, 512], F32) for _ in range(2)]
                pyd = palloc(st, "pyd", [128, 512], F32)
                pyo = palloc(st, "pyo", [128, 512], F32)
                psn = palloc(st, "psn", [128, 512], F32)
                pyd2 = [pcv[1], pyd]
                kpyd = ["pcv1", "pyd"]
                cbm2 = [alloc(st, "cbm2", [128, 2, 128], BF16) for _ in range(2)]
                cst2 = [alloc(st, "cst2", [128, 24], F32) for _ in range(2)]
                E2 = [alloc(st, "E2", [128, 24], F32) for _ in range(2)]
                MT2 = [alloc(st, "MT2", [128, 8, 128], BF16) for _ in range(2)]
                xdt2 = [alloc(st, "xdt2", [128, 512], BF16) for _ in range(2)]
                xw2 = [alloc(st, "xw2", [128, 512], BF16) for _ in range(2)]
                ytmp2 = alloc(st, "ytmp2", [128, 512], F32)

                dma("sync", cf[:], c_ssdf_d.ap(), (), ["cf"])
                dma("sync", cb[:], c_ssdb_d.ap(), (), ["cb"])
                dma("sync", dtb[:], DA(dtb_d, l * 16, [[0, 128], [1, 16]]), (), ["dtb"])
                dma("sync", alg[:], DA(alog_d, l * 16, [[0, 128], [1, 16]]), (), ["alg"])
                dma("sync", dsk[:], DA(dskip_d, l * 8, [[0, 128], [1, 8]]), (), ["dsk"])
                dma("sync", snw[:], DA(ssdnw_d, l * 512, [[0, 128], [1, 512]]), (), ["snw"])
                for j in range(4):
                    dma("sync", cwt[:, :, j:j + 1], DA(convw_d, (l * 4 + j) * 1024, [[1, 128], [128, 8], [1, 1]]), (), ["cwt"], slow=True)
                dma("sync", cbt[:].unsqueeze(2), DA(convb_d, l * 1024, [[1, 128], [128, 8], [1, 1]]), (), ["cbt"], slow=True)
                for ch in range(8):
                    for j in range(4):
                        tsc("gpsimd", dg[:, ch, j, :], ident[:], cwt[:, ch, j:j + 1], None, ALU.mult, None,
                            ["ident", "cwt"], [("dg", ch)])
                for c3 in range(3):
                    dma("sync", dtr[:, c3 * 11:(c3 + 1) * 11, :],
                        DA(dtr_d, c3 * 11 * 128 * 16, [[16, 128], [128 * 16, 11], [1, 16]]), ["dtr"], ["dtrl"])
                bcd = dtb[:].unsqueeze(1).broadcast_to([128, NCH, 16])
                tt("vector", tA[:], dtr[:], bcd, ALU.add, ["dtrl", "dtb"], ["tA"])
                act(tB[:], tA[:], AF.Abs, ["tA"], ["tB"])
                act(tB[:], tB[:], AF.Exp, ["tB"], ["tB"], scale=-1.0)
                tsc("vector", tB[:], tB[:], 1.0, None, ALU.add, None, ["tB"], ["tB"])
                act(tB[:], tB[:], AF.Ln, ["tB"], ["tB"])
                tsc("vector", tA[:], tA[:], 0.0, None, ALU.max, None, ["tA"], ["tA"])
                tt("vector", dtt[:], tA[:], tB[:], ALU.add, ["tA", "tB"], ["dtt"])
                act(alg[:], alg[:], AF.Exp, ["alg"], ["alg"])
                tsc("vector", alg[:], alg[:], -1.0, None, ALU.mult, None, ["alg"], ["alg"])
                tt("vector", da[:], dtt[:], alg[:].unsqueeze(1).broadcast_to([128, NCH, 16]), ALU.mult, ["dtt", "alg"], ["da"])

                for i in range(NT):
                    t0 = i * TS
                    xi = xin[i % 2]
                    kx = "xin%d" % (i % 2)
                    xt_ = xTt[i % 2]
                    kxt = "xTt%d" % (i % 2)
                    dma("sync", xi[:], DA(fm_d, 2048 * TW + 8 + t0 - 2, [[TW, 128], [128 * TW, 8], [1, TS + 3]]),
                        [("fm", 4), ("fm", 5), "fm_margin"], [kx])
                    for ch in range(8):
                        p = pcv[ch % 2]
                        kp = "pcv%d" % (ch % 2)
                        for j in range(4):
                            mm(p[:, 0:TS], dg[:, ch, j, :], xi[:, ch, j:j + TS], j == 0, j == 3, [("dg", ch), kx], [kp])
                        if ch < 4:
                            dst, kdst = xt_[:, ch, :], (kxt, ch)
                        elif ch < 6:
                            dst, kdst = BT[:, ch - 4, t0:t0 + TS], ("BT", i)
                        else:
                            dst, kdst = CT[:, ch - 6, t0:t0 + TS], ("CT", i)
                        act(dst, p[:, 0:TS], AF.Silu, [kp, "cbt"], [kdst], bias=cbt[:, ch:ch + 1])
                    if i == 0:
                        memset("gpsimd", xt_[:, :, 0:NPAD], 0.0, [(kxt, ch) for ch in range(4)])
                    for cl in range(3):
                        c = 3 * i + cl
                        lt = slice(cl * 128, (cl + 1) * 128)
                        for ch in range(4):
                            identr(ptr[:, ch * 128:(ch + 1) * 128], xt_[:, ch, lt], [(kxt, q) for q in range(4)], ["ptr"])
                        for g in range(2):
                            identr(ptr[:, 512 + g * 128:512 + (g + 1) * 128], BT[:, g, t0 + cl * 128:t0 + (cl + 1) * 128],
                                   [("BT", i)], ["ptr"])
                        cp("vector", xtm[:, c, :], ptr[:, 0:512], ["ptr"], [("xtm", c)])
                        cp("scalar", btm[:, c, :], ptr[:, 512:768], ["ptr"], [("btm", c)])

                ydx = ytmp2
                for d_ in range(2):
                    memset("gpsimd", H[:], 0.0, ["H"])
                    memset("gpsimd", Hb[:], 0.0, ["Hb"])
                    order = list(range(NCH)) if d_ == 0 else list(range(NCH - 1, -1, -1))

                    def stage_a(n_, d_=d_, order=order):
                        c = order[n_]
                        i = c // 3
                        b2 = n_ % 2
                        cols = slice(c * 128, (c + 1) * 128)
                        dac = da[:, c, d_ * 8:(d_ + 1) * 8]
                        for g in range(2):
                            mm(pcb[:, g * 128:(g + 1) * 128], BT[:, g, cols], CT[:, g, cols], True, True,
                               [("BT", i), ("CT", i)], ["pcv0"])
                        mm(pcb[:, 256:264], cf[:, d_, :], dac, True, True, ["cf", "da"], ["pcv0"])
                        mm(pcb[:, 264:272], cf[:, 2, :], dac, True, True, ["cf", "da"], ["pcv0"])
                        tt("vector", cbm2[b2][:], pcb[:, 0:256].rearrange("p (g s) -> p g s", g=2),
                           cb[:, 4 + d_, :].unsqueeze(1).broadcast_to([128, 2, 128]), ALU.mult, ["pcv0", "cb"], ["cbm%d" % b2])
                        cp("vector", cst2[b2][:, 0:16], pcb[:, 256:272], ["pcv0"], [("cst%d" % b2, 0)])
                        tt("vector", cst2[b2][:, 16:24], cst2[b2][:, 8:16], cst2[b2][:, 0:8], ALU.subtract,
                           [("cst%d" % b2, 0)], [("cst%d" % b2, 1)])
                        act(E2[b2][:], cst2[b2][:], AF.Exp, [("cst%d" % b2, 0), ("cst%d" % b2, 1)], ["E%d" % b2])
                        tt("gpsimd", R[:], cb[:, d_, :].unsqueeze(1).broadcast_to([128, 8, 128]),
                           dac.unsqueeze(2).broadcast_to([128, 8, 128]), ALU.mult, ["cb", "da"], ["R"])
                        for hh in range(2):
                            mm(pseg[hh][:, :], cb[:, 2 + d_, :], R[:, hh * 4:(hh + 1) * 4, :], True, True, ["cb", "R"],
                               [("pseg", hh)])
                            act(eseg[:, hh * 4:(hh + 1) * 4, :], pseg[hh][:, :].rearrange("p (h l) -> p h l", h=4), AF.Exp,
                                [("pseg", hh)], [("eseg", hh)])
                            tt("vector" if hh else "gpsimd", MT2[b2][:, hh * 4:(hh + 1) * 4, :], eseg[:, hh * 4:(hh + 1) * 4, :],
                               cbm2[b2][:, hh, :].unsqueeze(1).broadcast_to([128, 4, 128]), ALU.mult,
                               [("eseg", hh), "cbm%d" % b2], [("MT%d" % b2, hh)])
                        tt("gpsimd", xdt2[b2][:].rearrange("p (h d) -> p h d", h=8),
                           xtm[:, c, :].rearrange("p (h d) -> p h d", h=8),
                           dtt[:, c, d_ * 8:(d_ + 1) * 8].unsqueeze(2).broadcast_to([128, 8, 64]), ALU.mult,
                           [("xtm", c), "dtt"], ["xdt%d" % b2])
                        tt("gpsimd", xw2[b2][:].rearrange("p (h d) -> p h d", h=8),
                           xdt2[b2][:].rearrange("p (h d) -> p h d", h=8),
                           E2[b2][:, 16:24].unsqueeze(2).broadcast_to([128, 8, 64]), ALU.mult, ["xdt%d" % b2, "E%d" % b2],
                           ["xw%d" % b2])
                        for h in range(8):
                            mm(pyd2[b2][:, h * 64:(h + 1) * 64], MT2[b2][:, h, :], xdt2[b2][:, h * 64:(h + 1) * 64], True, True,
                               [("MT%d" % b2, h // 4), "xdt%d" % b2], [kpyd[b2]])

                    def stage_b(n_, d_=d_, order=order):
                        c = order[n_]
                        i = c // 3
                        b2 = n_ % 2
                        cols = slice(c * 128, (c + 1) * 128)
                        Eb = E2[b2]
                        kE = "E%d" % b2
                        for g in range(2):
                            mm(pyo[:, g * 256:(g + 1) * 256], CT[:, g, cols], Hb[:, g * 256:(g + 1) * 256], True, True,
                               [("CT", i), "Hb"], ["pyo"])
                        tt("vector", ytmp[:].rearrange("p (h d) -> p h d", h=8), pyo[:, :].rearrange("p (h d) -> p h d", h=8),
                           Eb[:, 0:8].unsqueeze(2).broadcast_to([128, 8, 64]), ALU.mult, ["pyo", kE], ["ytmp"])
                        ycb = yc[b2]
                        kyc = "yc%d" % b2
                        tt("vector", ycb[:], pyd2[b2][:, :], ytmp[:], ALU.add, [kpyd[b2], "ytmp"], [kyc])
                        for g in range(2):
                            mm(psn[:, g * 256:(g + 1) * 256], btm[:, c, g * 128:(g + 1) * 128], xw2[b2][:, g * 256:(g + 1) * 256],
                               True, True, [("btm", c), "xw%d" % b2], ["psn"])
                        tt("gpsimd", Ht[:].rearrange("p (h d) -> p h d", h=8), H[:].rearrange("p (h d) -> p h d", h=8),
                           Eb[:, 8:16].unsqueeze(2).broadcast_to([128, 8, 64]), ALU.mult, ["H", kE], ["Ht"])
                        tt("vector", H[:], psn[:, :], Ht[:], ALU.add, ["psn", "Ht"], ["H"])
                        cp("scalar", Hb[:], H[:], ["H"], ["Hb"])
                        if d_ == 0:
                            dma("sync", yf_d.ap()[c * 128:(c + 1) * 128, :], ycb[:], [kyc], ["yf"])
                        else:
                            yfb = yfl[b2]
                            kyf = "yfl%d" % b2
                            szb = szl[b2]
                            ksz = "szl%d" % b2
                            dma("sync", yfb[:], yf_d.ap()[c * 128:(c + 1) * 128, :], ["yf"], [kyf])
                            dma("sync", szb[:], sz_d.ap()[c * 128:(c + 1) * 128, :], ["sz"], [ksz])
                            tt("gpsimd", ydx[:].rearrange("p (h d) -> p h d", h=8),
                               xtm[:, c, :].rearrange("p (h d) -> p h d", h=8),
                               dsk[:].unsqueeze(2).broadcast_to([128, 8, 64]), ALU.mult, [("xtm", c), "dsk"], ["ydx"])
                            tt("vector", yg[:], ycb[:], yfb[:], ALU.add, [kyc, kyf], ["yg"])
                            tt("vector", yg[:], yg[:], ydx[:], ALU.add, ["yg", "ydx"], ["yg"])
                            tt("vector", yg[:], yg[:], szb[:], ALU.mult, ["yg", ksz], ["yg"])
                            act(yjunk[:], yg[:], AF.Square, ["yg"], ["yjunk", ("ystat", 0)], accum_out=ystat[:, 0:1])
                            tsc("vector", ystat[:, 1:2], ystat[:, 0:1], 1.0 / 512, EPS, ALU.mult, ALU.add, [("ystat", 0)], [("ystat", 1)])
                            tt("gpsimd", ystat[:, 3:4], ystat[:, 1:2], mhalf[:, 0:1], ALU.pow, [("ystat", 1), "mhalf"], [("ystat", 3)])
                            stt(ynb[:], yg[:], ystat[:, 3:4], snw[:], ALU.mult, ALU.mult, ["yg", ("ystat", 3), "snw"], ["ynb"])
                            ti = c // 3
                            cl = c % 3
                            ys = yst[ti % 2]
                            kys = "yst%d" % (ti % 2)
                            for ch in range(4):
                                identr(ptr[:, ch * 128:(ch + 1) * 128], ynb[:, ch * 128:(ch + 1) * 128], ["ynb"], ["ptr"])
                            cp("scalar", ys[:, :, cl * 128:(cl + 1) * 128], ptr[:, 0:512].rearrange("p (j t) -> p j t", j=4),
                               ["ptr"], [(kys, cl)])
                            if cl == 0:
                                dma("sync", DA(mixT_d, 1024 * TP + ti * TS, [[TP, 128], [128 * TP, 4], [1, TS]]), ys[:],
                                    [(kys, q) for q in range(3)], [("mixT", 2)])

                    stage_a(0)
                    for n_ in range(NCH):
                        if n_ + 1 < NCH:
                            stage_a(n_ + 1)
                        stage_b(n_)
            S.barrier()

        def phase_p4(l):
            with contextlib.ExitStack() as st:
                KT2 = alloc(st, "KT2", [128, 2, TP], BF16)
                Vp = alloc(st, "Vp", [128, NCH, 2, 192], BF16)
                vmask = alloc(st, "vmask", [128, 64], BF16)
                QT = [alloc(st, "QT", [128, 4, TS], BF16) for _ in range(2)]
                sg = [alloc(st, "sga", [128, 4, TS], BF16) for _ in range(2)]
                P = [alloc(st, "P", [128, 2, TS], BF16) for _ in range(3)]
                rec = alloc(st, "rec", [128, TS], F32)
                yt = alloc(st, "yt", [128, TS], F32)
                ost = [alloc(st, "osta", [128, 4, TS], BF16) for _ in range(2)]
                Sp = [palloc(st, "Sp", [128, 2, 512], F32) for _ in range(2)]
                Oe = [palloc(st, "Oe", [128, 512], F32) for _ in range(2)]
                Oo = [palloc(st, "Oo", [128, 512], F32) for _ in range(2)]
                dma("sync", vmask[:], c_vmask_d.ap(), (), ["vmask"])
                for g in range(2):
                    for hs in range(2):
                        dma("sync", KT2[hs * 64:(hs + 1) * 64, g, :], DA(kT_d, g * 64 * TP, [[TP, 64], [1, TP]]), ["kT"], ["KT2"])
                memset("gpsimd", Vp[:], 1.0, ["Vp"])
                for c3 in range(3):
                    for g in range(2):
                        dma("sync", Vp[:, c3 * 11:(c3 + 1) * 11, g, 64:128],
                            DA(vtm_d, c3 * 11 * 128 * 128 + g * 64, [[128, 128], [128 * 128, 11], [1, 64]]), ["vtm"], ["Vp"])
                for g in range(2):
                    for o0 in (0, 128):
                        cp("gpsimd", Vp[:, 0, g, o0:o0 + 64], vmask[:], ["vmask"], ["Vp"])
                npair = 0
                nS = [0]

                def p4_load(i):
                    dma("sync", QT[i % 2][:], DA(qT_d, i * TS, [[TP, 128], [128 * TP, 4], [1, TS]]), ["qT"], ["QT%d" % (i % 2)])
                    dma("sync", sg[i % 2][:], DA(fm_d, 3072 * TW + 8 + i * TS, [[TW, 128], [128 * TW, 4], [1, TS]]),
                        [("fm", 6)], ["sga%d" % (i % 2)])

                for i in range(NT):
                    t0 = i * TS
                    q_ = QT[i % 2]
                    kq = "QT%d" % (i % 2)
                    s_ = sg[i % 2]
                    ksg = "sga%d" % (i % 2)
                    o_ = ost[i % 2]
                    ko = "osta%d" % (i % 2)
                    if i == 0:
                        p4_load(0)
                    if i + 1 < NT:
                        p4_load(i + 1)
                    for j in range(4):
                        g = j // 2
                        oe = Oe[npair % 2]
                        oo = Oo[npair % 2]
                        koe = "Oe%d" % (npair % 2)
                        koo = "Oo%d" % (npair % 2)
                        npair += 1
                        pend = {}

                        def qk(kc, g=g, j=j, q_=q_, kq=kq, pend=pend):
                            n_ = nS[0]
                            nS[0] += 1
                            sp = Sp[n_ % 2]
                            ksp = "Sp%d" % (n_ % 2)
                            pb = P[n_ % 3]
                            kpb = "P%d" % (n_ % 3)
                            kcols = slice(kc * 128, (kc + 1) * 128)
                            mm(sp[:, 0, 0:TS], KT2[0:64, g, kcols], q_[0:64, j, :], True, True, ["KT2", kq], [ksp])
                            mm(sp[:, 1, 0:TS], KT2[64:128, g, kcols], q_[64:128, j, :], True, True, ["KT2", kq], [ksp])
                            act(pb[:, :, :], sp[:, :, 0:TS], AF.Exp, [ksp], [kpb])
                            pend[kc] = (pb, kpb)

                        def pvm(kc, g=g, oe=oe, oo=oo, koe=koe, koo=koo, pend=pend):
                            pb, kpb = pend.pop(kc)
                            mm(oe[:, 0:TS], Vp[:, kc, g, 64:192], pb[:, 0, :], kc == 0, kc == NCH - 1, ["Vp", kpb], [koe])
                            mm(oo[:, 0:TS], Vp[:, kc, g, 0:128], pb[:, 1, :], kc == 0, kc == NCH - 1, ["Vp", kpb], [koo])

                        qk(0)
                        for kc in range(NCH):
                            if kc + 1 < NCH:
                                qk(kc + 1)
                            pvm(kc)
                        recip(rec[0:64, :], oe[64:128, 0:TS], [koe], [("rec", 0)])
                        recip(rec[64:128, :], oo[0:64, 0:TS], [koo], [("rec", 1)])
                        tt("vector", yt[0:64, :], oe[0:64, 0:TS], rec[0:64, :], ALU.mult, [koe, ("rec", 0)], [("yt", 0)])
                        tt("vector", yt[64:128, :], oo[64:128, 0:TS], rec[64:128, :], ALU.mult, [koo, ("rec", 1)], [("yt", 1)])
                        tt("gpsimd", o_[:, j, :], yt[:], s_[:, j, :], ALU.mult, [("yt", 0), ("yt", 1), ksg], [(ko, j)])
                    dma("sync", DA(mixT_d, 1536 * TP + t0, [[TP, 128], [128 * TP, 4], [1, TS]]), o_[:],
                        [(ko, j) for j in range(4)], [("mixT", 3)])
            S.barrier()

        def phase_p5(l, last):
            with contextlib.ExitStack() as st:
                wo = alloc(st, "wo", [128, 16, D], BF16)
                mx = [alloc(st, "mx", [128, 16, TS], BF16) for _ in range(2)]
                hc = [alloc(st, "hc5", [128, D], F32) for _ in range(4)]
                ho = [alloc(st, "ho5", [128, D], F32) for _ in range(2)]
                po = [[palloc(st, "po", [128, 512], F32) for _ in range(2)] for _ in range(2)]
                for hh in range(2):
                    dma("gpsimd", wo[:, :, hh * 512:(hh + 1) * 512],
                        DA(wout_d, l * 2048 * D + hh * 512, [[D, 128], [128 * D, 16], [1, 512]]), (), [("wo", hh)])
                def p5_load(i):
                    for q4 in range(4):
                        dma("sync", mx[i % 2][:, q4 * 4:(q4 + 1) * 4, :],
                            DA(mixT_d, q4 * 512 * TP + i * TS, [[TP, 128], [128 * TP, 4], [1, TS]]), [("mixT", q4)], ["mx%d" % (i % 2)])

                def p5_hload(c):
                    if c >= NCH or (last and c == 0):
                        return
                    hcb = hc[c % 4]
                    khc = "hc5%d" % (c % 4)
                    if l == 0 and c == 0:
                        memset("gpsimd", hcb[:], 0.0, [khc])
                        dma("sync", hcb[NPAD:128, :], meta_d.ap(), (), [khc])
                    elif l == 0:
                        dma("sync", hcb[:], x_d.ap()[(c - 1) * 128:c * 128, :], (), [khc])
                    else:
                        dma("sync", hcb[:], hbuf_d.ap()[c * 128:(c + 1) * 128, :], ["hbuf", "hbuf_pad"], [khc])

                p5_load(0)
                p5_hload(0)
                p5_hload(1)
                for i in range(NT):
                    t0 = i * TS
                    m_ = mx[i % 2]
                    km = "mx%d" % (i % 2)
                    if i + 1 < NT:
                        p5_load(i + 1)
                    for cl in range(3):
                        c = 3 * i + cl
                        p5_hload(c + 2)
                        if last and c == 0:
                            continue
                        hcb = hc[c % 4]
                        khc = "hc5%d" % (c % 4)
                        hob = ho[c % 2]
                        kho = "ho5%d" % (c % 2)
                        for hh in range(2):
                            p = po[c % 2][hh]
                            kp = ("po", c % 2, hh)
                            for ec in range(16):
                                mm(p[:, :], m_[:, ec, cl * 128:(cl + 1) * 128], wo[:, ec, hh * 512:(hh + 1) * 512],
                                   ec == 0, ec == 15, [km, ("wo", hh)], [kp])
                            tt("vector", hob[:, hh * 512:(hh + 1) * 512], p[:, :], hcb[:, hh * 512:(hh + 1) * 512], ALU.add,
                               [kp, khc], [(kho, hh)])
                        if last:
                            o = dma("sync", out_d.ap()[(c - 1) * 128:c * 128, :], hob[:], [(kho, 0), (kho, 1)], ["out"])
                            final_ops.append(o)
                        elif c == 0:
                            dma("sync", hbuf_d.ap()[NPAD:128, :], hob[NPAD:128, :], [(kho, 0), (kho, 1)], ["hbuf"])
                        else:
                            dma("sync", hbuf_d.ap()[c * 128:(c + 1) * 128, :], hob[:], [(kho, 0), (kho, 1)], ["hbuf"])
            S.barrier()

        for l in range(n_layers):
            if "0" in phases:
                phase_p0(l)
            if "1" in phases:
                phase_p1(l)
            if "2" in phases:
                phase_p2(l)
            if "3" in phases:
                phase_p3(l)
            if "4" in phases:
                phase_p4(l)
            if "5" in phases:
                phase_p5(l, l == n_layers - 1)
        final_ops[:] = [o for o in final_ops if not o.skipped]
        S.limit = None
        print("n_ops", len(S.ops))
        if not final_ops:
            zo = dma("sync", out_d.ap()[0:128, :], zf[:], ["zf"], ["out"])
            final_ops.append(zo)
        S.emit(nc, final_ops)
    return nc


_PARAM_NAMES = ("meta_tokens", "norm_w", "w_in", "w_out", "pool_w", "pool_scale", "fourier_w", "conv_w", "conv_b",
                "dt_bias", "a_log", "d_skip", "ssd_norm_w", "q_norm_w", "k_norm_w")


def make_in_maps(inputs, cores):
    consts = host_consts()
    shared = {k: np.ascontiguousarray(np.asarray(inputs[k], dtype=np.float32)) for k in _PARAM_NAMES}
    shared.update(consts)
    x = np.asarray(inputs["x"], dtype=np.float32)
    maps = []
    for b in cores:
        m = dict(shared)
        m["x"] = np.ascontiguousarray(x[b])
        maps.append(m)
    return maps


def kernel(**inputs):
    nc = build()
    in_maps = make_in_maps(inputs, list(range(8)))
    res = run_bass_kernel_spmd(nc, in_maps, core_ids=list(range(8)))
    return np.stack([np.asarray(r["out"], dtype=np.float32) for r in res.results], axis=0)
```

```python
import contextlib
import numpy as np
import ml_dtypes
import concourse.bass as bass
import concourse.mybir as mybir
from concourse.bass_utils import run_bass_kernel_spmd

F32 = mybir.dt.float32
BF16 = mybir.dt.bfloat16
AF = mybir.ActivationFunctionType
ALU = mybir.AluOpType
AX = mybir.AxisListType

D = 1024
NTOK = 4096
NMETA = 16
L = NTOK + NMETA
NPAD = 112
TP = 4224
NCH = 33
NT = 11
TS = 384
TW = TP + 16
DEPTH = 4
INC = 4880
EPS = 1e-6
NPC = 3
PCC = 11

ENGS = ("tensor", "vector", "scalar", "gpsimd", "sync")
N_DMA_SEMS = 24
PSUM_KEYS = frozenset(["pT", "pf0", "pf1", "pz", "pq", "pkv", "ptq", "py", "acc", "pcv0", "pcv1", "ptr", "pseg", "pyd",
                       "pyo", "psn", "Sp0", "Sp1", "Oe0", "Oe1", "Oo0", "Oo1", "po"])


class Op:
    __slots__ = ("eng", "fn", "deps", "is_dma", "needed", "sem", "val", "nobar", "skipped")

    def __init__(self, eng, fn, deps, is_dma):
        self.eng = eng
        self.fn = fn
        self.deps = deps
        self.is_dma = is_dma
        self.needed = False
        self.sem = None
        self.val = 0
        self.nobar = False
        self.skipped = False


class Sched:
    def __init__(self):
        self.ops = []
        self.last_w = {}
        self.readers = {}
        self.last_eng = {}
        self.dmas = []
        self.limit = None

    def _add(self, eng, fn, reads, writes, is_dma, extra=()):
        ex = tuple(k for k in reads if (k[0] if isinstance(k, tuple) else k) in PSUM_KEYS)
        if ex:
            writes = tuple(writes) + tuple(k for k in ex if k not in writes)
        deps = list(extra)
        for k in reads:
            w = self.last_w.get(k)
            if w is not None:
                deps.append(w)
        for k in writes:
            w = self.last_w.get(k)
            if w is not None:
                deps.append(w)
            deps.extend(self.readers.get(k, ()))
        op = Op(eng, fn, deps, is_dma)
        if self.limit is not None and len(self.ops) >= self.limit:
            op.skipped = True
            return op
        self.ops.append(op)
        for k in writes:
            self.last_w[k] = op
            self.readers[k] = []
        for k in reads:
            if k in writes:
                continue
            lst = self.readers.setdefault(k, [])
            if not is_dma:
                lst[:] = [o for o in lst if o.is_dma or o.eng != eng]
            lst.append(op)
        if is_dma:
            self.dmas.append(op)
        else:
            self.last_eng[eng] = op
        return op

    def op(self, eng, fn, reads=(), writes=()):
        return self._add(eng, fn, tuple(reads), tuple(writes), False)

    def dma(self, eng, fn, reads=(), writes=(), nobar=False):
        o = self._add(eng, fn, tuple(reads), tuple(writes), True)
        o.nobar = nobar
        return o

    def barrier(self):
        deps = [o for o in self.last_eng.values()] + [o for o in self.dmas if not o.nobar]
        self.dmas = [o for o in self.dmas if o.nobar]
        for e in ENGS:
            self._add(e, lambda eng: eng.nop(), (), (), False, extra=deps)

    def emit(self, nc, final_wait_ops=()):
        ops = self.ops
        for o in ops:
            for d in o.deps:
                if d.eng == "tensor" and o.eng == "tensor" and not d.is_dma and not o.is_dma:
                    continue
                d.needed = True
        for o in final_wait_ops:
            o.needed = True
        with contextlib.ExitStack() as st:
            esem = {e: st.enter_context(nc.semaphore("s_" + e)) for e in ENGS}
            dsem = [st.enter_context(nc.semaphore("d_%d" % i)) for i in range(N_DMA_SEMS)]
            ecnt = {e: 0 for e in ENGS}
            dcnt = [0] * N_DMA_SEMS
            dnext = 0
            prev_on_dsem = [None] * N_DMA_SEMS
            for o in ops:
                if o.is_dma:
                    i = dnext
                    dnext = (dnext + 1) % N_DMA_SEMS
                    dcnt[i] += 16
                    o.sem = ("d", i)
                    o.val = dcnt[i]
                    if prev_on_dsem[i] is not None:
                        o.deps.append(prev_on_dsem[i])
                    prev_on_dsem[i] = o
                elif o.needed:
                    ecnt[o.eng] += 1
                    o.sem = ("e", o.eng)
                    o.val = ecnt[o.eng]
            per = {e: [o for o in ops if o.eng == e] for e in ENGS}
            final = list(final_wait_ops)
            blk = st.enter_context(nc.Block())

            def semh(s):
                return esem[s[1]] if s[0] == "e" else dsem[s[1]]

            def run(e, eng):
                seen = {}
                for o in per[e]:
                    need = {}
                    for d in o.deps:
                        if d.sem is None:
                            continue
                        if (not d.is_dma) and (not o.is_dma) and d.eng == "tensor" and e == "tensor":
                            continue
                        if need.get(d.sem, 0) < d.val:
                            need[d.sem] = d.val
                    for s, v in need.items():
                        if seen.get(s, 0) < v:
                            eng.wait_ge(semh(s), v)
                            seen[s] = v
                    ins = o.fn(eng)
                    if o.is_dma:
                        ins.then_inc(semh(o.sem), 16)
                    elif o.sem is not None:
                        ins.then_inc(semh(o.sem), 1)
                if e == "sync":
                    for o in final:
                        if seen.get(o.sem, 0) < o.val:
                            eng.wait_ge(semh(o.sem), o.val)
                            seen[o.sem] = o.val

            @blk.tensor
            def _(eng):
                run("tensor", eng)

            @blk.vector
            def _(eng):
                run("vector", eng)

            @blk.scalar
            def _(eng):
                run("scalar", eng)

            @blk.gpsimd
            def _(eng):
                run("gpsimd", eng)

            @blk.sync
            def _(eng):
                run("sync", eng)
        return ecnt, dcnt


_CONSTS = None


def host_consts():
    global _CONSTS
    if _CONSTS is not None:
        return _CONSTS
    bf = ml_dtypes.bfloat16
    c = {}
    c["c_ident"] = np.eye(128, dtype=np.float32).astype(bf)
    tphys = np.arange(TP)
    tlog = tphys - NPAD
    row = np.where(tlog >= NMETA, (tlog - NMETA) // 64, 0).astype(np.float64)
    col = np.where(tlog >= NMETA, (tlog - NMETA) % 64, 0).astype(np.float64)
    freqs = (10000.0 ** (-np.arange(0, 32, 2, dtype=np.float32) / 32.0)).astype(np.float32)
    ang = np.concatenate([row[:, None].astype(np.float32) * freqs[None], col[:, None].astype(np.float32) * freqs[None]],
                         axis=-1).astype(np.float32)
    cos2 = np.repeat(np.cos(ang), 2, axis=-1).astype(np.float32)
    sin2 = np.repeat(np.sin(ang), 2, axis=-1).astype(np.float32)
    rope = np.stack([cos2, sin2], 0).reshape(2, NCH, 128, 64).transpose(2, 0, 1, 3)
    c["c_rope"] = np.ascontiguousarray(rope, dtype=np.float32)
    k = np.arange(128)[:, None]
    l_ = np.arange(128)[None, :]
    trif = (k <= l_).astype(np.float32)
    trib = (k >= l_).astype(np.float32)
    ones = np.ones((128, 128), np.float32)
    c["c_ssd_f32"] = np.ascontiguousarray(np.stack([trif, trib, ones], 1), dtype=np.float32)
    t1f = (k > l_).astype(np.float32)
    t1b = (k < l_).astype(np.float32)
    c["c_ssd_bf"] = np.ascontiguousarray(np.stack([trif, trib, t1f, t1b, trif, trib], 1)).astype(bf)
    m = np.arange(512)
    a = 2.0 * np.pi * ((m[:, None] * m[None, :]) % 512) / 512.0
    cc = np.cos(a).reshape(4, 128, 512).transpose(1, 0, 2)
    sc = np.sin(a).reshape(4, 128, 512).transpose(1, 0, 2)
    c["c_dft_c"] = np.ascontiguousarray(np.stack([cc, sc], 1)).astype(bf)
    scale = 1.0 / np.sqrt(float(L) * 512.0)
    tl = (np.arange(TP) - NPAD).astype(np.int64)
    valid = (tl >= 0)
    prod = (np.maximum(tl, 0)[:, None] * np.maximum(tl, 0)[None, :]) % L
    angp = (2.0 * np.pi / L) * prod.astype(np.float64)
    vm = (valid[:, None] & valid[None, :])
    tabs = []
    for fn_, sgn in ((np.cos, 1.0), (np.sin, -1.0)):
        t_ = (fn_(angp) * (sgn * scale) * vm).astype(np.float32).astype(bf)
        t_ = t_.reshape(NPC, PCC, 128, NT, TS)
        t_ = t_.transpose(3, 0, 2, 1, 4)
        tabs.append(np.ascontiguousarray(t_).reshape(NT, NPC, 128, PCC * TS))
    c["c_tab"] = np.ascontiguousarray(np.stack(tabs, 0))
    pinv = np.zeros((128, 4, 2, 8), np.float32)
    for g, w in enumerate((2, 4, 8, 16)):
        for side in range(2):
            for q in range(8):
                t = q if side == 0 else L - 8 + q
                lo = min(max(t - w // 2, 0), L)
                hi = min(max(t - w // 2 + w, 0), L)
                pinv[:, g, side, q] = 1.0 / float(hi - lo)
    c["c_pool"] = pinv
    vmask = np.ones((128, 64), np.float32)
    vmask[:NPAD] = 0.0
    c["c_vmask"] = vmask.astype(bf)
    _CONSTS = c
    return c


def build(n_layers=DEPTH, dbg=False, phases="012345", limit=None):
    nc = bass.Bass("TRN2", target_bir_lowering=False)
    S = Sched()
    S.limit = limit
    uid = [0]

    def din(name, shape, dt):
        return nc.dram_tensor(name, list(shape), dt, kind="ExternalInput")

    x_d = din("x", [NTOK, D], F32)
    meta_d = din("meta_tokens", [NMETA, D], F32)
    normw_d = din("norm_w", [DEPTH, D], F32)
    win_d = din("w_in", [DEPTH, D, INC], F32)
    wout_d = din("w_out", [DEPTH, 2048, D], F32)
    poolw_d = din("pool_w", [DEPTH, 4, 128, 128], F32)
    pools_d = din("pool_scale", [DEPTH, 512], F32)
    fw_d = din("fourier_w", [DEPTH, 512, 512], F32)
    convw_d = din("conv_w", [DEPTH, 4, 1024], F32)
    convb_d = din("conv_b", [DEPTH, 1024], F32)
    dtb_d = din("dt_bias", [DEPTH, 2, 8], F32)
    alog_d = din("a_log", [DEPTH, 2, 8], F32)
    dskip_d = din("d_skip", [DEPTH, 8], F32)
    ssdnw_d = din("ssd_norm_w", [DEPTH, 512], F32)
    qnw_d = din("q_norm_w", [DEPTH, 64], F32)
    knw_d = din("k_norm_w", [DEPTH, 64], F32)
    c_ident_d = din("c_ident", [128, 128], BF16)
    c_rope_d = din("c_rope", [128, 2, NCH, 64], F32)
    c_ssdf_d = din("c_ssd_f32", [128, 3, 128], F32)
    c_ssdb_d = din("c_ssd_bf", [128, 6, 128], BF16)
    c_dftc_d = din("c_dft_c", [128, 2, 4, 512], BF16)
    c_tab_d = din("c_tab", [2, NT, NPC, 128, PCC * TS], BF16)
    c_pool_d = din("c_pool", [128, 4, 2, 8], F32)
    c_vmask_d = din("c_vmask", [128, 64], BF16)
    out_d = nc.dram_tensor("out", [NTOK, D], F32, kind="ExternalOutput")
    skind = "ExternalOutput" if dbg else "Internal"

    def dscr(name, shape, dt):
        return nc.dram_tensor(name, list(shape), dt, kind=skind)

    hbuf_d = dscr("hbuf", [TP, D], F32)
    fm_d = dscr("fm", [3584, TW], BF16)
    qT_d = dscr("qT", [512, TP], BF16)
    kT_d = dscr("kT", [128, TP], BF16)
    vtm_d = dscr("vtm", [TP, 128], BF16)
    sz_d = dscr("sz", [TP, 512], BF16)
    dtr_d = dscr("dtr", [TP, 16], F32)
    yf_d = dscr("yf", [TP, 512], F32)
    mixT_d = dscr("mixT", [2048, TP], BF16)

    def DA(h, off, dims):
        return bass.AP(h, off, [list(d) for d in dims])

    def mm(out, lhsT, rhs, start, stop, r, w):
        return S.op("tensor", lambda e: e.matmul(out, lhsT=lhsT, rhs=rhs, start=start, stop=stop), r, w)

    def act(out, in_, func, r, w, bias=None, scale=None, accum_out=None, eng="scalar"):
        kw = {}
        if bias is not None:
            kw["bias"] = bias
        if scale is not None:
            kw["scale"] = scale
        if accum_out is not None:
            kw["accum_out"] = accum_out
        return S.op("scalar", lambda e: e.activation(out=out, in_=in_, func=func, **kw), r, w)

    def tt(eng, out, in0, in1, op, r, w):
        return S.op(eng, lambda e: e.tensor_tensor(out=out, in0=in0, in1=in1, op=op), r, w)

    def tsc(eng, out, in0, s1, s2, op0, op1, r, w):
        if op1 is None:
            return S.op(eng, lambda e: e.tensor_scalar(out=out, in0=in0, scalar1=s1, scalar2=None, op0=op0), r, w)
        return S.op(eng, lambda e: e.tensor_scalar(out=out, in0=in0, scalar1=s1, scalar2=s2, op0=op0, op1=op1), r, w)

    def stt(out, in0, scalar, in1, op0, op1, r, w):
        return S.op("vector", lambda e: e.scalar_tensor_tensor(out=out, in0=in0, scalar=scalar, in1=in1, op0=op0, op1=op1), r, w)

    def cp(eng, out, in_, r, w):
        if eng == "scalar":
            return S.op("scalar", lambda e: e.activation(out=out, in_=in_, func=AF.Copy), r, w)
        return S.op(eng, lambda e: e.tensor_copy(out=out, in_=in_), r, w)

    def recip(out, in_, r, w):
        return S.op("vector", lambda e: e.reciprocal(out=out, in_=in_), r, w)

    def memset(eng, ap, val, w):
        return S.op(eng, lambda e: e.memset(ap, val), (), w)

    def dma(eng, out, in_, r, w, nobar=False, slow=False):
        if slow:
            return S.dma(eng, lambda e: e.dma_start(out=out, in_=in_, allow_slow_non_contiguous=True), r, w, nobar=nobar)
        return S.dma(eng, lambda e: e.dma_start(out=out, in_=in_), r, w, nobar=nobar)

    final_ops = []

    with contextlib.ExitStack() as top:
        def alloc(st, name, shape, dt):
            uid[0] += 1
            return st.enter_context(nc.sbuf_tensor("%s_%d" % (name, uid[0]), list(shape), dt))

        def palloc(st, name, shape, dt):
            uid[0] += 1
            return st.enter_context(nc.psum_tensor("%s_%d" % (name, uid[0]), list(shape), dt))

        ident = alloc(top, "ident", [128, 128], BF16)
        zero = alloc(top, "zero", [128, 16], BF16)
        dma("sync", ident[:], c_ident_d.ap(), (), ["ident"])
        memset("gpsimd", zero[:], 0.0, ["zero"])
        for r0, nj in ((0, 4), (2048, 8)):
            for c0 in (0, 8 + TP):
                dma("sync", DA(fm_d, r0 * TW + c0, [[TW, 128], [128 * TW, nj], [1, 8]]),
                    zero[:, 0:8].unsqueeze(1).broadcast_to([128, nj, 8]), ["zero"], ["fm_margin"])
        mhalf = alloc(top, "mhalf", [128, 16], F32)
        memset("gpsimd", mhalf[:], -0.5, ["mhalf"])
        zf = alloc(top, "zf", [128, D], F32)
        memset("gpsimd", zf[:], 0.0, ["zf"])
        dma("sync", hbuf_d.ap()[0:NPAD, :], zf[0:NPAD, :], ["zf"], ["hbuf_pad"])
        S.barrier()

        def identr(o, i_, r, w):
            return S.op("tensor", lambda e: e.transpose(o, i_, ident[:]), list(r) + ["ident"], w)

        def phase_p0(l):
            with contextlib.ExitStack() as st:
                W = alloc(st, "W", [128, 8, INC], BF16)
                pieces = ((0, 1024, 0), (1024, 1024, 1024), (2048, 1024, 2048), (4368, 512, 3072),
                          (3072, 512, 3584), (3600, 512, 4096), (4112, 256, 4608), (3584, 16, 4864))
                for (s0, n, d0) in pieces:
                    dma("gpsimd", W[:, :, d0:d0 + n],
                        DA(win_d, l * D * INC + s0, [[INC, 128], [128 * INC, 8], [1, n]]), (), [("W", d0)])
                wkeys = [("W", p[2]) for p in pieces]
                hc = [alloc(st, "hc", [128, D], F32) for _ in range(3)]
                junk = alloc(st, "junk", [128, D], BF16)
                nwb = alloc(st, "nwb", [128, D], F32)
                hnb = [alloc(st, "hnb", [128, D], BF16) for _ in range(3)]
                hnT = [alloc(st, "hnT", [128, 8, TS], BF16) for _ in range(2)]
                stt_t = [alloc(st, "stat", [128, 48], F32) for _ in range(2)]
                fst = [alloc(st, "fst", [128, 4, TS], BF16) for _ in range(3)]
                zst = [alloc(st, "zst", [128, 3, 512], BF16) for _ in range(2)]
                qTst = [alloc(st, "qTst", [128, 4, TS], BF16) for _ in range(2)]
                kTst = [alloc(st, "kTst", [128, TS], BF16) for _ in range(2)]
                vst = [alloc(st, "vst", [128, 3, 128], BF16) for _ in range(2)]
                dtst = [alloc(st, "dtst", [128, 3, 16], F32) for _ in range(2)]
                sq = alloc(st, "sq", [128, 640], F32)
                qn = alloc(st, "qn", [128, 640], F32)
                qw = alloc(st, "qw", [128, 640], F32)
                m1 = alloc(st, "m1", [128, 640], F32)
                m2 = alloc(st, "m2", [128, 640], F32)
                qr = [alloc(st, "qr", [128, 640], BF16) for _ in range(2)]
                wqk = alloc(st, "wqk", [128, 640], F32)
                rope = alloc(st, "rope", [128, 2, NCH, 64], F32)
                pT = palloc(st, "pT", [128, 1024], BF16)
                pf = [palloc(st, "pf", [128, 512], F32) for _ in range(2)]
                pz = palloc(st, "pz", [128, 512], F32)
                pq = palloc(st, "pq", [128, 512], F32)
                pkv = palloc(st, "pkv", [128, 512], F32)
                ptq = palloc(st, "ptq", [128, 1024], BF16)

                dma("sync", rope[:], c_rope_d.ap(), (), ["rope"])
                dma("sync", nwb[:], DA(normw_d, l * D, [[0, 128], [1, D]]), (), ["nwb"])
                dma("sync", wqk[:, 0:512].rearrange("p (h d) -> p h d", h=8),
                    DA(qnw_d, l * 64, [[0, 128], [0, 8], [1, 64]]), (), ["wqk"])
                dma("sync", wqk[:, 512:640].rearrange("p (h d) -> p h d", h=2),
                    DA(knw_d, l * 64, [[0, 128], [0, 2], [1, 64]]), (), ["wqk"])
                tsc("gpsimd", wqk[:, 0:512], wqk[:, 0:512], 0.125, None, ALU.mult, None, ["wqk"], ["wqk"])

                nfm = [0]

                def norm_load(c):
                    hcb = hc[c % 3]
                    khc = "hc%d" % (c % 3)
                    if l == 0 and c == 0:
                        memset("gpsimd", hcb[:], 0.0, [khc])
                        dma("sync", hcb[NPAD:128, :], meta_d.ap(), (), [khc])
                    elif l == 0:
                        dma("sync", hcb[:], x_d.ap()[(c - 1) * 128:c * 128, :], (), [khc])
                    else:
                        dma("sync", hcb[:], hbuf_d.ap()[c * 128:(c + 1) * 128, :], ["hbuf", "hbuf_pad"], [khc])

                def norm_front(c):
                    hcb = hc[c % 3]
                    khc = "hc%d" % (c % 3)
                    sta = stt_t[c % 2]
                    ks = "stat%d" % (c % 2)
                    act(junk[:], hcb[:], AF.Square, [khc], ["junk", (ks, 0)], accum_out=sta[:, 0:1])
                    tsc("vector", sta[:, 1:2], sta[:, 0:1], 1.0 / D, EPS, ALU.mult, ALU.add, [(ks, 0)], [(ks, 1)])
                    tt("gpsimd", sta[:, 3:4], sta[:, 1:2], mhalf[:, 0:1], ALU.pow, [(ks, 1), "mhalf"], [(ks, 3)])
                    stt(hnb[c % 3][:], hcb[:], sta[:, 3:4], nwb[:], ALU.mult, ALU.mult, [khc, (ks, 3), "nwb"], ["hnb%d" % (c % 3)])

                def norm_back(c):
                    i_, cl = c // 3, c % 3
                    hT = hnT[i_ % 2]
                    kh = "hnT%d" % (i_ % 2)
                    hb = hnb[c % 3]
                    khb = "hnb%d" % (c % 3)
                    for j in range(8):
                        identr(pT[:, j * 128:(j + 1) * 128], hb[:, j * 128:(j + 1) * 128], [khb], ["pT"])
                    cp("scalar", hT[:, :, cl * 128:(cl + 1) * 128], pT[:].rearrange("p (j t) -> p j t", j=8),
                       ["pT"], [(kh, cl)])

                for c in range(3):
                    norm_load(c)
                for c in range(3):
                    norm_front(c)
                for c in range(3):
                    norm_back(c)
                for i in range(NT):
                    t0 = i * TS
                    hT = hnT[i % 2]
                    kh = "hnT%d" % (i % 2)
                    hkeys = [(kh, 0), (kh, 1), (kh, 2)]
                    if i + 1 < NT:
                        for cl in range(3):
                            norm_load(3 * (i + 1) + cl)
                    def fm_seg(seg, i=i, t0=t0, hT=hT, hkeys=hkeys):
                        fs = fst[seg % 3]
                        kfs = "fst%d" % (seg % 3)
                        gate = seg in (1, 3, 6)
                        for j in range(4):
                            p = pf[nfm[0] % 2]
                            kp = "pf%d" % (nfm[0] % 2)
                            col0 = seg * 512 + j * 128
                            for dk in range(8):
                                mm(p[:, 0:TS], W[:, dk, col0:col0 + 128], hT[:, dk, :], dk == 0, dk == 7,
                                   wkeys + hkeys, [kp])
                            if gate:
                                act(fs[:, j, :], p[:, 0:TS], AF.Silu, [kp], [kfs])
                            else:
                                cp("scalar", fs[:, j, :], p[:, 0:TS], [kp], [kfs])
                            nfm[0] += 1
                        dma("sync", DA(fm_d, seg * 512 * TW + 8 + t0, [[TW, 128], [128 * TW, 4], [1, TS]]), fs[:],
                            [kfs, "fm_margin"], [("fm", seg)])
                        if seg in (0, 2, 4) and i + 1 < NT:
                            norm_front(3 * (i + 1) + seg // 2)
                    zs = zst[i % 2]
                    kzs = "zst%d" % (i % 2)
                    qTs = qTst[i % 2]
                    kqs = "qTst%d" % (i % 2)
                    kTs = kTst[i % 2]
                    kks = "kTst%d" % (i % 2)
                    vs = vst[i % 2]
                    kvs = "vst%d" % (i % 2)
                    dts = dtst[i % 2]
                    kds = "dtst%d" % (i % 2)

                    def qtrans(cl, i=i, qTs=qTs, kqs=kqs, kTs=kTs, kks=kks):
                        c = 3 * i + cl
                        lt = slice(cl * 128, (cl + 1) * 128)
                        qrb = qr[c % 2]
                        kqr = "qr%d" % (c % 2)
                        for j in range(5):
                            identr(ptq[:, j * 128:(j + 1) * 128], qrb[:, j * 128:(j + 1) * 128],
                                   [(kqr, 0), (kqr, 1)], ["ptq"])
                        cp("scalar", qTs[:, :, lt], ptq[:, 0:512].rearrange("p (j t) -> p j t", j=4), ["ptq"], [kqs])
                        cp("vector", kTs[:, lt], ptq[:, 512:640], ["ptq"], [kks])

                    for cl in range(3):
                        c = 3 * i + cl
                        lt = slice(cl * 128, (cl + 1) * 128)
                        if cl >= 1:
                            fm_seg(2 * cl - 2)
                            fm_seg(2 * cl - 1)
                        for dk in range(8):
                            mm(pz[:, :], hT[:, dk, lt], W[:, dk, 3584:4096], dk == 0, dk == 7, wkeys + hkeys, ["pz"])
                        act(zs[:, cl, :], pz[:, :], AF.Silu, ["pz"], [kzs])
                        for dk in range(8):
                            mm(pq[:, :], hT[:, dk, lt], W[:, dk, 4096:4608], dk == 0, dk == 7, wkeys + hkeys, ["pq"])
                        for dk in range(8):
                            mm(pkv[:, 0:272], hT[:, dk, lt], W[:, dk, 4608:4880], dk == 0, dk == 7, wkeys + hkeys, ["pkv"])
                        if cl >= 1:
                            qtrans(cl - 1)
                        cp("vector", vs[:, cl, :], pkv[:, 128:256], ["pkv"], [kvs])
                        cp("vector", dts[:, cl, :], pkv[:, 256:272], ["pkv"], [kds])
                        sta = stt_t[c % 2]
                        ks = "stat%d" % (c % 2)
                        act(sq[:, 0:512], pq[:, :], AF.Square, ["pq"], [("sq", 0)])
                        act(sq[:, 512:640], pkv[:, 0:128], AF.Square, ["pkv"], [("sq", 1)])
                        S.op("vector", lambda e, o=sta[:, 8:18], a=sq[:].rearrange("p (h d) -> p h d", h=10):
                             e.tensor_reduce(out=o, in_=a, axis=AX.X, op=ALU.add), [("sq", 0), ("sq", 1)], [(ks, 8)])
                        tsc("vector", sta[:, 18:28], sta[:, 8:18], 1.0 / 64, EPS, ALU.mult, ALU.add, [(ks, 8)], [(ks, 18)])
                        tt("gpsimd", sta[:, 38:48], sta[:, 18:28], mhalf[:, 0:10], ALU.pow, [(ks, 18), "mhalf"], [(ks, 38)])
                        tt("vector", qn[:, 0:512].rearrange("p (h d) -> p h d", h=8),
                           pq[:, :].rearrange("p (h d) -> p h d", h=8),
                           sta[:, 38:46].unsqueeze(2).broadcast_to([128, 8, 64]), ALU.mult, ["pq", (ks, 38)], [("qn", 0)])
                        tt("vector", qn[:, 512:640].rearrange("p (h d) -> p h d", h=2),
                           pkv[:, 0:128].rearrange("p (h d) -> p h d", h=2),
                           sta[:, 46:48].unsqueeze(2).broadcast_to([128, 2, 64]), ALU.mult, ["pkv", (ks, 38)], [("qn", 1)])
                        tt("gpsimd", qw[:], qn[:], wqk[:], ALU.mult, [("qn", 0), ("qn", 1), "wqk"], ["qw"])
                        qw3 = qw[:].rearrange("p (h d) -> p h d", h=10)
                        tt("gpsimd", m1[:].rearrange("p (h d) -> p h d", h=10), qw3,
                           rope[:, 0, c, :].unsqueeze(1).broadcast_to([128, 10, 64]), ALU.mult, ["qw", "rope"], ["m1"])
                        tt("gpsimd", m2[:].rearrange("p (h d) -> p h d", h=10), qw3,
                           rope[:, 1, c, :].unsqueeze(1).broadcast_to([128, 10, 64]), ALU.mult, ["qw", "rope"], ["m2"])
                        qrb = qr[c % 2]
                        kqr = "qr%d" % (c % 2)

                        def pv(tile_, off):
                            return bass.AP(tile_, off, [[640, 128], [2, 320]])
                        tt("vector", pv(qrb, 0), pv(m1, 0), pv(m2, 1), ALU.subtract, ["m1", "m2"], [(kqr, 0)])
                        tt("gpsimd", pv(qrb, 1), pv(m2, 0), pv(m1, 1), ALU.add, ["m1", "m2"], [(kqr, 1)])
                    for seg in (4, 5, 6):
                        fm_seg(seg)
                    if i + 1 < NT:
                        for cl in range(3):
                            norm_back(3 * (i + 1) + cl)
                    qtrans(2)
                    dma("sync", DA(sz_d, t0 * 512, [[512, 128], [128 * 512, 3], [1, 512]]), zs[:], [kzs], ["sz"])
                    dma("sync", DA(qT_d, t0, [[TP, 128], [128 * TP, 4], [1, TS]]), qTs[:], [kqs], ["qT"])
                    dma("sync", DA(kT_d, t0, [[TP, 128], [1, TS]]), kTs[:], [kks], ["kT"])
                    dma("sync", DA(vtm_d, t0 * 128, [[128, 128], [128 * 128, 3], [1, 128]]), vs[:], [kvs], ["vtm"])
                    dma("sync", DA(dtr_d, t0 * 16, [[16, 128], [128 * 16, 3], [1, 16]]), dts[:], [kds], ["dtr"])
            S.barrier()

        def phase_p12(l):
            with contextlib.ExitStack() as st:
                pw = alloc(st, "pw", [128, 4, 128], BF16)
                psc = alloc(st, "psc", [128, 4], F32)
                cinv = alloc(st, "cinv", [128, 4, 2, 8], F32)
                ub = [alloc(st, "ub", [128, 4, 400], BF16) for _ in range(2)]
                sgp = [alloc(st, "sg", [128, 4, TS], BF16) for _ in range(2)]
                T2 = alloc(st, "T2", [128, 4, 400], F32)
                T4 = alloc(st, "T4", [128, 4, 400], F32)
                T8 = alloc(st, "T8", [128, 4, 400], F32)
                T16 = alloc(st, "T16", [128, 4, 400], F32)
                tmp = alloc(st, "ptmp", [128, 4, 8], F32)
                dT = [alloc(st, "dT", [128, 4, TS], BF16) for _ in range(2)]
                ostp = [alloc(st, "ost", [128, 4, TS], BF16) for _ in range(2)]
                Ap = alloc(st, "Ap", [128, NCH, 512], BF16)
                Bp = alloc(st, "Bp", [128, NCH, 512], BF16)
                ring = [alloc(st, "ring", [128, PCC, TS], BF16) for _ in range(4)]
                cs = alloc(st, "dftc", [128, 2, 4, 512], BF16)
                wf = alloc(st, "wf", [128, 4, 512], BF16)
                cw = alloc(st, "cw", [128, 2, 4, 512], BF16)
                uT = [alloc(st, "uT", [128, 4, TS], BF16) for _ in range(2)]
                sg = [alloc(st, "sgf", [128, 4, TS], BF16) for _ in range(2)]
                ost = [alloc(st, "ostf", [128, 4, TS], BF16) for _ in range(2)]
                acc = [palloc(st, "acc", [128, 512], F32) for _ in range(4)]
                py = [palloc(st, "py", [128, 512], F32) for _ in range(4)]
                banks = [(acc[k], ("acc", 0, k)) for k in range(4)] + [(py[g], ("py", g)) for g in range(4)]

                dma("gpsimd", pw[:], DA(poolw_d, l * 4 * 128 * 128, [[128, 128], [128 * 128, 4], [1, 128]]), (), ["pw"])
                dma("sync", psc[:].unsqueeze(2), DA(pools_d, l * 512, [[1, 128], [128, 4], [1, 1]]), (), ["psc"], slow=True)
                dma("sync", cinv[:], c_pool_d.ap(), (), ["cinv"])
                dma("sync", cs[:], c_dftc_d.ap(), (), ["dftc"])
                dma("gpsimd", wf[:], DA(fw_d, l * 512 * 512, [[512, 128], [128 * 512, 4], [1, 512]]), (), ["wf"])
                Ts = (T2, T4, T8, T16)

                def p1_front(i):
                    t0 = i * TS
                    u = ub[i % 2]
                    ku = "ub%d" % (i % 2)
                    d_ = dT[i % 2]
                    kd = "dT%d" % (i % 2)
                    dma("sync", u[:], DA(fm_d, t0, [[TW, 128], [128 * TW, 4], [1, 400]]), [("fm", 0), "fm_margin"], [ku])
                    dma("sync", sgp[i % 2][:], DA(fm_d, 512 * TW + 8 + t0, [[TW, 128], [128 * TW, 4], [1, TS]]), [("fm", 1)],
                        ["sg%d" % (i % 2)])
                    tt("vector", T2[:, :, 0:399], u[:, :, 0:399], u[:, :, 1:400], ALU.add, [ku], ["T2"])
                    tt("gpsimd", T4[:, 1:4, 0:397], T2[:, 1:4, 0:397], T2[:, 1:4, 2:399], ALU.add, ["T2"], ["T4"])
                    tt("vector", T8[:, 2:4, 0:393], T4[:, 2:4, 0:393], T4[:, 2:4, 4:397], ALU.add, ["T4"], ["T8"])
                    tt("gpsimd", T16[:, 3, 0:385], T8[:, 3, 0:385], T8[:, 3, 8:393], ALU.add, ["T8"], ["T16"])
                    for g, w_ in enumerate((2, 4, 8, 16)):
                        half = w_ // 2
                        Pg = Ts[g]
                        kP = ("T2", "T4", "T8", "T16")[g]
                        o8 = 8 - half
                        stt(d_[:, g, :], Pg[:, g, o8:o8 + TS], 1.0 / w_, u[:, g, 8:8 + TS], ALU.mult, ALU.subtract,
                            [kP, ku], [(kd, g)])
                        for side, tile_i, m0 in ((0, 0, NPAD), (1, NT - 1, TS - 8)):
                            if i != tile_i:
                                continue
                            tt("vector", tmp[:, g, :], Pg[:, g, o8 + m0:o8 + m0 + 8], cinv[:, g, side, :], ALU.mult,
                               [kP, "cinv"], [("ptmp", g)])
                            tt("vector", d_[:, g, m0:m0 + 8], tmp[:, g, :], u[:, g, 8 + m0:8 + m0 + 8], ALU.subtract,
                               [("ptmp", g), ku], [(kd, g)])

                def p1_back(i):
                    t0 = i * TS
                    d_ = dT[i % 2]
                    kd = "dT%d" % (i % 2)
                    s_ = sgp[i % 2]
                    ksg = "sg%d" % (i % 2)
                    o_ = ostp[i % 2]
                    ko = "ost%d" % (i % 2)
                    for g in range(4):
                        mm(py[g][:, 0:TS], pw[:, g, :], d_[:, g, :], True, True, ["pw", (kd, g)], [("py", g)])
                    for g in range(4):
                        stt(o_[:, g, :], py[g][:, 0:TS], psc[:, g:g + 1], s_[:, g, :], ALU.mult, ALU.mult,
                            [("py", g), "psc", ksg], [(ko, g)])
                    dma("sync", DA(mixT_d, t0, [[TP, 128], [128 * TP, 4], [1, TS]]), o_[:],
                        [(ko, g) for g in range(4)], [("mixT", 0)])

                n = 0
                for tb in range(2):
                    for ci in range(4):
                        p, kp = banks[n % 8]
                        for mk in range(4):
                            mm(p[:, :], cs[:, tb, mk, ci * 128:(ci + 1) * 128], wf[:, mk, :], mk == 0, mk == 3,
                               ["dftc", "wf"], [kp])
                        cp("scalar" if n % 2 else "vector", cw[:, tb, ci, :], p[:, :], [kp], [("cw", tb, ci)])
                        n += 1
                cwk = [("cw", tb, ci) for tb in range(2) for ci in range(4)]
                for i in range(NT):
                    t0 = i * TS
                    u = uT[i % 2]
                    ku = "uT%d" % (i % 2)
                    dma("sync", u[:], DA(fm_d, 1024 * TW + 8 + t0, [[TW, 128], [128 * TW, 4], [1, TS]]), [("fm", 2)], [ku])
                    for cl in range(3):
                        c = 3 * i + cl
                        for tb, dst in ((0, Ap), (1, Bp)):
                            p, kp = banks[(tb * 3 + cl) % 8]
                            for ck in range(4):
                                mm(p[:, :], u[:, ck, cl * 128:(cl + 1) * 128], cw[:, tb, ck, :], ck == 0, ck == 3,
                                   [ku] + cwk, [kp])
                            cp("scalar" if tb else "vector", dst[:, c, :], p[:, :], [kp], [("ApBp", c)])
                apk = [("ApBp", c) for c in range(NCH)]
                npiece = 0
                for kt in range(NT):
                    t0 = kt * TS
                    s_ = sg[kt % 2]
                    ksg = "sgf%d" % (kt % 2)
                    o_ = ost[kt % 2]
                    ko = "ostf%d" % (kt % 2)
                    dma("sync", s_[:], DA(fm_d, 1536 * TW + 8 + t0, [[TW, 128], [128 * TW, 4], [1, TS]]), [("fm", 3)], [ksg])
                    p1_front(kt)
                    for pc in range(NPC):
                        for tb in range(2):
                            rb = ring[npiece % 4]
                            kr = "ring%d" % (npiece % 4)
                            npiece += 1
                            off = (((tb * NT + kt) * NPC + pc) * 128) * (PCC * TS)
                            dma("sync", rb[:], DA(c_tab_d, off, [[PCC * TS, 128], [TS, PCC], [1, TS]]), (), [kr])
                            src = Bp if tb else Ap
                            for tci in range(PCC):
                                tc_ = pc * PCC + tci
                                first = (pc == 0 and tb == 0 and tci == 0)
                                last = (pc == NPC - 1 and tb == 1 and tci == PCC - 1)
                                for dc in range(4):
                                    mm(acc[dc][:, 0:TS], src[:, tc_, dc * 128:(dc + 1) * 128], rb[:, tci, :], first, last,
                                       apk + [kr], [("acc", 0, dc)])
                    for dc in range(4):
                        tt("vector", o_[:, dc, :], acc[dc][:, 0:TS], s_[:, dc, :], ALU.mult,
                           [("acc", 0, dc), ksg], [(ko, dc)])
                    dma("sync", DA(mixT_d, 512 * TP + t0, [[TP, 128], [128 * TP, 4], [1, TS]]), o_[:],
                        [(ko, dc) for dc in range(4)], [("mixT", 1)])
                    p1_back(kt)
            S.barrier()

        def phase_p3(l):
            with contextlib.ExitStack() as st:
                BT = alloc(st, "BT", [128, 2, TP], BF16)
                CT = alloc(st, "CT", [128, 2, TP], BF16)
                xtm = alloc(st, "xtm", [128, NCH, 512], BF16)
                btm = alloc(st, "btm", [128, NCH, 256], BF16)
                dtr = alloc(st, "dtr", [128, NCH, 16], F32)
                dtt = alloc(st, "dtt", [128, NCH, 16], F32)
                da = alloc(st, "da", [128, NCH, 16], F32)
                tA = alloc(st, "tA", [128, NCH, 16], F32)
                tB = alloc(st, "tB", [128, NCH, 16], F32)
                dtb = alloc(st, "dtb", [128, 16], F32)
                alg = alloc(st, "alg", [128, 16], F32)
                dsk = alloc(st, "dsk", [128, 8], F32)
                snw = alloc(st, "snw", [128, 512], F32)
                cwt = alloc(st, "cwt", [128, 8, 4], F32)
                cbt = alloc(st, "cbt", [128, 8], F32)
                dg = alloc(st, "dg", [128, 8, 4, 128], BF16)
                cf = alloc(st, "cf", [128, 3, 128], F32)
                cb = alloc(st, "cb", [128, 6, 128], BF16)
                xin = [alloc(st, "xin", [128, 8, TS + 3], BF16) for _ in range(2)]
                xTt = [alloc(st, "xTt", [128, 4, TS], BF16) for _ in range(2)]
                H = alloc(st, "H", [128, 512], F32)
                Hb = alloc(st, "Hb", [128, 512], BF16)
                Ht = alloc(st, "Ht", [128, 512], F32)
                cst = alloc(st, "cst", [128, 24], F32)
                E = alloc(st, "E", [128, 24], F32)
                R = alloc(st, "R", [128, 8, 128], BF16)
                cbm = alloc(st, "cbm", [128, 2, 128], BF16)
                eseg = alloc(st, "eseg", [128, 8, 128], BF16)
                MT = alloc(st, "MT", [128, 8, 128], BF16)
                xdt = alloc(st, "xdt", [128, 512], BF16)
                xw = alloc(st, "xw", [128, 512], BF16)
                ytmp = alloc(st, "ytmp", [128, 512], F32)
                yc = [alloc(st, "yc", [128, 512], F32) for _ in range(2)]
                yfl = [alloc(st, "yfl", [128, 512], F32) for _ in range(2)]
                szl = [alloc(st, "szl", [128, 512], BF16) for _ in range(2)]
                yg = alloc(st, "yg", [128, 512], F32)
                yjunk = alloc(st, "yjunk", [128, 512], BF16)
                ynb = alloc(st, "ynb", [128, 512], BF16)
                ystat = alloc(st, "ystat", [128, 4], F32)
                yst = [alloc(st, "yst", [128, 4, TS], BF16) for _ in range(2)]
                pcv = [palloc(st, "pcv", [128, 512], F32) for _ in range(2)]
                ptr = palloc(st, "ptr", [128, 1024], BF16)
                pcb = pcv[0]
                pseg = [palloc(st, "pseg", [128, 512], F32) for _ in range(2)]
                pyd = palloc(st, "pyd", [128, 512], F32)
                pyo = palloc(st, "pyo", [128, 512], F32)
                psn = palloc(st, "psn", [128, 512], F32)
                pyd2 = [pcv[1], pyd]
                kpyd = ["pcv1", "pyd"]
                cbm2 = [alloc(st, "cbm2", [128, 2, 128], BF16) for _ in range(2)]
                cst2 = [alloc(st, "cst2", [128, 24], F32) for _ in range(2)]
                E2 = [alloc(st, "E2", [128, 24], F32) for _ in range(2)]
                MT2 = [alloc(st, "MT2", [128, 8, 128], BF16) for _ in range(2)]
                xdt2 = [alloc(st, "xdt2", [128, 512], BF16) for _ in range(2)]
                xw2 = [alloc(st, "xw2", [128, 512], BF16) for _ in range(2)]
                ytmp2 = alloc(st, "ytmp2", [128, 512], F32)

                dma("sync", cf[:], c_ssdf_d.ap(), (), ["cf"])
                dma("sync", cb[:], c_ssdb_d.ap(), (), ["cb"])
                dma("sync", dtb[:], DA(dtb_d, l * 16, [[0, 128], [1, 16]]), (), ["dtb"])
                dma("sync", alg[:], DA(alog_d, l * 16, [[0, 128], [1, 16]]), (), ["alg"])
                dma("sync", dsk[:], DA(dskip_d, l * 8, [[0, 128], [1, 8]]), (), ["dsk"])
                dma("sync", snw[:], DA(ssdnw_d, l * 512, [[0, 128], [1, 512]]), (), ["snw"])
                for j in range(4):
                    dma("sync", cwt[:, :, j:j + 1], DA(convw_d, (l * 4 + j) * 1024, [[1, 128], [128, 8], [1, 1]]), (), ["cwt"], slow=True)
                dma("sync", cbt[:].unsqueeze(2), DA(convb_d, l * 1024, [[1, 128], [128, 8], [1, 1]]), (), ["cbt"], slow=True)
                for ch in range(8):
                    for j in range(4):
                        tsc("gpsimd", dg[:, ch, j, :], ident[:], cwt[:, ch, j:j + 1], None, ALU.mult, None,
                            ["ident", "cwt"], [("dg", ch)])
                for c3 in range(3):
                    dma("sync", dtr[:, c3 * 11:(c3 + 1) * 11, :],
                        DA(dtr_d, c3 * 11 * 128 * 16, [[16, 128], [128 * 16, 11], [1, 16]]), ["dtr"], ["dtrl"])
                bcd = dtb[:].unsqueeze(1).broadcast_to([128, NCH, 16])
                tt("vector", tA[:], dtr[:], bcd, ALU.add, ["dtrl", "dtb"], ["tA"])
                act(tB[:], tA[:], AF.Abs, ["tA"], ["tB"])
                act(tB[:], tB[:], AF.Exp, ["tB"], ["tB"], scale=-1.0)
                tsc("vector", tB[:], tB[:], 1.0, None, ALU.add, None, ["tB"], ["tB"])
                act(tB[:], tB[:], AF.Ln, ["tB"], ["tB"])
                tsc("vector", tA[:], tA[:], 0.0, None, ALU.max, None, ["tA"], ["tA"])
                tt("vector", dtt[:], tA[:], tB[:], ALU.add, ["tA", "tB"], ["dtt"])
                act(alg[:], alg[:], AF.Exp, ["alg"], ["alg"])
                tsc("vector", alg[:], alg[:], -1.0, None, ALU.mult, None, ["alg"], ["alg"])
                tt("vector", da[:], dtt[:], alg[:].unsqueeze(1).broadcast_to([128, NCH, 16]), ALU.mult, ["dtt", "alg"], ["da"])

                for i in range(NT):
                    t0 = i * TS
                    xi = xin[i % 2]
                    kx = "xin%d" % (i % 2)
                    xt_ = xTt[i % 2]
                    kxt = "xTt%d" % (i % 2)
                    dma("sync", xi[:], DA(fm_d, 2048 * TW + 8 + t0 - 2, [[TW, 128], [128 * TW, 8], [1, TS + 3]]),
                        [("fm", 4), ("fm", 5), "fm_margin"], [kx])
                    for ch in range(8):
                        p = pcv[ch % 2]
                        kp = "pcv%d" % (ch % 2)
                        for j in range(4):
                            mm(p[:, 0:TS], dg[:, ch, j, :], xi[:, ch, j:j + TS], j == 0, j == 3, [("dg", ch), kx], [kp])
                        if ch < 4:
                            dst, kdst = xt_[:, ch, :], (kxt, ch)
                        elif ch < 6:
                            dst, kdst = BT[:, ch - 4, t0:t0 + TS], ("BT", i)
                        else:
                            dst, kdst = CT[:, ch - 6, t0:t0 + TS], ("CT", i)
                        act(dst, p[:, 0:TS], AF.Silu, [kp, "cbt"], [kdst], bias=cbt[:, ch:ch + 1])
                    if i == 0:
                        memset("gpsimd", xt_[:, :, 0:NPAD], 0.0, [(kxt, ch) for ch in range(4)])
                    for cl in range(3):
                        c = 3 * i + cl
                        lt = slice(cl * 128, (cl + 1) * 128)
                        for ch in range(4):
                            identr(ptr[:, ch * 128:(ch + 1) * 128], xt_[:, ch, lt], [(kxt, q) for q in range(4)], ["ptr"])
                        for g in range(2):
                            identr(ptr[:, 512 + g * 128:512 + (g + 1) * 128], BT[:, g, t0 + cl * 128:t0 + (cl + 1) * 128],
                                   [("BT", i)], ["ptr"])
                        cp("vector", xtm[:, c, :], ptr[:, 0:512], ["ptr"], [("xtm", c)])
                        cp("scalar", btm[:, c, :], ptr[:, 512:768], ["ptr"], [("btm", c)])

                ydx = ytmp2
                for d_ in range(2):
                    memset("gpsimd", H[:], 0.0, ["H"])
                    memset("gpsimd", Hb[:], 0.0, ["Hb"])
                    order = list(range(NCH)) if d_ == 0 else list(range(NCH - 1, -1, -1))

                    def stage_a(n_, d_=d_, order=order):
                        c = order[n_]
                        i = c // 3
                        b2 = n_ % 2
                        cols = slice(c * 128, (c + 1) * 128)
                        dac = da[:, c, d_ * 8:(d_ + 1) * 8]
                        for g in range(2):
                            mm(pcb[:, g * 128:(g + 1) * 128], BT[:, g, cols], CT[:, g, cols], True, True,
                               [("BT", i), ("CT", i)], ["pcv0"])
                        mm(pcb[:, 256:264], cf[:, d_, :], dac, True, True, ["cf", "da"], ["pcv0"])
                        mm(pcb[:, 264:272], cf[:, 2, :], dac, True, True, ["cf", "da"], ["pcv0"])
                        tt("vector", cbm2[b2][:], pcb[:, 0:256].rearrange("p (g s) -> p g s", g=2),
                           cb[:, 4 + d_, :].unsqueeze(1).broadcast_to([128, 2, 128]), ALU.mult, ["pcv0", "cb"], ["cbm%d" % b2])
                        cp("vector", cst2[b2][:, 0:16], pcb[:, 256:272], ["pcv0"], [("cst%d" % b2, 0)])
                        tt("vector", cst2[b2][:, 16:24], cst2[b2][:, 8:16], cst2[b2][:, 0:8], ALU.subtract,
                           [("cst%d" % b2, 0)], [("cst%d" % b2, 1)])
                        act(E2[b2][:], cst2[b2][:], AF.Exp, [("cst%d" % b2, 0), ("cst%d" % b2, 1)], ["E%d" % b2])
                        tt("gpsimd", R[:], cb[:, d_, :].unsqueeze(1).broadcast_to([128, 8, 128]),
                           dac.unsqueeze(2).broadcast_to([128, 8, 128]), ALU.mult, ["cb", "da"], ["R"])
                        for hh in range(2):
                            mm(pseg[hh][:, :], cb[:, 2 + d_, :], R[:, hh * 4:(hh + 1) * 4, :], True, True, ["cb", "R"],
                               [("pseg", hh)])
                            act(eseg[:, hh * 4:(hh + 1) * 4, :], pseg[hh][:, :].rearrange("p (h l) -> p h l", h=4), AF.Exp,
                                [("pseg", hh)], [("eseg", hh)])
                            tt("vector" if hh else "gpsimd", MT2[b2][:, hh * 4:(hh + 1) * 4, :], eseg[:, hh * 4:(hh + 1) * 4, :],
                               cbm2[b2][:, hh, :].unsqueeze(1).broadcast_to([128, 4, 128]), ALU.mult,
                               [("eseg", hh), "cbm%d" % b2], [("MT%d" % b2, hh)])
                        tt("gpsimd", xdt2[b2][:].rearrange("p (h d) -> p h d", h=8),
                           xtm[:, c, :].rearrange("p (h d) -> p h d", h=8),
                           dtt[:, c, d_ * 8:(d_ + 1) * 8].unsqueeze(2).broadcast_to([128, 8, 64]), ALU.mult,
                           [("xtm", c), "dtt"], ["xdt%d" % b2])
                        tt("gpsimd", xw2[b2][:].rearrange("p (h d) -> p h d", h=8),
                           xdt2[b2][:].rearrange("p (h d) -> p h d", h=8),
                           E2[b2][:, 16:24].unsqueeze(2).broadcast_to([128, 8, 64]), ALU.mult, ["xdt%d" % b2, "E%d" % b2],
                           ["xw%d" % b2])
                        for h in range(8):
                            mm(pyd2[b2][:, h * 64:(h + 1) * 64], MT2[b2][:, h, :], xdt2[b2][:, h * 64:(h + 1) * 64], True, True,
                               [("MT%d" % b2, h // 4), "xdt%d" % b2], [kpyd[b2]])

                    def stage_b(n_, d_=d_, order=order):
                        c = order[n_]
                        i = c // 3
                        b2 = n_ % 2
                        cols = slice(c * 128, (c + 1) * 128)
                        Eb = E2[b2]
                        kE = "E%d" % b2
                        for g in range(2):
                            mm(pyo[:, g * 256:(g + 1) * 256], CT[:, g, cols], Hb[:, g * 256:(g + 1) * 256], True, True,
                               [("CT", i), "Hb"], ["pyo"])
                        tt("vector", ytmp[:].rearrange("p (h d) -> p h d", h=8), pyo[:, :].rearrange("p (h d) -> p h d", h=8),
                           Eb[:, 0:8].unsqueeze(2).broadcast_to([128, 8, 64]), ALU.mult, ["pyo", kE], ["ytmp"])
                        ycb = yc[b2]
                        kyc = "yc%d" % b2
                        tt("vector", ycb[:], pyd2[b2][:, :], ytmp[:], ALU.add, [kpyd[b2], "ytmp"], [kyc])
                        for g in range(2):
                            mm(psn[:, g * 256:(g + 1) * 256], btm[:, c, g * 128:(g + 1) * 128], xw2[b2][:, g * 256:(g + 1) * 256],
                               True, True, [("btm", c), "xw%d" % b2], ["psn"])
                        tt("gpsimd", Ht[:].rearrange("p (h d) -> p h d", h=8), H[:].rearrange("p (h d) -> p h d", h=8),
                           Eb[:, 8:16].unsqueeze(2).broadcast_to([128, 8, 64]), ALU.mult, ["H", kE], ["Ht"])
                        tt("vector", H[:], psn[:, :], Ht[:], ALU.add, ["psn", "Ht"], ["H"])
                        cp("scalar", Hb[:], H[:], ["H"], ["Hb"])
                        if d_ == 0:
                            dma("sync", yf_d.ap()[c * 128:(c + 1) * 128, :], ycb[:], [kyc], ["yf"])
                        else:
                            yfb = yfl[b2]
                            kyf = "yfl%d" % b2
                            szb = szl[b2]
                            ksz = "szl%d" % b2
                            dma("sync", yfb[:], yf_d.ap()[c * 128:(c + 1) * 128, :], ["yf"], [kyf])
                            dma("sync", szb[:], sz_d.ap()[c * 128:(c + 1) * 128, :], ["sz"], [ksz])
                            tt("gpsimd", ydx[:].rearrange("p (h d) -> p h d", h=8),
                               xtm[:, c, :].rearrange("p (h d) -> p h d", h=8),
                               dsk[:].unsqueeze(2).broadcast_to([128, 8, 64]), ALU.mult, [("xtm", c), "dsk"], ["ydx"])
                            tt("vector", yg[:], ycb[:], yfb[:], ALU.add, [kyc, kyf], ["yg"])
                            tt("vector", yg[:], yg[:], ydx[:], ALU.add, ["yg", "ydx"], ["yg"])
                            tt("vector", yg[:], yg[:], szb[:], ALU.mult, ["yg", ksz], ["yg"])
                            act(yjunk[:], yg[:], AF.Square, ["yg"], ["yjunk", ("ystat", 0)], accum_out=ystat[:, 0:1])
                            tsc("vector", ystat[:, 1:2], ystat[:, 0:1], 1.0 / 512, EPS, ALU.mult, ALU.add, [("ystat", 0)], [("ystat", 1)])
                            tt("gpsimd", ystat[:, 3:4], ystat[:, 1:2], mhalf[:, 0:1], ALU.pow, [("ystat", 1), "mhalf"], [("ystat", 3)])
                            stt(ynb[:], yg[:], ystat[:, 3:4], snw[:], ALU.mult, ALU.mult, ["yg", ("ystat", 3), "snw"], ["ynb"])
                            ti = c // 3
                            cl = c % 3
                            ys = yst[ti % 2]
                            kys = "yst%d" % (ti % 2)
                            for ch in range(4):
                                identr(ptr[:, ch * 128:(ch + 1) * 128], ynb[:, ch * 128:(ch + 1) * 128], ["ynb"], ["ptr"])
                            cp("scalar", ys[:, :, cl * 128:(cl + 1) * 128], ptr[:, 0:512].rearrange("p (j t) -> p j t", j=4),
                               ["ptr"], [(kys, cl)])
                            if cl == 0:
                                dma("sync", DA(mixT_d, 1024 * TP + ti * TS, [[TP, 128], [128 * TP, 4], [1, TS]]), ys[:],
                                    [(kys, q) for q in range(3)], [("mixT", 2)])

                    stage_a(0)
                    for n_ in range(NCH):
                        if n_ + 1 < NCH:
                            stage_a(n_ + 1)
                        stage_b(n_)
            S.barrier()

        def phase_p4(l):
            with contextlib.ExitStack() as st:
                KT2 = alloc(st, "KT2", [128, 2, TP], BF16)
                Vp = alloc(st, "Vp", [128, NCH, 2, 192], BF16)
                vmask = alloc(st, "vmask", [128, 64], BF16)
                QT = [alloc(st, "QT", [128, 4, TS], BF16) for _ in range(2)]
                sg = [alloc(st, "sga", [128, 4, TS], BF16) for _ in range(2)]
                P = [alloc(st, "P", [128, 2, TS], BF16) for _ in range(3)]
                rec = alloc(st, "rec", [128, TS], F32)
                yt = alloc(st, "yt", [128, TS], F32)
                ost = [alloc(st, "osta", [128, 4, TS], BF16) for _ in range(2)]
                Sp = [palloc(st, "Sp", [128, 2, 512], F32) for _ in range(2)]
                Oe = [palloc(st, "Oe", [128, 512], F32) for _ in range(2)]
                Oo = [palloc(st, "Oo", [128, 512], F32) for _ in range(2)]
                dma("sync", vmask[:], c_vmask_d.ap(), (), ["vmask"])
                for g in range(2):
                    for hs in range(2):
                        dma("sync", KT2[hs * 64:(hs + 1) * 64, g, :], DA(kT_d, g * 64 * TP, [[TP, 64], [1, TP]]), ["kT"], ["KT2"])
                memset("gpsimd", Vp[:], 1.0, ["Vp"])
                for c3 in range(3):
                    for g in range(2):
                        dma("sync", Vp[:, c3 * 11:(c3 + 1) * 11, g, 64:128],
                            DA(vtm_d, c3 * 11 * 128 * 128 + g * 64, [[128, 128], [128 * 128, 11], [1, 64]]), ["vtm"], ["Vp"])
                for g in range(2):
                    for o0 in (0, 128):
                        cp("gpsimd", Vp[:, 0, g, o0:o0 + 64], vmask[:], ["vmask"], ["Vp"])
                npair = 0
                nS = [0]

                def p4_load(i):
                    dma("sync", QT[i % 2][:], DA(qT_d, i * TS, [[TP, 128], [128 * TP, 4], [1, TS]]), ["qT"], ["QT%d" % (i % 2)])
                    dma("sync", sg[i % 2][:], DA(fm_d, 3072 * TW + 8 + i * TS, [[TW, 128], [128 * TW, 4], [1, TS]]),
                        [("fm", 6)], ["sga%d" % (i % 2)])

                for i in range(NT):
                    t0 = i * TS
                    q_ = QT[i % 2]
                    kq = "QT%d" % (i % 2)
                    s_ = sg[i % 2]
                    ksg = "sga%d" % (i % 2)
                    o_ = ost[i % 2]
                    ko = "osta%d" % (i % 2)
                    if i == 0:
                        p4_load(0)
                    if i + 1 < NT:
                        p4_load(i + 1)
                    for j in range(4):
                        g = j // 2
                        oe = Oe[npair % 2]
                        oo = Oo[npair % 2]
                        koe = "Oe%d" % (npair % 2)
                        koo = "Oo%d" % (npair % 2)
                        npair += 1
                        pend = {}

                        def qk(kc, g=g, j=j, q_=q_, kq=kq, pend=pend):
                            n_ = nS[0]
                            nS[0] += 1
                            sp = Sp[n_ % 2]
                            ksp = "Sp%d" % (n_ % 2)
                            pb = P[n_ % 3]
                            kpb = "P%d" % (n_ % 3)
                            kcols = slice(kc * 128, (kc + 1) * 128)
                            mm(sp[:, 0, 0:TS], KT2[0:64, g, kcols], q_[0:64, j, :], True, True, ["KT2", kq], [ksp])
                            mm(sp[:, 1, 0:TS], KT2[64:128, g, kcols], q_[64:128, j, :], True, True, ["KT2", kq], [ksp])
                            act(pb[:, :, :], sp[:, :, 0:TS], AF.Exp, [ksp], [kpb])
                            pend[kc] = (pb, kpb)

                        def pvm(kc, g=g, oe=oe, oo=oo, koe=koe, koo=koo, pend=pend):
                            pb, kpb = pend.pop(kc)
                            mm(oe[:, 0:TS], Vp[:, kc, g, 64:192], pb[:, 0, :], kc == 0, kc == NCH - 1, ["Vp", kpb], [koe])
                            mm(oo[:, 0:TS], Vp[:, kc, g, 0:128], pb[:, 1, :], kc == 0, kc == NCH - 1, ["Vp", kpb], [koo])

                        qk(0)
                        for kc in range(NCH):
                            if kc + 1 < NCH:
                                qk(kc + 1)
                            pvm(kc)
                        recip(rec[0:64, :], oe[64:128, 0:TS], [koe], [("rec", 0)])
                        recip(rec[64:128, :], oo[0:64, 0:TS], [koo], [("rec", 1)])
                        tt("vector", yt[0:64, :], oe[0:64, 0:TS], rec[0:64, :], ALU.mult, [koe, ("rec", 0)], [("yt", 0)])
                        tt("vector", yt[64:128, :], oo[64:128, 0:TS], rec[64:128, :], ALU.mult, [koo, ("rec", 1)], [("yt", 1)])
                        tt("gpsimd", o_[:, j, :], yt[:], s_[:, j, :], ALU.mult, [("yt", 0), ("yt", 1), ksg], [(ko, j)])
                    dma("sync", DA(mixT_d, 1536 * TP + t0, [[TP, 128], [128 * TP, 4], [1, TS]]), o_[:],
                        [(ko, j) for j in range(4)], [("mixT", 3)])
            S.barrier()

        def phase_p5(l, last):
            with contextlib.ExitStack() as st:
                wo = alloc(st, "wo", [128, 16, D], BF16)
                mx = [alloc(st, "mx", [128, 16, TS], BF16) for _ in range(2)]
                hc = [alloc(st, "hc5", [128, D], F32) for _ in range(4)]
                ho = [alloc(st, "ho5", [128, D], F32) for _ in range(2)]
                po = [[palloc(st, "po", [128, 512], F32) for _ in range(2)] for _ in range(2)]
                for hh in range(2):
                    dma("gpsimd", wo[:, :, hh * 512:(hh + 1) * 512],
                        DA(wout_d, l * 2048 * D + hh * 512, [[D, 128], [128 * D, 16], [1, 512]]), (), [("wo", hh)])
                def p5_load(i):
                    for q4 in range(4):
                        dma("sync", mx[i % 2][:, q4 * 4:(q4 + 1) * 4, :],
                            DA(mixT_d, q4 * 512 * TP + i * TS, [[TP, 128], [128 * TP, 4], [1, TS]]), [("mixT", q4)], ["mx%d" % (i % 2)])

                def p5_hload(c):
                    if c >= NCH or (last and c == 0):
                        return
                    hcb = hc[c % 4]
                    khc = "hc5%d" % (c % 4)
                    if l == 0 and c == 0:
                        memset("gpsimd", hcb[:], 0.0, [khc])
                        dma("sync", hcb[NPAD:128, :], meta_d.ap(), (), [khc])
                    elif l == 0:
                        dma("sync", hcb[:], x_d.ap()[(c - 1) * 128:c * 128, :], (), [khc])
                    else:
                        dma("sync", hcb[:], hbuf_d.ap()[c * 128:(c + 1) * 128, :], ["hbuf", "hbuf_pad"], [khc])

                p5_load(0)
                p5_hload(0)
                p5_hload(1)
                for i in range(NT):
                    t0 = i * TS
                    m_ = mx[i % 2]
                    km = "mx%d" % (i % 2)
                    if i + 1 < NT:
                        p5_load(i + 1)
                    for cl in range(3):
                        c = 3 * i + cl
                        p5_hload(c + 2)
                        if last and c == 0:
                            continue
                        hcb = hc[c % 4]
                        khc = "hc5%d" % (c % 4)
                        hob = ho[c % 2]
                        kho = "ho5%d" % (c % 2)
                        for hh in range(2):
                            p = po[c % 2][hh]
                            kp = ("po", c % 2, hh)
                            for ec in range(16):
                                mm(p[:, :], m_[:, ec, cl * 128:(cl + 1) * 128], wo[:, ec, hh * 512:(hh + 1) * 512],
                                   ec == 0, ec == 15, [km, ("wo", hh)], [kp])
                            tt("vector", hob[:, hh * 512:(hh + 1) * 512], p[:, :], hcb[:, hh * 512:(hh + 1) * 512], ALU.add,
                               [kp, khc], [(kho, hh)])
                        if last:
                            o = dma("sync", out_d.ap()[(c - 1) * 128:c * 128, :], hob[:], [(kho, 0), (kho, 1)], ["out"])
                            final_ops.append(o)
                        elif c == 0:
                            dma("sync", hbuf_d.ap()[NPAD:128, :], hob[NPAD:128, :], [(kho, 0), (kho, 1)], ["hbuf"])
                        else:
                            dma("sync", hbuf_d.ap()[c * 128:(c + 1) * 128, :], hob[:], [(kho, 0), (kho, 1)], ["hbuf"])
            S.barrier()

        for l in range(n_layers):
            if "0" in phases:
                phase_p0(l)
            if "1" in phases or "2" in phases:
                phase_p12(l)
            if "3" in phases:
                phase_p3(l)
            if "4" in phases:
                phase_p4(l)
            if "5" in phases:
                phase_p5(l, l == n_layers - 1)
        final_ops[:] = [o for o in final_ops if not o.skipped]
        S.limit = None
        print("n_ops", len(S.ops))
        if not final_ops:
            zo = dma("sync", out_d.ap()[0:128, :], zf[:], ["zf"], ["out"])
            final_ops.append(zo)
        S.emit(nc, final_ops)
    return nc


_PARAM_NAMES = ("meta_tokens", "norm_w", "w_in", "w_out", "pool_w", "pool_scale", "fourier_w", "conv_w", "conv_b",
                "dt_bias", "a_log", "d_skip", "ssd_norm_w", "q_norm_w", "k_norm_w")


def make_in_maps(inputs, cores):
    consts = host_consts()
    shared = {k: np.ascontiguousarray(np.asarray(inputs[k], dtype=np.float32)) for k in _PARAM_NAMES}
    shared.update(consts)
    x = np.asarray(inputs["x"], dtype=np.float32)
    maps = []
    for b in cores:
        m = dict(shared)
        m["x"] = np.ascontiguousarray(x[b])
        maps.append(m)
    return maps


def kernel(**inputs):
    nc = build()
    in_maps = make_in_maps(inputs, list(range(8)))
    res = run_bass_kernel_spmd(nc, in_maps, core_ids=list(range(8)))
    return np.stack([np.asarray(r["out"], dtype=np.float32) for r in res.results], axis=0)
```

```python
import contextlib
import numpy as np
import ml_dtypes
import concourse.bass as bass
import concourse.mybir as mybir
from concourse.bass_utils import run_bass_kernel_spmd

F32 = mybir.dt.float32
BF16 = mybir.dt.bfloat16
AF = mybir.ActivationFunctionType
ALU = mybir.AluOpType
AX = mybir.AxisListType

D = 1024
NTOK = 4096
NMETA = 16
L = NTOK + NMETA
NPAD = 112
TP = 4224
NCH = 33
NT = 11
TS = 384
TW = TP + 16
DEPTH = 4
INC = 4880
EPS = 1e-6
NPC = 3
PCC = 11

ENGS = ("tensor", "vector", "scalar", "gpsimd", "sync")
N_DMA_SEMS = 28
N_SW_SEMS = 8
PSUM_KEYS = frozenset(["pT", "pf0", "pf1", "pz", "pq", "pkv", "ptq", "py", "acc", "pcv0", "pcv1", "ptr", "pseg", "pyd",
                       "pyo", "psn", "Sp0", "Sp1", "Oe0", "Oe1", "Oo0", "Oo1", "po"])


class Op:
    __slots__ = ("eng", "fn", "deps", "is_dma", "needed", "sem", "val", "nobar", "skipped")

    def __init__(self, eng, fn, deps, is_dma):
        self.eng = eng
        self.fn = fn
        self.deps = deps
        self.is_dma = is_dma
        self.needed = False
        self.sem = None
        self.val = 0
        self.nobar = False
        self.skipped = False


class Sched:
    def __init__(self):
        self.ops = []
        self.last_w = {}
        self.readers = {}
        self.last_eng = {}
        self.dmas = []
        self.limit = None

    def _add(self, eng, fn, reads, writes, is_dma, extra=()):
        ex = tuple(k for k in reads if (k[0] if isinstance(k, tuple) else k) in PSUM_KEYS)
        if ex:
            writes = tuple(writes) + tuple(k for k in ex if k not in writes)
        deps = list(extra)
        for k in reads:
            w = self.last_w.get(k)
            if w is not None:
                deps.append(w)
        for k in writes:
            w = self.last_w.get(k)
            if w is not None:
                deps.append(w)
            deps.extend(self.readers.get(k, ()))
        op = Op(eng, fn, deps, is_dma)
        if self.limit is not None and len(self.ops) >= self.limit:
            op.skipped = True
            return op
        self.ops.append(op)
        for k in writes:
            self.last_w[k] = op
            self.readers[k] = []
        for k in reads:
            if k in writes:
                continue
            lst = self.readers.setdefault(k, [])
            if not is_dma:
                lst[:] = [o for o in lst if o.is_dma or o.eng != eng]
            lst.append(op)
        if is_dma:
            self.dmas.append(op)
        else:
            self.last_eng[eng] = op
        return op

    def op(self, eng, fn, reads=(), writes=()):
        return self._add(eng, fn, tuple(reads), tuple(writes), False)

    def dma(self, eng, fn, reads=(), writes=(), nobar=False):
        o = self._add(eng, fn, tuple(reads), tuple(writes), True)
        o.nobar = nobar
        return o

    def barrier(self):
        deps = [o for o in self.last_eng.values()] + [o for o in self.dmas if not o.nobar]
        self.dmas = [o for o in self.dmas if o.nobar]
        for e in ENGS:
            self._add(e, lambda eng: eng.nop(), (), (), False, extra=deps)

    def emit(self, nc, final_wait_ops=()):
        ops = self.ops
        for o in ops:
            for d in o.deps:
                if d.eng == "tensor" and o.eng == "tensor" and not d.is_dma and not o.is_dma:
                    continue
                d.needed = True
        for o in final_wait_ops:
            o.needed = True
        with contextlib.ExitStack() as st:
            esem = {e: st.enter_context(nc.semaphore("s_" + e)) for e in ENGS}
            dsem = [st.enter_context(nc.semaphore("d_%d" % i)) for i in range(N_DMA_SEMS)]
            ecnt = {e: 0 for e in ENGS}
            dcnt = [0] * N_DMA_SEMS
            dnext = 0
            prev_on_dsem = [None] * N_DMA_SEMS
            swnext = 0
            for o in ops:
                if o.is_dma:
                    if o.eng == "gpsimd":
                        i = N_DMA_SEMS - N_SW_SEMS + swnext
                        swnext = (swnext + 1) % N_SW_SEMS
                    else:
                        i = dnext
                        dnext = (dnext + 1) % (N_DMA_SEMS - N_SW_SEMS)
                    dcnt[i] += 16
                    o.sem = ("d", i)
                    o.val = dcnt[i]
                    if prev_on_dsem[i] is not None:
                        o.deps.append(prev_on_dsem[i])
                    prev_on_dsem[i] = o
                elif o.needed:
                    ecnt[o.eng] += 1
                    o.sem = ("e", o.eng)
                    o.val = ecnt[o.eng]
            per = {e: [o for o in ops if o.eng == e] for e in ENGS}
            final = list(final_wait_ops)
            blk = st.enter_context(nc.Block())

            def semh(s):
                return esem[s[1]] if s[0] == "e" else dsem[s[1]]

            def run(e, eng):
                seen = {}
                for o in per[e]:
                    need = {}
                    for d in o.deps:
                        if d.sem is None:
                            continue
                        if (not d.is_dma) and (not o.is_dma) and d.eng == "tensor" and e == "tensor":
                            continue
                        if need.get(d.sem, 0) < d.val:
                            need[d.sem] = d.val
                    for s, v in need.items():
                        if seen.get(s, 0) < v:
                            eng.wait_ge(semh(s), v)
                            seen[s] = v
                    ins = o.fn(eng)
                    if o.is_dma:
                        ins.then_inc(semh(o.sem), 16)
                    elif o.sem is not None:
                        ins.then_inc(semh(o.sem), 1)
                if e == "sync":
                    for o in final:
                        if seen.get(o.sem, 0) < o.val:
                            eng.wait_ge(semh(o.sem), o.val)
                            seen[o.sem] = o.val

            @blk.tensor
            def _(eng):
                run("tensor", eng)

            @blk.vector
            def _(eng):
                run("vector", eng)

            @blk.scalar
            def _(eng):
                run("scalar", eng)

            @blk.gpsimd
            def _(eng):
                run("gpsimd", eng)

            @blk.sync
            def _(eng):
                run("sync", eng)
        return ecnt, dcnt


_CONSTS = None


def host_consts():
    global _CONSTS
    if _CONSTS is not None:
        return _CONSTS
    bf = ml_dtypes.bfloat16
    c = {}
    c["c_ident"] = np.eye(128, dtype=np.float32).astype(bf)
    tphys = np.arange(TP)
    tlog = tphys - NPAD
    row = np.where(tlog >= NMETA, (tlog - NMETA) // 64, 0).astype(np.float64)
    col = np.where(tlog >= NMETA, (tlog - NMETA) % 64, 0).astype(np.float64)
    freqs = (10000.0 ** (-np.arange(0, 32, 2, dtype=np.float32) / 32.0)).astype(np.float32)
    ang = np.concatenate([row[:, None].astype(np.float32) * freqs[None], col[:, None].astype(np.float32) * freqs[None]],
                         axis=-1).astype(np.float32)
    cos2 = np.repeat(np.cos(ang), 2, axis=-1).astype(np.float32)
    sin2 = np.repeat(np.sin(ang), 2, axis=-1).astype(np.float32)
    rope = np.stack([cos2, sin2], 0).reshape(2, NCH, 128, 64).transpose(2, 0, 1, 3)
    c["c_rope"] = np.ascontiguousarray(rope, dtype=np.float32)
    k = np.arange(128)[:, None]
    l_ = np.arange(128)[None, :]
    trif = (k <= l_).astype(np.float32)
    trib = (k >= l_).astype(np.float32)
    ones = np.ones((128, 128), np.float32)
    c["c_ssd_f32"] = np.ascontiguousarray(np.stack([trif, trib, ones], 1), dtype=np.float32)
    t1f = (k > l_).astype(np.float32)
    t1b = (k < l_).astype(np.float32)
    c["c_ssd_bf"] = np.ascontiguousarray(np.stack([trif, trib, t1f, t1b, trif, trib], 1)).astype(bf)
    m = np.arange(512)
    a = 2.0 * np.pi * ((m[:, None] * m[None, :]) % 512) / 512.0
    cc = np.cos(a).reshape(4, 128, 512).transpose(1, 0, 2)
    sc = np.sin(a).reshape(4, 128, 512).transpose(1, 0, 2)
    c["c_dft_c"] = np.ascontiguousarray(np.stack([cc, sc], 1)).astype(bf)
    scale = 1.0 / np.sqrt(float(L) * 512.0)
    tl = (np.arange(TP) - NPAD).astype(np.int64)
    valid = (tl >= 0)
    prod = (np.maximum(tl, 0)[:, None] * np.maximum(tl, 0)[None, :]) % L
    angp = (2.0 * np.pi / L) * prod.astype(np.float64)
    vm = (valid[:, None] & valid[None, :])
    tabs = []
    for fn_, sgn in ((np.cos, 1.0), (np.sin, -1.0)):
        t_ = (fn_(angp) * (sgn * scale) * vm).astype(np.float32).astype(bf)
        t_ = t_.reshape(NPC, PCC, 128, NT, TS)
        t_ = t_.transpose(3, 0, 2, 1, 4)
        tabs.append(np.ascontiguousarray(t_).reshape(NT, NPC, 128, PCC * TS))
    c["c_tab"] = np.ascontiguousarray(np.stack(tabs, 0))
    pinv = np.zeros((128, 4, 2, 8), np.float32)
    for g, w in enumerate((2, 4, 8, 16)):
        for side in range(2):
            for q in range(8):
                t = q if side == 0 else L - 8 + q
                lo = min(max(t - w // 2, 0), L)
                hi = min(max(t - w // 2 + w, 0), L)
                pinv[:, g, side, q] = 1.0 / float(hi - lo)
    c["c_pool"] = pinv
    vmask = np.ones((128, 64), np.float32)
    vmask[:NPAD] = 0.0
    c["c_vmask"] = vmask.astype(bf)
    _CONSTS = c
    return c


def build(n_layers=DEPTH, dbg=False, phases="012345", limit=None):
    nc = bass.Bass("TRN2", target_bir_lowering=False)
    S = Sched()
    S.limit = limit
    uid = [0]

    def din(name, shape, dt):
        return nc.dram_tensor(name, list(shape), dt, kind="ExternalInput")

    x_d = din("x", [NTOK, D], F32)
    meta_d = din("meta_tokens", [NMETA, D], F32)
    normw_d = din("norm_w", [DEPTH, D], F32)
    win_d = din("w_in", [DEPTH, D, INC], F32)
    wout_d = din("w_out", [DEPTH, 2048, D], F32)
    poolw_d = din("pool_w", [DEPTH, 4, 128, 128], F32)
    pools_d = din("pool_scale", [DEPTH, 512], F32)
    fw_d = din("fourier_w", [DEPTH, 512, 512], F32)
    convw_d = din("conv_w", [DEPTH, 4, 1024], F32)
    convb_d = din("conv_b", [DEPTH, 1024], F32)
    dtb_d = din("dt_bias", [DEPTH, 2, 8], F32)
    alog_d = din("a_log", [DEPTH, 2, 8], F32)
    dskip_d = din("d_skip", [DEPTH, 8], F32)
    ssdnw_d = din("ssd_norm_w", [DEPTH, 512], F32)
    qnw_d = din("q_norm_w", [DEPTH, 64], F32)
    knw_d = din("k_norm_w", [DEPTH, 64], F32)
    c_ident_d = din("c_ident", [128, 128], BF16)
    c_rope_d = din("c_rope", [128, 2, NCH, 64], F32)
    c_ssdf_d = din("c_ssd_f32", [128, 3, 128], F32)
    c_ssdb_d = din("c_ssd_bf", [128, 6, 128], BF16)
    c_dftc_d = din("c_dft_c", [128, 2, 4, 512], BF16)
    c_tab_d = din("c_tab", [2, NT, NPC, 128, PCC * TS], BF16)
    c_pool_d = din("c_pool", [128, 4, 2, 8], F32)
    c_vmask_d = din("c_vmask", [128, 64], BF16)
    out_d = nc.dram_tensor("out", [NTOK, D], F32, kind="ExternalOutput")
    skind = "ExternalOutput" if dbg else "Internal"

    def dscr(name, shape, dt):
        return nc.dram_tensor(name, list(shape), dt, kind=skind)

    hbuf_d = dscr("hbuf", [TP, D], F32)
    fm_d = dscr("fm", [3584, TW], BF16)
    qT_d = dscr("qT", [512, TP], BF16)
    kT_d = dscr("kT", [128, TP], BF16)
    vtm_d = dscr("vtm", [TP, 128], BF16)
    sz_d = dscr("sz", [TP, 512], BF16)
    dtr_d = dscr("dtr", [TP, 16], F32)
    yf_d = dscr("yf", [TP, 512], F32)
    mixT_d = dscr("mixT", [2048, TP], BF16)

    def DA(h, off, dims):
        return bass.AP(h, off, [list(d) for d in dims])

    def mm(out, lhsT, rhs, start, stop, r, w):
        return S.op("tensor", lambda e: e.matmul(out, lhsT=lhsT, rhs=rhs, start=start, stop=stop), r, w)

    def act(out, in_, func, r, w, bias=None, scale=None, accum_out=None, eng="scalar"):
        kw = {}
        if bias is not None:
            kw["bias"] = bias
        if scale is not None:
            kw["scale"] = scale
        if accum_out is not None:
            kw["accum_out"] = accum_out
        return S.op("scalar", lambda e: e.activation(out=out, in_=in_, func=func, **kw), r, w)

    def tt(eng, out, in0, in1, op, r, w):
        return S.op(eng, lambda e: e.tensor_tensor(out=out, in0=in0, in1=in1, op=op), r, w)

    def tsc(eng, out, in0, s1, s2, op0, op1, r, w):
        if op1 is None:
            return S.op(eng, lambda e: e.tensor_scalar(out=out, in0=in0, scalar1=s1, scalar2=None, op0=op0), r, w)
        return S.op(eng, lambda e: e.tensor_scalar(out=out, in0=in0, scalar1=s1, scalar2=s2, op0=op0, op1=op1), r, w)

    def stt(out, in0, scalar, in1, op0, op1, r, w):
        return S.op("vector", lambda e: e.scalar_tensor_tensor(out=out, in0=in0, scalar=scalar, in1=in1, op0=op0, op1=op1), r, w)

    def cp(eng, out, in_, r, w):
        if eng == "scalar":
            return S.op("scalar", lambda e: e.activation(out=out, in_=in_, func=AF.Copy), r, w)
        return S.op(eng, lambda e: e.tensor_copy(out=out, in_=in_), r, w)

    def recip(out, in_, r, w):
        return S.op("vector", lambda e: e.reciprocal(out=out, in_=in_), r, w)

    def memset(eng, ap, val, w):
        return S.op(eng, lambda e: e.memset(ap, val), (), w)

    def dma(eng, out, in_, r, w, nobar=False, slow=False):
        if slow:
            return S.dma(eng, lambda e: e.dma_start(out=out, in_=in_, allow_slow_non_contiguous=True), r, w, nobar=nobar)
        return S.dma(eng, lambda e: e.dma_start(out=out, in_=in_), r, w, nobar=nobar)

    final_ops = []

    with contextlib.ExitStack() as top:
        def alloc(st, name, shape, dt):
            uid[0] += 1
            return st.enter_context(nc.sbuf_tensor("%s_%d" % (name, uid[0]), list(shape), dt))

        def palloc(st, name, shape, dt):
            uid[0] += 1
            return st.enter_context(nc.psum_tensor("%s_%d" % (name, uid[0]), list(shape), dt))

        ident = alloc(top, "ident", [128, 128], BF16)
        zero = alloc(top, "zero", [128, 16], BF16)
        dma("sync", ident[:], c_ident_d.ap(), (), ["ident"])
        memset("gpsimd", zero[:], 0.0, ["zero"])
        for r0, nj in ((0, 4), (2048, 8)):
            for c0 in (0, 8 + TP):
                dma("sync", DA(fm_d, r0 * TW + c0, [[TW, 128], [128 * TW, nj], [1, 8]]),
                    zero[:, 0:8].unsqueeze(1).broadcast_to([128, nj, 8]), ["zero"], ["fm_margin"])
        mhalf = alloc(top, "mhalf", [128, 16], F32)
        memset("gpsimd", mhalf[:], -0.5, ["mhalf"])
        zf = alloc(top, "zf", [128, D], F32)
        memset("gpsimd", zf[:], 0.0, ["zf"])
        dma("sync", hbuf_d.ap()[0:NPAD, :], zf[0:NPAD, :], ["zf"], ["hbuf_pad"])
        S.barrier()

        def identr(o, i_, r, w):
            return S.op("tensor", lambda e: e.transpose(o, i_, ident[:]), list(r) + ["ident"], w)

        def phase_p0(l):
            with contextlib.ExitStack() as st:
                W = alloc(st, "W", [128, 8, INC], BF16)
                pieces = ((0, 1024, 0), (1024, 1024, 1024), (2048, 1024, 2048), (4368, 512, 3072),
                          (3072, 512, 3584), (3600, 512, 4096), (4112, 256, 4608), (3584, 16, 4864))
                for (s0, n, d0) in pieces:
                    dma("gpsimd", W[:, :, d0:d0 + n],
                        DA(win_d, l * D * INC + s0, [[INC, 128], [128 * INC, 8], [1, n]]), (), [("W", d0)])
                wkeys = [("W", p[2]) for p in pieces]
                hc = [alloc(st, "hc", [128, D], F32) for _ in range(3)]
                junk = alloc(st, "junk", [128, D], BF16)
                nwb = alloc(st, "nwb", [128, D], F32)
                hnb = [alloc(st, "hnb", [128, D], BF16) for _ in range(3)]
                hnT = [alloc(st, "hnT", [128, 8, TS], BF16) for _ in range(2)]
                stt_t = [alloc(st, "stat", [128, 48], F32) for _ in range(2)]
                fst = [alloc(st, "fst", [128, 4, TS], BF16) for _ in range(3)]
                zst = [alloc(st, "zst", [128, 3, 512], BF16) for _ in range(2)]
                qTst = [alloc(st, "qTst", [128, 4, TS], BF16) for _ in range(2)]
                kTst = [alloc(st, "kTst", [128, TS], BF16) for _ in range(2)]
                vst = [alloc(st, "vst", [128, 3, 128], BF16) for _ in range(2)]
                dtst = [alloc(st, "dtst", [128, 3, 16], F32) for _ in range(2)]
                sq = alloc(st, "sq", [128, 640], F32)
                qn = alloc(st, "qn", [128, 640], F32)
                qw = alloc(st, "qw", [128, 640], F32)
                m1 = alloc(st, "m1", [128, 640], F32)
                m2 = alloc(st, "m2", [128, 640], F32)
                qr = [alloc(st, "qr", [128, 640], BF16) for _ in range(2)]
                wqk = alloc(st, "wqk", [128, 640], F32)
                rope = alloc(st, "rope", [128, 2, NCH, 64], F32)
                pT = palloc(st, "pT", [128, 1024], BF16)
                pf = [palloc(st, "pf", [128, 512], F32) for _ in range(2)]
                pz = palloc(st, "pz", [128, 512], F32)
                pq = palloc(st, "pq", [128, 512], F32)
                pkv = palloc(st, "pkv", [128, 512], F32)
                ptq = palloc(st, "ptq", [128, 1024], BF16)

                dma("sync", rope[:], c_rope_d.ap(), (), ["rope"])
                dma("sync", nwb[:], DA(normw_d, l * D, [[0, 128], [1, D]]), (), ["nwb"])
                dma("sync", wqk[:, 0:512].rearrange("p (h d) -> p h d", h=8),
                    DA(qnw_d, l * 64, [[0, 128], [0, 8], [1, 64]]), (), ["wqk"])
                dma("sync", wqk[:, 512:640].rearrange("p (h d) -> p h d", h=2),
                    DA(knw_d, l * 64, [[0, 128], [0, 2], [1, 64]]), (), ["wqk"])
                tsc("gpsimd", wqk[:, 0:512], wqk[:, 0:512], 0.125, None, ALU.mult, None, ["wqk"], ["wqk"])

                nfm = [0]

                def norm_load(c):
                    hcb = hc[c % 3]
                    khc = "hc%d" % (c % 3)
                    if l == 0 and c == 0:
                        memset("gpsimd", hcb[:], 0.0, [khc])
                        dma("sync", hcb[NPAD:128, :], meta_d.ap(), (), [khc])
                    elif l == 0:
                        dma("sync", hcb[:], x_d.ap()[(c - 1) * 128:c * 128, :], (), [khc])
                    else:
                        dma("sync", hcb[:], hbuf_d.ap()[c * 128:(c + 1) * 128, :], ["hbuf", "hbuf_pad"], [khc])

                def norm_front(c):
                    hcb = hc[c % 3]
                    khc = "hc%d" % (c % 3)
                    sta = stt_t[c % 2]
                    ks = "stat%d" % (c % 2)
                    act(junk[:], hcb[:], AF.Square, [khc], ["junk", (ks, 0)], accum_out=sta[:, 0:1])
                    tsc("vector", sta[:, 1:2], sta[:, 0:1], 1.0 / D, EPS, ALU.mult, ALU.add, [(ks, 0)], [(ks, 1)])
                    tt("gpsimd", sta[:, 3:4], sta[:, 1:2], mhalf[:, 0:1], ALU.pow, [(ks, 1), "mhalf"], [(ks, 3)])
                    stt(hnb[c % 3][:], hcb[:], sta[:, 3:4], nwb[:], ALU.mult, ALU.mult, [khc, (ks, 3), "nwb"], ["hnb%d" % (c % 3)])

                def norm_back(c):
                    i_, cl = c // 3, c % 3
                    hT = hnT[i_ % 2]
                    kh = "hnT%d" % (i_ % 2)
                    hb = hnb[c % 3]
                    khb = "hnb%d" % (c % 3)
                    for j in range(8):
                        identr(pT[:, j * 128:(j + 1) * 128], hb[:, j * 128:(j + 1) * 128], [khb], ["pT"])
                    cp("scalar", hT[:, :, cl * 128:(cl + 1) * 128], pT[:].rearrange("p (j t) -> p j t", j=8),
                       ["pT"], [(kh, cl)])

                for c in range(3):
                    norm_load(c)
                for c in range(3):
                    norm_front(c)
                for c in range(3):
                    norm_back(c)
                for i in range(NT):
                    t0 = i * TS
                    hT = hnT[i % 2]
                    kh = "hnT%d" % (i % 2)
                    hkeys = [(kh, 0), (kh, 1), (kh, 2)]
                    if i + 1 < NT:
                        for cl in range(3):
                            norm_load(3 * (i + 1) + cl)
                    def fm_seg(seg, i=i, t0=t0, hT=hT, hkeys=hkeys):
                        fs = fst[seg % 3]
                        kfs = "fst%d" % (seg % 3)
                        gate = seg in (1, 3, 6)
                        for j in range(4):
                            p = pf[nfm[0] % 2]
                            kp = "pf%d" % (nfm[0] % 2)
                            col0 = seg * 512 + j * 128
                            for dk in range(8):
                                mm(p[:, 0:TS], W[:, dk, col0:col0 + 128], hT[:, dk, :], dk == 0, dk == 7,
                                   wkeys + hkeys, [kp])
                            if gate:
                                act(fs[:, j, :], p[:, 0:TS], AF.Silu, [kp], [kfs])
                            else:
                                cp("scalar", fs[:, j, :], p[:, 0:TS], [kp], [kfs])
                            nfm[0] += 1
                        dma("sync", DA(fm_d, seg * 512 * TW + 8 + t0, [[TW, 128], [128 * TW, 4], [1, TS]]), fs[:],
                            [kfs, "fm_margin"], [("fm", seg)])
                        if seg in (0, 2, 4) and i + 1 < NT:
                            norm_front(3 * (i + 1) + seg // 2)
                    zs = zst[i % 2]
                    kzs = "zst%d" % (i % 2)
                    qTs = qTst[i % 2]
                    kqs = "qTst%d" % (i % 2)
                    kTs = kTst[i % 2]
                    kks = "kTst%d" % (i % 2)
                    vs = vst[i % 2]
                    kvs = "vst%d" % (i % 2)
                    dts = dtst[i % 2]
                    kds = "dtst%d" % (i % 2)

                    def qtrans(cl, i=i, qTs=qTs, kqs=kqs, kTs=kTs, kks=kks):
                        c = 3 * i + cl
                        lt = slice(cl * 128, (cl + 1) * 128)
                        qrb = qr[c % 2]
                        kqr = "qr%d" % (c % 2)
                        for j in range(5):
                            identr(ptq[:, j * 128:(j + 1) * 128], qrb[:, j * 128:(j + 1) * 128],
                                   [(kqr, 0), (kqr, 1)], ["ptq"])
                        cp("scalar", qTs[:, :, lt], ptq[:, 0:512].rearrange("p (j t) -> p j t", j=4), ["ptq"], [kqs])
                        cp("vector", kTs[:, lt], ptq[:, 512:640], ["ptq"], [kks])

                    for cl in range(3):
                        c = 3 * i + cl
                        lt = slice(cl * 128, (cl + 1) * 128)
                        if cl >= 1:
                            fm_seg(2 * cl - 2)
                            fm_seg(2 * cl - 1)
                        for dk in range(8):
                            mm(pz[:, :], hT[:, dk, lt], W[:, dk, 3584:4096], dk == 0, dk == 7, wkeys + hkeys, ["pz"])
                        act(zs[:, cl, :], pz[:, :], AF.Silu, ["pz"], [kzs])
                        for dk in range(8):
                            mm(pq[:, :], hT[:, dk, lt], W[:, dk, 4096:4608], dk == 0, dk == 7, wkeys + hkeys, ["pq"])
                        for dk in range(8):
                            mm(pkv[:, 0:272], hT[:, dk, lt], W[:, dk, 4608:4880], dk == 0, dk == 7, wkeys + hkeys, ["pkv"])
                        if cl >= 1:
                            qtrans(cl - 1)
                        cp("vector", vs[:, cl, :], pkv[:, 128:256], ["pkv"], [kvs])
                        cp("vector", dts[:, cl, :], pkv[:, 256:272], ["pkv"], [kds])
                        sta = stt_t[c % 2]
                        ks = "stat%d" % (c % 2)
                        act(sq[:, 0:512], pq[:, :], AF.Square, ["pq"], [("sq", 0)])
                        act(sq[:, 512:640], pkv[:, 0:128], AF.Square, ["pkv"], [("sq", 1)])
                        S.op("vector", lambda e, o=sta[:, 8:18], a=sq[:].rearrange("p (h d) -> p h d", h=10):
                             e.tensor_reduce(out=o, in_=a, axis=AX.X, op=ALU.add), [("sq", 0), ("sq", 1)], [(ks, 8)])
                        tsc("vector", sta[:, 18:28], sta[:, 8:18], 1.0 / 64, EPS, ALU.mult, ALU.add, [(ks, 8)], [(ks, 18)])
                        tt("gpsimd", sta[:, 38:48], sta[:, 18:28], mhalf[:, 0:10], ALU.pow, [(ks, 18), "mhalf"], [(ks, 38)])
                        tt("vector", qn[:, 0:512].rearrange("p (h d) -> p h d", h=8),
                           pq[:, :].rearrange("p (h d) -> p h d", h=8),
                           sta[:, 38:46].unsqueeze(2).broadcast_to([128, 8, 64]), ALU.mult, ["pq", (ks, 38)], [("qn", 0)])
                        tt("vector", qn[:, 512:640].rearrange("p (h d) -> p h d", h=2),
                           pkv[:, 0:128].rearrange("p (h d) -> p h d", h=2),
                           sta[:, 46:48].unsqueeze(2).broadcast_to([128, 2, 64]), ALU.mult, ["pkv", (ks, 38)], [("qn", 1)])
                        tt("gpsimd", qw[:], qn[:], wqk[:], ALU.mult, [("qn", 0), ("qn", 1), "wqk"], ["qw"])
                        qw3 = qw[:].rearrange("p (h d) -> p h d", h=10)
                        tt("gpsimd", m1[:].rearrange("p (h d) -> p h d", h=10), qw3,
                           rope[:, 0, c, :].unsqueeze(1).broadcast_to([128, 10, 64]), ALU.mult, ["qw", "rope"], ["m1"])
                        tt("gpsimd", m2[:].rearrange("p (h d) -> p h d", h=10), qw3,
                           rope[:, 1, c, :].unsqueeze(1).broadcast_to([128, 10, 64]), ALU.mult, ["qw", "rope"], ["m2"])
                        qrb = qr[c % 2]
                        kqr = "qr%d" % (c % 2)

                        def pv(tile_, off):
                            return bass.AP(tile_, off, [[640, 128], [2, 320]])
                        tt("vector", pv(qrb, 0), pv(m1, 0), pv(m2, 1), ALU.subtract, ["m1", "m2"], [(kqr, 0)])
                        tt("gpsimd", pv(qrb, 1), pv(m2, 0), pv(m1, 1), ALU.add, ["m1", "m2"], [(kqr, 1)])
                    for seg in (4, 5, 6):
                        fm_seg(seg)
                    if i + 1 < NT:
                        for cl in range(3):
                            norm_back(3 * (i + 1) + cl)
                    qtrans(2)
                    dma("sync", DA(sz_d, t0 * 512, [[512, 128], [128 * 512, 3], [1, 512]]), zs[:], [kzs], ["sz"])
                    dma("sync", DA(qT_d, t0, [[TP, 128], [128 * TP, 4], [1, TS]]), qTs[:], [kqs], ["qT"])
                    dma("sync", DA(kT_d, t0, [[TP, 128], [1, TS]]), kTs[:], [kks], ["kT"])
                    dma("sync", DA(vtm_d, t0 * 128, [[128, 128], [128 * 128, 3], [1, 128]]), vs[:], [kvs], ["vtm"])
                    dma("sync", DA(dtr_d, t0 * 16, [[16, 128], [128 * 16, 3], [1, 16]]), dts[:], [kds], ["dtr"])
            S.barrier()

        def phase_p12(l):
            with contextlib.ExitStack() as st:
                pw = alloc(st, "pw", [128, 4, 128], BF16)
                psc = alloc(st, "psc", [128, 4], F32)
                cinv = alloc(st, "cinv", [128, 4, 2, 8], F32)
                ub = [alloc(st, "ub", [128, 4, 400], BF16) for _ in range(2)]
                sgp = [alloc(st, "sg", [128, 4, TS], BF16) for _ in range(2)]
                T2 = alloc(st, "T2", [128, 4, 400], F32)
                T4 = alloc(st, "T4", [128, 4, 400], F32)
                T8 = alloc(st, "T8", [128, 4, 400], F32)
                T16 = alloc(st, "T16", [128, 4, 400], F32)
                tmp = alloc(st, "ptmp", [128, 4, 8], F32)
                dT = [alloc(st, "dT", [128, 4, TS], BF16) for _ in range(2)]
                ostp = [alloc(st, "ost", [128, 4, TS], BF16) for _ in range(2)]
                Ap = alloc(st, "Ap", [128, NCH, 512], BF16)
                Bp = alloc(st, "Bp", [128, NCH, 512], BF16)
                ring = [alloc(st, "ring", [128, PCC, TS], BF16) for _ in range(4)]
                cs = alloc(st, "dftc", [128, 2, 4, 512], BF16)
                wf = alloc(st, "wf", [128, 4, 512], BF16)
                cw = alloc(st, "cw", [128, 2, 4, 512], BF16)
                uT = [alloc(st, "uT", [128, 4, TS], BF16) for _ in range(2)]
                sg = [alloc(st, "sgf", [128, 4, TS], BF16) for _ in range(2)]
                ost = [alloc(st, "ostf", [128, 4, TS], BF16) for _ in range(2)]
                acc = [palloc(st, "acc", [128, 512], F32) for _ in range(4)]
                py = [palloc(st, "py", [128, 512], F32) for _ in range(4)]
                banks = [(acc[k], ("acc", 0, k)) for k in range(4)] + [(py[g], ("py", g)) for g in range(4)]

                dma("gpsimd", pw[:], DA(poolw_d, l * 4 * 128 * 128, [[128, 128], [128 * 128, 4], [1, 128]]), (), ["pw"])
                dma("sync", psc[:].unsqueeze(2), DA(pools_d, l * 512, [[1, 128], [128, 4], [1, 1]]), (), ["psc"], slow=True)
                dma("sync", cinv[:], c_pool_d.ap(), (), ["cinv"])
                dma("sync", cs[:], c_dftc_d.ap(), (), ["dftc"])
                dma("gpsimd", wf[:], DA(fw_d, l * 512 * 512, [[512, 128], [128 * 512, 4], [1, 512]]), (), ["wf"])
                Ts = (T2, T4, T8, T16)

                def p1_front(i):
                    t0 = i * TS
                    u = ub[i % 2]
                    ku = "ub%d" % (i % 2)
                    d_ = dT[i % 2]
                    kd = "dT%d" % (i % 2)
                    dma("sync", u[:], DA(fm_d, t0, [[TW, 128], [128 * TW, 4], [1, 400]]), [("fm", 0), "fm_margin"], [ku])
                    dma("sync", sgp[i % 2][:], DA(fm_d, 512 * TW + 8 + t0, [[TW, 128], [128 * TW, 4], [1, TS]]), [("fm", 1)],
                        ["sg%d" % (i % 2)])
                    tt("vector", T2[:, :, 0:399], u[:, :, 0:399], u[:, :, 1:400], ALU.add, [ku], ["T2"])
                    tt("gpsimd", T4[:, 1:4, 0:397], T2[:, 1:4, 0:397], T2[:, 1:4, 2:399], ALU.add, ["T2"], ["T4"])
                    tt("vector", T8[:, 2:4, 0:393], T4[:, 2:4, 0:393], T4[:, 2:4, 4:397], ALU.add, ["T4"], ["T8"])
                    tt("gpsimd", T16[:, 3, 0:385], T8[:, 3, 0:385], T8[:, 3, 8:393], ALU.add, ["T8"], ["T16"])
                    for g, w_ in enumerate((2, 4, 8, 16)):
                        half = w_ // 2
                        Pg = Ts[g]
                        kP = ("T2", "T4", "T8", "T16")[g]
                        o8 = 8 - half
                        stt(d_[:, g, :], Pg[:, g, o8:o8 + TS], 1.0 / w_, u[:, g, 8:8 + TS], ALU.mult, ALU.subtract,
                            [kP, ku], [(kd, g)])
                        for side, tile_i, m0 in ((0, 0, NPAD), (1, NT - 1, TS - 8)):
                            if i != tile_i:
                                continue
                            tt("vector", tmp[:, g, :], Pg[:, g, o8 + m0:o8 + m0 + 8], cinv[:, g, side, :], ALU.mult,
                               [kP, "cinv"], [("ptmp", g)])
                            tt("vector", d_[:, g, m0:m0 + 8], tmp[:, g, :], u[:, g, 8 + m0:8 + m0 + 8], ALU.subtract,
                               [("ptmp", g), ku], [(kd, g)])

                def p1_back(i):
                    t0 = i * TS
                    d_ = dT[i % 2]
                    kd = "dT%d" % (i % 2)
                    s_ = sgp[i % 2]
                    ksg = "sg%d" % (i % 2)
                    o_ = ostp[i % 2]
                    ko = "ost%d" % (i % 2)
                    for g in range(4):
                        mm(py[g][:, 0:TS], pw[:, g, :], d_[:, g, :], True, True, ["pw", (kd, g)], [("py", g)])
                    for g in range(4):
                        stt(o_[:, g, :], py[g][:, 0:TS], psc[:, g:g + 1], s_[:, g, :], ALU.mult, ALU.mult,
                            [("py", g), "psc", ksg], [(ko, g)])
                    dma("sync", DA(mixT_d, t0, [[TP, 128], [128 * TP, 4], [1, TS]]), o_[:],
                        [(ko, g) for g in range(4)], [("mixT", 0)])

                n = 0
                for tb in range(2):
                    for ci in range(4):
                        p, kp = banks[n % 8]
                        for mk in range(4):
                            mm(p[:, :], cs[:, tb, mk, ci * 128:(ci + 1) * 128], wf[:, mk, :], mk == 0, mk == 3,
                               ["dftc", "wf"], [kp])
                        cp("scalar" if n % 2 else "vector", cw[:, tb, ci, :], p[:, :], [kp], [("cw", tb, ci)])
                        n += 1
                cwk = [("cw", tb, ci) for tb in range(2) for ci in range(4)]
                for i in range(NT):
                    t0 = i * TS
                    u = uT[i % 2]
                    ku = "uT%d" % (i % 2)
                    dma("sync", u[:], DA(fm_d, 1024 * TW + 8 + t0, [[TW, 128], [128 * TW, 4], [1, TS]]), [("fm", 2)], [ku])
                    for cl in range(3):
                        c = 3 * i + cl
                        for tb, dst in ((0, Ap), (1, Bp)):
                            p, kp = banks[(tb * 3 + cl) % 8]
                            for ck in range(4):
                                mm(p[:, :], u[:, ck, cl * 128:(cl + 1) * 128], cw[:, tb, ck, :], ck == 0, ck == 3,
                                   [ku] + cwk, [kp])
                            cp("scalar" if tb else "vector", dst[:, c, :], p[:, :], [kp], [("ApBp", c)])
                apk = [("ApBp", c) for c in range(NCH)]
                npiece = 0
                for kt in range(NT):
                    t0 = kt * TS
                    s_ = sg[kt % 2]
                    ksg = "sgf%d" % (kt % 2)
                    o_ = ost[kt % 2]
                    ko = "ostf%d" % (kt % 2)
                    dma("sync", s_[:], DA(fm_d, 1536 * TW + 8 + t0, [[TW, 128], [128 * TW, 4], [1, TS]]), [("fm", 3)], [ksg])
                    p1_front(kt)
                    for pc in range(NPC):
                        for tb in range(2):
                            rb = ring[npiece % 4]
                            kr = "ring%d" % (npiece % 4)
                            npiece += 1
                            off = (((tb * NT + kt) * NPC + pc) * 128) * (PCC * TS)
                            dma("sync", rb[:], DA(c_tab_d, off, [[PCC * TS, 128], [TS, PCC], [1, TS]]), (), [kr])
                            src = Bp if tb else Ap
                            for tci in range(PCC):
                                tc_ = pc * PCC + tci
                                first = (pc == 0 and tb == 0 and tci == 0)
                                last = (pc == NPC - 1 and tb == 1 and tci == PCC - 1)
                                for dc in range(4):
                                    mm(acc[dc][:, 0:TS], src[:, tc_, dc * 128:(dc + 1) * 128], rb[:, tci, :], first, last,
                                       apk + [kr], [("acc", 0, dc)])
                    for dc in range(4):
                        tt("vector", o_[:, dc, :], acc[dc][:, 0:TS], s_[:, dc, :], ALU.mult,
                           [("acc", 0, dc), ksg], [(ko, dc)])
                    dma("sync", DA(mixT_d, 512 * TP + t0, [[TP, 128], [128 * TP, 4], [1, TS]]), o_[:],
                        [(ko, dc) for dc in range(4)], [("mixT", 1)])
                    p1_back(kt)
            S.barrier()

        def phase_p3(l):
            with contextlib.ExitStack() as st:
                BT = alloc(st, "BT", [128, 2, TP], BF16)
                CT = alloc(st, "CT", [128, 2, TP], BF16)
                xtm = alloc(st, "xtm", [128, NCH, 512], BF16)
                btm = alloc(st, "btm", [128, NCH, 256], BF16)
                dtr = alloc(st, "dtr", [128, NCH, 16], F32)
                dtt = alloc(st, "dtt", [128, NCH, 16], F32)
                da = alloc(st, "da", [128, NCH, 16], F32)
                tA = alloc(st, "tA", [128, NCH, 16], F32)
                tB = alloc(st, "tB", [128, NCH, 16], F32)
                dtb = alloc(st, "dtb", [128, 16], F32)
                alg = alloc(st, "alg", [128, 16], F32)
                dsk = alloc(st, "dsk", [128, 8], F32)
                snw = alloc(st, "snw", [128, 512], F32)
                cwt = alloc(st, "cwt", [128, 8, 4], F32)
                cbt = alloc(st, "cbt", [128, 8], F32)
                dg = alloc(st, "dg", [128, 8, 4, 128], BF16)
                cf = alloc(st, "cf", [128, 3, 128], F32)
                cb = alloc(st, "cb", [128, 6, 128], BF16)
                xin = [alloc(st, "xin", [128, 8, TS + 3], BF16) for _ in range(2)]
                xTt = [alloc(st, "xTt", [128, 4, TS], BF16) for _ in range(2)]
                H = alloc(st, "H", [128, 512], F32)
                Hb = alloc(st, "Hb", [128, 512], BF16)
                Ht = alloc(st, "Ht", [128, 512], F32)
                cst = alloc(st, "cst", [128, 24], F32)
                E = alloc(st, "E", [128, 24], F32)
                R = alloc(st, "R", [128, 8, 128], BF16)
                cbm = alloc(st, "cbm", [128, 2, 128], BF16)
                eseg = alloc(st, "eseg", [128, 8, 128], BF16)
                MT = alloc(st, "MT", [128, 8, 128], BF16)
                xdt = alloc(st, "xdt", [128, 512], BF16)
                xw = alloc(st, "xw", [128, 512], BF16)
                ytmp = alloc(st, "ytmp", [128, 512], F32)
                yc = [alloc(st, "yc", [128, 512], F32) for _ in range(2)]
                yfl = [alloc(st, "yfl", [128, 512], F32) for _ in range(2)]
                szl = [alloc(st, "szl", [128, 512], BF16) for _ in range(2)]
                yg = alloc(st, "yg", [128, 512], F32)
                yjunk = alloc(st, "yjunk", [128, 512], BF16)
                ynb = alloc(st, "ynb", [128, 512], BF16)
                ystat = alloc(st, "ystat", [128, 4], F32)
                yst = [alloc(st, "yst", [128, 4, TS], BF16) for _ in range(2)]
                pcv = [palloc(st, "pcv", [128, 512], F32) for _ in range(2)]
                ptr = palloc(st, "ptr", [128, 1024], BF16)
                pcb = pcv[0]
                pseg = [palloc(st, "pseg", [128, 512], F32) for _ in range(2)]
                pyd = palloc(st, "pyd", [128, 512], F32)
                pyo = palloc(st, "pyo", [128, 512], F32)
                psn = palloc(st, "psn", [128, 512], F32)
                pyd2 = [pcv[1], pyd]
                kpyd = ["pcv1", "pyd"]
                cbm2 = [alloc(st, "cbm2", [128, 2, 128], BF16) for _ in range(2)]
                cst2 = [alloc(st, "cst2", [128, 24], F32) for _ in range(2)]
                E2 = [alloc(st, "E2", [128, 24], F32) for _ in range(2)]
                MT2 = [alloc(st, "MT2", [128, 8, 128], BF16) for _ in range(2)]
                xdt2 = [alloc(st, "xdt2", [128, 512], BF16) for _ in range(2)]
                xw2 = [alloc(st, "xw2", [128, 512], BF16) for _ in range(2)]
                ytmp2 = alloc(st, "ytmp2", [128, 512], F32)

                dma("sync", cf[:], c_ssdf_d.ap(), (), ["cf"])
                dma("sync", cb[:], c_ssdb_d.ap(), (), ["cb"])
                dma("sync", dtb[:], DA(dtb_d, l * 16, [[0, 128], [1, 16]]), (), ["dtb"])
                dma("sync", alg[:], DA(alog_d, l * 16, [[0, 128], [1, 16]]), (), ["alg"])
                dma("sync", dsk[:], DA(dskip_d, l * 8, [[0, 128], [1, 8]]), (), ["dsk"])
                dma("sync", snw[:], DA(ssdnw_d, l * 512, [[0, 128], [1, 512]]), (), ["snw"])
                for j in range(4):
                    dma("sync", cwt[:, :, j:j + 1], DA(convw_d, (l * 4 + j) * 1024, [[1, 128], [128, 8], [1, 1]]), (), ["cwt"], slow=True)
                dma("sync", cbt[:].unsqueeze(2), DA(convb_d, l * 1024, [[1, 128], [128, 8], [1, 1]]), (), ["cbt"], slow=True)
                for ch in range(8):
                    for j in range(4):
                        tsc("gpsimd", dg[:, ch, j, :], ident[:], cwt[:, ch, j:j + 1], None, ALU.mult, None,
                            ["ident", "cwt"], [("dg", ch)])
                for c3 in range(3):
                    dma("sync", dtr[:, c3 * 11:(c3 + 1) * 11, :],
                        DA(dtr_d, c3 * 11 * 128 * 16, [[16, 128], [128 * 16, 11], [1, 16]]), ["dtr"], ["dtrl"])
                bcd = dtb[:].unsqueeze(1).broadcast_to([128, NCH, 16])
                tt("vector", tA[:], dtr[:], bcd, ALU.add, ["dtrl", "dtb"], ["tA"])
                act(tB[:], tA[:], AF.Abs, ["tA"], ["tB"])
                act(tB[:], tB[:], AF.Exp, ["tB"], ["tB"], scale=-1.0)
                tsc("vector", tB[:], tB[:], 1.0, None, ALU.add, None, ["tB"], ["tB"])
                act(tB[:], tB[:], AF.Ln, ["tB"], ["tB"])
                tsc("vector", tA[:], tA[:], 0.0, None, ALU.max, None, ["tA"], ["tA"])
                tt("vector", dtt[:], tA[:], tB[:], ALU.add, ["tA", "tB"], ["dtt"])
                act(alg[:], alg[:], AF.Exp, ["alg"], ["alg"])
                tsc("vector", alg[:], alg[:], -1.0, None, ALU.mult, None, ["alg"], ["alg"])
                tt("vector", da[:], dtt[:], alg[:].unsqueeze(1).broadcast_to([128, NCH, 16]), ALU.mult, ["dtt", "alg"], ["da"])

                for i in range(NT):
                    t0 = i * TS
                    xi = xin[i % 2]
                    kx = "xin%d" % (i % 2)
                    xt_ = xTt[i % 2]
                    kxt = "xTt%d" % (i % 2)
                    dma("sync", xi[:], DA(fm_d, 2048 * TW + 8 + t0 - 2, [[TW, 128], [128 * TW, 8], [1, TS + 3]]),
                        [("fm", 4), ("fm", 5), "fm_margin"], [kx])
                    for ch in range(8):
                        p = pcv[ch % 2]
                        kp = "pcv%d" % (ch % 2)
                        for j in range(4):
                            mm(p[:, 0:TS], dg[:, ch, j, :], xi[:, ch, j:j + TS], j == 0, j == 3, [("dg", ch), kx], [kp])
                        if ch < 4:
                            dst, kdst = xt_[:, ch, :], (kxt, ch)
                        elif ch < 6:
                            dst, kdst = BT[:, ch - 4, t0:t0 + TS], ("BT", i)
                        else:
                            dst, kdst = CT[:, ch - 6, t0:t0 + TS], ("CT", i)
                        act(dst, p[:, 0:TS], AF.Silu, [kp, "cbt"], [kdst], bias=cbt[:, ch:ch + 1])
                    if i == 0:
                        memset("gpsimd", xt_[:, :, 0:NPAD], 0.0, [(kxt, ch) for ch in range(4)])
                    for cl in range(3):
                        c = 3 * i + cl
                        lt = slice(cl * 128, (cl + 1) * 128)
                        for ch in range(4):
                            identr(ptr[:, ch * 128:(ch + 1) * 128], xt_[:, ch, lt], [(kxt, q) for q in range(4)], ["ptr"])
                        for g in range(2):
                            identr(ptr[:, 512 + g * 128:512 + (g + 1) * 128], BT[:, g, t0 + cl * 128:t0 + (cl + 1) * 128],
                                   [("BT", i)], ["ptr"])
                        cp("vector", xtm[:, c, :], ptr[:, 0:512], ["ptr"], [("xtm", c)])
                        cp("scalar", btm[:, c, :], ptr[:, 512:768], ["ptr"], [("btm", c)])

                ydx = ytmp2
                for d_ in range(2):
                    memset("gpsimd", H[:], 0.0, ["H"])
                    memset("gpsimd", Hb[:], 0.0, ["Hb"])
                    order = list(range(NCH)) if d_ == 0 else list(range(NCH - 1, -1, -1))

                    def stage_a(n_, d_=d_, order=order):
                        c = order[n_]
                        i = c // 3
                        b2 = n_ % 2
                        cols = slice(c * 128, (c + 1) * 128)
                        dac = da[:, c, d_ * 8:(d_ + 1) * 8]
                        for g in range(2):
                            mm(pcb[:, g * 128:(g + 1) * 128], BT[:, g, cols], CT[:, g, cols], True, True,
                               [("BT", i), ("CT", i)], ["pcv0"])
                        mm(pcb[:, 256:264], cf[:, d_, :], dac, True, True, ["cf", "da"], ["pcv0"])
                        mm(pcb[:, 264:272], cf[:, 2, :], dac, True, True, ["cf", "da"], ["pcv0"])
                        tt("vector", cbm2[b2][:], pcb[:, 0:256].rearrange("p (g s) -> p g s", g=2),
                           cb[:, 4 + d_, :].unsqueeze(1).broadcast_to([128, 2, 128]), ALU.mult, ["pcv0", "cb"], ["cbm%d" % b2])
                        cp("vector", cst2[b2][:, 0:16], pcb[:, 256:272], ["pcv0"], [("cst%d" % b2, 0)])
                        tt("vector", cst2[b2][:, 16:24], cst2[b2][:, 8:16], cst2[b2][:, 0:8], ALU.subtract,
                           [("cst%d" % b2, 0)], [("cst%d" % b2, 1)])
                        act(E2[b2][:], cst2[b2][:], AF.Exp, [("cst%d" % b2, 0), ("cst%d" % b2, 1)], ["E%d" % b2])
                        tt("gpsimd", R[:], cb[:, d_, :].unsqueeze(1).broadcast_to([128, 8, 128]),
                           dac.unsqueeze(2).broadcast_to([128, 8, 128]), ALU.mult, ["cb", "da"], ["R"])
                        for hh in range(2):
                            mm(pseg[hh][:, :], cb[:, 2 + d_, :], R[:, hh * 4:(hh + 1) * 4, :], True, True, ["cb", "R"],
                               [("pseg", hh)])
                            act(eseg[:, hh * 4:(hh + 1) * 4, :], pseg[hh][:, :].rearrange("p (h l) -> p h l", h=4), AF.Exp,
                                [("pseg", hh)], [("eseg", hh)])
                            tt("vector" if hh else "gpsimd", MT2[b2][:, hh * 4:(hh + 1) * 4, :], eseg[:, hh * 4:(hh + 1) * 4, :],
                               cbm2[b2][:, hh, :].unsqueeze(1).broadcast_to([128, 4, 128]), ALU.mult,
                               [("eseg", hh), "cbm%d" % b2], [("MT%d" % b2, hh)])
                        tt("gpsimd", xdt2[b2][:].rearrange("p (h d) -> p h d", h=8),
                           xtm[:, c, :].rearrange("p (h d) -> p h d", h=8),
                           dtt[:, c, d_ * 8:(d_ + 1) * 8].unsqueeze(2).broadcast_to([128, 8, 64]), ALU.mult,
                           [("xtm", c), "dtt"], ["xdt%d" % b2])
                        tt("gpsimd", xw2[b2][:].rearrange("p (h d) -> p h d", h=8),
                           xdt2[b2][:].rearrange("p (h d) -> p h d", h=8),
                           E2[b2][:, 16:24].unsqueeze(2).broadcast_to([128, 8, 64]), ALU.mult, ["xdt%d" % b2, "E%d" % b2],
                           ["xw%d" % b2])
                        for h in range(8):
                            mm(pyd2[b2][:, h * 64:(h + 1) * 64], MT2[b2][:, h, :], xdt2[b2][:, h * 64:(h + 1) * 64], True, True,
                               [("MT%d" % b2, h // 4), "xdt%d" % b2], [kpyd[b2]])

                    def stage_b(n_, d_=d_, order=order):
                        c = order[n_]
                        i = c // 3
                        b2 = n_ % 2
                        cols = slice(c * 128, (c + 1) * 128)
                        Eb = E2[b2]
                        kE = "E%d" % b2
                        for g in range(2):
                            mm(pyo[:, g * 256:(g + 1) * 256], CT[:, g, cols], Hb[:, g * 256:(g + 1) * 256], True, True,
                               [("CT", i), "Hb"], ["pyo"])
                        tt("vector", ytmp[:].rearrange("p (h d) -> p h d", h=8), pyo[:, :].rearrange("p (h d) -> p h d", h=8),
                           Eb[:, 0:8].unsqueeze(2).broadcast_to([128, 8, 64]), ALU.mult, ["pyo", kE], ["ytmp"])
                        ycb = yc[b2]
                        kyc = "yc%d" % b2
                        tt("vector", ycb[:], pyd2[b2][:, :], ytmp[:], ALU.add, [kpyd[b2], "ytmp"], [kyc])
                        for g in range(2):
                            mm(psn[:, g * 256:(g + 1) * 256], btm[:, c, g * 128:(g + 1) * 128], xw2[b2][:, g * 256:(g + 1) * 256],
                               True, True, [("btm", c), "xw%d" % b2], ["psn"])
                        tt("gpsimd", Ht[:].rearrange("p (h d) -> p h d", h=8), H[:].rearrange("p (h d) -> p h d", h=8),
                           Eb[:, 8:16].unsqueeze(2).broadcast_to([128, 8, 64]), ALU.mult, ["H", kE], ["Ht"])
                        tt("vector", H[:], psn[:, :], Ht[:], ALU.add, ["psn", "Ht"], ["H"])
                        cp("scalar", Hb[:], H[:], ["H"], ["Hb"])
                        if d_ == 0:
                            dma("sync", yf_d.ap()[c * 128:(c + 1) * 128, :], ycb[:], [kyc], ["yf"])
                        else:
                            yfb = yfl[b2]
                            kyf = "yfl%d" % b2
                            szb = szl[b2]
                            ksz = "szl%d" % b2
                            dma("sync", yfb[:], yf_d.ap()[c * 128:(c + 1) * 128, :], ["yf"], [kyf])
                            dma("sync", szb[:], sz_d.ap()[c * 128:(c + 1) * 128, :], ["sz"], [ksz])
                            tt("gpsimd", ydx[:].rearrange("p (h d) -> p h d", h=8),
                               xtm[:, c, :].rearrange("p (h d) -> p h d", h=8),
                               dsk[:].unsqueeze(2).broadcast_to([128, 8, 64]), ALU.mult, [("xtm", c), "dsk"], ["ydx"])
                            tt("vector", yg[:], ycb[:], yfb[:], ALU.add, [kyc, kyf], ["yg"])
                            tt("vector", yg[:], yg[:], ydx[:], ALU.add, ["yg", "ydx"], ["yg"])
                            tt("vector", yg[:], yg[:], szb[:], ALU.mult, ["yg", ksz], ["yg"])
                            act(yjunk[:], yg[:], AF.Square, ["yg"], ["yjunk", ("ystat", 0)], accum_out=ystat[:, 0:1])
                            tsc("vector", ystat[:, 1:2], ystat[:, 0:1], 1.0 / 512, EPS, ALU.mult, ALU.add, [("ystat", 0)], [("ystat", 1)])
                            tt("gpsimd", ystat[:, 3:4], ystat[:, 1:2], mhalf[:, 0:1], ALU.pow, [("ystat", 1), "mhalf"], [("ystat", 3)])
                            stt(ynb[:], yg[:], ystat[:, 3:4], snw[:], ALU.mult, ALU.mult, ["yg", ("ystat", 3), "snw"], ["ynb"])
                            ti = c // 3
                            cl = c % 3
                            ys = yst[ti % 2]
                            kys = "yst%d" % (ti % 2)
                            for ch in range(4):
                                identr(ptr[:, ch * 128:(ch + 1) * 128], ynb[:, ch * 128:(ch + 1) * 128], ["ynb"], ["ptr"])
                            cp("scalar", ys[:, :, cl * 128:(cl + 1) * 128], ptr[:, 0:512].rearrange("p (j t) -> p j t", j=4),
                               ["ptr"], [(kys, cl)])
                            if cl == 0:
                                dma("sync", DA(mixT_d, 1024 * TP + ti * TS, [[TP, 128], [128 * TP, 4], [1, TS]]), ys[:],
                                    [(kys, q) for q in range(3)], [("mixT", 2)])

                    stage_a(0)
                    for n_ in range(NCH):
                        if n_ + 1 < NCH:
                            stage_a(n_ + 1)
                        stage_b(n_)
            S.barrier()

        def phase_p4(l):
            TQ = 512
            qtiles = [(0, 128)] + [(128 + k * TQ, TQ) for k in range(8)]
            with contextlib.ExitStack() as st:
                KT2 = alloc(st, "KT2", [128, 2, TP], BF16)
                Vp = alloc(st, "Vp", [128, NCH, 2, 192], BF16)
                vmask = alloc(st, "vmask", [128, 64], BF16)
                QT = [alloc(st, "QT", [128, 4, TQ], BF16) for _ in range(2)]
                sg = [alloc(st, "sga", [128, 4, TQ], BF16) for _ in range(2)]
                P = [alloc(st, "P", [128, 2, TQ], BF16) for _ in range(3)]
                rec = alloc(st, "rec", [128, TQ], F32)
                yt = alloc(st, "yt", [128, TQ], F32)
                ost = [alloc(st, "osta", [128, 4, TQ], BF16) for _ in range(2)]
                Sp = [palloc(st, "Sp", [128, 2, 512], F32) for _ in range(2)]
                Oe = [palloc(st, "Oe", [128, 512], F32) for _ in range(2)]
                Oo = [palloc(st, "Oo", [128, 512], F32) for _ in range(2)]
                dma("sync", vmask[:], c_vmask_d.ap(), (), ["vmask"])
                for g in range(2):
                    for hs in range(2):
                        dma("sync", KT2[hs * 64:(hs + 1) * 64, g, :], DA(kT_d, g * 64 * TP, [[TP, 64], [1, TP]]), ["kT"], ["KT2"])
                memset("gpsimd", Vp[:], 1.0, ["Vp"])
                for c3 in range(3):
                    for g in range(2):
                        dma("sync", Vp[:, c3 * 11:(c3 + 1) * 11, g, 64:128],
                            DA(vtm_d, c3 * 11 * 128 * 128 + g * 64, [[128, 128], [128 * 128, 11], [1, 64]]), ["vtm"], ["Vp"])
                for g in range(2):
                    for o0 in (0, 128):
                        cp("gpsimd", Vp[:, 0, g, o0:o0 + 64], vmask[:], ["vmask"], ["Vp"])
                npair = 0
                nS = [0]

                def p4_load(i):
                    t0, tw = qtiles[i]
                    dma("sync", QT[i % 2][:, :, 0:tw], DA(qT_d, t0, [[TP, 128], [128 * TP, 4], [1, tw]]), ["qT"], ["QT%d" % (i % 2)])
                    dma("sync", sg[i % 2][:, :, 0:tw], DA(fm_d, 3072 * TW + 8 + t0, [[TW, 128], [128 * TW, 4], [1, tw]]),
                        [("fm", 6)], ["sga%d" % (i % 2)])

                for i in range(len(qtiles)):
                    t0, tw = qtiles[i]
                    q_ = QT[i % 2]
                    kq = "QT%d" % (i % 2)
                    s_ = sg[i % 2]
                    ksg = "sga%d" % (i % 2)
                    o_ = ost[i % 2]
                    ko = "osta%d" % (i % 2)
                    if i == 0:
                        p4_load(0)
                    if i + 1 < len(qtiles):
                        p4_load(i + 1)
                    for j in range(4):
                        g = j // 2
                        oe = Oe[npair % 2]
                        oo = Oo[npair % 2]
                        koe = "Oe%d" % (npair % 2)
                        koo = "Oo%d" % (npair % 2)
                        npair += 1
                        pend = {}

                        def qk(kc, g=g, j=j, q_=q_, kq=kq, pend=pend, tw=tw):
                            n_ = nS[0]
                            nS[0] += 1
                            sp = Sp[n_ % 2]
                            ksp = "Sp%d" % (n_ % 2)
                            pb = P[n_ % 3]
                            kpb = "P%d" % (n_ % 3)
                            kcols = slice(kc * 128, (kc + 1) * 128)
                            mm(sp[:, 0, 0:tw], KT2[0:64, g, kcols], q_[0:64, j, 0:tw], True, True, ["KT2", kq], [ksp])
                            mm(sp[:, 1, 0:tw], KT2[64:128, g, kcols], q_[64:128, j, 0:tw], True, True, ["KT2", kq], [ksp])
                            act(pb[:, :, 0:tw], sp[:, :, 0:tw], AF.Exp, [ksp], [kpb])
                            pend[kc] = (pb, kpb)

                        def pvm(kc, g=g, oe=oe, oo=oo, koe=koe, koo=koo, pend=pend, tw=tw):
                            pb, kpb = pend.pop(kc)
                            mm(oe[:, 0:tw], Vp[:, kc, g, 64:192], pb[:, 0, 0:tw], kc == 0, kc == NCH - 1, ["Vp", kpb], [koe])
                            mm(oo[:, 0:tw], Vp[:, kc, g, 0:128], pb[:, 1, 0:tw], kc == 0, kc == NCH - 1, ["Vp", kpb], [koo])

                        qk(0)
                        for kc in range(NCH):
                            if kc + 1 < NCH:
                                qk(kc + 1)
                            pvm(kc)
                        recip(rec[0:64, 0:tw], oe[64:128, 0:tw], [koe], [("rec", 0)])
                        recip(rec[64:128, 0:tw], oo[0:64, 0:tw], [koo], [("rec", 1)])
                        tt("vector", yt[0:64, 0:tw], oe[0:64, 0:tw], rec[0:64, 0:tw], ALU.mult, [koe, ("rec", 0)], [("yt", 0)])
                        tt("vector", yt[64:128, 0:tw], oo[64:128, 0:tw], rec[64:128, 0:tw], ALU.mult, [koo, ("rec", 1)], [("yt", 1)])
                        tt("gpsimd", o_[:, j, 0:tw], yt[:, 0:tw], s_[:, j, 0:tw], ALU.mult, [("yt", 0), ("yt", 1), ksg], [(ko, j)])
                    dma("sync", DA(mixT_d, 1536 * TP + t0, [[TP, 128], [128 * TP, 4], [1, tw]]), o_[:, :, 0:tw],
                        [(ko, j) for j in range(4)], [("mixT", 3)])
            S.barrier()

        def phase_p5(l, last):
            with contextlib.ExitStack() as st:
                wo = alloc(st, "wo", [128, 16, D], BF16)
                mx = [alloc(st, "mx", [128, 16, TS], BF16) for _ in range(2)]
                hc = [alloc(st, "hc5", [128, D], F32) for _ in range(4)]
                ho = [alloc(st, "ho5", [128, D], F32) for _ in range(2)]
                po = [[palloc(st, "po", [128, 512], F32) for _ in range(2)] for _ in range(2)]
                for hh in range(2):
                    dma("gpsimd", wo[:, :, hh * 512:(hh + 1) * 512],
                        DA(wout_d, l * 2048 * D + hh * 512, [[D, 128], [128 * D, 16], [1, 512]]), (), [("wo", hh)])
                def p5_load(i):
                    for q4 in range(4):
                        dma("sync", mx[i % 2][:, q4 * 4:(q4 + 1) * 4, :],
                            DA(mixT_d, q4 * 512 * TP + i * TS, [[TP, 128], [128 * TP, 4], [1, TS]]), [("mixT", q4)], ["mx%d" % (i % 2)])

                def p5_hload(c):
                    if c >= NCH or (last and c == 0):
                        return
                    hcb = hc[c % 4]
                    khc = "hc5%d" % (c % 4)
                    if l == 0 and c == 0:
                        memset("gpsimd", hcb[:], 0.0, [khc])
                        dma("sync", hcb[NPAD:128, :], meta_d.ap(), (), [khc])
                    elif l == 0:
                        dma("sync", hcb[:], x_d.ap()[(c - 1) * 128:c * 128, :], (), [khc])
                    else:
                        dma("sync", hcb[:], hbuf_d.ap()[c * 128:(c + 1) * 128, :], ["hbuf", "hbuf_pad"], [khc])

                p5_load(0)
                p5_hload(0)
                p5_hload(1)
                for i in range(NT):
                    t0 = i * TS
                    m_ = mx[i % 2]
                    km = "mx%d" % (i % 2)
                    if i + 1 < NT:
                        p5_load(i + 1)
                    for cl in range(3):
                        c = 3 * i + cl
                        p5_hload(c + 2)
                        if last and c == 0:
                            continue
                        hcb = hc[c % 4]
                        khc = "hc5%d" % (c % 4)
                        hob = ho[c % 2]
                        kho = "ho5%d" % (c % 2)
                        for hh in range(2):
                            p = po[c % 2][hh]
                            kp = ("po", c % 2, hh)
                            for ec in range(16):
                                mm(p[:, :], m_[:, ec, cl * 128:(cl + 1) * 128], wo[:, ec, hh * 512:(hh + 1) * 512],
                                   ec == 0, ec == 15, [km, ("wo", hh)], [kp])
                            tt("vector", hob[:, hh * 512:(hh + 1) * 512], p[:, :], hcb[:, hh * 512:(hh + 1) * 512], ALU.add,
                               [kp, khc], [(kho, hh)])
                        if last:
                            o = dma("sync", out_d.ap()[(c - 1) * 128:c * 128, :], hob[:], [(kho, 0), (kho, 1)], ["out"])
                            final_ops.append(o)
                        elif c == 0:
                            dma("sync", hbuf_d.ap()[NPAD:128, :], hob[NPAD:128, :], [(kho, 0), (kho, 1)], ["hbuf"])
                        else:
                            dma("sync", hbuf_d.ap()[c * 128:(c + 1) * 128, :], hob[:], [(kho, 0), (kho, 1)], ["hbuf"])
            S.barrier()

        for l in range(n_layers):
            if "0" in phases:
                phase_p0(l)
            if "1" in phases or "2" in phases:
                phase_p12(l)
            if "3" in phases:
                phase_p3(l)
            if "4" in phases:
                phase_p4(l)
            if "5" in phases:
                phase_p5(l, l == n_layers - 1)
        final_ops[:] = [o for o in final_ops if not o.skipped]
        S.limit = None
        print("n_ops", len(S.ops))
        if not final_ops:
            zo = dma("sync", out_d.ap()[0:128, :], zf[:], ["zf"], ["out"])
            final_ops.append(zo)
        S.emit(nc, final_ops)
    return nc


_PARAM_NAMES = ("meta_tokens", "norm_w", "w_in", "w_out", "pool_w", "pool_scale", "fourier_w", "conv_w", "conv_b",
                "dt_bias", "a_log", "d_skip", "ssd_norm_w", "q_norm_w", "k_norm_w")


def make_in_maps(inputs, cores):
    consts = host_consts()
    shared = {k: np.ascontiguousarray(np.asarray(inputs[k], dtype=np.float32)) for k in _PARAM_NAMES}
    shared.update(consts)
    x = np.asarray(inputs["x"], dtype=np.float32)
    maps = []
    for b in cores:
        m = dict(shared)
        m["x"] = np.ascontiguousarray(x[b])
        maps.append(m)
    return maps


def kernel(**inputs):
    nc = build()
    in_maps = make_in_maps(inputs, list(range(8)))
    res = run_bass_kernel_spmd(nc, in_maps, core_ids=list(range(8)))
    return np.stack([np.asarray(r["out"], dtype=np.float32) for r in res.results], axis=0)
```

```python
import contextlib
import numpy as np
import ml_dtypes
import concourse.bass as bass
import concourse.mybir as mybir
from concourse.bass_utils import run_bass_kernel_spmd

F32 = mybir.dt.float32
BF16 = mybir.dt.bfloat16
AF = mybir.ActivationFunctionType
ALU = mybir.AluOpType
AX = mybir.AxisListType

D = 1024
NTOK = 4096
NMETA = 16
L = NTOK + NMETA
NPAD = 112
TP = 4224
NCH = 33
NT = 11
TS = 384
TW = TP + 16
DEPTH = 4
INC = 4880
EPS = 1e-6
NPC = 3
PCC = 11

ENGS = ("tensor", "vector", "scalar", "gpsimd", "sync")
N_DMA_SEMS = 28
N_SW_SEMS = 8
PSUM_KEYS = frozenset(["pT", "pf0", "pf1", "pz", "pq", "pkv", "ptq", "py", "acc", "pcv0", "pcv1", "ptr", "pseg", "pyd",
                       "pyo", "psn", "Sp0", "Sp1", "Oe0", "Oe1", "Oo0", "Oo1", "po"])


class Op:
    __slots__ = ("eng", "fn", "deps", "is_dma", "needed", "sem", "val", "nobar", "skipped")

    def __init__(self, eng, fn, deps, is_dma):
        self.eng = eng
        self.fn = fn
        self.deps = deps
        self.is_dma = is_dma
        self.needed = False
        self.sem = None
        self.val = 0
        self.nobar = False
        self.skipped = False


class Sched:
    def __init__(self):
        self.ops = []
        self.last_w = {}
        self.readers = {}
        self.last_eng = {}
        self.dmas = []
        self.limit = None

    def _add(self, eng, fn, reads, writes, is_dma, extra=()):
        ex = tuple(k for k in reads if (k[0] if isinstance(k, tuple) else k) in PSUM_KEYS)
        if ex:
            writes = tuple(writes) + tuple(k for k in ex if k not in writes)
        deps = list(extra)
        for k in reads:
            w = self.last_w.get(k)
            if w is not None:
                deps.append(w)
        for k in writes:
            w = self.last_w.get(k)
            if w is not None:
                deps.append(w)
            deps.extend(self.readers.get(k, ()))
        op = Op(eng, fn, deps, is_dma)
        if self.limit is not None and len(self.ops) >= self.limit:
            op.skipped = True
            return op
        self.ops.append(op)
        for k in writes:
            self.last_w[k] = op
            self.readers[k] = []
        for k in reads:
            if k in writes:
                continue
            lst = self.readers.setdefault(k, [])
            if not is_dma:
                lst[:] = [o for o in lst if o.is_dma or o.eng != eng]
            lst.append(op)
        if is_dma:
            self.dmas.append(op)
        else:
            self.last_eng[eng] = op
        return op

    def op(self, eng, fn, reads=(), writes=()):
        return self._add(eng, fn, tuple(reads), tuple(writes), False)

    def dma(self, eng, fn, reads=(), writes=(), nobar=False):
        o = self._add(eng, fn, tuple(reads), tuple(writes), True)
        o.nobar = nobar
        return o

    def barrier(self):
        deps = [o for o in self.last_eng.values()] + [o for o in self.dmas if not o.nobar]
        self.dmas = [o for o in self.dmas if o.nobar]
        for e in ENGS:
            self._add(e, lambda eng: eng.nop(), (), (), False, extra=deps)

    def emit(self, nc, final_wait_ops=()):
        ops = self.ops
        for o in ops:
            for d in o.deps:
                if d.eng == "tensor" and o.eng == "tensor" and not d.is_dma and not o.is_dma:
                    continue
                d.needed = True
        for o in final_wait_ops:
            o.needed = True
        with contextlib.ExitStack() as st:
            esem = {e: st.enter_context(nc.semaphore("s_" + e)) for e in ENGS}
            dsem = [st.enter_context(nc.semaphore("d_%d" % i)) for i in range(N_DMA_SEMS)]
            ecnt = {e: 0 for e in ENGS}
            dcnt = [0] * N_DMA_SEMS
            dnext = 0
            prev_on_dsem = [None] * N_DMA_SEMS
            swnext = 0
            for o in ops:
                if o.is_dma:
                    if o.eng == "gpsimd":
                        i = N_DMA_SEMS - N_SW_SEMS + swnext
                        swnext = (swnext + 1) % N_SW_SEMS
                    else:
                        i = dnext
                        dnext = (dnext + 1) % (N_DMA_SEMS - N_SW_SEMS)
                    dcnt[i] += 16
                    o.sem = ("d", i)
                    o.val = dcnt[i]
                    if prev_on_dsem[i] is not None:
                        o.deps.append(prev_on_dsem[i])
                    prev_on_dsem[i] = o
                elif o.needed:
                    ecnt[o.eng] += 1
                    o.sem = ("e", o.eng)
                    o.val = ecnt[o.eng]
            per = {e: [o for o in ops if o.eng == e] for e in ENGS}
            final = list(final_wait_ops)
            blk = st.enter_context(nc.Block())

            def semh(s):
                return esem[s[1]] if s[0] == "e" else dsem[s[1]]

            def run(e, eng):
                seen = {}
                for o in per[e]:
                    need = {}
                    for d in o.deps:
                        if d.sem is None:
                            continue
                        if (not d.is_dma) and (not o.is_dma) and d.eng == "tensor" and e == "tensor":
                            continue
                        if need.get(d.sem, 0) < d.val:
                            need[d.sem] = d.val
                    for s, v in need.items():
                        if seen.get(s, 0) < v:
                            eng.wait_ge(semh(s), v)
                            seen[s] = v
                    ins = o.fn(eng)
                    if o.is_dma:
                        ins.then_inc(semh(o.sem), 16)
                    elif o.sem is not None:
                        ins.then_inc(semh(o.sem), 1)
                if e == "sync":
                    for o in final:
                        if seen.get(o.sem, 0) < o.val:
                            eng.wait_ge(semh(o.sem), o.val)
                            seen[o.sem] = o.val

            @blk.tensor
            def _(eng):
                run("tensor", eng)

            @blk.vector
            def _(eng):
                run("vector", eng)

            @blk.scalar
            def _(eng):
                run("scalar", eng)

            @blk.gpsimd
            def _(eng):
                run("gpsimd", eng)

            @blk.sync
            def _(eng):
                run("sync", eng)
        return ecnt, dcnt


_CONSTS = None


def host_consts():
    global _CONSTS
    if _CONSTS is not None:
        return _CONSTS
    bf = ml_dtypes.bfloat16
    c = {}
    c["c_ident"] = np.eye(128, dtype=np.float32).astype(bf)
    tphys = np.arange(TP)
    tlog = tphys - NPAD
    row = np.where(tlog >= NMETA, (tlog - NMETA) // 64, 0).astype(np.float64)
    col = np.where(tlog >= NMETA, (tlog - NMETA) % 64, 0).astype(np.float64)
    freqs = (10000.0 ** (-np.arange(0, 32, 2, dtype=np.float32) / 32.0)).astype(np.float32)
    ang = np.concatenate([row[:, None].astype(np.float32) * freqs[None], col[:, None].astype(np.float32) * freqs[None]],
                         axis=-1).astype(np.float32)
    cos2 = np.repeat(np.cos(ang), 2, axis=-1).astype(np.float32)
    sin2 = np.repeat(np.sin(ang), 2, axis=-1).astype(np.float32)
    rope = np.stack([cos2, sin2], 0).reshape(2, NCH, 128, 64).transpose(2, 0, 1, 3)
    c["c_rope"] = np.ascontiguousarray(rope, dtype=np.float32)
    k = np.arange(128)[:, None]
    l_ = np.arange(128)[None, :]
    trif = (k <= l_).astype(np.float32)
    trib = (k >= l_).astype(np.float32)
    ones = np.ones((128, 128), np.float32)
    c["c_ssd_f32"] = np.ascontiguousarray(np.stack([trif, trib, ones], 1), dtype=np.float32)
    t1f = (k > l_).astype(np.float32)
    t1b = (k < l_).astype(np.float32)
    c["c_ssd_bf"] = np.ascontiguousarray(np.stack([trif, trib, t1f, t1b, trif, trib], 1)).astype(bf)
    m = np.arange(512)
    a = 2.0 * np.pi * ((m[:, None] * m[None, :]) % 512) / 512.0
    cc = np.cos(a).reshape(4, 128, 512).transpose(1, 0, 2)
    sc = np.sin(a).reshape(4, 128, 512).transpose(1, 0, 2)
    c["c_dft_c"] = np.ascontiguousarray(np.stack([cc, sc], 1)).astype(bf)
    scale = 1.0 / np.sqrt(float(L) * 512.0)
    tl = (np.arange(TP) - NPAD).astype(np.int64)
    valid = (tl >= 0)
    prod = (np.maximum(tl, 0)[:, None] * np.maximum(tl, 0)[None, :]) % L
    angp = (2.0 * np.pi / L) * prod.astype(np.float64)
    vm = (valid[:, None] & valid[None, :])
    tabs = []
    for fn_, sgn in ((np.cos, 1.0), (np.sin, -1.0)):
        t_ = (fn_(angp) * (sgn * scale) * vm).astype(np.float32).astype(bf)
        t_ = t_.reshape(NPC, PCC, 128, NT, TS)
        t_ = t_.transpose(3, 0, 2, 1, 4)
        tabs.append(np.ascontiguousarray(t_).reshape(NT, NPC, 128, PCC * TS))
    c["c_tab"] = np.ascontiguousarray(np.stack(tabs, 0))
    pinv = np.zeros((128, 4, 2, 8), np.float32)
    for g, w in enumerate((2, 4, 8, 16)):
        for side in range(2):
            for q in range(8):
                t = q if side == 0 else L - 8 + q
                lo = min(max(t - w // 2, 0), L)
                hi = min(max(t - w // 2 + w, 0), L)
                pinv[:, g, side, q] = 1.0 / float(hi - lo)
    c["c_pool"] = pinv
    vmask = np.ones((128, 64), np.float32)
    vmask[:NPAD] = 0.0
    c["c_vmask"] = vmask.astype(bf)
    _CONSTS = c
    return c


def build(n_layers=DEPTH, dbg=False, phases="012345", limit=None):
    nc = bass.Bass("TRN2", target_bir_lowering=False)
    S = Sched()
    S.limit = limit
    uid = [0]

    def din(name, shape, dt):
        return nc.dram_tensor(name, list(shape), dt, kind="ExternalInput")

    x_d = din("x", [NTOK, D], F32)
    meta_d = din("meta_tokens", [NMETA, D], F32)
    normw_d = din("norm_w", [DEPTH, D], F32)
    win_d = din("w_in", [DEPTH, D, INC], F32)
    wout_d = din("w_out", [DEPTH, 2048, D], F32)
    poolw_d = din("pool_w", [DEPTH, 4, 128, 128], F32)
    pools_d = din("pool_scale", [DEPTH, 512], F32)
    fw_d = din("fourier_w", [DEPTH, 512, 512], F32)
    convw_d = din("conv_w", [DEPTH, 4, 1024], F32)
    convb_d = din("conv_b", [DEPTH, 1024], F32)
    dtb_d = din("dt_bias", [DEPTH, 2, 8], F32)
    alog_d = din("a_log", [DEPTH, 2, 8], F32)
    dskip_d = din("d_skip", [DEPTH, 8], F32)
    ssdnw_d = din("ssd_norm_w", [DEPTH, 512], F32)
    qnw_d = din("q_norm_w", [DEPTH, 64], F32)
    knw_d = din("k_norm_w", [DEPTH, 64], F32)
    c_ident_d = din("c_ident", [128, 128], BF16)
    c_rope_d = din("c_rope", [128, 2, NCH, 64], F32)
    c_ssdf_d = din("c_ssd_f32", [128, 3, 128], F32)
    c_ssdb_d = din("c_ssd_bf", [128, 6, 128], BF16)
    c_dftc_d = din("c_dft_c", [128, 2, 4, 512], BF16)
    c_tab_d = din("c_tab", [2, NT, NPC, 128, PCC * TS], BF16)
    c_pool_d = din("c_pool", [128, 4, 2, 8], F32)
    c_vmask_d = din("c_vmask", [128, 64], BF16)
    out_d = nc.dram_tensor("out", [NTOK, D], F32, kind="ExternalOutput")
    skind = "ExternalOutput" if dbg else "Internal"

    def dscr(name, shape, dt):
        return nc.dram_tensor(name, list(shape), dt, kind=skind)

    hbuf_d = dscr("hbuf", [TP, D], F32)
    fm_d = dscr("fm", [3584, TW], BF16)
    qT_d = dscr("qT", [512, TP], BF16)
    kT_d = dscr("kT", [128, TP], BF16)
    vtm_d = dscr("vtm", [TP, 128], BF16)
    sz_d = dscr("sz", [TP, 512], BF16)
    dtr_d = dscr("dtr", [TP, 16], F32)
    yf_d = dscr("yf", [TP, 512], F32)
    mixT_d = dscr("mixT", [2048, TP], BF16)

    def DA(h, off, dims):
        return bass.AP(h, off, [list(d) for d in dims])

    def mm(out, lhsT, rhs, start, stop, r, w):
        return S.op("tensor", lambda e: e.matmul(out, lhsT=lhsT, rhs=rhs, start=start, stop=stop), r, w)

    def act(out, in_, func, r, w, bias=None, scale=None, accum_out=None, eng="scalar"):
        kw = {}
        if bias is not None:
            kw["bias"] = bias
        if scale is not None:
            kw["scale"] = scale
        if accum_out is not None:
            kw["accum_out"] = accum_out
        return S.op("scalar", lambda e: e.activation(out=out, in_=in_, func=func, **kw), r, w)

    def tt(eng, out, in0, in1, op, r, w):
        return S.op(eng, lambda e: e.tensor_tensor(out=out, in0=in0, in1=in1, op=op), r, w)

    def tsc(eng, out, in0, s1, s2, op0, op1, r, w):
        if op1 is None:
            return S.op(eng, lambda e: e.tensor_scalar(out=out, in0=in0, scalar1=s1, scalar2=None, op0=op0), r, w)
        return S.op(eng, lambda e: e.tensor_scalar(out=out, in0=in0, scalar1=s1, scalar2=s2, op0=op0, op1=op1), r, w)

    def stt(out, in0, scalar, in1, op0, op1, r, w):
        return S.op("vector", lambda e: e.scalar_tensor_tensor(out=out, in0=in0, scalar=scalar, in1=in1, op0=op0, op1=op1), r, w)

    def cp(eng, out, in_, r, w):
        if eng == "scalar":
            return S.op("scalar", lambda e: e.activation(out=out, in_=in_, func=AF.Copy), r, w)
        return S.op(eng, lambda e: e.tensor_copy(out=out, in_=in_), r, w)

    def recip(out, in_, r, w):
        return S.op("vector", lambda e: e.reciprocal(out=out, in_=in_), r, w)

    def memset(eng, ap, val, w):
        return S.op(eng, lambda e: e.memset(ap, val), (), w)

    def dma(eng, out, in_, r, w, nobar=False, slow=False):
        if slow:
            return S.dma(eng, lambda e: e.dma_start(out=out, in_=in_, allow_slow_non_contiguous=True), r, w, nobar=nobar)
        return S.dma(eng, lambda e: e.dma_start(out=out, in_=in_), r, w, nobar=nobar)

    final_ops = []

    with contextlib.ExitStack() as top:
        def alloc(st, name, shape, dt):
            uid[0] += 1
            return st.enter_context(nc.sbuf_tensor("%s_%d" % (name, uid[0]), list(shape), dt))

        def palloc(st, name, shape, dt):
            uid[0] += 1
            return st.enter_context(nc.psum_tensor("%s_%d" % (name, uid[0]), list(shape), dt))

        ident = alloc(top, "ident", [128, 128], BF16)
        zero = alloc(top, "zero", [128, 16], BF16)
        dma("sync", ident[:], c_ident_d.ap(), (), ["ident"])
        memset("gpsimd", zero[:], 0.0, ["zero"])
        for r0, nj in ((0, 4), (2048, 8)):
            for c0 in (0, 8 + TP):
                dma("sync", DA(fm_d, r0 * TW + c0, [[TW, 128], [128 * TW, nj], [1, 8]]),
                    zero[:, 0:8].unsqueeze(1).broadcast_to([128, nj, 8]), ["zero"], ["fm_margin"])
        mhalf = alloc(top, "mhalf", [128, 16], F32)
        memset("gpsimd", mhalf[:], -0.5, ["mhalf"])
        zf = alloc(top, "zf", [128, D], F32)
        memset("gpsimd", zf[:], 0.0, ["zf"])
        dma("sync", hbuf_d.ap()[0:NPAD, :], zf[0:NPAD, :], ["zf"], ["hbuf_pad"])
        S.barrier()

        def identr(o, i_, r, w):
            return S.op("tensor", lambda e: e.transpose(o, i_, ident[:]), list(r) + ["ident"], w)

        W_PIECES = ((0, 1024, 0), (1024, 1024, 1024), (2048, 1024, 2048), (4368, 512, 3072),
                    (3072, 512, 3584), (3600, 512, 4096), (4112, 256, 4608), (3584, 16, 4864))

        def load_w(st, l):
            W = alloc(st, "W", [128, 8, INC], BF16)
            for (s0, n, d0) in W_PIECES:
                dma("gpsimd", W[:, :, d0:d0 + n],
                    DA(win_d, l * D * INC + s0, [[INC, 128], [128 * INC, 8], [1, n]]), (), [("W", d0)], nobar=True)
            return W

        def phase_p0(l, W):
            with contextlib.ExitStack() as st:
                pieces = W_PIECES
                wkeys = [("W", p[2]) for p in pieces]
                hc = [alloc(st, "hc", [128, D], F32) for _ in range(3)]
                junk = alloc(st, "junk", [128, D], BF16)
                nwb = alloc(st, "nwb", [128, D], F32)
                hnb = [alloc(st, "hnb", [128, D], BF16) for _ in range(3)]
                hnT = [alloc(st, "hnT", [128, 8, TS], BF16) for _ in range(2)]
                stt_t = [alloc(st, "stat", [128, 48], F32) for _ in range(2)]
                fst = [alloc(st, "fst", [128, 4, TS], BF16) for _ in range(3)]
                zst = [alloc(st, "zst", [128, 3, 512], BF16) for _ in range(2)]
                qTst = [alloc(st, "qTst", [128, 4, TS], BF16) for _ in range(2)]
                kTst = [alloc(st, "kTst", [128, TS], BF16) for _ in range(2)]
                vst = [alloc(st, "vst", [128, 3, 128], BF16) for _ in range(2)]
                dtst = [alloc(st, "dtst", [128, 3, 16], F32) for _ in range(2)]
                sq = alloc(st, "sq", [128, 640], F32)
                qn = alloc(st, "qn", [128, 640], F32)
                qw = alloc(st, "qw", [128, 640], F32)
                m1 = alloc(st, "m1", [128, 640], F32)
                m2 = alloc(st, "m2", [128, 640], F32)
                qr = [alloc(st, "qr", [128, 640], BF16) for _ in range(2)]
                wqk = alloc(st, "wqk", [128, 640], F32)
                rope = alloc(st, "rope", [128, 2, NCH, 64], F32)
                pT = palloc(st, "pT", [128, 1024], BF16)
                pf = [palloc(st, "pf", [128, 512], F32) for _ in range(2)]
                pz = palloc(st, "pz", [128, 512], F32)
                pq = palloc(st, "pq", [128, 512], F32)
                pkv = palloc(st, "pkv", [128, 512], F32)
                ptq = palloc(st, "ptq", [128, 1024], BF16)

                dma("sync", rope[:], c_rope_d.ap(), (), ["rope"])
                dma("sync", nwb[:], DA(normw_d, l * D, [[0, 128], [1, D]]), (), ["nwb"])
                dma("sync", wqk[:, 0:512].rearrange("p (h d) -> p h d", h=8),
                    DA(qnw_d, l * 64, [[0, 128], [0, 8], [1, 64]]), (), ["wqk"])
                dma("sync", wqk[:, 512:640].rearrange("p (h d) -> p h d", h=2),
                    DA(knw_d, l * 64, [[0, 128], [0, 2], [1, 64]]), (), ["wqk"])
                tsc("gpsimd", wqk[:, 0:512], wqk[:, 0:512], 0.125, None, ALU.mult, None, ["wqk"], ["wqk"])

                nfm = [0]

                def norm_load(c):
                    hcb = hc[c % 3]
                    khc = "hc%d" % (c % 3)
                    if l == 0 and c == 0:
                        memset("gpsimd", hcb[:], 0.0, [khc])
                        dma("sync", hcb[NPAD:128, :], meta_d.ap(), (), [khc])
                    elif l == 0:
                        dma("sync", hcb[:], x_d.ap()[(c - 1) * 128:c * 128, :], (), [khc])
                    else:
                        dma("sync", hcb[:], hbuf_d.ap()[c * 128:(c + 1) * 128, :], ["hbuf", "hbuf_pad"], [khc])

                def norm_front(c):
                    hcb = hc[c % 3]
                    khc = "hc%d" % (c % 3)
                    sta = stt_t[c % 2]
                    ks = "stat%d" % (c % 2)
                    act(junk[:], hcb[:], AF.Square, [khc], ["junk", (ks, 0)], accum_out=sta[:, 0:1])
                    tsc("vector", sta[:, 1:2], sta[:, 0:1], 1.0 / D, EPS, ALU.mult, ALU.add, [(ks, 0)], [(ks, 1)])
                    tt("gpsimd", sta[:, 3:4], sta[:, 1:2], mhalf[:, 0:1], ALU.pow, [(ks, 1), "mhalf"], [(ks, 3)])
                    stt(hnb[c % 3][:], hcb[:], sta[:, 3:4], nwb[:], ALU.mult, ALU.mult, [khc, (ks, 3), "nwb"], ["hnb%d" % (c % 3)])

                def norm_back(c):
                    i_, cl = c // 3, c % 3
                    hT = hnT[i_ % 2]
                    kh = "hnT%d" % (i_ % 2)
                    hb = hnb[c % 3]
                    khb = "hnb%d" % (c % 3)
                    for j in range(8):
                        identr(pT[:, j * 128:(j + 1) * 128], hb[:, j * 128:(j + 1) * 128], [khb], ["pT"])
                    cp("scalar", hT[:, :, cl * 128:(cl + 1) * 128], pT[:].rearrange("p (j t) -> p j t", j=8),
                       ["pT"], [(kh, cl)])

                for c in range(3):
                    norm_load(c)
                for c in range(3):
                    norm_front(c)
                for c in range(3):
                    norm_back(c)
                for i in range(NT):
                    t0 = i * TS
                    hT = hnT[i % 2]
                    kh = "hnT%d" % (i % 2)
                    hkeys = [(kh, 0), (kh, 1), (kh, 2)]
                    if i + 1 < NT:
                        for cl in range(3):
                            norm_load(3 * (i + 1) + cl)
                    def fm_seg(seg, i=i, t0=t0, hT=hT, hkeys=hkeys):
                        fs = fst[seg % 3]
                        kfs = "fst%d" % (seg % 3)
                        gate = seg in (1, 3, 6)
                        for j in range(4):
                            p = pf[nfm[0] % 2]
                            kp = "pf%d" % (nfm[0] % 2)
                            col0 = seg * 512 + j * 128
                            for dk in range(8):
                                mm(p[:, 0:TS], W[:, dk, col0:col0 + 128], hT[:, dk, :], dk == 0, dk == 7,
                                   wkeys + hkeys, [kp])
                            if gate:
                                act(fs[:, j, :], p[:, 0:TS], AF.Silu, [kp], [kfs])
                            else:
                                cp("scalar", fs[:, j, :], p[:, 0:TS], [kp], [kfs])
                            nfm[0] += 1
                        dma("sync", DA(fm_d, seg * 512 * TW + 8 + t0, [[TW, 128], [128 * TW, 4], [1, TS]]), fs[:],
                            [kfs, "fm_margin"], [("fm", seg)])
                        if seg in (0, 2, 4) and i + 1 < NT:
                            norm_front(3 * (i + 1) + seg // 2)
                    zs = zst[i % 2]
                    kzs = "zst%d" % (i % 2)
                    qTs = qTst[i % 2]
                    kqs = "qTst%d" % (i % 2)
                    kTs = kTst[i % 2]
                    kks = "kTst%d" % (i % 2)
                    vs = vst[i % 2]
                    kvs = "vst%d" % (i % 2)
                    dts = dtst[i % 2]
                    kds = "dtst%d" % (i % 2)

                    def qtrans(cl, i=i, qTs=qTs, kqs=kqs, kTs=kTs, kks=kks):
                        c = 3 * i + cl
                        lt = slice(cl * 128, (cl + 1) * 128)
                        qrb = qr[c % 2]
                        kqr = "qr%d" % (c % 2)
                        for j in range(5):
                            identr(ptq[:, j * 128:(j + 1) * 128], qrb[:, j * 128:(j + 1) * 128],
                                   [(kqr, 0), (kqr, 1)], ["ptq"])
                        cp("scalar", qTs[:, :, lt], ptq[:, 0:512].rearrange("p (j t) -> p j t", j=4), ["ptq"], [kqs])
                        cp("vector", kTs[:, lt], ptq[:, 512:640], ["ptq"], [kks])

                    for cl in range(3):
                        c = 3 * i + cl
                        lt = slice(cl * 128, (cl + 1) * 128)
                        if cl >= 1:
                            fm_seg(2 * cl - 2)
                            fm_seg(2 * cl - 1)
                        for dk in range(8):
                            mm(pz[:, :], hT[:, dk, lt], W[:, dk, 3584:4096], dk == 0, dk == 7, wkeys + hkeys, ["pz"])
                        act(zs[:, cl, :], pz[:, :], AF.Silu, ["pz"], [kzs])
                        for dk in range(8):
                            mm(pq[:, :], hT[:, dk, lt], W[:, dk, 4096:4608], dk == 0, dk == 7, wkeys + hkeys, ["pq"])
                        for dk in range(8):
                            mm(pkv[:, 0:272], hT[:, dk, lt], W[:, dk, 4608:4880], dk == 0, dk == 7, wkeys + hkeys, ["pkv"])
                        if cl >= 1:
                            qtrans(cl - 1)
                        cp("vector", vs[:, cl, :], pkv[:, 128:256], ["pkv"], [kvs])
                        cp("vector", dts[:, cl, :], pkv[:, 256:272], ["pkv"], [kds])
                        sta = stt_t[c % 2]
                        ks = "stat%d" % (c % 2)
                        act(sq[:, 0:512], pq[:, :], AF.Square, ["pq"], [("sq", 0)])
                        act(sq[:, 512:640], pkv[:, 0:128], AF.Square, ["pkv"], [("sq", 1)])
                        S.op("vector", lambda e, o=sta[:, 8:18], a=sq[:].rearrange("p (h d) -> p h d", h=10):
                             e.tensor_reduce(out=o, in_=a, axis=AX.X, op=ALU.add), [("sq", 0), ("sq", 1)], [(ks, 8)])
                        tsc("vector", sta[:, 18:28], sta[:, 8:18], 1.0 / 64, EPS, ALU.mult, ALU.add, [(ks, 8)], [(ks, 18)])
                        tt("gpsimd", sta[:, 38:48], sta[:, 18:28], mhalf[:, 0:10], ALU.pow, [(ks, 18), "mhalf"], [(ks, 38)])
                        tt("vector", qn[:, 0:512].rearrange("p (h d) -> p h d", h=8),
                           pq[:, :].rearrange("p (h d) -> p h d", h=8),
                           sta[:, 38:46].unsqueeze(2).broadcast_to([128, 8, 64]), ALU.mult, ["pq", (ks, 38)], [("qn", 0)])
                        tt("vector", qn[:, 512:640].rearrange("p (h d) -> p h d", h=2),
                           pkv[:, 0:128].rearrange("p (h d) -> p h d", h=2),
                           sta[:, 46:48].unsqueeze(2).broadcast_to([128, 2, 64]), ALU.mult, ["pkv", (ks, 38)], [("qn", 1)])
                        tt("gpsimd", qw[:], qn[:], wqk[:], ALU.mult, [("qn", 0), ("qn", 1), "wqk"], ["qw"])
                        qw3 = qw[:].rearrange("p (h d) -> p h d", h=10)
                        tt("gpsimd", m1[:].rearrange("p (h d) -> p h d", h=10), qw3,
                           rope[:, 0, c, :].unsqueeze(1).broadcast_to([128, 10, 64]), ALU.mult, ["qw", "rope"], ["m1"])
                        tt("gpsimd", m2[:].rearrange("p (h d) -> p h d", h=10), qw3,
                           rope[:, 1, c, :].unsqueeze(1).broadcast_to([128, 10, 64]), ALU.mult, ["qw", "rope"], ["m2"])
                        qrb = qr[c % 2]
                        kqr = "qr%d" % (c % 2)

                        def pv(tile_, off):
                            return bass.AP(tile_, off, [[640, 128], [2, 320]])
                        tt("vector", pv(qrb, 0), pv(m1, 0), pv(m2, 1), ALU.subtract, ["m1", "m2"], [(kqr, 0)])
                        tt("gpsimd", pv(qrb, 1), pv(m2, 0), pv(m1, 1), ALU.add, ["m1", "m2"], [(kqr, 1)])
                    for seg in (4, 5, 6):
                        fm_seg(seg)
                    if i + 1 < NT:
                        for cl in range(3):
                            norm_back(3 * (i + 1) + cl)
                    qtrans(2)
                    dma("sync", DA(sz_d, t0 * 512, [[512, 128], [128 * 512, 3], [1, 512]]), zs[:], [kzs], ["sz"])
                    dma("sync", DA(qT_d, t0, [[TP, 128], [128 * TP, 4], [1, TS]]), qTs[:], [kqs], ["qT"])
                    dma("sync", DA(kT_d, t0, [[TP, 128], [1, TS]]), kTs[:], [kks], ["kT"])
                    dma("sync", DA(vtm_d, t0 * 128, [[128, 128], [128 * 128, 3], [1, 128]]), vs[:], [kvs], ["vtm"])
                    dma("sync", DA(dtr_d, t0 * 16, [[16, 128], [128 * 16, 3], [1, 16]]), dts[:], [kds], ["dtr"])
            S.barrier()

        def phase_p12(l):
            with contextlib.ExitStack() as st:
                pw = alloc(st, "pw", [128, 4, 128], BF16)
                psc = alloc(st, "psc", [128, 4], F32)
                cinv = alloc(st, "cinv", [128, 4, 2, 8], F32)
                ub = [alloc(st, "ub", [128, 4, 400], BF16) for _ in range(2)]
                sgp = [alloc(st, "sg", [128, 4, TS], BF16) for _ in range(2)]
                T2 = alloc(st, "T2", [128, 4, 400], F32)
                T4 = alloc(st, "T4", [128, 4, 400], F32)
                T8 = alloc(st, "T8", [128, 4, 400], F32)
                T16 = alloc(st, "T16", [128, 4, 400], F32)
                tmp = alloc(st, "ptmp", [128, 4, 8], F32)
                dT = [alloc(st, "dT", [128, 4, TS], BF16) for _ in range(2)]
                ostp = [alloc(st, "ost", [128, 4, TS], BF16) for _ in range(2)]
                Ap = alloc(st, "Ap", [128, NCH, 512], BF16)
                Bp = alloc(st, "Bp", [128, NCH, 512], BF16)
                ring = [alloc(st, "ring", [128, PCC, TS], BF16) for _ in range(4)]
                cs = alloc(st, "dftc", [128, 2, 4, 512], BF16)
                wf = alloc(st, "wf", [128, 4, 512], BF16)
                cw = alloc(st, "cw", [128, 2, 4, 512], BF16)
                uT = [alloc(st, "uT", [128, 4, TS], BF16) for _ in range(2)]
                sg = [alloc(st, "sgf", [128, 4, TS], BF16) for _ in range(2)]
                ost = [alloc(st, "ostf", [128, 4, TS], BF16) for _ in range(2)]
                acc = [palloc(st, "acc", [128, 512], F32) for _ in range(4)]
                py = [palloc(st, "py", [128, 512], F32) for _ in range(4)]
                banks = [(acc[k], ("acc", 0, k)) for k in range(4)] + [(py[g], ("py", g)) for g in range(4)]

                dma("gpsimd", pw[:], DA(poolw_d, l * 4 * 128 * 128, [[128, 128], [128 * 128, 4], [1, 128]]), (), ["pw"])
                dma("sync", psc[:].unsqueeze(2), DA(pools_d, l * 512, [[1, 128], [128, 4], [1, 1]]), (), ["psc"], slow=True)
                dma("sync", cinv[:], c_pool_d.ap(), (), ["cinv"])
                dma("sync", cs[:], c_dftc_d.ap(), (), ["dftc"])
                dma("gpsimd", wf[:], DA(fw_d, l * 512 * 512, [[512, 128], [128 * 512, 4], [1, 512]]), (), ["wf"])
                Ts = (T2, T4, T8, T16)

                def p1_front(i):
                    t0 = i * TS
                    u = ub[i % 2]
                    ku = "ub%d" % (i % 2)
                    d_ = dT[i % 2]
                    kd = "dT%d" % (i % 2)
                    dma("sync", u[:], DA(fm_d, t0, [[TW, 128], [128 * TW, 4], [1, 400]]), [("fm", 0), "fm_margin"], [ku])
                    dma("sync", sgp[i % 2][:], DA(fm_d, 512 * TW + 8 + t0, [[TW, 128], [128 * TW, 4], [1, TS]]), [("fm", 1)],
                        ["sg%d" % (i % 2)])
                    tt("vector", T2[:, :, 0:399], u[:, :, 0:399], u[:, :, 1:400], ALU.add, [ku], ["T2"])
                    tt("gpsimd", T4[:, 1:4, 0:397], T2[:, 1:4, 0:397], T2[:, 1:4, 2:399], ALU.add, ["T2"], ["T4"])
                    tt("vector", T8[:, 2:4, 0:393], T4[:, 2:4, 0:393], T4[:, 2:4, 4:397], ALU.add, ["T4"], ["T8"])
                    tt("gpsimd", T16[:, 3, 0:385], T8[:, 3, 0:385], T8[:, 3, 8:393], ALU.add, ["T8"], ["T16"])
                    for g, w_ in enumerate((2, 4, 8, 16)):
                        half = w_ // 2
                        Pg = Ts[g]
                        kP = ("T2", "T4", "T8", "T16")[g]
                        o8 = 8 - half
                        stt(d_[:, g, :], Pg[:, g, o8:o8 + TS], 1.0 / w_, u[:, g, 8:8 + TS], ALU.mult, ALU.subtract,
                            [kP, ku], [(kd, g)])
                        for side, tile_i, m0 in ((0, 0, NPAD), (1, NT - 1, TS - 8)):
                            if i != tile_i:
                                continue
                            tt("vector", tmp[:, g, :], Pg[:, g, o8 + m0:o8 + m0 + 8], cinv[:, g, side, :], ALU.mult,
                               [kP, "cinv"], [("ptmp", g)])
                            tt("vector", d_[:, g, m0:m0 + 8], tmp[:, g, :], u[:, g, 8 + m0:8 + m0 + 8], ALU.subtract,
                               [("ptmp", g), ku], [(kd, g)])

                def p1_back(i):
                    t0 = i * TS
                    d_ = dT[i % 2]
                    kd = "dT%d" % (i % 2)
                    s_ = sgp[i % 2]
                    ksg = "sg%d" % (i % 2)
                    o_ = ostp[i % 2]
                    ko = "ost%d" % (i % 2)
                    for g in range(4):
                        mm(py[g][:, 0:TS], pw[:, g, :], d_[:, g, :], True, True, ["pw", (kd, g)], [("py", g)])
                    for g in range(4):
                        stt(o_[:, g, :], py[g][:, 0:TS], psc[:, g:g + 1], s_[:, g, :], ALU.mult, ALU.mult,
                            [("py", g), "psc", ksg], [(ko, g)])
                    dma("sync", DA(mixT_d, t0, [[TP, 128], [128 * TP, 4], [1, TS]]), o_[:],
                        [(ko, g) for g in range(4)], [("mixT", 0)])

                n = 0
                for tb in range(2):
                    for ci in range(4):
                        p, kp = banks[n % 8]
                        for mk in range(4):
                            mm(p[:, :], cs[:, tb, mk, ci * 128:(ci + 1) * 128], wf[:, mk, :], mk == 0, mk == 3,
                               ["dftc", "wf"], [kp])
                        cp("scalar" if n % 2 else "vector", cw[:, tb, ci, :], p[:, :], [kp], [("cw", tb, ci)])
                        n += 1
                cwk = [("cw", tb, ci) for tb in range(2) for ci in range(4)]
                for i in range(NT):
                    t0 = i * TS
                    u = uT[i % 2]
                    ku = "uT%d" % (i % 2)
                    dma("sync", u[:], DA(fm_d, 1024 * TW + 8 + t0, [[TW, 128], [128 * TW, 4], [1, TS]]), [("fm", 2)], [ku])
                    for cl in range(3):
                        c = 3 * i + cl
                        for tb, dst in ((0, Ap), (1, Bp)):
                            p, kp = banks[(tb * 3 + cl) % 8]
                            for ck in range(4):
                                mm(p[:, :], u[:, ck, cl * 128:(cl + 1) * 128], cw[:, tb, ck, :], ck == 0, ck == 3,
                                   [ku] + cwk, [kp])
                            cp("scalar" if tb else "vector", dst[:, c, :], p[:, :], [kp], [("ApBp", c)])
                apk = [("ApBp", c) for c in range(NCH)]
                npiece = 0
                for kt in range(NT):
                    t0 = kt * TS
                    s_ = sg[kt % 2]
                    ksg = "sgf%d" % (kt % 2)
                    o_ = ost[kt % 2]
                    ko = "ostf%d" % (kt % 2)
                    dma("sync", s_[:], DA(fm_d, 1536 * TW + 8 + t0, [[TW, 128], [128 * TW, 4], [1, TS]]), [("fm", 3)], [ksg])
                    p1_front(kt)
                    for pc in range(NPC):
                        for tb in range(2):
                            rb = ring[npiece % 4]
                            kr = "ring%d" % (npiece % 4)
                            npiece += 1
                            off = (((tb * NT + kt) * NPC + pc) * 128) * (PCC * TS)
                            dma("sync", rb[:], DA(c_tab_d, off, [[PCC * TS, 128], [TS, PCC], [1, TS]]), (), [kr])
                            src = Bp if tb else Ap
                            for tci in range(PCC):
                                tc_ = pc * PCC + tci
                                first = (pc == 0 and tb == 0 and tci == 0)
                                last = (pc == NPC - 1 and tb == 1 and tci == PCC - 1)
                                for dc in range(4):
                                    mm(acc[dc][:, 0:TS], src[:, tc_, dc * 128:(dc + 1) * 128], rb[:, tci, :], first, last,
                                       apk + [kr], [("acc", 0, dc)])
                    for dc in range(4):
                        tt("vector", o_[:, dc, :], acc[dc][:, 0:TS], s_[:, dc, :], ALU.mult,
                           [("acc", 0, dc), ksg], [(ko, dc)])
                    dma("sync", DA(mixT_d, 512 * TP + t0, [[TP, 128], [128 * TP, 4], [1, TS]]), o_[:],
                        [(ko, dc) for dc in range(4)], [("mixT", 1)])
                    p1_back(kt)
            S.barrier()

        def phase_p3(l):
            with contextlib.ExitStack() as st:
                BT = alloc(st, "BT", [128, 2, TP], BF16)
                CT = alloc(st, "CT", [128, 2, TP], BF16)
                xtm = alloc(st, "xtm", [128, NCH, 512], BF16)
                btm = alloc(st, "btm", [128, NCH, 256], BF16)
                dtr = alloc(st, "dtr", [128, NCH, 16], F32)
                dtt = alloc(st, "dtt", [128, NCH, 16], F32)
                da = alloc(st, "da", [128, NCH, 16], F32)
                tA = alloc(st, "tA", [128, NCH, 16], F32)
                tB = alloc(st, "tB", [128, NCH, 16], F32)
                dtb = alloc(st, "dtb", [128, 16], F32)
                alg = alloc(st, "alg", [128, 16], F32)
                dsk = alloc(st, "dsk", [128, 8], F32)
                snw = alloc(st, "snw", [128, 512], F32)
                cwt = alloc(st, "cwt", [128, 8, 4], F32)
                cbt = alloc(st, "cbt", [128, 8], F32)
                dg = alloc(st, "dg", [128, 8, 4, 128], BF16)
                cf = alloc(st, "cf", [128, 3, 128], F32)
                cb = alloc(st, "cb", [128, 6, 128], BF16)
                xin = [alloc(st, "xin", [128, 8, TS + 3], BF16) for _ in range(2)]
                xTt = [alloc(st, "xTt", [128, 4, TS], BF16) for _ in range(2)]
                H = alloc(st, "H", [128, 512], F32)
                Hb = alloc(st, "Hb", [128, 512], BF16)
                Ht = alloc(st, "Ht", [128, 512], F32)
                cst = alloc(st, "cst", [128, 24], F32)
                E = alloc(st, "E", [128, 24], F32)
                R = alloc(st, "R", [128, 8, 128], BF16)
                cbm = alloc(st, "cbm", [128, 2, 128], BF16)
                eseg = alloc(st, "eseg", [128, 8, 128], BF16)
                MT = alloc(st, "MT", [128, 8, 128], BF16)
                xdt = alloc(st, "xdt", [128, 512], BF16)
                xw = alloc(st, "xw", [128, 512], BF16)
                ytmp = alloc(st, "ytmp", [128, 512], F32)
                yc = [alloc(st, "yc", [128, 512], F32) for _ in range(2)]
                yfl = [alloc(st, "yfl", [128, 512], F32) for _ in range(2)]
                szl = [alloc(st, "szl", [128, 512], BF16) for _ in range(2)]
                yg = alloc(st, "yg", [128, 512], F32)
                yjunk = alloc(st, "yjunk", [128, 512], BF16)
                ynb = alloc(st, "ynb", [128, 512], BF16)
                ystat = alloc(st, "ystat", [128, 4], F32)
                yst = [alloc(st, "yst", [128, 4, TS], BF16) for _ in range(2)]
                pcv = [palloc(st, "pcv", [128, 512], F32) for _ in range(2)]
                ptr = palloc(st, "ptr", [128, 1024], BF16)
                pcb = pcv[0]
                pseg = [palloc(st, "pseg", [128, 512], F32) for _ in range(2)]
                pyd = palloc(st, "pyd", [128, 512], F32)
                pyo = palloc(st, "pyo", [128, 512], F32)
                psn = palloc(st, "psn", [128, 512], F32)
                pyd2 = [pcv[1], pyd]
                kpyd = ["pcv1", "pyd"]
                cbm2 = [alloc(st, "cbm2", [128, 2, 128], BF16) for _ in range(2)]
                cst2 = [alloc(st, "cst2", [128, 24], F32) for _ in range(2)]
                E2 = [alloc(st, "E2", [128, 24], F32) for _ in range(2)]
                MT2 = [alloc(st, "MT2", [128, 8, 128], BF16) for _ in range(2)]
                xdt2 = [alloc(st, "xdt2", [128, 512], BF16) for _ in range(2)]
                xw2 = [alloc(st, "xw2", [128, 512], BF16) for _ in range(2)]
                ytmp2 = alloc(st, "ytmp2", [128, 512], F32)

                dma("sync", cf[:], c_ssdf_d.ap(), (), ["cf"])
                dma("sync", cb[:], c_ssdb_d.ap(), (), ["cb"])
                dma("sync", dtb[:], DA(dtb_d, l * 16, [[0, 128], [1, 16]]), (), ["dtb"])
                dma("sync", alg[:], DA(alog_d, l * 16, [[0, 128], [1, 16]]), (), ["alg"])
                dma("sync", dsk[:], DA(dskip_d, l * 8, [[0, 128], [1, 8]]), (), ["dsk"])
                dma("sync", snw[:], DA(ssdnw_d, l * 512, [[0, 128], [1, 512]]), (), ["snw"])
                for j in range(4):
                    dma("sync", cwt[:, :, j:j + 1], DA(convw_d, (l * 4 + j) * 1024, [[1, 128], [128, 8], [1, 1]]), (), ["cwt"], slow=True)
                dma("sync", cbt[:].unsqueeze(2), DA(convb_d, l * 1024, [[1, 128], [128, 8], [1, 1]]), (), ["cbt"], slow=True)
                for ch in range(8):
                    for j in range(4):
                        tsc("gpsimd", dg[:, ch, j, :], ident[:], cwt[:, ch, j:j + 1], None, ALU.mult, None,
                            ["ident", "cwt"], [("dg", ch)])
                for c3 in range(3):
                    dma("sync", dtr[:, c3 * 11:(c3 + 1) * 11, :],
                        DA(dtr_d, c3 * 11 * 128 * 16, [[16, 128], [128 * 16, 11], [1, 16]]), ["dtr"], ["dtrl"])
                bcd = dtb[:].unsqueeze(1).broadcast_to([128, NCH, 16])
                tt("vector", tA[:], dtr[:], bcd, ALU.add, ["dtrl", "dtb"], ["tA"])
                act(tB[:], tA[:], AF.Abs, ["tA"], ["tB"])
                act(tB[:], tB[:], AF.Exp, ["tB"], ["tB"], scale=-1.0)
                tsc("vector", tB[:], tB[:], 1.0, None, ALU.add, None, ["tB"], ["tB"])
                act(tB[:], tB[:], AF.Ln, ["tB"], ["tB"])
                tsc("vector", tA[:], tA[:], 0.0, None, ALU.max, None, ["tA"], ["tA"])
                tt("vector", dtt[:], tA[:], tB[:], ALU.add, ["tA", "tB"], ["dtt"])
                act(alg[:], alg[:], AF.Exp, ["alg"], ["alg"])
                tsc("vector", alg[:], alg[:], -1.0, None, ALU.mult, None, ["alg"], ["alg"])
                tt("vector", da[:], dtt[:], alg[:].unsqueeze(1).broadcast_to([128, NCH, 16]), ALU.mult, ["dtt", "alg"], ["da"])

                for i in range(NT):
                    t0 = i * TS
                    xi = xin[i % 2]
                    kx = "xin%d" % (i % 2)
                    xt_ = xTt[i % 2]
                    kxt = "xTt%d" % (i % 2)
                    dma("sync", xi[:], DA(fm_d, 2048 * TW + 8 + t0 - 2, [[TW, 128], [128 * TW, 8], [1, TS + 3]]),
                        [("fm", 4), ("fm", 5), "fm_margin"], [kx])
                    for ch in range(8):
                        p = pcv[ch % 2]
                        kp = "pcv%d" % (ch % 2)
                        for j in range(4):
                            mm(p[:, 0:TS], dg[:, ch, j, :], xi[:, ch, j:j + TS], j == 0, j == 3, [("dg", ch), kx], [kp])
                        if ch < 4:
                            dst, kdst = xt_[:, ch, :], (kxt, ch)
                        elif ch < 6:
                            dst, kdst = BT[:, ch - 4, t0:t0 + TS], ("BT", i)
                        else:
                            dst, kdst = CT[:, ch - 6, t0:t0 + TS], ("CT", i)
                        act(dst, p[:, 0:TS], AF.Silu, [kp, "cbt"], [kdst], bias=cbt[:, ch:ch + 1])
                    if i == 0:
                        memset("gpsimd", xt_[:, :, 0:NPAD], 0.0, [(kxt, ch) for ch in range(4)])
                    for cl in range(3):
                        c = 3 * i + cl
                        lt = slice(cl * 128, (cl + 1) * 128)
                        for ch in range(4):
                            identr(ptr[:, ch * 128:(ch + 1) * 128], xt_[:, ch, lt], [(kxt, q) for q in range(4)], ["ptr"])
                        for g in range(2):
                            identr(ptr[:, 512 + g * 128:512 + (g + 1) * 128], BT[:, g, t0 + cl * 128:t0 + (cl + 1) * 128],
                                   [("BT", i)], ["ptr"])
                        cp("vector", xtm[:, c, :], ptr[:, 0:512], ["ptr"], [("xtm", c)])
                        cp("scalar", btm[:, c, :], ptr[:, 512:768], ["ptr"], [("btm", c)])

                ydx = ytmp2
                for d_ in range(2):
                    memset("gpsimd", H[:], 0.0, ["H"])
                    memset("gpsimd", Hb[:], 0.0, ["Hb"])
                    order = list(range(NCH)) if d_ == 0 else list(range(NCH - 1, -1, -1))

                    def stage_a(n_, d_=d_, order=order):
                        c = order[n_]
                        i = c // 3
                        b2 = n_ % 2
                        cols = slice(c * 128, (c + 1) * 128)
                        dac = da[:, c, d_ * 8:(d_ + 1) * 8]
                        for g in range(2):
                            mm(pcb[:, g * 128:(g + 1) * 128], BT[:, g, cols], CT[:, g, cols], True, True,
                               [("BT", i), ("CT", i)], ["pcv0"])
                        mm(pcb[:, 256:264], cf[:, d_, :], dac, True, True, ["cf", "da"], ["pcv0"])
                        mm(pcb[:, 264:272], cf[:, 2, :], dac, True, True, ["cf", "da"], ["pcv0"])
                        tt("vector", cbm2[b2][:], pcb[:, 0:256].rearrange("p (g s) -> p g s", g=2),
                           cb[:, 4 + d_, :].unsqueeze(1).broadcast_to([128, 2, 128]), ALU.mult, ["pcv0", "cb"], ["cbm%d" % b2])
                        cp("vector", cst2[b2][:, 0:16], pcb[:, 256:272], ["pcv0"], [("cst%d" % b2, 0)])
                        tt("vector", cst2[b2][:, 16:24], cst2[b2][:, 8:16], cst2[b2][:, 0:8], ALU.subtract,
                           [("cst%d" % b2, 0)], [("cst%d" % b2, 1)])
                        act(E2[b2][:], cst2[b2][:], AF.Exp, [("cst%d" % b2, 0), ("cst%d" % b2, 1)], ["E%d" % b2])
                        tt("gpsimd", R[:], cb[:, d_, :].unsqueeze(1).broadcast_to([128, 8, 128]),
                           dac.unsqueeze(2).broadcast_to([128, 8, 128]), ALU.mult, ["cb", "da"], ["R"])
                        for hh in range(2):
                            mm(pseg[hh][:, :], cb[:, 2 + d_, :], R[:, hh * 4:(hh + 1) * 4, :], True, True, ["cb", "R"],
                               [("pseg", hh)])
                            act(eseg[:, hh * 4:(hh + 1) * 4, :], pseg[hh][:, :].rearrange("p (h l) -> p h l", h=4), AF.Exp,
                                [("pseg", hh)], [("eseg", hh)])
                            tt("vector" if hh else "gpsimd", MT2[b2][:, hh * 4:(hh + 1) * 4, :], eseg[:, hh * 4:(hh + 1) * 4, :],
                               cbm2[b2][:, hh, :].unsqueeze(1).broadcast_to([128, 4, 128]), ALU.mult,
                               [("eseg", hh), "cbm%d" % b2], [("MT%d" % b2, hh)])
                        tt("gpsimd", xdt2[b2][:].rearrange("p (h d) -> p h d", h=8),
                           xtm[:, c, :].rearrange("p (h d) -> p h d", h=8),
                           dtt[:, c, d_ * 8:(d_ + 1) * 8].unsqueeze(2).broadcast_to([128, 8, 64]), ALU.mult,
                           [("xtm", c), "dtt"], ["xdt%d" % b2])
                        tt("gpsimd", xw2[b2][:].rearrange("p (h d) -> p h d", h=8),
                           xdt2[b2][:].rearrange("p (h d) -> p h d", h=8),
                           E2[b2][:, 16:24].unsqueeze(2).broadcast_to([128, 8, 64]), ALU.mult, ["xdt%d" % b2, "E%d" % b2],
                           ["xw%d" % b2])
                        for h in range(8):
                            mm(pyd2[b2][:, h * 64:(h + 1) * 64], MT2[b2][:, h, :], xdt2[b2][:, h * 64:(h + 1) * 64], True, True,
                               [("MT%d" % b2, h // 4), "xdt%d" % b2], [kpyd[b2]])

                    def stage_b(n_, d_=d_, order=order):
                        c = order[n_]
                        i = c // 3
                        b2 = n_ % 2
                        cols = slice(c * 128, (c + 1) * 128)
                        Eb = E2[b2]
                        kE = "E%d" % b2
                        for g in range(2):
                            mm(pyo[:, g * 256:(g + 1) * 256], CT[:, g, cols], Hb[:, g * 256:(g + 1) * 256], True, True,
                               [("CT", i), "Hb"], ["pyo"])
                        tt("vector", ytmp[:].rearrange("p (h d) -> p h d", h=8), pyo[:, :].rearrange("p (h d) -> p h d", h=8),
                           Eb[:, 0:8].unsqueeze(2).broadcast_to([128, 8, 64]), ALU.mult, ["pyo", kE], ["ytmp"])
                        ycb = yc[b2]
                        kyc = "yc%d" % b2
                        tt("vector", ycb[:], pyd2[b2][:, :], ytmp[:], ALU.add, [kpyd[b2], "ytmp"], [kyc])
                        for g in range(2):
                            mm(psn[:, g * 256:(g + 1) * 256], btm[:, c, g * 128:(g + 1) * 128], xw2[b2][:, g * 256:(g + 1) * 256],
                               True, True, [("btm", c), "xw%d" % b2], ["psn"])
                        tt("gpsimd", Ht[:].rearrange("p (h d) -> p h d", h=8), H[:].rearrange("p (h d) -> p h d", h=8),
                           Eb[:, 8:16].unsqueeze(2).broadcast_to([128, 8, 64]), ALU.mult, ["H", kE], ["Ht"])
                        tt("vector", H[:], psn[:, :], Ht[:], ALU.add, ["psn", "Ht"], ["H"])
                        cp("scalar", Hb[:], H[:], ["H"], ["Hb"])
                        if d_ == 0:
                            dma("sync", yf_d.ap()[c * 128:(c + 1) * 128, :], ycb[:], [kyc], ["yf"])
                        else:
                            yfb = yfl[b2]
                            kyf = "yfl%d" % b2
                            szb = szl[b2]
                            ksz = "szl%d" % b2
                            dma("sync", yfb[:], yf_d.ap()[c * 128:(c + 1) * 128, :], ["yf"], [kyf])
                            dma("sync", szb[:], sz_d.ap()[c * 128:(c + 1) * 128, :], ["sz"], [ksz])
                            tt("gpsimd", ydx[:].rearrange("p (h d) -> p h d", h=8),
                               xtm[:, c, :].rearrange("p (h d) -> p h d", h=8),
                               dsk[:].unsqueeze(2).broadcast_to([128, 8, 64]), ALU.mult, [("xtm", c), "dsk"], ["ydx"])
                            tt("vector", yg[:], ycb[:], yfb[:], ALU.add, [kyc, kyf], ["yg"])
                            tt("vector", yg[:], yg[:], ydx[:], ALU.add, ["yg", "ydx"], ["yg"])
                            tt("vector", yg[:], yg[:], szb[:], ALU.mult, ["yg", ksz], ["yg"])
                            act(yjunk[:], yg[:], AF.Square, ["yg"], ["yjunk", ("ystat", 0)], accum_out=ystat[:, 0:1])
                            tsc("vector", ystat[:, 1:2], ystat[:, 0:1], 1.0 / 512, EPS, ALU.mult, ALU.add, [("ystat", 0)], [("ystat", 1)])
                            tt("gpsimd", ystat[:, 3:4], ystat[:, 1:2], mhalf[:, 0:1], ALU.pow, [("ystat", 1), "mhalf"], [("ystat", 3)])
                            stt(ynb[:], yg[:], ystat[:, 3:4], snw[:], ALU.mult, ALU.mult, ["yg", ("ystat", 3), "snw"], ["ynb"])
                            ti = c // 3
                            cl = c % 3
                            ys = yst[ti % 2]
                            kys = "yst%d" % (ti % 2)
                            for ch in range(4):
                                identr(ptr[:, ch * 128:(ch + 1) * 128], ynb[:, ch * 128:(ch + 1) * 128], ["ynb"], ["ptr"])
                            cp("scalar", ys[:, :, cl * 128:(cl + 1) * 128], ptr[:, 0:512].rearrange("p (j t) -> p j t", j=4),
                               ["ptr"], [(kys, cl)])
                            if cl == 0:
                                dma("sync", DA(mixT_d, 1024 * TP + ti * TS, [[TP, 128], [128 * TP, 4], [1, TS]]), ys[:],
                                    [(kys, q) for q in range(3)], [("mixT", 2)])

                    stage_a(0)
                    for n_ in range(NCH):
                        if n_ + 1 < NCH:
                            stage_a(n_ + 1)
                        stage_b(n_)
            S.barrier()

        def phase_p4(l):
            TQ = 512
            qtiles = [(0, 128)] + [(128 + k * TQ, TQ) for k in range(8)]
            with contextlib.ExitStack() as st:
                KT2 = alloc(st, "KT2", [128, 2, TP], BF16)
                Vp = alloc(st, "Vp", [128, NCH, 2, 192], BF16)
                vmask = alloc(st, "vmask", [128, 64], BF16)
                QT = [alloc(st, "QT", [128, 4, TQ], BF16) for _ in range(2)]
                sg = [alloc(st, "sga", [128, 4, TQ], BF16) for _ in range(2)]
                P = [alloc(st, "P", [128, 2, TQ], BF16) for _ in range(3)]
                rec = alloc(st, "rec", [128, TQ], F32)
                yt = alloc(st, "yt", [128, TQ], F32)
                ost = [alloc(st, "osta", [128, 4, TQ], BF16) for _ in range(2)]
                Sp = [palloc(st, "Sp", [128, 2, 512], F32) for _ in range(2)]
                Oe = [palloc(st, "Oe", [128, 512], F32) for _ in range(2)]
                Oo = [palloc(st, "Oo", [128, 512], F32) for _ in range(2)]
                dma("sync", vmask[:], c_vmask_d.ap(), (), ["vmask"])
                for g in range(2):
                    for hs in range(2):
                        dma("sync", KT2[hs * 64:(hs + 1) * 64, g, :], DA(kT_d, g * 64 * TP, [[TP, 64], [1, TP]]), ["kT"],
                            [("KT2", g, hs)])
                vkeys = [("Vp", c3, g) for c3 in range(3) for g in range(2)]
                memset("gpsimd", Vp[:], 1.0, vkeys)
                for g in range(2):
                    for o0 in (0, 128):
                        cp("gpsimd", Vp[:, 0, g, o0:o0 + 64], vmask[:], ["vmask"], [("Vp", 0, g)])
                for c3 in range(3):
                    for g in range(2):
                        dma("sync", Vp[:, c3 * 11:(c3 + 1) * 11, g, 64:128],
                            DA(vtm_d, c3 * 11 * 128 * 128 + g * 64, [[128, 128], [128 * 128, 11], [1, 64]]), ["vtm"],
                            [("Vp", c3, g)])
                ktk = [("KT2", g, hs) for g in range(2) for hs in range(2)]
                npair = 0
                nS = [0]

                def p4_load(i):
                    t0, tw = qtiles[i]
                    dma("sync", QT[i % 2][:, :, 0:tw], DA(qT_d, t0, [[TP, 128], [128 * TP, 4], [1, tw]]), ["qT"], ["QT%d" % (i % 2)])
                    dma("sync", sg[i % 2][:, :, 0:tw], DA(fm_d, 3072 * TW + 8 + t0, [[TW, 128], [128 * TW, 4], [1, tw]]),
                        [("fm", 6)], ["sga%d" % (i % 2)])

                for i in range(len(qtiles)):
                    t0, tw = qtiles[i]
                    q_ = QT[i % 2]
                    kq = "QT%d" % (i % 2)
                    s_ = sg[i % 2]
                    ksg = "sga%d" % (i % 2)
                    o_ = ost[i % 2]
                    ko = "osta%d" % (i % 2)
                    if i == 0:
                        p4_load(0)
                    if i + 1 < len(qtiles):
                        p4_load(i + 1)
                    for j in range(4):
                        g = j // 2
                        oe = Oe[npair % 2]
                        oo = Oo[npair % 2]
                        koe = "Oe%d" % (npair % 2)
                        koo = "Oo%d" % (npair % 2)
                        npair += 1
                        pend = {}

                        def qk(kc, g=g, j=j, q_=q_, kq=kq, pend=pend, tw=tw):
                            n_ = nS[0]
                            nS[0] += 1
                            sp = Sp[n_ % 2]
                            ksp = "Sp%d" % (n_ % 2)
                            pb = P[n_ % 3]
                            kpb = "P%d" % (n_ % 3)
                            kcols = slice(kc * 128, (kc + 1) * 128)
                            mm(sp[:, 0, 0:tw], KT2[0:64, g, kcols], q_[0:64, j, 0:tw], True, True, ktk + [kq], [ksp])
                            mm(sp[:, 1, 0:tw], KT2[64:128, g, kcols], q_[64:128, j, 0:tw], True, True, ktk + [kq], [ksp])
                            act(pb[:, :, 0:tw], sp[:, :, 0:tw], AF.Exp, [ksp], [kpb])
                            pend[kc] = (pb, kpb)

                        def pvm(kc, g=g, oe=oe, oo=oo, koe=koe, koo=koo, pend=pend, tw=tw):
                            pb, kpb = pend.pop(kc)
                            mm(oe[:, 0:tw], Vp[:, kc, g, 64:192], pb[:, 0, 0:tw], kc == 0, kc == NCH - 1, vkeys + [kpb], [koe])
                            mm(oo[:, 0:tw], Vp[:, kc, g, 0:128], pb[:, 1, 0:tw], kc == 0, kc == NCH - 1, vkeys + [kpb], [koo])

                        qk(0)
                        for kc in range(NCH):
                            if kc + 1 < NCH:
                                qk(kc + 1)
                            pvm(kc)
                        recip(rec[0:64, 0:tw], oe[64:128, 0:tw], [koe], [("rec", 0)])
                        recip(rec[64:128, 0:tw], oo[0:64, 0:tw], [koo], [("rec", 1)])
                        tt("vector", yt[0:64, 0:tw], oe[0:64, 0:tw], rec[0:64, 0:tw], ALU.mult, [koe, ("rec", 0)], [("yt", 0)])
                        tt("vector", yt[64:128, 0:tw], oo[64:128, 0:tw], rec[64:128, 0:tw], ALU.mult, [koo, ("rec", 1)], [("yt", 1)])
                        tt("gpsimd", o_[:, j, 0:tw], yt[:, 0:tw], s_[:, j, 0:tw], ALU.mult, [("yt", 0), ("yt", 1), ksg], [(ko, j)])
                    dma("sync", DA(mixT_d, 1536 * TP + t0, [[TP, 128], [128 * TP, 4], [1, tw]]), o_[:, :, 0:tw],
                        [(ko, j) for j in range(4)], [("mixT", 3)])
            S.barrier()

        def phase_p5(l, last):
            with contextlib.ExitStack() as st:
                wo = alloc(st, "wo", [128, 16, D], BF16)
                mx = [alloc(st, "mx", [128, 16, TS], BF16) for _ in range(2)]
                hc = [alloc(st, "hc5", [128, D], F32) for _ in range(4)]
                ho = [alloc(st, "ho5", [128, D], F32) for _ in range(2)]
                po = [[palloc(st, "po", [128, 512], F32) for _ in range(2)] for _ in range(2)]
                for hh in range(2):
                    dma("gpsimd", wo[:, :, hh * 512:(hh + 1) * 512],
                        DA(wout_d, l * 2048 * D + hh * 512, [[D, 128], [128 * D, 16], [1, 512]]), (), [("wo", hh)])
                def p5_load(i):
                    for q4 in range(4):
                        dma("sync", mx[i % 2][:, q4 * 4:(q4 + 1) * 4, :],
                            DA(mixT_d, q4 * 512 * TP + i * TS, [[TP, 128], [128 * TP, 4], [1, TS]]), [("mixT", q4)], ["mx%d" % (i % 2)])

                def p5_hload(c):
                    if c >= NCH or (last and c == 0):
                        return
                    hcb = hc[c % 4]
                    khc = "hc5%d" % (c % 4)
                    if l == 0 and c == 0:
                        memset("gpsimd", hcb[:], 0.0, [khc])
                        dma("sync", hcb[NPAD:128, :], meta_d.ap(), (), [khc])
                    elif l == 0:
                        dma("sync", hcb[:], x_d.ap()[(c - 1) * 128:c * 128, :], (), [khc])
                    else:
                        dma("sync", hcb[:], hbuf_d.ap()[c * 128:(c + 1) * 128, :], ["hbuf", "hbuf_pad"], [khc])

                p5_load(0)
                p5_hload(0)
                p5_hload(1)
                for i in range(NT):
                    t0 = i * TS
                    m_ = mx[i % 2]
                    km = "mx%d" % (i % 2)
                    if i + 1 < NT:
                        p5_load(i + 1)
                    for cl in range(3):
                        c = 3 * i + cl
                        p5_hload(c + 2)
                        if last and c == 0:
                            continue
                        hcb = hc[c % 4]
                        khc = "hc5%d" % (c % 4)
                        hob = ho[c % 2]
                        kho = "ho5%d" % (c % 2)
                        for hh in range(2):
                            p = po[c % 2][hh]
                            kp = ("po", c % 2, hh)
                            for ec in range(16):
                                mm(p[:, :], m_[:, ec, cl * 128:(cl + 1) * 128], wo[:, ec, hh * 512:(hh + 1) * 512],
                                   ec == 0, ec == 15, [km, ("wo", hh)], [kp])
                            tt("vector", hob[:, hh * 512:(hh + 1) * 512], p[:, :], hcb[:, hh * 512:(hh + 1) * 512], ALU.add,
                               [kp, khc], [(kho, hh)])
                        if last:
                            o = dma("sync", out_d.ap()[(c - 1) * 128:c * 128, :], hob[:], [(kho, 0), (kho, 1)], ["out"])
                            final_ops.append(o)
                        elif c == 0:
                            dma("sync", hbuf_d.ap()[NPAD:128, :], hob[NPAD:128, :], [(kho, 0), (kho, 1)], ["hbuf"])
                        else:
                            dma("sync", hbuf_d.ap()[c * 128:(c + 1) * 128, :], hob[:], [(kho, 0), (kho, 1)], ["hbuf"])
            S.barrier()

        wst = contextlib.ExitStack()
        Wcur = load_w(wst, 0) if "0" in phases else None
        for l in range(n_layers):
            if "0" in phases:
                phase_p0(l, Wcur)
                wst.close()
            if "1" in phases or "2" in phases:
                phase_p12(l)
            if "3" in phases:
                phase_p3(l)
            if "4" in phases:
                phase_p4(l)
            if "5" in phases:
                if "0" in phases and l + 1 < n_layers:
                    wst = contextlib.ExitStack()
                    Wcur = load_w(wst, l + 1)
                phase_p5(l, l == n_layers - 1)
        final_ops[:] = [o for o in final_ops if not o.skipped]
        S.limit = None
        print("n_ops", len(S.ops))
        if not final_ops:
            zo = dma("sync", out_d.ap()[0:128, :], zf[:], ["zf"], ["out"])
            final_ops.append(zo)
        S.emit(nc, final_ops)
    return nc


_PARAM_NAMES = ("meta_tokens", "norm_w", "w_in", "w_out", "pool_w", "pool_scale", "fourier_w", "conv_w", "conv_b",
                "dt_bias", "a_log", "d_skip", "ssd_norm_w", "q_norm_w", "k_norm_w")


def make_in_maps(inputs, cores):
    consts = host_consts()
    shared = {k: np.ascontiguousarray(np.asarray(inputs[k], dtype=np.float32)) for k in _PARAM_NAMES}
    shared.update(consts)
    x = np.asarray(inputs["x"], dtype=np.float32)
    maps = []
    for b in cores:
        m = dict(shared)
        m["x"] = np.ascontiguousarray(x[b])
        maps.append(m)
    return maps


def kernel(**inputs):
    nc = build()
    in_maps = make_in_maps(inputs, list(range(8)))
    res = run_bass_kernel_spmd(nc, in_maps, core_ids=list(range(8)))
    return np.stack([np.asarray(r["out"], dtype=np.float32) for r in res.results], axis=0)
```

```python
import contextlib
import numpy as np
import ml_dtypes
import concourse.bass as bass
import concourse.mybir as mybir
from concourse.bass_utils import run_bass_kernel_spmd

F32 = mybir.dt.float32
BF16 = mybir.dt.bfloat16
AF = mybir.ActivationFunctionType
ALU = mybir.AluOpType
AX = mybir.AxisListType

D = 1024
NTOK = 4096
NMETA = 16
L = NTOK + NMETA
NPAD = 112
TP = 4224
NCH = 33
NT = 11
TS = 384
TW = TP + 16
DEPTH = 4
INC = 4880
EPS = 1e-6
NPC = 3
PCC = 11

ENGS = ("tensor", "vector", "scalar", "gpsimd", "sync")
N_DMA_SEMS = 28
N_SW_SEMS = 8
PSUM_KEYS = frozenset(["pT", "pf0", "pf1", "pz", "pq", "pkv", "ptq", "py", "acc", "pcv0", "pcv1", "ptr", "pseg", "pyd",
                       "pyo", "psn", "Sp0", "Sp1", "Oe0", "Oe1", "Oo0", "Oo1", "po"])


class Op:
    __slots__ = ("eng", "fn", "deps", "is_dma", "needed", "sem", "val", "nobar", "skipped")

    def __init__(self, eng, fn, deps, is_dma):
        self.eng = eng
        self.fn = fn
        self.deps = deps
        self.is_dma = is_dma
        self.needed = False
        self.sem = None
        self.val = 0
        self.nobar = False
        self.skipped = False


class Sched:
    def __init__(self):
        self.ops = []
        self.last_w = {}
        self.readers = {}
        self.last_eng = {}
        self.dmas = []
        self.limit = None

    def _add(self, eng, fn, reads, writes, is_dma, extra=()):
        ex = tuple(k for k in reads if (k[0] if isinstance(k, tuple) else k) in PSUM_KEYS)
        if ex:
            writes = tuple(writes) + tuple(k for k in ex if k not in writes)
        deps = list(extra)
        for k in reads:
            w = self.last_w.get(k)
            if w is not None:
                deps.append(w)
        for k in writes:
            w = self.last_w.get(k)
            if w is not None:
                deps.append(w)
            deps.extend(self.readers.get(k, ()))
        op = Op(eng, fn, deps, is_dma)
        if self.limit is not None and len(self.ops) >= self.limit:
            op.skipped = True
            return op
        self.ops.append(op)
        for k in writes:
            self.last_w[k] = op
            self.readers[k] = []
        for k in reads:
            if k in writes:
                continue
            lst = self.readers.setdefault(k, [])
            if not is_dma:
                lst[:] = [o for o in lst if o.is_dma or o.eng != eng]
            lst.append(op)
        if is_dma:
            self.dmas.append(op)
        else:
            self.last_eng[eng] = op
        return op

    def op(self, eng, fn, reads=(), writes=()):
        return self._add(eng, fn, tuple(reads), tuple(writes), False)

    def dma(self, eng, fn, reads=(), writes=(), nobar=False):
        o = self._add(eng, fn, tuple(reads), tuple(writes), True)
        o.nobar = nobar
        return o

    def barrier(self):
        deps = [o for o in self.last_eng.values()] + [o for o in self.dmas if not o.nobar]
        self.dmas = [o for o in self.dmas if o.nobar]
        for e in ENGS:
            self._add(e, lambda eng: eng.nop(), (), (), False, extra=deps)

    def emit(self, nc, final_wait_ops=()):
        ops = self.ops
        for o in ops:
            for d in o.deps:
                if d.eng == "tensor" and o.eng == "tensor" and not d.is_dma and not o.is_dma:
                    continue
                d.needed = True
        for o in final_wait_ops:
            o.needed = True
        with contextlib.ExitStack() as st:
            esem = {e: st.enter_context(nc.semaphore("s_" + e)) for e in ENGS}
            dsem = [st.enter_context(nc.semaphore("d_%d" % i)) for i in range(N_DMA_SEMS)]
            ecnt = {e: 0 for e in ENGS}
            dcnt = [0] * N_DMA_SEMS
            dnext = 0
            prev_on_dsem = [None] * N_DMA_SEMS
            swnext = 0
            for o in ops:
                if o.is_dma:
                    if o.eng == "gpsimd":
                        i = N_DMA_SEMS - N_SW_SEMS + swnext
                        swnext = (swnext + 1) % N_SW_SEMS
                    else:
                        i = dnext
                        dnext = (dnext + 1) % (N_DMA_SEMS - N_SW_SEMS)
                    dcnt[i] += 16
                    o.sem = ("d", i)
                    o.val = dcnt[i]
                    if prev_on_dsem[i] is not None:
                        o.deps.append(prev_on_dsem[i])
                    prev_on_dsem[i] = o
                elif o.needed:
                    ecnt[o.eng] += 1
                    o.sem = ("e", o.eng)
                    o.val = ecnt[o.eng]
            per = {e: [o for o in ops if o.eng == e] for e in ENGS}
            final = list(final_wait_ops)
            blk = st.enter_context(nc.Block())

            def semh(s):
                return esem[s[1]] if s[0] == "e" else dsem[s[1]]

            def run(e, eng):
                seen = {}
                for o in per[e]:
                    need = {}
                    for d in o.deps:
                        if d.sem is None:
                            continue
                        if (not d.is_dma) and (not o.is_dma) and d.eng == "tensor" and e == "tensor":
                            continue
                        if need.get(d.sem, 0) < d.val:
                            need[d.sem] = d.val
                    for s, v in need.items():
                        if seen.get(s, 0) < v:
                            eng.wait_ge(semh(s), v)
                            seen[s] = v
                    ins = o.fn(eng)
                    if o.is_dma:
                        ins.then_inc(semh(o.sem), 16)
                    elif o.sem is not None:
                        ins.then_inc(semh(o.sem), 1)
                if e == "sync":
                    for o in final:
                        if seen.get(o.sem, 0) < o.val:
                            eng.wait_ge(semh(o.sem), o.val)
                            seen[o.sem] = o.val

            @blk.tensor
            def _(eng):
                run("tensor", eng)

            @blk.vector
            def _(eng):
                run("vector", eng)

            @blk.scalar
            def _(eng):
                run("scalar", eng)

            @blk.gpsimd
            def _(eng):
                run("gpsimd", eng)

            @blk.sync
            def _(eng):
                run("sync", eng)
        return ecnt, dcnt


_CONSTS = None


def host_consts():
    global _CONSTS
    if _CONSTS is not None:
        return _CONSTS
    bf = ml_dtypes.bfloat16
    c = {}
    c["c_ident"] = np.eye(128, dtype=np.float32).astype(bf)
    tphys = np.arange(TP)
    tlog = tphys - NPAD
    row = np.where(tlog >= NMETA, (tlog - NMETA) // 64, 0).astype(np.float64)
    col = np.where(tlog >= NMETA, (tlog - NMETA) % 64, 0).astype(np.float64)
    freqs = (10000.0 ** (-np.arange(0, 32, 2, dtype=np.float32) / 32.0)).astype(np.float32)
    ang = np.concatenate([row[:, None].astype(np.float32) * freqs[None], col[:, None].astype(np.float32) * freqs[None]],
                         axis=-1).astype(np.float32)
    cos2 = np.repeat(np.cos(ang), 2, axis=-1).astype(np.float32)
    sin2 = np.repeat(np.sin(ang), 2, axis=-1).astype(np.float32)
    rope = np.stack([cos2, sin2], 0).reshape(2, NCH, 128, 64).transpose(2, 0, 1, 3)
    c["c_rope"] = np.ascontiguousarray(rope, dtype=np.float32)
    k = np.arange(128)[:, None]
    l_ = np.arange(128)[None, :]
    trif = (k <= l_).astype(np.float32)
    trib = (k >= l_).astype(np.float32)
    ones = np.ones((128, 128), np.float32)
    c["c_ssd_f32"] = np.ascontiguousarray(np.stack([trif, trib, ones], 1), dtype=np.float32)
    t1f = (k > l_).astype(np.float32)
    t1b = (k < l_).astype(np.float32)
    c["c_ssd_bf"] = np.ascontiguousarray(np.stack([trif, trib, t1f, t1b, trif, trib], 1)).astype(bf)
    m = np.arange(512)
    a = 2.0 * np.pi * ((m[:, None] * m[None, :]) % 512) / 512.0
    cc = np.cos(a).reshape(4, 128, 512).transpose(1, 0, 2)
    sc = np.sin(a).reshape(4, 128, 512).transpose(1, 0, 2)
    c["c_dft_c"] = np.ascontiguousarray(np.stack([cc, sc], 1)).astype(bf)
    scale = 1.0 / np.sqrt(float(L) * 512.0)
    tl = (np.arange(TP) - NPAD).astype(np.int64)
    valid = (tl >= 0)
    prod = (np.maximum(tl, 0)[:, None] * np.maximum(tl, 0)[None, :]) % L
    angp = (2.0 * np.pi / L) * prod.astype(np.float64)
    vm = (valid[:, None] & valid[None, :])
    tabs = []
    for fn_, sgn in ((np.cos, 1.0), (np.sin, -1.0)):
        t_ = (fn_(angp) * (sgn * scale) * vm).astype(np.float32).astype(bf)
        t_ = t_.reshape(NPC, PCC, 128, NT, TS)
        t_ = t_.transpose(3, 0, 2, 1, 4)
        tabs.append(np.ascontiguousarray(t_).reshape(NT, NPC, 128, PCC * TS))
    c["c_tab"] = np.ascontiguousarray(np.stack(tabs, 0))
    pinv = np.zeros((128, 4, 2, 8), np.float32)
    for g, w in enumerate((2, 4, 8, 16)):
        for side in range(2):
            for q in range(8):
                t = q if side == 0 else L - 8 + q
                lo = min(max(t - w // 2, 0), L)
                hi = min(max(t - w // 2 + w, 0), L)
                pinv[:, g, side, q] = 1.0 / float(hi - lo)
    c["c_pool"] = pinv
    vmask = np.ones((128, 64), np.float32)
    vmask[:NPAD] = 0.0
    c["c_vmask"] = vmask.astype(bf)
    _CONSTS = c
    return c


def build(n_layers=DEPTH, dbg=False, phases="012345", limit=None):
    nc = bass.Bass("TRN2", target_bir_lowering=False)
    S = Sched()
    S.limit = limit
    uid = [0]

    def din(name, shape, dt):
        return nc.dram_tensor(name, list(shape), dt, kind="ExternalInput")

    x_d = din("x", [NTOK, D], F32)
    meta_d = din("meta_tokens", [NMETA, D], F32)
    normw_d = din("norm_w", [DEPTH, D], F32)
    win_d = din("w_in", [DEPTH, D, INC], F32)
    wout_d = din("w_out", [DEPTH, 2048, D], F32)
    poolw_d = din("pool_w", [DEPTH, 4, 128, 128], F32)
    pools_d = din("pool_scale", [DEPTH, 512], F32)
    fw_d = din("fourier_w", [DEPTH, 512, 512], F32)
    convw_d = din("conv_w", [DEPTH, 4, 1024], F32)
    convb_d = din("conv_b", [DEPTH, 1024], F32)
    dtb_d = din("dt_bias", [DEPTH, 2, 8], F32)
    alog_d = din("a_log", [DEPTH, 2, 8], F32)
    dskip_d = din("d_skip", [DEPTH, 8], F32)
    ssdnw_d = din("ssd_norm_w", [DEPTH, 512], F32)
    qnw_d = din("q_norm_w", [DEPTH, 64], F32)
    knw_d = din("k_norm_w", [DEPTH, 64], F32)
    c_ident_d = din("c_ident", [128, 128], BF16)
    c_rope_d = din("c_rope", [128, 2, NCH, 64], F32)
    c_ssdf_d = din("c_ssd_f32", [128, 3, 128], F32)
    c_ssdb_d = din("c_ssd_bf", [128, 6, 128], BF16)
    c_dftc_d = din("c_dft_c", [128, 2, 4, 512], BF16)
    c_tab_d = din("c_tab", [2, NT, NPC, 128, PCC * TS], BF16)
    c_pool_d = din("c_pool", [128, 4, 2, 8], F32)
    c_vmask_d = din("c_vmask", [128, 64], BF16)
    out_d = nc.dram_tensor("out", [NTOK, D], F32, kind="ExternalOutput")
    skind = "ExternalOutput" if dbg else "Internal"

    def dscr(name, shape, dt):
        return nc.dram_tensor(name, list(shape), dt, kind=skind)

    hbuf_d = dscr("hbuf", [TP, D], F32)
    fm_d = dscr("fm", [3584, TW], BF16)
    qT_d = dscr("qT", [512, TP], BF16)
    kT_d = dscr("kT", [128, TP], BF16)
    vtm_d = dscr("vtm", [TP, 128], BF16)
    sz_d = dscr("sz", [TP, 512], BF16)
    dtr_d = dscr("dtr", [TP, 16], F32)
    yf_d = dscr("yf", [TP, 512], F32)
    mixT_d = dscr("mixT", [2048, TP], BF16)

    def DA(h, off, dims):
        return bass.AP(h, off, [list(d) for d in dims])

    def mm(out, lhsT, rhs, start, stop, r, w):
        return S.op("tensor", lambda e: e.matmul(out, lhsT=lhsT, rhs=rhs, start=start, stop=stop), r, w)

    def act(out, in_, func, r, w, bias=None, scale=None, accum_out=None, eng="scalar"):
        kw = {}
        if bias is not None:
            kw["bias"] = bias
        if scale is not None:
            kw["scale"] = scale
        if accum_out is not None:
            kw["accum_out"] = accum_out
        return S.op("scalar", lambda e: e.activation(out=out, in_=in_, func=func, **kw), r, w)

    def tt(eng, out, in0, in1, op, r, w):
        return S.op(eng, lambda e: e.tensor_tensor(out=out, in0=in0, in1=in1, op=op), r, w)

    def tsc(eng, out, in0, s1, s2, op0, op1, r, w):
        if op1 is None:
            return S.op(eng, lambda e: e.tensor_scalar(out=out, in0=in0, scalar1=s1, scalar2=None, op0=op0), r, w)
        return S.op(eng, lambda e: e.tensor_scalar(out=out, in0=in0, scalar1=s1, scalar2=s2, op0=op0, op1=op1), r, w)

    def stt(out, in0, scalar, in1, op0, op1, r, w):
        return S.op("vector", lambda e: e.scalar_tensor_tensor(out=out, in0=in0, scalar=scalar, in1=in1, op0=op0, op1=op1), r, w)

    def cp(eng, out, in_, r, w):
        if eng == "scalar":
            return S.op("scalar", lambda e: e.activation(out=out, in_=in_, func=AF.Copy), r, w)
        return S.op(eng, lambda e: e.tensor_copy(out=out, in_=in_), r, w)

    def recip(out, in_, r, w):
        return S.op("vector", lambda e: e.reciprocal(out=out, in_=in_), r, w)

    def memset(eng, ap, val, w):
        return S.op(eng, lambda e: e.memset(ap, val), (), w)

    def dma(eng, out, in_, r, w, nobar=False, slow=False):
        if slow:
            return S.dma(eng, lambda e: e.dma_start(out=out, in_=in_, allow_slow_non_contiguous=True), r, w, nobar=nobar)
        return S.dma(eng, lambda e: e.dma_start(out=out, in_=in_), r, w, nobar=nobar)

    final_ops = []

    with contextlib.ExitStack() as top:
        def alloc(st, name, shape, dt):
            uid[0] += 1
            return st.enter_context(nc.sbuf_tensor("%s_%d" % (name, uid[0]), list(shape), dt))

        def palloc(st, name, shape, dt):
            uid[0] += 1
            return st.enter_context(nc.psum_tensor("%s_%d" % (name, uid[0]), list(shape), dt))

        ident = alloc(top, "ident", [128, 128], BF16)
        zero = alloc(top, "zero", [128, 16], BF16)
        dma("sync", ident[:], c_ident_d.ap(), (), ["ident"])
        memset("gpsimd", zero[:], 0.0, ["zero"])
        for r0, nj in ((0, 4), (2048, 8)):
            for c0 in (0, 8 + TP):
                dma("sync", DA(fm_d, r0 * TW + c0, [[TW, 128], [128 * TW, nj], [1, 8]]),
                    zero[:, 0:8].unsqueeze(1).broadcast_to([128, nj, 8]), ["zero"], ["fm_margin"])
        mhalf = alloc(top, "mhalf", [128, 16], F32)
        memset("gpsimd", mhalf[:], -0.5, ["mhalf"])
        zf = alloc(top, "zf", [128, D], F32)
        memset("gpsimd", zf[:], 0.0, ["zf"])
        dma("sync", hbuf_d.ap()[0:NPAD, :], zf[0:NPAD, :], ["zf"], ["hbuf_pad"])
        S.barrier()

        def identr(o, i_, r, w):
            return S.op("tensor", lambda e: e.transpose(o, i_, ident[:]), list(r) + ["ident"], w)

        W_PIECES = ((0, 1024, 0), (1024, 1024, 1024), (2048, 1024, 2048), (4368, 512, 3072),
                    (3072, 512, 3584), (3600, 512, 4096), (4112, 256, 4608), (3584, 16, 4864))

        def load_w(st, l):
            W = alloc(st, "W", [128, 8, INC], BF16)
            for (s0, n, d0) in W_PIECES:
                dma("gpsimd", W[:, :, d0:d0 + n],
                    DA(win_d, l * D * INC + s0, [[INC, 128], [128 * INC, 8], [1, n]]), (), [("W", d0)], nobar=True)
            return W

        def phase_p0(l, W):
            with contextlib.ExitStack() as st:
                pieces = W_PIECES
                wkeys = [("W", p[2]) for p in pieces]
                hc = [alloc(st, "hc", [128, D], F32) for _ in range(3)]
                junk = alloc(st, "junk", [128, D], BF16)
                nwb = alloc(st, "nwb", [128, D], F32)
                hnb = [alloc(st, "hnb", [128, D], BF16) for _ in range(3)]
                hnT = [alloc(st, "hnT", [128, 8, TS], BF16) for _ in range(2)]
                stt_t = [alloc(st, "stat", [128, 48], F32) for _ in range(2)]
                fst = [alloc(st, "fst", [128, 4, TS], BF16) for _ in range(3)]
                zst = [alloc(st, "zst", [128, 3, 512], BF16) for _ in range(2)]
                qTst = [alloc(st, "qTst", [128, 4, TS], BF16) for _ in range(2)]
                kTst = [alloc(st, "kTst", [128, TS], BF16) for _ in range(2)]
                vst = [alloc(st, "vst", [128, 3, 128], BF16) for _ in range(2)]
                dtst = [alloc(st, "dtst", [128, 3, 16], F32) for _ in range(2)]
                sq = alloc(st, "sq", [128, 640], F32)
                qn = alloc(st, "qn", [128, 640], F32)
                qw = alloc(st, "qw", [128, 640], F32)
                m1 = alloc(st, "m1", [128, 640], F32)
                m2 = alloc(st, "m2", [128, 640], F32)
                qr = [alloc(st, "qr", [128, 640], BF16) for _ in range(2)]
                wqk = alloc(st, "wqk", [128, 640], F32)
                rope = alloc(st, "rope", [128, 2, NCH, 64], F32)
                pT = palloc(st, "pT", [128, 1024], BF16)
                pf = [palloc(st, "pf", [128, 512], F32) for _ in range(2)]
                pz = palloc(st, "pz", [128, 512], F32)
                pq = palloc(st, "pq", [128, 512], F32)
                pkv = palloc(st, "pkv", [128, 512], F32)
                ptq = palloc(st, "ptq", [128, 1024], BF16)

                dma("sync", rope[:], c_rope_d.ap(), (), ["rope"])
                dma("sync", nwb[:], DA(normw_d, l * D, [[0, 128], [1, D]]), (), ["nwb"])
                dma("sync", wqk[:, 0:512].rearrange("p (h d) -> p h d", h=8),
                    DA(qnw_d, l * 64, [[0, 128], [0, 8], [1, 64]]), (), ["wqk"])
                dma("sync", wqk[:, 512:640].rearrange("p (h d) -> p h d", h=2),
                    DA(knw_d, l * 64, [[0, 128], [0, 2], [1, 64]]), (), ["wqk"])
                tsc("gpsimd", wqk[:, 0:512], wqk[:, 0:512], 0.125, None, ALU.mult, None, ["wqk"], ["wqk"])

                nfm = [0]

                def norm_load(c):
                    hcb = hc[c % 3]
                    khc = "hc%d" % (c % 3)
                    if l == 0 and c == 0:
                        memset("gpsimd", hcb[:], 0.0, [khc])
                        dma("sync", hcb[NPAD:128, :], meta_d.ap(), (), [khc])
                    elif l == 0:
                        dma("sync", hcb[:], x_d.ap()[(c - 1) * 128:c * 128, :], (), [khc])
                    else:
                        dma("sync", hcb[:], hbuf_d.ap()[c * 128:(c + 1) * 128, :], ["hbuf", "hbuf_pad"], [khc])

                def norm_front(c):
                    hcb = hc[c % 3]
                    khc = "hc%d" % (c % 3)
                    sta = stt_t[c % 2]
                    ks = "stat%d" % (c % 2)
                    act(junk[:], hcb[:], AF.Square, [khc], ["junk", (ks, 0)], accum_out=sta[:, 0:1])
                    tsc("vector", sta[:, 1:2], sta[:, 0:1], 1.0 / D, EPS, ALU.mult, ALU.add, [(ks, 0)], [(ks, 1)])
                    tt("gpsimd", sta[:, 3:4], sta[:, 1:2], mhalf[:, 0:1], ALU.pow, [(ks, 1), "mhalf"], [(ks, 3)])
                    stt(hnb[c % 3][:], hcb[:], sta[:, 3:4], nwb[:], ALU.mult, ALU.mult, [khc, (ks, 3), "nwb"], ["hnb%d" % (c % 3)])

                def norm_back(c):
                    i_, cl = c // 3, c % 3
                    hT = hnT[i_ % 2]
                    kh = "hnT%d" % (i_ % 2)
                    hb = hnb[c % 3]
                    khb = "hnb%d" % (c % 3)
                    for j in range(8):
                        identr(pT[:, j * 128:(j + 1) * 128], hb[:, j * 128:(j + 1) * 128], [khb], ["pT"])
                    cp("scalar", hT[:, :, cl * 128:(cl + 1) * 128], pT[:].rearrange("p (j t) -> p j t", j=8),
                       ["pT"], [(kh, cl)])

                for c in range(3):
                    norm_load(c)
                for c in range(3):
                    norm_front(c)
                for c in range(3):
                    norm_back(c)
                for i in range(NT):
                    t0 = i * TS
                    hT = hnT[i % 2]
                    kh = "hnT%d" % (i % 2)
                    hkeys = [(kh, 0), (kh, 1), (kh, 2)]
                    if i + 1 < NT:
                        for cl in range(3):
                            norm_load(3 * (i + 1) + cl)
                    def fm_seg(seg, i=i, t0=t0, hT=hT, hkeys=hkeys):
                        fs = fst[seg % 3]
                        kfs = "fst%d" % (seg % 3)
                        gate = seg in (1, 3, 6)
                        for j in range(4):
                            p = pf[nfm[0] % 2]
                            kp = "pf%d" % (nfm[0] % 2)
                            col0 = seg * 512 + j * 128
                            for dk in range(8):
                                mm(p[:, 0:TS], W[:, dk, col0:col0 + 128], hT[:, dk, :], dk == 0, dk == 7,
                                   wkeys + hkeys, [kp])
                            if gate:
                                act(fs[:, j, :], p[:, 0:TS], AF.Silu, [kp], [kfs])
                            else:
                                cp("scalar", fs[:, j, :], p[:, 0:TS], [kp], [kfs])
                            nfm[0] += 1
                        dma("sync", DA(fm_d, seg * 512 * TW + 8 + t0, [[TW, 128], [128 * TW, 4], [1, TS]]), fs[:],
                            [kfs, "fm_margin"], [("fm", seg)])
                        if seg in (0, 2, 4) and i + 1 < NT:
                            norm_front(3 * (i + 1) + seg // 2)
                    zs = zst[i % 2]
                    kzs = "zst%d" % (i % 2)
                    qTs = qTst[i % 2]
                    kqs = "qTst%d" % (i % 2)
                    kTs = kTst[i % 2]
                    kks = "kTst%d" % (i % 2)
                    vs = vst[i % 2]
                    kvs = "vst%d" % (i % 2)
                    dts = dtst[i % 2]
                    kds = "dtst%d" % (i % 2)

                    def qtrans(cl, i=i, qTs=qTs, kqs=kqs, kTs=kTs, kks=kks):
                        c = 3 * i + cl
                        lt = slice(cl * 128, (cl + 1) * 128)
                        qrb = qr[c % 2]
                        kqr = "qr%d" % (c % 2)
                        for j in range(5):
                            identr(ptq[:, j * 128:(j + 1) * 128], qrb[:, j * 128:(j + 1) * 128],
                                   [(kqr, 0), (kqr, 1)], ["ptq"])
                        cp("scalar", qTs[:, :, lt], ptq[:, 0:512].rearrange("p (j t) -> p j t", j=4), ["ptq"], [kqs])
                        cp("vector", kTs[:, lt], ptq[:, 512:640], ["ptq"], [kks])

                    for cl in range(3):
                        c = 3 * i + cl
                        lt = slice(cl * 128, (cl + 1) * 128)
                        if cl >= 1:
                            fm_seg(2 * cl - 2)
                            fm_seg(2 * cl - 1)
                        for dk in range(8):
                            mm(pz[:, :], hT[:, dk, lt], W[:, dk, 3584:4096], dk == 0, dk == 7, wkeys + hkeys, ["pz"])
                        act(zs[:, cl, :], pz[:, :], AF.Silu, ["pz"], [kzs])
                        for dk in range(8):
                            mm(pq[:, :], hT[:, dk, lt], W[:, dk, 4096:4608], dk == 0, dk == 7, wkeys + hkeys, ["pq"])
                        for dk in range(8):
                            mm(pkv[:, 0:272], hT[:, dk, lt], W[:, dk, 4608:4880], dk == 0, dk == 7, wkeys + hkeys, ["pkv"])
                        if cl >= 1:
                            qtrans(cl - 1)
                        cp("vector", vs[:, cl, :], pkv[:, 128:256], ["pkv"], [kvs])
                        cp("vector", dts[:, cl, :], pkv[:, 256:272], ["pkv"], [kds])
                        sta = stt_t[c % 2]
                        ks = "stat%d" % (c % 2)
                        act(sq[:, 0:512], pq[:, :], AF.Square, ["pq"], [("sq", 0)])
                        act(sq[:, 512:640], pkv[:, 0:128], AF.Square, ["pkv"], [("sq", 1)])
                        S.op("vector", lambda e, o=sta[:, 8:18], a=sq[:].rearrange("p (h d) -> p h d", h=10):
                             e.tensor_reduce(out=o, in_=a, axis=AX.X, op=ALU.add), [("sq", 0), ("sq", 1)], [(ks, 8)])
                        tsc("vector", sta[:, 18:28], sta[:, 8:18], 1.0 / 64, EPS, ALU.mult, ALU.add, [(ks, 8)], [(ks, 18)])
                        tt("gpsimd", sta[:, 38:48], sta[:, 18:28], mhalf[:, 0:10], ALU.pow, [(ks, 18), "mhalf"], [(ks, 38)])
                        tt("vector", qn[:, 0:512].rearrange("p (h d) -> p h d", h=8),
                           pq[:, :].rearrange("p (h d) -> p h d", h=8),
                           sta[:, 38:46].unsqueeze(2).broadcast_to([128, 8, 64]), ALU.mult, ["pq", (ks, 38)], [("qn", 0)])
                        tt("vector", qn[:, 512:640].rearrange("p (h d) -> p h d", h=2),
                           pkv[:, 0:128].rearrange("p (h d) -> p h d", h=2),
                           sta[:, 46:48].unsqueeze(2).broadcast_to([128, 2, 64]), ALU.mult, ["pkv", (ks, 38)], [("qn", 1)])
                        tt("gpsimd", qw[:], qn[:], wqk[:], ALU.mult, [("qn", 0), ("qn", 1), "wqk"], ["qw"])
                        qw3 = qw[:].rearrange("p (h d) -> p h d", h=10)
                        tt("gpsimd", m1[:].rearrange("p (h d) -> p h d", h=10), qw3,
                           rope[:, 0, c, :].unsqueeze(1).broadcast_to([128, 10, 64]), ALU.mult, ["qw", "rope"], ["m1"])
                        tt("gpsimd", m2[:].rearrange("p (h d) -> p h d", h=10), qw3,
                           rope[:, 1, c, :].unsqueeze(1).broadcast_to([128, 10, 64]), ALU.mult, ["qw", "rope"], ["m2"])
                        qrb = qr[c % 2]
                        kqr = "qr%d" % (c % 2)

                        def pv(tile_, off):
                            return bass.AP(tile_, off, [[640, 128], [2, 320]])
                        tt("vector", pv(qrb, 0), pv(m1, 0), pv(m2, 1), ALU.subtract, ["m1", "m2"], [(kqr, 0)])
                        tt("gpsimd", pv(qrb, 1), pv(m2, 0), pv(m1, 1), ALU.add, ["m1", "m2"], [(kqr, 1)])
                    for seg in (4, 5, 6):
                        fm_seg(seg)
                    if i + 1 < NT:
                        for cl in range(3):
                            norm_back(3 * (i + 1) + cl)
                    qtrans(2)
                    dma("sync", DA(sz_d, t0 * 512, [[512, 128], [128 * 512, 3], [1, 512]]), zs[:], [kzs], ["sz"])
                    dma("sync", DA(qT_d, t0, [[TP, 128], [128 * TP, 4], [1, TS]]), qTs[:], [kqs], ["qT"])
                    dma("sync", DA(kT_d, t0, [[TP, 128], [1, TS]]), kTs[:], [kks], ["kT"])
                    dma("sync", DA(vtm_d, t0 * 128, [[128, 128], [128 * 128, 3], [1, 128]]), vs[:], [kvs], ["vtm"])
                    dma("sync", DA(dtr_d, t0 * 16, [[16, 128], [128 * 16, 3], [1, 16]]), dts[:], [kds], ["dtr"])
            S.barrier()

        def phase_p12(l):
            with contextlib.ExitStack() as st:
                pw = alloc(st, "pw", [128, 4, 128], BF16)
                psc = alloc(st, "psc", [128, 4], F32)
                cinv = alloc(st, "cinv", [128, 4, 2, 8], F32)
                ub = [alloc(st, "ub", [128, 4, 400], BF16) for _ in range(2)]
                sgp = [alloc(st, "sg", [128, 4, TS], BF16) for _ in range(2)]
                T2 = alloc(st, "T2", [128, 4, 400], F32)
                T4 = alloc(st, "T4", [128, 4, 400], F32)
                T8 = alloc(st, "T8", [128, 4, 400], F32)
                T16 = alloc(st, "T16", [128, 4, 400], F32)
                tmp = alloc(st, "ptmp", [128, 4, 8], F32)
                dT = [alloc(st, "dT", [128, 4, TS], BF16) for _ in range(2)]
                ostp = [alloc(st, "ost", [128, 4, TS], BF16) for _ in range(2)]
                Ap = alloc(st, "Ap", [128, NCH, 512], BF16)
                Bp = alloc(st, "Bp", [128, NCH, 512], BF16)
                ring = [alloc(st, "ring", [128, PCC, TS], BF16) for _ in range(4)]
                cs = alloc(st, "dftc", [128, 2, 4, 512], BF16)
                wf = alloc(st, "wf", [128, 4, 512], BF16)
                cw = alloc(st, "cw", [128, 2, 4, 512], BF16)
                uT = [alloc(st, "uT", [128, 4, TS], BF16) for _ in range(2)]
                sg = [alloc(st, "sgf", [128, 4, TS], BF16) for _ in range(2)]
                ost = [alloc(st, "ostf", [128, 4, TS], BF16) for _ in range(2)]
                acc = [palloc(st, "acc", [128, 512], F32) for _ in range(4)]
                py = [palloc(st, "py", [128, 512], F32) for _ in range(4)]
                banks = [(acc[k], ("acc", 0, k)) for k in range(4)] + [(py[g], ("py", g)) for g in range(4)]

                dma("gpsimd", pw[:], DA(poolw_d, l * 4 * 128 * 128, [[128, 128], [128 * 128, 4], [1, 128]]), (), ["pw"])
                dma("sync", psc[:].unsqueeze(2), DA(pools_d, l * 512, [[1, 128], [128, 4], [1, 1]]), (), ["psc"], slow=True)
                dma("sync", cinv[:], c_pool_d.ap(), (), ["cinv"])
                dma("sync", cs[:], c_dftc_d.ap(), (), ["dftc"])
                dma("gpsimd", wf[:], DA(fw_d, l * 512 * 512, [[512, 128], [128 * 512, 4], [1, 512]]), (), ["wf"])
                Ts = (T2, T4, T8, T16)

                def p1_front(i):
                    t0 = i * TS
                    u = ub[i % 2]
                    ku = "ub%d" % (i % 2)
                    d_ = dT[i % 2]
                    kd = "dT%d" % (i % 2)
                    dma("sync", u[:], DA(fm_d, t0, [[TW, 128], [128 * TW, 4], [1, 400]]), [("fm", 0), "fm_margin"], [ku])
                    dma("sync", sgp[i % 2][:], DA(fm_d, 512 * TW + 8 + t0, [[TW, 128], [128 * TW, 4], [1, TS]]), [("fm", 1)],
                        ["sg%d" % (i % 2)])
                    tt("vector", T2[:, :, 0:399], u[:, :, 0:399], u[:, :, 1:400], ALU.add, [ku], ["T2"])
                    tt("gpsimd", T4[:, 1:4, 0:397], T2[:, 1:4, 0:397], T2[:, 1:4, 2:399], ALU.add, ["T2"], ["T4"])
                    tt("vector", T8[:, 2:4, 0:393], T4[:, 2:4, 0:393], T4[:, 2:4, 4:397], ALU.add, ["T4"], ["T8"])
                    tt("gpsimd", T16[:, 3, 0:385], T8[:, 3, 0:385], T8[:, 3, 8:393], ALU.add, ["T8"], ["T16"])
                    for g, w_ in enumerate((2, 4, 8, 16)):
                        half = w_ // 2
                        Pg = Ts[g]
                        kP = ("T2", "T4", "T8", "T16")[g]
                        o8 = 8 - half
                        stt(d_[:, g, :], Pg[:, g, o8:o8 + TS], 1.0 / w_, u[:, g, 8:8 + TS], ALU.mult, ALU.subtract,
                            [kP, ku], [(kd, g)])
                        for side, tile_i, m0 in ((0, 0, NPAD), (1, NT - 1, TS - 8)):
                            if i != tile_i:
                                continue
                            tt("vector", tmp[:, g, :], Pg[:, g, o8 + m0:o8 + m0 + 8], cinv[:, g, side, :], ALU.mult,
                               [kP, "cinv"], [("ptmp", g)])
                            tt("vector", d_[:, g, m0:m0 + 8], tmp[:, g, :], u[:, g, 8 + m0:8 + m0 + 8], ALU.subtract,
                               [("ptmp", g), ku], [(kd, g)])

                def p1_back(i):
                    t0 = i * TS
                    d_ = dT[i % 2]
                    kd = "dT%d" % (i % 2)
                    s_ = sgp[i % 2]
                    ksg = "sg%d" % (i % 2)
                    o_ = ostp[i % 2]
                    ko = "ost%d" % (i % 2)
                    for g in range(4):
                        mm(py[g][:, 0:TS], pw[:, g, :], d_[:, g, :], True, True, ["pw", (kd, g)], [("py", g)])
                    for g in range(4):
                        stt(o_[:, g, :], py[g][:, 0:TS], psc[:, g:g + 1], s_[:, g, :], ALU.mult, ALU.mult,
                            [("py", g), "psc", ksg], [(ko, g)])
                    dma("sync", DA(mixT_d, t0, [[TP, 128], [128 * TP, 4], [1, TS]]), o_[:],
                        [(ko, g) for g in range(4)], [("mixT", 0)])

                n = 0
                for tb in range(2):
                    for ci in range(4):
                        p, kp = banks[n % 8]
                        for mk in range(4):
                            mm(p[:, :], cs[:, tb, mk, ci * 128:(ci + 1) * 128], wf[:, mk, :], mk == 0, mk == 3,
                               ["dftc", "wf"], [kp])
                        cp("scalar" if n % 2 else "vector", cw[:, tb, ci, :], p[:, :], [kp], [("cw", tb, ci)])
                        n += 1
                cwk = [("cw", tb, ci) for tb in range(2) for ci in range(4)]
                for i in range(NT):
                    t0 = i * TS
                    u = uT[i % 2]
                    ku = "uT%d" % (i % 2)
                    dma("sync", u[:], DA(fm_d, 1024 * TW + 8 + t0, [[TW, 128], [128 * TW, 4], [1, TS]]), [("fm", 2)], [ku])
                    for cl in range(3):
                        c = 3 * i + cl
                        for tb, dst in ((0, Ap), (1, Bp)):
                            p, kp = banks[(tb * 3 + cl) % 8]
                            for ck in range(4):
                                mm(p[:, :], u[:, ck, cl * 128:(cl + 1) * 128], cw[:, tb, ck, :], ck == 0, ck == 3,
                                   [ku] + cwk, [kp])
                            cp("scalar" if tb else "vector", dst[:, c, :], p[:, :], [kp], [("ApBp", c)])
                apk = [("ApBp", c) for c in range(NCH)]
                npiece = 0
                for kt in range(NT):
                    t0 = kt * TS
                    s_ = sg[kt % 2]
                    ksg = "sgf%d" % (kt % 2)
                    o_ = ost[kt % 2]
                    ko = "ostf%d" % (kt % 2)
                    dma("sync", s_[:], DA(fm_d, 1536 * TW + 8 + t0, [[TW, 128], [128 * TW, 4], [1, TS]]), [("fm", 3)], [ksg])
                    p1_front(kt)
                    for pc in range(NPC):
                        for tb in range(2):
                            rb = ring[npiece % 4]
                            kr = "ring%d" % (npiece % 4)
                            npiece += 1
                            off = (((tb * NT + kt) * NPC + pc) * 128) * (PCC * TS)
                            dma("sync", rb[:], DA(c_tab_d, off, [[PCC * TS, 128], [TS, PCC], [1, TS]]), (), [kr])
                            src = Bp if tb else Ap
                            for tci in range(PCC):
                                tc_ = pc * PCC + tci
                                first = (pc == 0 and tb == 0 and tci == 0)
                                last = (pc == NPC - 1 and tb == 1 and tci == PCC - 1)
                                for dc in range(4):
                                    mm(acc[dc][:, 0:TS], src[:, tc_, dc * 128:(dc + 1) * 128], rb[:, tci, :], first, last,
                                       apk + [kr], [("acc", 0, dc)])
                    for dc in range(4):
                        tt("vector", o_[:, dc, :], acc[dc][:, 0:TS], s_[:, dc, :], ALU.mult,
                           [("acc", 0, dc), ksg], [(ko, dc)])
                    dma("sync", DA(mixT_d, 512 * TP + t0, [[TP, 128], [128 * TP, 4], [1, TS]]), o_[:],
                        [(ko, dc) for dc in range(4)], [("mixT", 1)])
                    p1_back(kt)
            S.barrier()

        def phase_p3(l):
            with contextlib.ExitStack() as st:
                BT = alloc(st, "BT", [128, 2, TP], BF16)
                CT = alloc(st, "CT", [128, 2, TP], BF16)
                xtm = alloc(st, "xtm", [128, NCH, 512], BF16)
                btm = alloc(st, "btm", [128, NCH, 256], BF16)
                dtr = alloc(st, "dtr", [128, NCH, 16], F32)
                dtt = alloc(st, "dtt", [128, NCH, 16], F32)
                da = alloc(st, "da", [128, NCH, 16], F32)
                tA = alloc(st, "tA", [128, NCH, 16], F32)
                tB = alloc(st, "tB", [128, NCH, 16], F32)
                dtb = alloc(st, "dtb", [128, 16], F32)
                alg = alloc(st, "alg", [128, 16], F32)
                dsk = alloc(st, "dsk", [128, 8], F32)
                snw = alloc(st, "snw", [128, 512], F32)
                cwt = alloc(st, "cwt", [128, 8, 4], F32)
                cbt = alloc(st, "cbt", [128, 8], F32)
                dg = alloc(st, "dg", [128, 8, 4, 128], BF16)
                cf = alloc(st, "cf", [128, 3, 128], F32)
                cb = alloc(st, "cb", [128, 6, 128], BF16)
                xin = [alloc(st, "xin", [128, 8, TS + 3], BF16) for _ in range(2)]
                xTt = [alloc(st, "xTt", [128, 4, TS], BF16) for _ in range(2)]
                H = alloc(st, "H", [128, 512], F32)
                Hb = alloc(st, "Hb", [128, 512], BF16)
                Ht = alloc(st, "Ht", [128, 512], F32)
                cst = alloc(st, "cst", [128, 24], F32)
                E = alloc(st, "E", [128, 24], F32)
                R = alloc(st, "R", [128, 8, 128], BF16)
                cbm = alloc(st, "cbm", [128, 2, 128], BF16)
                eseg = alloc(st, "eseg", [128, 8, 128], BF16)
                MT = alloc(st, "MT", [128, 8, 128], BF16)
                xdt = alloc(st, "xdt", [128, 512], BF16)
                xw = alloc(st, "xw", [128, 512], BF16)
                ytmp = alloc(st, "ytmp", [128, 512], F32)
                yc = [alloc(st, "yc", [128, 512], F32) for _ in range(2)]
                yfl = [alloc(st, "yfl", [128, 512], F32) for _ in range(2)]
                szl = [alloc(st, "szl", [128, 512], BF16) for _ in range(2)]
                yg = alloc(st, "yg", [128, 512], F32)
                yjunk = alloc(st, "yjunk", [128, 512], BF16)
                ynb = alloc(st, "ynb", [128, 512], BF16)
                ystat = alloc(st, "ystat", [128, 4], F32)
                yst = [alloc(st, "yst", [128, 4, TS], BF16) for _ in range(2)]
                pcv = [palloc(st, "pcv", [128, 512], F32) for _ in range(2)]
                ptr = palloc(st, "ptr", [128, 1024], BF16)
                pcb = pcv[0]
                pseg = [palloc(st, "pseg", [128, 512], F32) for _ in range(2)]
                pyd = palloc(st, "pyd", [128, 512], F32)
                pyo = palloc(st, "pyo", [128, 512], F32)
                psn = palloc(st, "psn", [128, 512], F32)
                pyd2 = [pcv[1], pyd]
                kpyd = ["pcv1", "pyd"]
                cbm2 = [alloc(st, "cbm2", [128, 2, 128], BF16) for _ in range(2)]
                cst2 = [alloc(st, "cst2", [128, 24], F32) for _ in range(2)]
                E2 = [alloc(st, "E2", [128, 24], F32) for _ in range(2)]
                MT2 = [alloc(st, "MT2", [128, 8, 128], BF16) for _ in range(2)]
                xdt2 = [alloc(st, "xdt2", [128, 512], BF16) for _ in range(2)]
                xw2 = [alloc(st, "xw2", [128, 512], BF16) for _ in range(2)]
                ytmp2 = alloc(st, "ytmp2", [128, 512], F32)

                dma("sync", cf[:], c_ssdf_d.ap(), (), ["cf"])
                dma("sync", cb[:], c_ssdb_d.ap(), (), ["cb"])
                dma("sync", dtb[:], DA(dtb_d, l * 16, [[0, 128], [1, 16]]), (), ["dtb"])
                dma("sync", alg[:], DA(alog_d, l * 16, [[0, 128], [1, 16]]), (), ["alg"])
                dma("sync", dsk[:], DA(dskip_d, l * 8, [[0, 128], [1, 8]]), (), ["dsk"])
                dma("sync", snw[:], DA(ssdnw_d, l * 512, [[0, 128], [1, 512]]), (), ["snw"])
                for j in range(4):
                    dma("sync", cwt[:, :, j:j + 1], DA(convw_d, (l * 4 + j) * 1024, [[1, 128], [128, 8], [1, 1]]), (), ["cwt"], slow=True)
                dma("sync", cbt[:].unsqueeze(2), DA(convb_d, l * 1024, [[1, 128], [128, 8], [1, 1]]), (), ["cbt"], slow=True)
                for ch in range(8):
                    for j in range(4):
                        tsc("gpsimd", dg[:, ch, j, :], ident[:], cwt[:, ch, j:j + 1], None, ALU.mult, None,
                            ["ident", "cwt"], [("dg", ch)])
                for c3 in range(3):
                    dma("sync", dtr[:, c3 * 11:(c3 + 1) * 11, :],
                        DA(dtr_d, c3 * 11 * 128 * 16, [[16, 128], [128 * 16, 11], [1, 16]]), ["dtr"], ["dtrl"])
                bcd = dtb[:].unsqueeze(1).broadcast_to([128, NCH, 16])
                tt("vector", tA[:], dtr[:], bcd, ALU.add, ["dtrl", "dtb"], ["tA"])
                act(tB[:], tA[:], AF.Abs, ["tA"], ["tB"])
                act(tB[:], tB[:], AF.Exp, ["tB"], ["tB"], scale=-1.0)
                tsc("vector", tB[:], tB[:], 1.0, None, ALU.add, None, ["tB"], ["tB"])
                act(tB[:], tB[:], AF.Ln, ["tB"], ["tB"])
                tsc("vector", tA[:], tA[:], 0.0, None, ALU.max, None, ["tA"], ["tA"])
                tt("vector", dtt[:], tA[:], tB[:], ALU.add, ["tA", "tB"], ["dtt"])
                act(alg[:], alg[:], AF.Exp, ["alg"], ["alg"])
                tsc("vector", alg[:], alg[:], -1.0, None, ALU.mult, None, ["alg"], ["alg"])
                tt("vector", da[:], dtt[:], alg[:].unsqueeze(1).broadcast_to([128, NCH, 16]), ALU.mult, ["dtt", "alg"], ["da"])

                for i in range(NT):
                    t0 = i * TS
                    xi = xin[i % 2]
                    kx = "xin%d" % (i % 2)
                    xt_ = xTt[i % 2]
                    kxt = "xTt%d" % (i % 2)
                    dma("sync", xi[:], DA(fm_d, 2048 * TW + 8 + t0 - 2, [[TW, 128], [128 * TW, 8], [1, TS + 3]]),
                        [("fm", 4), ("fm", 5), "fm_margin"], [kx])
                    for ch in range(8):
                        p = pcv[ch % 2]
                        kp = "pcv%d" % (ch % 2)
                        for j in range(4):
                            mm(p[:, 0:TS], dg[:, ch, j, :], xi[:, ch, j:j + TS], j == 0, j == 3, [("dg", ch), kx], [kp])
                        if ch < 4:
                            dst, kdst = xt_[:, ch, :], (kxt, ch)
                        elif ch < 6:
                            dst, kdst = BT[:, ch - 4, t0:t0 + TS], ("BT", i)
                        else:
                            dst, kdst = CT[:, ch - 6, t0:t0 + TS], ("CT", i)
                        act(dst, p[:, 0:TS], AF.Silu, [kp, "cbt"], [kdst], bias=cbt[:, ch:ch + 1])
                    if i == 0:
                        memset("gpsimd", xt_[:, :, 0:NPAD], 0.0, [(kxt, ch) for ch in range(4)])
                    for cl in range(3):
                        c = 3 * i + cl
                        lt = slice(cl * 128, (cl + 1) * 128)
                        for ch in range(4):
                            identr(ptr[:, ch * 128:(ch + 1) * 128], xt_[:, ch, lt], [(kxt, q) for q in range(4)], ["ptr"])
                        for g in range(2):
                            identr(ptr[:, 512 + g * 128:512 + (g + 1) * 128], BT[:, g, t0 + cl * 128:t0 + (cl + 1) * 128],
                                   [("BT", i)], ["ptr"])
                        cp("vector", xtm[:, c, :], ptr[:, 0:512], ["ptr"], [("xtm", c)])
                        cp("scalar", btm[:, c, :], ptr[:, 512:768], ["ptr"], [("btm", c)])

                ydx = ytmp2
                for d_ in range(2):
                    memset("gpsimd", H[:], 0.0, ["H"])
                    memset("gpsimd", Hb[:], 0.0, ["Hb"])
                    order = list(range(NCH)) if d_ == 0 else list(range(NCH - 1, -1, -1))

                    def stage_a(n_, d_=d_, order=order):
                        c = order[n_]
                        i = c // 3
                        b2 = n_ % 2
                        cols = slice(c * 128, (c + 1) * 128)
                        dac = da[:, c, d_ * 8:(d_ + 1) * 8]
                        for g in range(2):
                            mm(pcb[:, g * 128:(g + 1) * 128], BT[:, g, cols], CT[:, g, cols], True, True,
                               [("BT", i), ("CT", i)], ["pcv0"])
                        mm(pcb[:, 256:264], cf[:, d_, :], dac, True, True, ["cf", "da"], ["pcv0"])
                        mm(pcb[:, 264:272], cf[:, 2, :], dac, True, True, ["cf", "da"], ["pcv0"])
                        tt("vector", cbm2[b2][:], pcb[:, 0:256].rearrange("p (g s) -> p g s", g=2),
                           cb[:, 4 + d_, :].unsqueeze(1).broadcast_to([128, 2, 128]), ALU.mult, ["pcv0", "cb"], ["cbm%d" % b2])
                        cp("vector", cst2[b2][:, 0:16], pcb[:, 256:272], ["pcv0"], [("cst%d" % b2, 0)])
                        tt("vector", cst2[b2][:, 16:24], cst2[b2][:, 8:16], cst2[b2][:, 0:8], ALU.subtract,
                           [("cst%d" % b2, 0)], [("cst%d" % b2, 1)])
                        act(E2[b2][:], cst2[b2][:], AF.Exp, [("cst%d" % b2, 0), ("cst%d" % b2, 1)], ["E%d" % b2])
                        tt("gpsimd", R[:], cb[:, d_, :].unsqueeze(1).broadcast_to([128, 8, 128]),
                           dac.unsqueeze(2).broadcast_to([128, 8, 128]), ALU.mult, ["cb", "da"], ["R"])
                        tt("gpsimd", xdt2[b2][:].rearrange("p (h d) -> p h d", h=8),
                           xtm[:, c, :].rearrange("p (h d) -> p h d", h=8),
                           dtt[:, c, d_ * 8:(d_ + 1) * 8].unsqueeze(2).broadcast_to([128, 8, 64]), ALU.mult,
                           [("xtm", c), "dtt"], ["xdt%d" % b2])
                        for hh in range(2):
                            mm(pseg[hh][:, :], cb[:, 2 + d_, :], R[:, hh * 4:(hh + 1) * 4, :], True, True, ["cb", "R"],
                               [("pseg", hh)])
                            act(eseg[:, hh * 4:(hh + 1) * 4, :], pseg[hh][:, :].rearrange("p (h l) -> p h l", h=4), AF.Exp,
                                [("pseg", hh)], [("eseg", hh)])
                            tt("vector", MT2[b2][:, hh * 4:(hh + 1) * 4, :], eseg[:, hh * 4:(hh + 1) * 4, :],
                               cbm2[b2][:, hh, :].unsqueeze(1).broadcast_to([128, 4, 128]), ALU.mult,
                               [("eseg", hh), "cbm%d" % b2], [("MT%d" % b2, hh)])
                        tt("gpsimd", xw2[b2][:].rearrange("p (h d) -> p h d", h=8),
                           xdt2[b2][:].rearrange("p (h d) -> p h d", h=8),
                           E2[b2][:, 16:24].unsqueeze(2).broadcast_to([128, 8, 64]), ALU.mult, ["xdt%d" % b2, "E%d" % b2],
                           ["xw%d" % b2])
                        for h in range(8):
                            mm(pyd2[b2][:, h * 64:(h + 1) * 64], MT2[b2][:, h, :], xdt2[b2][:, h * 64:(h + 1) * 64], True, True,
                               [("MT%d" % b2, h // 4), "xdt%d" % b2], [kpyd[b2]])

                    def stage_b(n_, d_=d_, order=order):
                        c = order[n_]
                        i = c // 3
                        b2 = n_ % 2
                        cols = slice(c * 128, (c + 1) * 128)
                        Eb = E2[b2]
                        kE = "E%d" % b2
                        for g in range(2):
                            mm(pyo[:, g * 256:(g + 1) * 256], CT[:, g, cols], Hb[:, g * 256:(g + 1) * 256], True, True,
                               [("CT", i), "Hb"], ["pyo"])
                        tt("vector", ytmp[:].rearrange("p (h d) -> p h d", h=8), pyo[:, :].rearrange("p (h d) -> p h d", h=8),
                           Eb[:, 0:8].unsqueeze(2).broadcast_to([128, 8, 64]), ALU.mult, ["pyo", kE], ["ytmp"])
                        ycb = yc[b2]
                        kyc = "yc%d" % b2
                        tt("vector", ycb[:], pyd2[b2][:, :], ytmp[:], ALU.add, [kpyd[b2], "ytmp"], [kyc])
                        for g in range(2):
                            mm(psn[:, g * 256:(g + 1) * 256], btm[:, c, g * 128:(g + 1) * 128], xw2[b2][:, g * 256:(g + 1) * 256],
                               True, True, [("btm", c), "xw%d" % b2], ["psn"])
                        tt("gpsimd", Ht[:].rearrange("p (h d) -> p h d", h=8), H[:].rearrange("p (h d) -> p h d", h=8),
                           Eb[:, 8:16].unsqueeze(2).broadcast_to([128, 8, 64]), ALU.mult, ["H", kE], ["Ht"])
                        tt("vector", H[:], psn[:, :], Ht[:], ALU.add, ["psn", "Ht"], ["H"])
                        cp("scalar", Hb[:], H[:], ["H"], ["Hb"])
                        if d_ == 0:
                            dma("sync", yf_d.ap()[c * 128:(c + 1) * 128, :], ycb[:], [kyc], ["yf"])
                        else:
                            yfb = yfl[b2]
                            kyf = "yfl%d" % b2
                            szb = szl[b2]
                            ksz = "szl%d" % b2
                            dma("sync", yfb[:], yf_d.ap()[c * 128:(c + 1) * 128, :], ["yf"], [kyf])
                            dma("sync", szb[:], sz_d.ap()[c * 128:(c + 1) * 128, :], ["sz"], [ksz])
                            tt("gpsimd", ydx[:].rearrange("p (h d) -> p h d", h=8),
                               xtm[:, c, :].rearrange("p (h d) -> p h d", h=8),
                               dsk[:].unsqueeze(2).broadcast_to([128, 8, 64]), ALU.mult, [("xtm", c), "dsk"], ["ydx"])
                            tt("vector", yg[:], ycb[:], yfb[:], ALU.add, [kyc, kyf], ["yg"])
                            tt("vector", yg[:], yg[:], ydx[:], ALU.add, ["yg", "ydx"], ["yg"])
                            tt("vector", yg[:], yg[:], szb[:], ALU.mult, ["yg", ksz], ["yg"])
                            act(yjunk[:], yg[:], AF.Square, ["yg"], ["yjunk", ("ystat", 0)], accum_out=ystat[:, 0:1])
                            tsc("vector", ystat[:, 1:2], ystat[:, 0:1], 1.0 / 512, EPS, ALU.mult, ALU.add, [("ystat", 0)], [("ystat", 1)])
                            tt("gpsimd", ystat[:, 3:4], ystat[:, 1:2], mhalf[:, 0:1], ALU.pow, [("ystat", 1), "mhalf"], [("ystat", 3)])
                            stt(ynb[:], yg[:], ystat[:, 3:4], snw[:], ALU.mult, ALU.mult, ["yg", ("ystat", 3), "snw"], ["ynb"])
                            ti = c // 3
                            cl = c % 3
                            ys = yst[ti % 2]
                            kys = "yst%d" % (ti % 2)
                            for ch in range(4):
                                identr(ptr[:, ch * 128:(ch + 1) * 128], ynb[:, ch * 128:(ch + 1) * 128], ["ynb"], ["ptr"])
                            cp("scalar", ys[:, :, cl * 128:(cl + 1) * 128], ptr[:, 0:512].rearrange("p (j t) -> p j t", j=4),
                               ["ptr"], [(kys, cl)])
                            if cl == 0:
                                dma("sync", DA(mixT_d, 1024 * TP + ti * TS, [[TP, 128], [128 * TP, 4], [1, TS]]), ys[:],
                                    [(kys, q) for q in range(3)], [("mixT", 2)])

                    stage_a(0)
                    for n_ in range(NCH):
                        if n_ + 1 < NCH:
                            stage_a(n_ + 1)
                        stage_b(n_)
            S.barrier()

        def phase_p4(l):
            TQ = 512
            qtiles = [(0, 128)] + [(128 + k * TQ, TQ) for k in range(8)]
            with contextlib.ExitStack() as st:
                KT2 = alloc(st, "KT2", [128, 2, TP], BF16)
                Vp = alloc(st, "Vp", [128, NCH, 2, 192], BF16)
                vmask = alloc(st, "vmask", [128, 64], BF16)
                QT = [alloc(st, "QT", [128, 4, TQ], BF16) for _ in range(2)]
                sg = [alloc(st, "sga", [128, 4, TQ], BF16) for _ in range(2)]
                P = [alloc(st, "P", [128, 2, TQ], BF16) for _ in range(3)]
                rec = alloc(st, "rec", [128, TQ], F32)
                yt = alloc(st, "yt", [128, TQ], F32)
                ost = [alloc(st, "osta", [128, 4, TQ], BF16) for _ in range(2)]
                Sp = [palloc(st, "Sp", [128, 2, 512], F32) for _ in range(2)]
                Oe = [palloc(st, "Oe", [128, 512], F32) for _ in range(2)]
                Oo = [palloc(st, "Oo", [128, 512], F32) for _ in range(2)]
                dma("sync", vmask[:], c_vmask_d.ap(), (), ["vmask"])
                for g in range(2):
                    for hs in range(2):
                        dma("sync", KT2[hs * 64:(hs + 1) * 64, g, :], DA(kT_d, g * 64 * TP, [[TP, 64], [1, TP]]), ["kT"],
                            [("KT2", g, hs)])
                vkeys = [("Vp", c3, g) for c3 in range(3) for g in range(2)]
                memset("gpsimd", Vp[:], 1.0, vkeys)
                for g in range(2):
                    for o0 in (0, 128):
                        cp("gpsimd", Vp[:, 0, g, o0:o0 + 64], vmask[:], ["vmask"], [("Vp", 0, g)])
                for c3 in range(3):
                    for g in range(2):
                        dma("sync", Vp[:, c3 * 11:(c3 + 1) * 11, g, 64:128],
                            DA(vtm_d, c3 * 11 * 128 * 128 + g * 64, [[128, 128], [128 * 128, 11], [1, 64]]), ["vtm"],
                            [("Vp", c3, g)])
                ktk = [("KT2", g, hs) for g in range(2) for hs in range(2)]
                npair = 0
                nS = [0]

                def p4_load(i):
                    t0, tw = qtiles[i]
                    dma("sync", QT[i % 2][:, :, 0:tw], DA(qT_d, t0, [[TP, 128], [128 * TP, 4], [1, tw]]), ["qT"], ["QT%d" % (i % 2)])
                    dma("sync", sg[i % 2][:, :, 0:tw], DA(fm_d, 3072 * TW + 8 + t0, [[TW, 128], [128 * TW, 4], [1, tw]]),
                        [("fm", 6)], ["sga%d" % (i % 2)])

                for i in range(len(qtiles)):
                    t0, tw = qtiles[i]
                    q_ = QT[i % 2]
                    kq = "QT%d" % (i % 2)
                    s_ = sg[i % 2]
                    ksg = "sga%d" % (i % 2)
                    o_ = ost[i % 2]
                    ko = "osta%d" % (i % 2)
                    if i == 0:
                        p4_load(0)
                    if i + 1 < len(qtiles):
                        p4_load(i + 1)
                    for j in range(4):
                        g = j // 2
                        oe = Oe[npair % 2]
                        oo = Oo[npair % 2]
                        koe = "Oe%d" % (npair % 2)
                        koo = "Oo%d" % (npair % 2)
                        npair += 1
                        pend = {}

                        def qk(kc, g=g, j=j, q_=q_, kq=kq, pend=pend, tw=tw):
                            n_ = nS[0]
                            nS[0] += 1
                            sp = Sp[n_ % 2]
                            ksp = "Sp%d" % (n_ % 2)
                            pb = P[n_ % 3]
                            kpb = "P%d" % (n_ % 3)
                            kcols = slice(kc * 128, (kc + 1) * 128)
                            mm(sp[:, 0, 0:tw], KT2[0:64, g, kcols], q_[0:64, j, 0:tw], True, True, ktk + [kq], [ksp])
                            mm(sp[:, 1, 0:tw], KT2[64:128, g, kcols], q_[64:128, j, 0:tw], True, True, ktk + [kq], [ksp])
                            act(pb[:, :, 0:tw], sp[:, :, 0:tw], AF.Exp, [ksp], [kpb])
                            pend[kc] = (pb, kpb)

                        def pvm(kc, g=g, oe=oe, oo=oo, koe=koe, koo=koo, pend=pend, tw=tw):
                            pb, kpb = pend.pop(kc)
                            mm(oe[:, 0:tw], Vp[:, kc, g, 64:192], pb[:, 0, 0:tw], kc == 0, kc == NCH - 1, vkeys + [kpb], [koe])
                            mm(oo[:, 0:tw], Vp[:, kc, g, 0:128], pb[:, 1, 0:tw], kc == 0, kc == NCH - 1, vkeys + [kpb], [koo])

                        qk(0)
                        for kc in range(NCH):
                            if kc + 1 < NCH:
                                qk(kc + 1)
                            pvm(kc)
                        recip(rec[0:64, 0:tw], oe[64:128, 0:tw], [koe], [("rec", 0)])
                        recip(rec[64:128, 0:tw], oo[0:64, 0:tw], [koo], [("rec", 1)])
                        tt("vector", yt[0:64, 0:tw], oe[0:64, 0:tw], rec[0:64, 0:tw], ALU.mult, [koe, ("rec", 0)], [("yt", 0)])
                        tt("vector", yt[64:128, 0:tw], oo[64:128, 0:tw], rec[64:128, 0:tw], ALU.mult, [koo, ("rec", 1)], [("yt", 1)])
                        tt("gpsimd", o_[:, j, 0:tw], yt[:, 0:tw], s_[:, j, 0:tw], ALU.mult, [("yt", 0), ("yt", 1), ksg], [(ko, j)])
                    dma("sync", DA(mixT_d, 1536 * TP + t0, [[TP, 128], [128 * TP, 4], [1, tw]]), o_[:, :, 0:tw],
                        [(ko, j) for j in range(4)], [("mixT", 3)])
            S.barrier()

        def phase_p5(l, last):
            with contextlib.ExitStack() as st:
                wo = alloc(st, "wo", [128, 16, D], BF16)
                mx = [alloc(st, "mx", [128, 16, TS], BF16) for _ in range(2)]
                hc = [alloc(st, "hc5", [128, D], F32) for _ in range(4)]
                ho = [alloc(st, "ho5", [128, D], F32) for _ in range(2)]
                po = [[palloc(st, "po", [128, 512], F32) for _ in range(2)] for _ in range(2)]
                for hh in range(2):
                    dma("gpsimd", wo[:, :, hh * 512:(hh + 1) * 512],
                        DA(wout_d, l * 2048 * D + hh * 512, [[D, 128], [128 * D, 16], [1, 512]]), (), [("wo", hh)])
                def p5_load(i):
                    for q4 in range(4):
                        dma("sync", mx[i % 2][:, q4 * 4:(q4 + 1) * 4, :],
                            DA(mixT_d, q4 * 512 * TP + i * TS, [[TP, 128], [128 * TP, 4], [1, TS]]), [("mixT", q4)], ["mx%d" % (i % 2)])

                def p5_hload(c):
                    if c >= NCH or (last and c == 0):
                        return
                    hcb = hc[c % 4]
                    khc = "hc5%d" % (c % 4)
                    if l == 0 and c == 0:
                        memset("gpsimd", hcb[:], 0.0, [khc])
                        dma("sync", hcb[NPAD:128, :], meta_d.ap(), (), [khc])
                    elif l == 0:
                        dma("sync", hcb[:], x_d.ap()[(c - 1) * 128:c * 128, :], (), [khc])
                    else:
                        dma("sync", hcb[:], hbuf_d.ap()[c * 128:(c + 1) * 128, :], ["hbuf", "hbuf_pad"], [khc])

                p5_load(0)
                p5_hload(0)
                p5_hload(1)
                for i in range(NT):
                    t0 = i * TS
                    m_ = mx[i % 2]
                    km = "mx%d" % (i % 2)
                    if i + 1 < NT:
                        p5_load(i + 1)
                    for cl in range(3):
                        c = 3 * i + cl
                        p5_hload(c + 2)
                        if last and c == 0:
                            continue
                        hcb = hc[c % 4]
                        khc = "hc5%d" % (c % 4)
                        hob = ho[c % 2]
                        kho = "ho5%d" % (c % 2)
                        for hh in range(2):
                            p = po[c % 2][hh]
                            kp = ("po", c % 2, hh)
                            for ec in range(16):
                                mm(p[:, :], m_[:, ec, cl * 128:(cl + 1) * 128], wo[:, ec, hh * 512:(hh + 1) * 512],
                                   ec == 0, ec == 15, [km, ("wo", hh)], [kp])
                            tt("vector", hob[:, hh * 512:(hh + 1) * 512], p[:, :], hcb[:, hh * 512:(hh + 1) * 512], ALU.add,
                               [kp, khc], [(kho, hh)])
                        if last:
                            o = dma("sync", out_d.ap()[(c - 1) * 128:c * 128, :], hob[:], [(kho, 0), (kho, 1)], ["out"])
                            final_ops.append(o)
                        elif c == 0:
                            dma("sync", hbuf_d.ap()[NPAD:128, :], hob[NPAD:128, :], [(kho, 0), (kho, 1)], ["hbuf"])
                        else:
                            dma("sync", hbuf_d.ap()[c * 128:(c + 1) * 128, :], hob[:], [(kho, 0), (kho, 1)], ["hbuf"])
            S.barrier()

        wst = contextlib.ExitStack()
        Wcur = load_w(wst, 0) if "0" in phases else None
        for l in range(n_layers):
            if "0" in phases:
                phase_p0(l, Wcur)
                wst.close()
            if "1" in phases or "2" in phases:
                phase_p12(l)
            if "3" in phases:
                phase_p3(l)
            if "4" in phases:
                phase_p4(l)
            if "5" in phases:
                if "0" in phases and l + 1 < n_layers:
                    wst = contextlib.ExitStack()
                    Wcur = load_w(wst, l + 1)
                phase_p5(l, l == n_layers - 1)
        final_ops[:] = [o for o in final_ops if not o.skipped]
        S.limit = None
        print("n_ops", len(S.ops))
        if not final_ops:
            zo = dma("sync", out_d.ap()[0:128, :], zf[:], ["zf"], ["out"])
            final_ops.append(zo)
        S.emit(nc, final_ops)
    return nc


_PARAM_NAMES = ("meta_tokens", "norm_w", "w_in", "w_out", "pool_w", "pool_scale", "fourier_w", "conv_w", "conv_b",
                "dt_bias", "a_log", "d_skip", "ssd_norm_w", "q_norm_w", "k_norm_w")


def make_in_maps(inputs, cores):
    consts = host_consts()
    shared = {k: np.ascontiguousarray(np.asarray(inputs[k], dtype=np.float32)) for k in _PARAM_NAMES}
    shared.update(consts)
    x = np.asarray(inputs["x"], dtype=np.float32)
    maps = []
    for b in cores:
        m = dict(shared)
        m["x"] = np.ascontiguousarray(x[b])
        maps.append(m)
    return maps


def kernel(**inputs):
    nc = build()
    in_maps = make_in_maps(inputs, list(range(8)))
    res = run_bass_kernel_spmd(nc, in_maps, core_ids=list(range(8)))
    return np.stack([np.asarray(r["out"], dtype=np.float32) for r in res.results], axis=0)
```

```python
import contextlib
import numpy as np
import ml_dtypes
import concourse.bass as bass
import concourse.mybir as mybir
from concourse.bass_utils import run_bass_kernel_spmd

F32 = mybir.dt.float32
BF16 = mybir.dt.bfloat16
AF = mybir.ActivationFunctionType
ALU = mybir.AluOpType
AX = mybir.AxisListType

D = 1024
NTOK = 4096
NMETA = 16
L = NTOK + NMETA
NPAD = 112
TP = 4224
NCH = 33
NT = 11
TS = 384
TW = TP + 16
DEPTH = 4
INC = 4880
EPS = 1e-6
NPC = 3
PCC = 11

ENGS = ("tensor", "vector", "scalar", "gpsimd", "sync")
N_DMA_SEMS = 28
N_SW_SEMS = 8
PSUM_KEYS = frozenset(["pT", "pf0", "pf1", "pz", "pq", "pkv", "ptq", "py", "acc", "pcv0", "pcv1", "ptr", "pseg", "pyd",
                       "pyo", "psn", "Sp0", "Sp1", "Oe0", "Oe1", "Oo0", "Oo1", "po"])


class Op:
    __slots__ = ("eng", "fn", "deps", "is_dma", "needed", "sem", "val", "nobar", "skipped")

    def __init__(self, eng, fn, deps, is_dma):
        self.eng = eng
        self.fn = fn
        self.deps = deps
        self.is_dma = is_dma
        self.needed = False
        self.sem = None
        self.val = 0
        self.nobar = False
        self.skipped = False


class Sched:
    def __init__(self):
        self.ops = []
        self.last_w = {}
        self.readers = {}
        self.last_eng = {}
        self.dmas = []
        self.limit = None

    def _add(self, eng, fn, reads, writes, is_dma, extra=()):
        ex = tuple(k for k in reads if (k[0] if isinstance(k, tuple) else k) in PSUM_KEYS)
        if ex:
            writes = tuple(writes) + tuple(k for k in ex if k not in writes)
        deps = list(extra)
        for k in reads:
            w = self.last_w.get(k)
            if w is not None:
                deps.append(w)
        for k in writes:
            w = self.last_w.get(k)
            if w is not None:
                deps.append(w)
            deps.extend(self.readers.get(k, ()))
        op = Op(eng, fn, deps, is_dma)
        if self.limit is not None and len(self.ops) >= self.limit:
            op.skipped = True
            return op
        self.ops.append(op)
        for k in writes:
            self.last_w[k] = op
            self.readers[k] = []
        for k in reads:
            if k in writes:
                continue
            lst = self.readers.setdefault(k, [])
            if not is_dma:
                lst[:] = [o for o in lst if o.is_dma or o.eng != eng]
            lst.append(op)
        if is_dma:
            self.dmas.append(op)
        else:
            self.last_eng[eng] = op
        return op

    def op(self, eng, fn, reads=(), writes=()):
        return self._add(eng, fn, tuple(reads), tuple(writes), False)

    def dma(self, eng, fn, reads=(), writes=(), nobar=False):
        o = self._add(eng, fn, tuple(reads), tuple(writes), True)
        o.nobar = nobar
        return o

    def barrier(self):
        deps = [o for o in self.last_eng.values()] + [o for o in self.dmas if not o.nobar]
        self.dmas = [o for o in self.dmas if o.nobar]
        for e in ENGS:
            self._add(e, lambda eng: eng.nop(), (), (), False, extra=deps)

    def emit(self, nc, final_wait_ops=()):
        ops = self.ops
        for o in ops:
            for d in o.deps:
                if d.eng == "tensor" and o.eng == "tensor" and not d.is_dma and not o.is_dma:
                    continue
                d.needed = True
        for o in final_wait_ops:
            o.needed = True
        with contextlib.ExitStack() as st:
            esem = {e: st.enter_context(nc.semaphore("s_" + e)) for e in ENGS}
            dsem = [st.enter_context(nc.semaphore("d_%d" % i)) for i in range(N_DMA_SEMS)]
            ecnt = {e: 0 for e in ENGS}
            dcnt = [0] * N_DMA_SEMS
            dnext = 0
            prev_on_dsem = [None] * N_DMA_SEMS
            swnext = 0
            for o in ops:
                if o.is_dma:
                    if o.eng == "gpsimd":
                        i = N_DMA_SEMS - N_SW_SEMS + swnext
                        swnext = (swnext + 1) % N_SW_SEMS
                    else:
                        i = dnext
                        dnext = (dnext + 1) % (N_DMA_SEMS - N_SW_SEMS)
                    dcnt[i] += 16
                    o.sem = ("d", i)
                    o.val = dcnt[i]
                    if prev_on_dsem[i] is not None:
                        o.deps.append(prev_on_dsem[i])
                    prev_on_dsem[i] = o
                elif o.needed:
                    ecnt[o.eng] += 1
                    o.sem = ("e", o.eng)
                    o.val = ecnt[o.eng]
            per = {e: [o for o in ops if o.eng == e] for e in ENGS}
            final = list(final_wait_ops)
            blk = st.enter_context(nc.Block())

            def semh(s):
                return esem[s[1]] if s[0] == "e" else dsem[s[1]]

            def run(e, eng):
                seen = {}
                for o in per[e]:
                    need = {}
                    for d in o.deps:
                        if d.sem is None:
                            continue
                        if (not d.is_dma) and (not o.is_dma) and d.eng == "tensor" and e == "tensor":
                            continue
                        if need.get(d.sem, 0) < d.val:
                            need[d.sem] = d.val
                    for s, v in need.items():
                        if seen.get(s, 0) < v:
                            eng.wait_ge(semh(s), v)
                            seen[s] = v
                    ins = o.fn(eng)
                    if o.is_dma:
                        ins.then_inc(semh(o.sem), 16)
                    elif o.sem is not None:
                        ins.then_inc(semh(o.sem), 1)
                if e == "sync":
                    for o in final:
                        if seen.get(o.sem, 0) < o.val:
                            eng.wait_ge(semh(o.sem), o.val)
                            seen[o.sem] = o.val

            @blk.tensor
            def _(eng):
                run("tensor", eng)

            @blk.vector
            def _(eng):
                run("vector", eng)

            @blk.scalar
            def _(eng):
                run("scalar", eng)

            @blk.gpsimd
            def _(eng):
                run("gpsimd", eng)

            @blk.sync
            def _(eng):
                run("sync", eng)
        return ecnt, dcnt


_CONSTS = None


def host_consts():
    global _CONSTS
    if _CONSTS is not None:
        return _CONSTS
    bf = ml_dtypes.bfloat16
    c = {}
    c["c_ident"] = np.eye(128, dtype=np.float32).astype(bf)
    tphys = np.arange(TP)
    tlog = tphys - NPAD
    row = np.where(tlog >= NMETA, (tlog - NMETA) // 64, 0).astype(np.float64)
    col = np.where(tlog >= NMETA, (tlog - NMETA) % 64, 0).astype(np.float64)
    freqs = (10000.0 ** (-np.arange(0, 32, 2, dtype=np.float32) / 32.0)).astype(np.float32)
    ang = np.concatenate([row[:, None].astype(np.float32) * freqs[None], col[:, None].astype(np.float32) * freqs[None]],
                         axis=-1).astype(np.float32)
    cos2 = np.repeat(np.cos(ang), 2, axis=-1).astype(np.float32)
    sin2 = np.repeat(np.sin(ang), 2, axis=-1).astype(np.float32)
    rope = np.stack([cos2, sin2], 0).reshape(2, NCH, 128, 64).transpose(2, 0, 1, 3)
    c["c_rope"] = np.ascontiguousarray(rope, dtype=np.float32)
    k = np.arange(128)[:, None]
    l_ = np.arange(128)[None, :]
    trif = (k <= l_).astype(np.float32)
    trib = (k >= l_).astype(np.float32)
    ones = np.ones((128, 128), np.float32)
    c["c_ssd_f32"] = np.ascontiguousarray(np.stack([trif, trib, ones], 1), dtype=np.float32)
    t1f = (k > l_).astype(np.float32)
    t1b = (k < l_).astype(np.float32)
    c["c_ssd_bf"] = np.ascontiguousarray(np.stack([trif, trib, t1f, t1b, trif, trib], 1)).astype(bf)
    m = np.arange(512)
    a = 2.0 * np.pi * ((m[:, None] * m[None, :]) % 512) / 512.0
    cc = np.cos(a).reshape(4, 128, 512).transpose(1, 0, 2)
    sc = np.sin(a).reshape(4, 128, 512).transpose(1, 0, 2)
    c["c_dft_c"] = np.ascontiguousarray(np.stack([cc, sc], 1)).astype(bf)
    scale = 1.0 / np.sqrt(float(L) * 512.0)
    tl = (np.arange(TP) - NPAD).astype(np.int64)
    valid = (tl >= 0)
    prod = (np.maximum(tl, 0)[:, None] * np.maximum(tl, 0)[None, :]) % L
    angp = (2.0 * np.pi / L) * prod.astype(np.float64)
    vm = (valid[:, None] & valid[None, :])
    tabs = []
    for fn_, sgn in ((np.cos, 1.0), (np.sin, -1.0)):
        t_ = (fn_(angp) * (sgn * scale) * vm).astype(np.float32).astype(bf)
        t_ = t_.reshape(NPC, PCC, 128, NT, TS)
        t_ = t_.transpose(3, 0, 2, 1, 4)
        tabs.append(np.ascontiguousarray(t_).reshape(NT, NPC, 128, PCC * TS))
    c["c_tab"] = np.ascontiguousarray(np.stack(tabs, 0))
    pinv = np.zeros((128, 4, 2, 8), np.float32)
    for g, w in enumerate((2, 4, 8, 16)):
        for side in range(2):
            for q in range(8):
                t = q if side == 0 else L - 8 + q
                lo = min(max(t - w // 2, 0), L)
                hi = min(max(t - w // 2 + w, 0), L)
                pinv[:, g, side, q] = 1.0 / float(hi - lo)
    c["c_pool"] = pinv
    vmask = np.ones((128, 64), np.float32)
    vmask[:NPAD] = 0.0
    c["c_vmask"] = vmask.astype(bf)
    _CONSTS = c
    return c


def build(n_layers=DEPTH, dbg=False, phases="012345", limit=None):
    nc = bass.Bass("TRN2", target_bir_lowering=False)
    S = Sched()
    S.limit = limit
    uid = [0]

    def din(name, shape, dt):
        return nc.dram_tensor(name, list(shape), dt, kind="ExternalInput")

    x_d = din("x", [NTOK, D], F32)
    meta_d = din("meta_tokens", [NMETA, D], F32)
    normw_d = din("norm_w", [DEPTH, D], F32)
    win_d = din("w_in", [DEPTH, D, INC], F32)
    wout_d = din("w_out", [DEPTH, 2048, D], F32)
    poolw_d = din("pool_w", [DEPTH, 4, 128, 128], F32)
    pools_d = din("pool_scale", [DEPTH, 512], F32)
    fw_d = din("fourier_w", [DEPTH, 512, 512], F32)
    convw_d = din("conv_w", [DEPTH, 4, 1024], F32)
    convb_d = din("conv_b", [DEPTH, 1024], F32)
    dtb_d = din("dt_bias", [DEPTH, 2, 8], F32)
    alog_d = din("a_log", [DEPTH, 2, 8], F32)
    dskip_d = din("d_skip", [DEPTH, 8], F32)
    ssdnw_d = din("ssd_norm_w", [DEPTH, 512], F32)
    qnw_d = din("q_norm_w", [DEPTH, 64], F32)
    knw_d = din("k_norm_w", [DEPTH, 64], F32)
    c_ident_d = din("c_ident", [128, 128], BF16)
    c_rope_d = din("c_rope", [128, 2, NCH, 64], F32)
    c_ssdf_d = din("c_ssd_f32", [128, 3, 128], F32)
    c_ssdb_d = din("c_ssd_bf", [128, 6, 128], BF16)
    c_dftc_d = din("c_dft_c", [128, 2, 4, 512], BF16)
    c_tab_d = din("c_tab", [2, NT, NPC, 128, PCC * TS], BF16)
    c_pool_d = din("c_pool", [128, 4, 2, 8], F32)
    c_vmask_d = din("c_vmask", [128, 64], BF16)
    out_d = nc.dram_tensor("out", [NTOK, D], F32, kind="ExternalOutput")
    skind = "ExternalOutput" if dbg else "Internal"

    def dscr(name, shape, dt):
        return nc.dram_tensor(name, list(shape), dt, kind=skind)

    hbuf_d = dscr("hbuf", [TP, D], F32)
    fm_d = dscr("fm", [3584, TW], BF16)
    qT_d = dscr("qT", [512, TP], BF16)
    kT_d = dscr("kT", [128, TP], BF16)
    vtm_d = dscr("vtm", [TP, 128], BF16)
    sz_d = dscr("sz", [TP, 512], BF16)
    dtr_d = dscr("dtr", [TP, 16], F32)
    yf_d = dscr("yf", [TP, 512], F32)
    mixT_d = dscr("mixT", [2048, TP], BF16)

    def DA(h, off, dims):
        return bass.AP(h, off, [list(d) for d in dims])

    def mm(out, lhsT, rhs, start, stop, r, w):
        return S.op("tensor", lambda e: e.matmul(out, lhsT=lhsT, rhs=rhs, start=start, stop=stop), r, w)

    def act(out, in_, func, r, w, bias=None, scale=None, accum_out=None, eng="scalar"):
        kw = {}
        if bias is not None:
            kw["bias"] = bias
        if scale is not None:
            kw["scale"] = scale
        if accum_out is not None:
            kw["accum_out"] = accum_out
        return S.op("scalar", lambda e: e.activation(out=out, in_=in_, func=func, **kw), r, w)

    def tt(eng, out, in0, in1, op, r, w):
        return S.op(eng, lambda e: e.tensor_tensor(out=out, in0=in0, in1=in1, op=op), r, w)

    def tsc(eng, out, in0, s1, s2, op0, op1, r, w):
        if op1 is None:
            return S.op(eng, lambda e: e.tensor_scalar(out=out, in0=in0, scalar1=s1, scalar2=None, op0=op0), r, w)
        return S.op(eng, lambda e: e.tensor_scalar(out=out, in0=in0, scalar1=s1, scalar2=s2, op0=op0, op1=op1), r, w)

    def stt(out, in0, scalar, in1, op0, op1, r, w):
        return S.op("vector", lambda e: e.scalar_tensor_tensor(out=out, in0=in0, scalar=scalar, in1=in1, op0=op0, op1=op1), r, w)

    def cp(eng, out, in_, r, w):
        if eng == "scalar":
            return S.op("scalar", lambda e: e.activation(out=out, in_=in_, func=AF.Copy), r, w)
        return S.op(eng, lambda e: e.tensor_copy(out=out, in_=in_), r, w)

    def recip(out, in_, r, w):
        return S.op("vector", lambda e: e.reciprocal(out=out, in_=in_), r, w)

    def memset(eng, ap, val, w):
        return S.op(eng, lambda e: e.memset(ap, val), (), w)

    def dma(eng, out, in_, r, w, nobar=False, slow=False):
        if slow:
            return S.dma(eng, lambda e: e.dma_start(out=out, in_=in_, allow_slow_non_contiguous=True), r, w, nobar=nobar)
        return S.dma(eng, lambda e: e.dma_start(out=out, in_=in_), r, w, nobar=nobar)

    final_ops = []

    with contextlib.ExitStack() as top:
        def alloc(st, name, shape, dt):
            uid[0] += 1
            return st.enter_context(nc.sbuf_tensor("%s_%d" % (name, uid[0]), list(shape), dt))

        def palloc(st, name, shape, dt):
            uid[0] += 1
            return st.enter_context(nc.psum_tensor("%s_%d" % (name, uid[0]), list(shape), dt))

        ident = alloc(top, "ident", [128, 128], BF16)
        zero = alloc(top, "zero", [128, 16], BF16)
        dma("sync", ident[:], c_ident_d.ap(), (), ["ident"])
        memset("gpsimd", zero[:], 0.0, ["zero"])
        for r0, nj in ((0, 4), (2048, 8)):
            for c0 in (0, 8 + TP):
                dma("sync", DA(fm_d, r0 * TW + c0, [[TW, 128], [128 * TW, nj], [1, 8]]),
                    zero[:, 0:8].unsqueeze(1).broadcast_to([128, nj, 8]), ["zero"], ["fm_margin"])
        mhalf = alloc(top, "mhalf", [128, 16], F32)
        memset("gpsimd", mhalf[:], -0.5, ["mhalf"])
        zf = alloc(top, "zf", [128, D], F32)
        memset("gpsimd", zf[:], 0.0, ["zf"])
        dma("sync", hbuf_d.ap()[0:NPAD, :], zf[0:NPAD, :], ["zf"], ["hbuf_pad"])
        S.barrier()

        def identr(o, i_, r, w):
            return S.op("tensor", lambda e: e.transpose(o, i_, ident[:]), list(r) + ["ident"], w)

        W_PIECES = ((0, 1024, 0), (1024, 1024, 1024), (2048, 1024, 2048), (4368, 512, 3072),
                    (3072, 512, 3584), (3600, 512, 4096), (4112, 256, 4608), (3584, 16, 4864))

        def load_w(st, l):
            W = alloc(st, "W", [128, 8, INC], BF16)
            for (s0, n, d0) in W_PIECES:
                dma("gpsimd", W[:, :, d0:d0 + n],
                    DA(win_d, l * D * INC + s0, [[INC, 128], [128 * INC, 8], [1, n]]), (), [("W", d0)], nobar=True)
            return W

        def phase_p0(l, W):
            with contextlib.ExitStack() as st:
                pieces = W_PIECES
                wkeys = [("W", p[2]) for p in pieces]
                hc = [alloc(st, "hc", [128, D], F32) for _ in range(3)]
                junk = alloc(st, "junk", [128, D], BF16)
                nwb = alloc(st, "nwb", [128, D], F32)
                hnb = [alloc(st, "hnb", [128, D], BF16) for _ in range(3)]
                hnT = [alloc(st, "hnT", [128, 8, TS], BF16) for _ in range(2)]
                stt_t = [alloc(st, "stat", [128, 48], F32) for _ in range(2)]
                fst = [alloc(st, "fst", [128, 4, TS], BF16) for _ in range(3)]
                zst = [alloc(st, "zst", [128, 3, 512], BF16) for _ in range(2)]
                qTst = [alloc(st, "qTst", [128, 4, TS], BF16) for _ in range(2)]
                kTst = [alloc(st, "kTst", [128, TS], BF16) for _ in range(2)]
                vst = [alloc(st, "vst", [128, 3, 128], BF16) for _ in range(2)]
                dtst = [alloc(st, "dtst", [128, 3, 16], F32) for _ in range(2)]
                sq = alloc(st, "sq", [128, 640], F32)
                qn = alloc(st, "qn", [128, 640], F32)
                qw = alloc(st, "qw", [128, 640], F32)
                m1 = alloc(st, "m1", [128, 640], F32)
                m2 = alloc(st, "m2", [128, 640], F32)
                qr = [alloc(st, "qr", [128, 640], BF16) for _ in range(2)]
                wqk = alloc(st, "wqk", [128, 640], F32)
                rope = alloc(st, "rope", [128, 2, NCH, 64], F32)
                pT = palloc(st, "pT", [128, 1024], BF16)
                pf = [palloc(st, "pf", [128, 512], F32) for _ in range(2)]
                pz = palloc(st, "pz", [128, 512], F32)
                pq = palloc(st, "pq", [128, 512], F32)
                pkv = palloc(st, "pkv", [128, 512], F32)
                ptq = palloc(st, "ptq", [128, 1024], BF16)

                dma("sync", rope[:], c_rope_d.ap(), (), ["rope"])
                dma("sync", nwb[:], DA(normw_d, l * D, [[0, 128], [1, D]]), (), ["nwb"])
                dma("sync", wqk[:, 0:512].rearrange("p (h d) -> p h d", h=8),
                    DA(qnw_d, l * 64, [[0, 128], [0, 8], [1, 64]]), (), ["wqk"])
                dma("sync", wqk[:, 512:640].rearrange("p (h d) -> p h d", h=2),
                    DA(knw_d, l * 64, [[0, 128], [0, 2], [1, 64]]), (), ["wqk"])
                tsc("gpsimd", wqk[:, 0:512], wqk[:, 0:512], 0.125, None, ALU.mult, None, ["wqk"], ["wqk"])

                nfm = [0]

                def norm_load(c):
                    hcb = hc[c % 3]
                    khc = "hc%d" % (c % 3)
                    if l == 0 and c == 0:
                        memset("gpsimd", hcb[:], 0.0, [khc])
                        dma("sync", hcb[NPAD:128, :], meta_d.ap(), (), [khc])
                    elif l == 0:
                        dma("sync", hcb[:], x_d.ap()[(c - 1) * 128:c * 128, :], (), [khc])
                    else:
                        dma("sync", hcb[:], hbuf_d.ap()[c * 128:(c + 1) * 128, :], ["hbuf", "hbuf_pad"], [khc])

                def norm_front(c):
                    hcb = hc[c % 3]
                    khc = "hc%d" % (c % 3)
                    sta = stt_t[c % 2]
                    ks = "stat%d" % (c % 2)
                    act(junk[:], hcb[:], AF.Square, [khc], ["junk", (ks, 0)], accum_out=sta[:, 0:1])
                    tsc("vector", sta[:, 1:2], sta[:, 0:1], 1.0 / D, EPS, ALU.mult, ALU.add, [(ks, 0)], [(ks, 1)])
                    tt("gpsimd", sta[:, 3:4], sta[:, 1:2], mhalf[:, 0:1], ALU.pow, [(ks, 1), "mhalf"], [(ks, 3)])
                    stt(hnb[c % 3][:], hcb[:], sta[:, 3:4], nwb[:], ALU.mult, ALU.mult, [khc, (ks, 3), "nwb"], ["hnb%d" % (c % 3)])

                def norm_back(c):
                    i_, cl = c // 3, c % 3
                    hT = hnT[i_ % 2]
                    kh = "hnT%d" % (i_ % 2)
                    hb = hnb[c % 3]
                    khb = "hnb%d" % (c % 3)
                    for j in range(8):
                        identr(pT[:, j * 128:(j + 1) * 128], hb[:, j * 128:(j + 1) * 128], [khb], ["pT"])
                    cp("scalar", hT[:, :, cl * 128:(cl + 1) * 128], pT[:].rearrange("p (j t) -> p j t", j=8),
                       ["pT"], [(kh, cl)])

                for c in range(3):
                    norm_load(c)
                for c in range(3):
                    norm_front(c)
                for c in range(3):
                    norm_back(c)
                for i in range(NT):
                    t0 = i * TS
                    hT = hnT[i % 2]
                    kh = "hnT%d" % (i % 2)
                    hkeys = [(kh, 0), (kh, 1), (kh, 2)]
                    if i + 1 < NT:
                        for cl in range(3):
                            norm_load(3 * (i + 1) + cl)
                    def fm_seg(seg, i=i, t0=t0, hT=hT, hkeys=hkeys):
                        fs = fst[seg % 3]
                        kfs = "fst%d" % (seg % 3)
                        gate = seg in (1, 3, 6)
                        for j in range(4):
                            p = pf[nfm[0] % 2]
                            kp = "pf%d" % (nfm[0] % 2)
                            col0 = seg * 512 + j * 128
                            for dk in range(8):
                                mm(p[:, 0:TS], W[:, dk, col0:col0 + 128], hT[:, dk, :], dk == 0, dk == 7,
                                   wkeys + hkeys, [kp])
                            if gate:
                                act(fs[:, j, :], p[:, 0:TS], AF.Silu, [kp], [kfs])
                            else:
                                cp("scalar", fs[:, j, :], p[:, 0:TS], [kp], [kfs])
                            nfm[0] += 1
                        dma("sync", DA(fm_d, seg * 512 * TW + 8 + t0, [[TW, 128], [128 * TW, 4], [1, TS]]), fs[:],
                            [kfs, "fm_margin"], [("fm", seg)])
                        if seg in (0, 2, 4) and i + 1 < NT:
                            norm_front(3 * (i + 1) + seg // 2)
                    zs = zst[i % 2]
                    kzs = "zst%d" % (i % 2)
                    qTs = qTst[i % 2]
                    kqs = "qTst%d" % (i % 2)
                    kTs = kTst[i % 2]
                    kks = "kTst%d" % (i % 2)
                    vs = vst[i % 2]
                    kvs = "vst%d" % (i % 2)
                    dts = dtst[i % 2]
                    kds = "dtst%d" % (i % 2)

                    def qtrans(cl, i=i, qTs=qTs, kqs=kqs, kTs=kTs, kks=kks):
                        c = 3 * i + cl
                        lt = slice(cl * 128, (cl + 1) * 128)
                        qrb = qr[c % 2]
                        kqr = "qr%d" % (c % 2)
                        for j in range(5):
                            identr(ptq[:, j * 128:(j + 1) * 128], qrb[:, j * 128:(j + 1) * 128],
                                   [(kqr, 0), (kqr, 1)], ["ptq"])
                        cp("scalar", qTs[:, :, lt], ptq[:, 0:512].rearrange("p (j t) -> p j t", j=4), ["ptq"], [kqs])
                        cp("vector", kTs[:, lt], ptq[:, 512:640], ["ptq"], [kks])

                    for cl in range(3):
                        c = 3 * i + cl
                        lt = slice(cl * 128, (cl + 1) * 128)
                        if cl >= 1:
                            fm_seg(2 * cl - 2)
                            fm_seg(2 * cl - 1)
                        for dk in range(8):
                            mm(pz[:, :], hT[:, dk, lt], W[:, dk, 3584:4096], dk == 0, dk == 7, wkeys + hkeys, ["pz"])
                        act(zs[:, cl, :], pz[:, :], AF.Silu, ["pz"], [kzs])
                        for dk in range(8):
                            mm(pq[:, :], hT[:, dk, lt], W[:, dk, 4096:4608], dk == 0, dk == 7, wkeys + hkeys, ["pq"])
                        for dk in range(8):
                            mm(pkv[:, 0:272], hT[:, dk, lt], W[:, dk, 4608:4880], dk == 0, dk == 7, wkeys + hkeys, ["pkv"])
                        if cl >= 1:
                            qtrans(cl - 1)
                        cp("vector", vs[:, cl, :], pkv[:, 128:256], ["pkv"], [kvs])
                        cp("vector", dts[:, cl, :], pkv[:, 256:272], ["pkv"], [kds])
                        sta = stt_t[c % 2]
                        ks = "stat%d" % (c % 2)
                        act(sq[:, 0:512], pq[:, :], AF.Square, ["pq"], [("sq", 0)])
                        act(sq[:, 512:640], pkv[:, 0:128], AF.Square, ["pkv"], [("sq", 1)])
                        S.op("vector", lambda e, o=sta[:, 8:18], a=sq[:].rearrange("p (h d) -> p h d", h=10):
                             e.tensor_reduce(out=o, in_=a, axis=AX.X, op=ALU.add), [("sq", 0), ("sq", 1)], [(ks, 8)])
                        tsc("vector", sta[:, 18:28], sta[:, 8:18], 1.0 / 64, EPS, ALU.mult, ALU.add, [(ks, 8)], [(ks, 18)])
                        tt("gpsimd", sta[:, 38:48], sta[:, 18:28], mhalf[:, 0:10], ALU.pow, [(ks, 18), "mhalf"], [(ks, 38)])
                        tt("vector", qn[:, 0:512].rearrange("p (h d) -> p h d", h=8),
                           pq[:, :].rearrange("p (h d) -> p h d", h=8),
                           sta[:, 38:46].unsqueeze(2).broadcast_to([128, 8, 64]), ALU.mult, ["pq", (ks, 38)], [("qn", 0)])
                        tt("vector", qn[:, 512:640].rearrange("p (h d) -> p h d", h=2),
                           pkv[:, 0:128].rearrange("p (h d) -> p h d", h=2),
                           sta[:, 46:48].unsqueeze(2).broadcast_to([128, 2, 64]), ALU.mult, ["pkv", (ks, 38)], [("qn", 1)])
                        tt("gpsimd", qw[:], qn[:], wqk[:], ALU.mult, [("qn", 0), ("qn", 1), "wqk"], ["qw"])
                        qw3 = qw[:].rearrange("p (h d) -> p h d", h=10)
                        tt("gpsimd", m1[:].rearrange("p (h d) -> p h d", h=10), qw3,
                           rope[:, 0, c, :].unsqueeze(1).broadcast_to([128, 10, 64]), ALU.mult, ["qw", "rope"], ["m1"])
                        tt("gpsimd", m2[:].rearrange("p (h d) -> p h d", h=10), qw3,
                           rope[:, 1, c, :].unsqueeze(1).broadcast_to([128, 10, 64]), ALU.mult, ["qw", "rope"], ["m2"])
                        qrb = qr[c % 2]
                        kqr = "qr%d" % (c % 2)

                        def pv(tile_, off):
                            return bass.AP(tile_, off, [[640, 128], [2, 320]])
                        tt("vector", pv(qrb, 0), pv(m1, 0), pv(m2, 1), ALU.subtract, ["m1", "m2"], [(kqr, 0)])
                        tt("gpsimd", pv(qrb, 1), pv(m2, 0), pv(m1, 1), ALU.add, ["m1", "m2"], [(kqr, 1)])
                    for seg in (4, 5, 6):
                        fm_seg(seg)
                    if i + 1 < NT:
                        for cl in range(3):
                            norm_back(3 * (i + 1) + cl)
                    qtrans(2)
                    dma("sync", DA(sz_d, t0 * 512, [[512, 128], [128 * 512, 3], [1, 512]]), zs[:], [kzs], ["sz"])
                    dma("sync", DA(qT_d, t0, [[TP, 128], [128 * TP, 4], [1, TS]]), qTs[:], [kqs], ["qT"])
                    dma("sync", DA(kT_d, t0, [[TP, 128], [1, TS]]), kTs[:], [kks], ["kT"])
                    dma("sync", DA(vtm_d, t0 * 128, [[128, 128], [128 * 128, 3], [1, 128]]), vs[:], [kvs], ["vtm"])
                    dma("sync", DA(dtr_d, t0 * 16, [[16, 128], [128 * 16, 3], [1, 16]]), dts[:], [kds], ["dtr"])
            S.barrier()

        def phase_p12(l):
            with contextlib.ExitStack() as st:
                pw = alloc(st, "pw", [128, 4, 128], BF16)
                psc = alloc(st, "psc", [128, 4], F32)
                cinv = alloc(st, "cinv", [128, 4, 2, 8], F32)
                ub = [alloc(st, "ub", [128, 4, 400], BF16) for _ in range(2)]
                sgp = [alloc(st, "sg", [128, 4, TS], BF16) for _ in range(2)]
                T2 = alloc(st, "T2", [128, 4, 400], F32)
                T4 = alloc(st, "T4", [128, 4, 400], F32)
                T8 = alloc(st, "T8", [128, 4, 400], F32)
                T16 = alloc(st, "T16", [128, 4, 400], F32)
                tmp = alloc(st, "ptmp", [128, 4, 8], F32)
                dT = [alloc(st, "dT", [128, 4, TS], BF16) for _ in range(2)]
                ostp = [alloc(st, "ost", [128, 4, TS], BF16) for _ in range(2)]
                Ap = alloc(st, "Ap", [128, NCH, 512], BF16)
                Bp = alloc(st, "Bp", [128, NCH, 512], BF16)
                ring = [alloc(st, "ring", [128, PCC, TS], BF16) for _ in range(4)]
                cs = alloc(st, "dftc", [128, 2, 4, 512], BF16)
                wf = alloc(st, "wf", [128, 4, 512], BF16)
                cw = alloc(st, "cw", [128, 2, 4, 512], BF16)
                uT = [alloc(st, "uT", [128, 4, TS], BF16) for _ in range(2)]
                sg = [alloc(st, "sgf", [128, 4, TS], BF16) for _ in range(2)]
                ost = [alloc(st, "ostf", [128, 4, TS], BF16) for _ in range(2)]
                acc = [palloc(st, "acc", [128, 512], F32) for _ in range(4)]
                py = [palloc(st, "py", [128, 512], F32) for _ in range(4)]
                banks = [(acc[k], ("acc", 0, k)) for k in range(4)] + [(py[g], ("py", g)) for g in range(4)]

                dma("gpsimd", pw[:], DA(poolw_d, l * 4 * 128 * 128, [[128, 128], [128 * 128, 4], [1, 128]]), (), ["pw"])
                dma("sync", psc[:].unsqueeze(2), DA(pools_d, l * 512, [[1, 128], [128, 4], [1, 1]]), (), ["psc"], slow=True)
                dma("sync", cinv[:], c_pool_d.ap(), (), ["cinv"])
                dma("sync", cs[:], c_dftc_d.ap(), (), ["dftc"])
                dma("gpsimd", wf[:], DA(fw_d, l * 512 * 512, [[512, 128], [128 * 512, 4], [1, 512]]), (), ["wf"])
                Ts = (T2, T4, T8, T16)

                def p1_front(i):
                    t0 = i * TS
                    u = ub[i % 2]
                    ku = "ub%d" % (i % 2)
                    d_ = dT[i % 2]
                    kd = "dT%d" % (i % 2)
                    dma("sync", u[:], DA(fm_d, t0, [[TW, 128], [128 * TW, 4], [1, 400]]), [("fm", 0), "fm_margin"], [ku])
                    dma("sync", sgp[i % 2][:], DA(fm_d, 512 * TW + 8 + t0, [[TW, 128], [128 * TW, 4], [1, TS]]), [("fm", 1)],
                        ["sg%d" % (i % 2)])
                    tt("vector", T2[:, :, 0:399], u[:, :, 0:399], u[:, :, 1:400], ALU.add, [ku], ["T2"])
                    tt("gpsimd", T4[:, 1:4, 0:397], T2[:, 1:4, 0:397], T2[:, 1:4, 2:399], ALU.add, ["T2"], ["T4"])
                    tt("vector", T8[:, 2:4, 0:393], T4[:, 2:4, 0:393], T4[:, 2:4, 4:397], ALU.add, ["T4"], ["T8"])
                    tt("gpsimd", T16[:, 3, 0:385], T8[:, 3, 0:385], T8[:, 3, 8:393], ALU.add, ["T8"], ["T16"])
                    for g, w_ in enumerate((2, 4, 8, 16)):
                        half = w_ // 2
                        Pg = Ts[g]
                        kP = ("T2", "T4", "T8", "T16")[g]
                        o8 = 8 - half
                        stt(d_[:, g, :], Pg[:, g, o8:o8 + TS], 1.0 / w_, u[:, g, 8:8 + TS], ALU.mult, ALU.subtract,
                            [kP, ku], [(kd, g)])
                        for side, tile_i, m0 in ((0, 0, NPAD), (1, NT - 1, TS - 8)):
                            if i != tile_i:
                                continue
                            tt("vector", tmp[:, g, :], Pg[:, g, o8 + m0:o8 + m0 + 8], cinv[:, g, side, :], ALU.mult,
                               [kP, "cinv"], [("ptmp", g)])
                            tt("vector", d_[:, g, m0:m0 + 8], tmp[:, g, :], u[:, g, 8 + m0:8 + m0 + 8], ALU.subtract,
                               [("ptmp", g), ku], [(kd, g)])

                def p1_back(i):
                    t0 = i * TS
                    d_ = dT[i % 2]
                    kd = "dT%d" % (i % 2)
                    s_ = sgp[i % 2]
                    ksg = "sg%d" % (i % 2)
                    o_ = ostp[i % 2]
                    ko = "ost%d" % (i % 2)
                    for g in range(4):
                        mm(py[g][:, 0:TS], pw[:, g, :], d_[:, g, :], True, True, ["pw", (kd, g)], [("py", g)])
                    for g in range(4):
                        stt(o_[:, g, :], py[g][:, 0:TS], psc[:, g:g + 1], s_[:, g, :], ALU.mult, ALU.mult,
                            [("py", g), "psc", ksg], [(ko, g)])
                    dma("sync", DA(mixT_d, t0, [[TP, 128], [128 * TP, 4], [1, TS]]), o_[:],
                        [(ko, g) for g in range(4)], [("mixT", 0)])

                n = 0
                for tb in range(2):
                    for ci in range(4):
                        p, kp = banks[n % 8]
                        for mk in range(4):
                            mm(p[:, :], cs[:, tb, mk, ci * 128:(ci + 1) * 128], wf[:, mk, :], mk == 0, mk == 3,
                               ["dftc", "wf"], [kp])
                        cp("scalar" if n % 2 else "vector", cw[:, tb, ci, :], p[:, :], [kp], [("cw", tb, ci)])
                        n += 1
                cwk = [("cw", tb, ci) for tb in range(2) for ci in range(4)]
                for i in range(NT):
                    t0 = i * TS
                    u = uT[i % 2]
                    ku = "uT%d" % (i % 2)
                    dma("sync", u[:], DA(fm_d, 1024 * TW + 8 + t0, [[TW, 128], [128 * TW, 4], [1, TS]]), [("fm", 2)], [ku])
                    for cl in range(3):
                        c = 3 * i + cl
                        for tb, dst in ((0, Ap), (1, Bp)):
                            p, kp = banks[(tb * 3 + cl) % 8]
                            for ck in range(4):
                                mm(p[:, :], u[:, ck, cl * 128:(cl + 1) * 128], cw[:, tb, ck, :], ck == 0, ck == 3,
                                   [ku] + cwk, [kp])
                            cp("scalar" if tb else "vector", dst[:, c, :], p[:, :], [kp], [("ApBp", c)])
                apk = [("ApBp", c) for c in range(NCH)]
                npiece = 0
                for kt in range(NT):
                    t0 = kt * TS
                    s_ = sg[kt % 2]
                    ksg = "sgf%d" % (kt % 2)
                    o_ = ost[kt % 2]
                    ko = "ostf%d" % (kt % 2)
                    dma("sync", s_[:], DA(fm_d, 1536 * TW + 8 + t0, [[TW, 128], [128 * TW, 4], [1, TS]]), [("fm", 3)], [ksg])
                    p1_front(kt)
                    for pc in range(NPC):
                        for tb in range(2):
                            rb = ring[npiece % 4]
                            kr = "ring%d" % (npiece % 4)
                            npiece += 1
                            off = (((tb * NT + kt) * NPC + pc) * 128) * (PCC * TS)
                            dma("sync", rb[:], DA(c_tab_d, off, [[PCC * TS, 128], [TS, PCC], [1, TS]]), (), [kr])
                            src = Bp if tb else Ap
                            for tci in range(PCC):
                                tc_ = pc * PCC + tci
                                first = (pc == 0 and tb == 0 and tci == 0)
                                last = (pc == NPC - 1 and tb == 1 and tci == PCC - 1)
                                for dc in range(4):
                                    mm(acc[dc][:, 0:TS], src[:, tc_, dc * 128:(dc + 1) * 128], rb[:, tci, :], first, last,
                                       apk + [kr], [("acc", 0, dc)])
                    for dc in range(4):
                        tt("vector", o_[:, dc, :], acc[dc][:, 0:TS], s_[:, dc, :], ALU.mult,
                           [("acc", 0, dc), ksg], [(ko, dc)])
                    dma("sync", DA(mixT_d, 512 * TP + t0, [[TP, 128], [128 * TP, 4], [1, TS]]), o_[:],
                        [(ko, dc) for dc in range(4)], [("mixT", 1)])
                    p1_back(kt)
            S.barrier()

        def phase_p3(l):
            with contextlib.ExitStack() as st:
                BT = alloc(st, "BT", [128, 2, TP], BF16)
                CT = alloc(st, "CT", [128, 2, TP], BF16)
                xtm = alloc(st, "xtm", [128, NCH, 512], BF16)
                btm = alloc(st, "btm", [128, NCH, 256], BF16)
                dtr = alloc(st, "dtr", [128, NCH, 16], F32)
                dtt = alloc(st, "dtt", [128, NCH, 16], F32)
                da = alloc(st, "da", [128, NCH, 16], F32)
                tA = alloc(st, "tA", [128, NCH, 16], F32)
                tB = alloc(st, "tB", [128, NCH, 16], F32)
                dtb = alloc(st, "dtb", [128, 16], F32)
                alg = alloc(st, "alg", [128, 16], F32)
                dsk = alloc(st, "dsk", [128, 8], F32)
                snw = alloc(st, "snw", [128, 512], F32)
                cwt = alloc(st, "cwt", [128, 8, 4], F32)
                cbt = alloc(st, "cbt", [128, 8], F32)
                dg = alloc(st, "dg", [128, 8, 4, 128], BF16)
                cf = alloc(st, "cf", [128, 3, 128], F32)
                cb = alloc(st, "cb", [128, 6, 128], BF16)
                xin = [alloc(st, "xin", [128, 8, TS + 3], BF16) for _ in range(2)]
                xTt = [alloc(st, "xTt", [128, 4, TS], BF16) for _ in range(2)]
                H = alloc(st, "H", [128, 512], F32)
                Hb = alloc(st, "Hb", [128, 512], BF16)
                Ht = alloc(st, "Ht", [128, 512], F32)
                cst = alloc(st, "cst", [128, 24], F32)
                E = alloc(st, "E", [128, 24], F32)
                R = alloc(st, "R", [128, 8, 128], BF16)
                cbm = alloc(st, "cbm", [128, 2, 128], BF16)
                eseg = alloc(st, "eseg", [128, 8, 128], BF16)
                MT = alloc(st, "MT", [128, 8, 128], BF16)
                xdt = alloc(st, "xdt", [128, 512], BF16)
                xw = alloc(st, "xw", [128, 512], BF16)
                ytmp = alloc(st, "ytmp", [128, 512], F32)
                yc = [alloc(st, "yc", [128, 512], F32) for _ in range(2)]
                yfl = [alloc(st, "yfl", [128, 512], F32) for _ in range(2)]
                szl = [alloc(st, "szl", [128, 512], BF16) for _ in range(2)]
                yg = alloc(st, "yg", [128, 512], F32)
                yjunk = alloc(st, "yjunk", [128, 512], BF16)
                ynb = alloc(st, "ynb", [128, 512], BF16)
                ystat = alloc(st, "ystat", [128, 4], F32)
                yst = [alloc(st, "yst", [128, 4, TS], BF16) for _ in range(2)]
                pcv = [palloc(st, "pcv", [128, 512], F32) for _ in range(2)]
                ptr = palloc(st, "ptr", [128, 1024], BF16)
                pcb = pcv[0]
                pseg = [palloc(st, "pseg", [128, 512], F32) for _ in range(2)]
                pyd = palloc(st, "pyd", [128, 512], F32)
                pyo = palloc(st, "pyo", [128, 512], F32)
                psn = palloc(st, "psn", [128, 512], F32)
                dgD = alloc(st, "dgD", [128, 8, 128], BF16)
                pyd2 = [pcv[1], pyd]
                kpyd = ["pcv1", "pyd"]
                cbm2 = [alloc(st, "cbm2", [128, 2, 128], BF16) for _ in range(2)]
                cst2 = [alloc(st, "cst2", [128, 24], F32) for _ in range(2)]
                E2 = [alloc(st, "E2", [128, 24], F32) for _ in range(2)]
                MT2 = [alloc(st, "MT2", [128, 8, 128], BF16) for _ in range(2)]
                xdt2 = [alloc(st, "xdt2", [128, 512], BF16) for _ in range(2)]
                xw2 = [alloc(st, "xw2", [128, 512], BF16) for _ in range(2)]
                ytmp2 = alloc(st, "ytmp2", [128, 512], F32)

                dma("sync", cf[:], c_ssdf_d.ap(), (), ["cf"])
                dma("sync", cb[:], c_ssdb_d.ap(), (), ["cb"])
                dma("sync", dtb[:], DA(dtb_d, l * 16, [[0, 128], [1, 16]]), (), ["dtb"])
                dma("sync", alg[:], DA(alog_d, l * 16, [[0, 128], [1, 16]]), (), ["alg"])
                dma("sync", dsk[:], DA(dskip_d, l * 8, [[0, 128], [1, 8]]), (), ["dsk"])
                snwT = alloc(st, "snwT", [128, 4], F32)
                dma("sync", snwT[:].unsqueeze(2), DA(ssdnw_d, l * 512, [[1, 128], [128, 4], [1, 1]]), (), ["snwT"], slow=True)
                for j in range(4):
                    dma("sync", cwt[:, :, j:j + 1], DA(convw_d, (l * 4 + j) * 1024, [[1, 128], [128, 8], [1, 1]]), (), ["cwt"], slow=True)
                dma("sync", cbt[:].unsqueeze(2), DA(convb_d, l * 1024, [[1, 128], [128, 8], [1, 1]]), (), ["cbt"], slow=True)
                for ch in range(8):
                    for j in range(4):
                        tsc("gpsimd", dg[:, ch, j, :], ident[:], cwt[:, ch, j:j + 1], None, ALU.mult, None,
                            ["ident", "cwt"], [("dg", ch)])
                for c3 in range(3):
                    dma("sync", dtr[:, c3 * 11:(c3 + 1) * 11, :],
                        DA(dtr_d, c3 * 11 * 128 * 16, [[16, 128], [128 * 16, 11], [1, 16]]), ["dtr"], ["dtrl"])
                bcd = dtb[:].unsqueeze(1).broadcast_to([128, NCH, 16])
                tt("vector", tA[:], dtr[:], bcd, ALU.add, ["dtrl", "dtb"], ["tA"])
                act(tB[:], tA[:], AF.Abs, ["tA"], ["tB"])
                act(tB[:], tB[:], AF.Exp, ["tB"], ["tB"], scale=-1.0)
                tsc("vector", tB[:], tB[:], 1.0, None, ALU.add, None, ["tB"], ["tB"])
                act(tB[:], tB[:], AF.Ln, ["tB"], ["tB"])
                tsc("vector", tA[:], tA[:], 0.0, None, ALU.max, None, ["tA"], ["tA"])
                tt("vector", dtt[:], tA[:], tB[:], ALU.add, ["tA", "tB"], ["dtt"])
                act(alg[:], alg[:], AF.Exp, ["alg"], ["alg"])
                tsc("vector", alg[:], alg[:], -1.0, None, ALU.mult, None, ["alg"], ["alg"])
                tt("vector", da[:], dtt[:], alg[:].unsqueeze(1).broadcast_to([128, NCH, 16]), ALU.mult, ["dtt", "alg"], ["da"])

                for i in range(NT):
                    t0 = i * TS
                    xi = xin[i % 2]
                    kx = "xin%d" % (i % 2)
                    xt_ = xTt[i % 2]
                    kxt = "xTt%d" % (i % 2)
                    dma("sync", xi[:], DA(fm_d, 2048 * TW + 8 + t0 - 2, [[TW, 128], [128 * TW, 8], [1, TS + 3]]),
                        [("fm", 4), ("fm", 5), "fm_margin"], [kx])
                    for ch in range(8):
                        p = pcv[ch % 2]
                        kp = "pcv%d" % (ch % 2)
                        for j in range(4):
                            mm(p[:, 0:TS], dg[:, ch, j, :], xi[:, ch, j:j + TS], j == 0, j == 3, [("dg", ch), kx], [kp])
                        if ch < 4:
                            dst, kdst = xt_[:, ch, :], (kxt, ch)
                        elif ch < 6:
                            dst, kdst = BT[:, ch - 4, t0:t0 + TS], ("BT", i)
                        else:
                            dst, kdst = CT[:, ch - 6, t0:t0 + TS], ("CT", i)
                        act(dst, p[:, 0:TS], AF.Silu, [kp, "cbt"], [kdst], bias=cbt[:, ch:ch + 1])
                    if i == 0:
                        memset("gpsimd", xt_[:, :, 0:NPAD], 0.0, [(kxt, ch) for ch in range(4)])
                    for cl in range(3):
                        c = 3 * i + cl
                        lt = slice(cl * 128, (cl + 1) * 128)
                        for ch in range(4):
                            identr(ptr[:, ch * 128:(ch + 1) * 128], xt_[:, ch, lt], [(kxt, q) for q in range(4)], ["ptr"])
                        for g in range(2):
                            identr(ptr[:, 512 + g * 128:512 + (g + 1) * 128], BT[:, g, t0 + cl * 128:t0 + (cl + 1) * 128],
                                   [("BT", i)], ["ptr"])
                        cp("vector", xtm[:, c, :], ptr[:, 0:512], ["ptr"], [("xtm", c)])
                        cp("scalar", btm[:, c, :], ptr[:, 512:768], ["ptr"], [("btm", c)])

                for h in range(8):
                    tsc("gpsimd", dgD[:, h, :], ident[:], dsk[:, h:h + 1], None, ALU.mult, None, ["ident", "dsk"], ["dgD"])
                ydx = ytmp2
                for d_ in range(2):
                    memset("gpsimd", H[:], 0.0, ["H"])
                    memset("gpsimd", Hb[:], 0.0, ["Hb"])
                    order = list(range(NCH)) if d_ == 0 else list(range(NCH - 1, -1, -1))

                    def stage_a(n_, d_=d_, order=order):
                        c = order[n_]
                        i = c // 3
                        b2 = n_ % 2
                        cols = slice(c * 128, (c + 1) * 128)
                        dac = da[:, c, d_ * 8:(d_ + 1) * 8]
                        for g in range(2):
                            mm(pcb[:, g * 128:(g + 1) * 128], BT[:, g, cols], CT[:, g, cols], True, True,
                               [("BT", i), ("CT", i)], ["pcv0"])
                        mm(pcb[:, 256:264], cf[:, d_, :], dac, True, True, ["cf", "da"], ["pcv0"])
                        mm(pcb[:, 264:272], cf[:, 2, :], dac, True, True, ["cf", "da"], ["pcv0"])
                        tt("vector", cbm2[b2][:], pcb[:, 0:256].rearrange("p (g s) -> p g s", g=2),
                           cb[:, 4 + d_, :].unsqueeze(1).broadcast_to([128, 2, 128]), ALU.mult, ["pcv0", "cb"], ["cbm%d" % b2])
                        cp("vector", cst2[b2][:, 0:16], pcb[:, 256:272], ["pcv0"], [("cst%d" % b2, 0)])
                        tt("vector", cst2[b2][:, 16:24], cst2[b2][:, 8:16], cst2[b2][:, 0:8], ALU.subtract,
                           [("cst%d" % b2, 0)], [("cst%d" % b2, 1)])
                        act(E2[b2][:], cst2[b2][:], AF.Exp, [("cst%d" % b2, 0), ("cst%d" % b2, 1)], ["E%d" % b2])
                        tt("gpsimd", R[:], cb[:, d_, :].unsqueeze(1).broadcast_to([128, 8, 128]),
                           dac.unsqueeze(2).broadcast_to([128, 8, 128]), ALU.mult, ["cb", "da"], ["R"])
                        tt("gpsimd", xdt2[b2][:].rearrange("p (h d) -> p h d", h=8),
                           xtm[:, c, :].rearrange("p (h d) -> p h d", h=8),
                           dtt[:, c, d_ * 8:(d_ + 1) * 8].unsqueeze(2).broadcast_to([128, 8, 64]), ALU.mult,
                           [("xtm", c), "dtt"], ["xdt%d" % b2])
                        for hh in range(2):
                            mm(pseg[hh][:, :], cb[:, 2 + d_, :], R[:, hh * 4:(hh + 1) * 4, :], True, True, ["cb", "R"],
                               [("pseg", hh)])
                            act(eseg[:, hh * 4:(hh + 1) * 4, :], pseg[hh][:, :].rearrange("p (h l) -> p h l", h=4), AF.Exp,
                                [("pseg", hh)], [("eseg", hh)])
                            tt("vector", MT2[b2][:, hh * 4:(hh + 1) * 4, :], eseg[:, hh * 4:(hh + 1) * 4, :],
                               cbm2[b2][:, hh, :].unsqueeze(1).broadcast_to([128, 4, 128]), ALU.mult,
                               [("eseg", hh), "cbm%d" % b2], [("MT%d" % b2, hh)])
                        tt("gpsimd", xw2[b2][:].rearrange("p (h d) -> p h d", h=8),
                           xdt2[b2][:].rearrange("p (h d) -> p h d", h=8),
                           E2[b2][:, 16:24].unsqueeze(2).broadcast_to([128, 8, 64]), ALU.mult, ["xdt%d" % b2, "E%d" % b2],
                           ["xw%d" % b2])
                        for h in range(8):
                            mm(pyd2[b2][:, h * 64:(h + 1) * 64], MT2[b2][:, h, :], xdt2[b2][:, h * 64:(h + 1) * 64], True, d_ == 0,
                               [("MT%d" % b2, h // 4), "xdt%d" % b2], [kpyd[b2]])
                            if d_ == 1:
                                mm(pyd2[b2][:, h * 64:(h + 1) * 64], dgD[:, h, :], xtm[:, c, h * 64:(h + 1) * 64], False, True,
                                   ["dgD", ("xtm", c)], [kpyd[b2]])

                    def stage_b(n_, d_=d_, order=order):
                        c = order[n_]
                        i = c // 3
                        b2 = n_ % 2
                        cols = slice(c * 128, (c + 1) * 128)
                        Eb = E2[b2]
                        kE = "E%d" % b2
                        for g in range(2):
                            mm(pyo[:, g * 256:(g + 1) * 256], CT[:, g, cols], Hb[:, g * 256:(g + 1) * 256], True, True,
                               [("CT", i), "Hb"], ["pyo"])
                        tt("vector", ytmp[:].rearrange("p (h d) -> p h d", h=8), pyo[:, :].rearrange("p (h d) -> p h d", h=8),
                           Eb[:, 0:8].unsqueeze(2).broadcast_to([128, 8, 64]), ALU.mult, ["pyo", kE], ["ytmp"])
                        ycb = yc[b2]
                        kyc = "yc%d" % b2
                        tt("vector", ycb[:], pyd2[b2][:, :], ytmp[:], ALU.add, [kpyd[b2], "ytmp"], [kyc])
                        for g in range(2):
                            mm(psn[:, g * 256:(g + 1) * 256], btm[:, c, g * 128:(g + 1) * 128], xw2[b2][:, g * 256:(g + 1) * 256],
                               True, True, [("btm", c), "xw%d" % b2], ["psn"])
                        tt("gpsimd", Ht[:].rearrange("p (h d) -> p h d", h=8), H[:].rearrange("p (h d) -> p h d", h=8),
                           Eb[:, 8:16].unsqueeze(2).broadcast_to([128, 8, 64]), ALU.mult, ["H", kE], ["Ht"])
                        tt("vector", H[:], psn[:, :], Ht[:], ALU.add, ["psn", "Ht"], ["H"])
                        cp("scalar", Hb[:], H[:], ["H"], ["Hb"])
                        if d_ == 0:
                            dma("sync", yf_d.ap()[c * 128:(c + 1) * 128, :], ycb[:], [kyc], ["yf"])
                        else:
                            yfb = yfl[b2]
                            kyf = "yfl%d" % b2
                            szb = szl[b2]
                            ksz = "szl%d" % b2
                            dma("sync", yfb[:], yf_d.ap()[c * 128:(c + 1) * 128, :], ["yf"], [kyf])
                            dma("sync", szb[:], sz_d.ap()[c * 128:(c + 1) * 128, :], ["sz"], [ksz])
                            tt("vector", yg[:], ycb[:], yfb[:], ALU.add, [kyc, kyf], ["yg"])
                            tt("vector", yg[:], yg[:], szb[:], ALU.mult, ["yg", ksz], ["yg"])
                            act(yjunk[:], yg[:], AF.Square, ["yg"], ["yjunk", ("ystat", 0)], accum_out=ystat[:, 0:1])
                            tsc("vector", ystat[:, 1:2], ystat[:, 0:1], 1.0 / 512, EPS, ALU.mult, ALU.add, [("ystat", 0)], [("ystat", 1)])
                            tt("gpsimd", ystat[:, 3:4], ystat[:, 1:2], mhalf[:, 0:1], ALU.pow, [("ystat", 1), "mhalf"], [("ystat", 3)])
                            act(ynb[:], yg[:], AF.Copy, ["yg", ("ystat", 3)], ["ynb"], scale=ystat[:, 3:4])
                            ti = c // 3
                            cl = c % 3
                            ys = yst[ti % 2]
                            kys = "yst%d" % (ti % 2)
                            for ch in range(4):
                                identr(ptr[:, ch * 128:(ch + 1) * 128], ynb[:, ch * 128:(ch + 1) * 128], ["ynb"], ["ptr"])
                            for j4 in range(4):
                                act(ys[:, j4, cl * 128:(cl + 1) * 128], ptr[:, j4 * 128:(j4 + 1) * 128], AF.Copy,
                                    ["ptr", "snwT"], [(kys, cl)], scale=snwT[:, j4:j4 + 1])
                            if cl == 0:
                                dma("sync", DA(mixT_d, 1024 * TP + ti * TS, [[TP, 128], [128 * TP, 4], [1, TS]]), ys[:],
                                    [(kys, q) for q in range(3)], [("mixT", 2)])

                    stage_a(0)
                    for n_ in range(NCH):
                        if n_ + 1 < NCH:
                            stage_a(n_ + 1)
                        stage_b(n_)
            S.barrier()

        def phase_p4(l):
            TQ = 512
            qtiles = [(0, 128)] + [(128 + k * TQ, TQ) for k in range(8)]
            with contextlib.ExitStack() as st:
                KT2 = alloc(st, "KT2", [128, 2, TP], BF16)
                Vp = alloc(st, "Vp", [128, NCH, 2, 192], BF16)
                vmask = alloc(st, "vmask", [128, 64], BF16)
                QT = [alloc(st, "QT", [128, 4, TQ], BF16) for _ in range(2)]
                sg = [alloc(st, "sga", [128, 4, TQ], BF16) for _ in range(2)]
                P = [alloc(st, "P", [128, 2, TQ], BF16) for _ in range(3)]
                rec = alloc(st, "rec", [128, TQ], F32)
                yt = alloc(st, "yt", [128, TQ], F32)
                ost = [alloc(st, "osta", [128, 4, TQ], BF16) for _ in range(2)]
                Sp = [palloc(st, "Sp", [128, 2, 512], F32) for _ in range(2)]
                Oe = [palloc(st, "Oe", [128, 512], F32) for _ in range(2)]
                Oo = [palloc(st, "Oo", [128, 512], F32) for _ in range(2)]
                dma("sync", vmask[:], c_vmask_d.ap(), (), ["vmask"])
                for g in range(2):
                    for hs in range(2):
                        dma("sync", KT2[hs * 64:(hs + 1) * 64, g, :], DA(kT_d, g * 64 * TP, [[TP, 64], [1, TP]]), ["kT"],
                            [("KT2", g, hs)])
                vkeys = [("Vp", c3, g) for c3 in range(3) for g in range(2)]
                memset("gpsimd", Vp[:], 1.0, vkeys)
                for g in range(2):
                    for o0 in (0, 128):
                        cp("gpsimd", Vp[:, 0, g, o0:o0 + 64], vmask[:], ["vmask"], [("Vp", 0, g)])
                for c3 in range(3):
                    for g in range(2):
                        dma("sync", Vp[:, c3 * 11:(c3 + 1) * 11, g, 64:128],
                            DA(vtm_d, c3 * 11 * 128 * 128 + g * 64, [[128, 128], [128 * 128, 11], [1, 64]]), ["vtm"],
                            [("Vp", c3, g)])
                ktk = [("KT2", g, hs) for g in range(2) for hs in range(2)]
                npair = 0
                nS = [0]

                def p4_load(i):
                    t0, tw = qtiles[i]
                    dma("sync", QT[i % 2][:, :, 0:tw], DA(qT_d, t0, [[TP, 128], [128 * TP, 4], [1, tw]]), ["qT"], ["QT%d" % (i % 2)])
                    dma("sync", sg[i % 2][:, :, 0:tw], DA(fm_d, 3072 * TW + 8 + t0, [[TW, 128], [128 * TW, 4], [1, tw]]),
                        [("fm", 6)], ["sga%d" % (i % 2)])

                for i in range(len(qtiles)):
                    t0, tw = qtiles[i]
                    q_ = QT[i % 2]
                    kq = "QT%d" % (i % 2)
                    s_ = sg[i % 2]
                    ksg = "sga%d" % (i % 2)
                    o_ = ost[i % 2]
                    ko = "osta%d" % (i % 2)
                    if i == 0:
                        p4_load(0)
                    if i + 1 < len(qtiles):
                        p4_load(i + 1)
                    for j in range(4):
                        g = j // 2
                        oe = Oe[npair % 2]
                        oo = Oo[npair % 2]
                        koe = "Oe%d" % (npair % 2)
                        koo = "Oo%d" % (npair % 2)
                        npair += 1
                        pend = {}

                        def qk(kc, g=g, j=j, q_=q_, kq=kq, pend=pend, tw=tw):
                            n_ = nS[0]
                            nS[0] += 1
                            sp = Sp[n_ % 2]
                            ksp = "Sp%d" % (n_ % 2)
                            pb = P[n_ % 3]
                            kpb = "P%d" % (n_ % 3)
                            kcols = slice(kc * 128, (kc + 1) * 128)
                            mm(sp[:, 0, 0:tw], KT2[0:64, g, kcols], q_[0:64, j, 0:tw], True, True, ktk + [kq], [ksp])
                            mm(sp[:, 1, 0:tw], KT2[64:128, g, kcols], q_[64:128, j, 0:tw], True, True, ktk + [kq], [ksp])
                            act(pb[:, :, 0:tw], sp[:, :, 0:tw], AF.Exp, [ksp], [kpb])
                            pend[kc] = (pb, kpb)

                        def pvm(kc, g=g, oe=oe, oo=oo, koe=koe, koo=koo, pend=pend, tw=tw):
                            pb, kpb = pend.pop(kc)
                            mm(oe[:, 0:tw], Vp[:, kc, g, 64:192], pb[:, 0, 0:tw], kc == 0, kc == NCH - 1, vkeys + [kpb], [koe])
                            mm(oo[:, 0:tw], Vp[:, kc, g, 0:128], pb[:, 1, 0:tw], kc == 0, kc == NCH - 1, vkeys + [kpb], [koo])

                        qk(0)
                        for kc in range(NCH):
                            if kc + 1 < NCH:
                                qk(kc + 1)
                            pvm(kc)
                        recip(rec[0:64, 0:tw], oe[64:128, 0:tw], [koe], [("rec", 0)])
                        recip(rec[64:128, 0:tw], oo[0:64, 0:tw], [koo], [("rec", 1)])
                        tt("vector", yt[0:64, 0:tw], oe[0:64, 0:tw], rec[0:64, 0:tw], ALU.mult, [koe, ("rec", 0)], [("yt", 0)])
                        tt("vector", yt[64:128, 0:tw], oo[64:128, 0:tw], rec[64:128, 0:tw], ALU.mult, [koo, ("rec", 1)], [("yt", 1)])
                        tt("gpsimd", o_[:, j, 0:tw], yt[:, 0:tw], s_[:, j, 0:tw], ALU.mult, [("yt", 0), ("yt", 1), ksg], [(ko, j)])
                    dma("sync", DA(mixT_d, 1536 * TP + t0, [[TP, 128], [128 * TP, 4], [1, tw]]), o_[:, :, 0:tw],
                        [(ko, j) for j in range(4)], [("mixT", 3)])
            S.barrier()

        def phase_p5(l, last):
            with contextlib.ExitStack() as st:
                wo = alloc(st, "wo", [128, 16, D], BF16)
                mx = [alloc(st, "mx", [128, 16, TS], BF16) for _ in range(2)]
                hc = [alloc(st, "hc5", [128, D], F32) for _ in range(4)]
                ho = [alloc(st, "ho5", [128, D], F32) for _ in range(2)]
                po = [[palloc(st, "po", [128, 512], F32) for _ in range(2)] for _ in range(2)]
                for hh in range(2):
                    dma("gpsimd", wo[:, :, hh * 512:(hh + 1) * 512],
                        DA(wout_d, l * 2048 * D + hh * 512, [[D, 128], [128 * D, 16], [1, 512]]), (), [("wo", hh)])
                def p5_load(i):
                    for q4 in range(4):
                        dma("sync", mx[i % 2][:, q4 * 4:(q4 + 1) * 4, :],
                            DA(mixT_d, q4 * 512 * TP + i * TS, [[TP, 128], [128 * TP, 4], [1, TS]]), [("mixT", q4)], ["mx%d" % (i % 2)])

                def p5_hload(c):
                    if c >= NCH or (last and c == 0):
                        return
                    hcb = hc[c % 4]
                    khc = "hc5%d" % (c % 4)
                    if l == 0 and c == 0:
                        memset("gpsimd", hcb[:], 0.0, [khc])
                        dma("sync", hcb[NPAD:128, :], meta_d.ap(), (), [khc])
                    elif l == 0:
                        dma("sync", hcb[:], x_d.ap()[(c - 1) * 128:c * 128, :], (), [khc])
                    else:
                        dma("sync", hcb[:], hbuf_d.ap()[c * 128:(c + 1) * 128, :], ["hbuf", "hbuf_pad"], [khc])

                p5_load(0)
                p5_hload(0)
                p5_hload(1)
                for i in range(NT):
                    t0 = i * TS
                    m_ = mx[i % 2]
                    km = "mx%d" % (i % 2)
                    if i + 1 < NT:
                        p5_load(i + 1)
                    for cl in range(3):
                        c = 3 * i + cl
                        p5_hload(c + 2)
                        if last and c == 0:
                            continue
                        hcb = hc[c % 4]
                        khc = "hc5%d" % (c % 4)
                        hob = ho[c % 2]
                        kho = "ho5%d" % (c % 2)
                        for hh in range(2):
                            p = po[c % 2][hh]
                            kp = ("po", c % 2, hh)
                            for ec in range(16):
                                mm(p[:, :], m_[:, ec, cl * 128:(cl + 1) * 128], wo[:, ec, hh * 512:(hh + 1) * 512],
                                   ec == 0, ec == 15, [km, ("wo", hh)], [kp])
                            tt("vector", hob[:, hh * 512:(hh + 1) * 512], p[:, :], hcb[:, hh * 512:(hh + 1) * 512], ALU.add,
                               [kp, khc], [(kho, hh)])
                        if last:
                            o = dma("sync", out_d.ap()[(c - 1) * 128:c * 128, :], hob[:], [(kho, 0), (kho, 1)], ["out"])
                            final_ops.append(o)
                        elif c == 0:
                            dma("sync", hbuf_d.ap()[NPAD:128, :], hob[NPAD:128, :], [(kho, 0), (kho, 1)], ["hbuf"])
                        else:
                            dma("sync", hbuf_d.ap()[c * 128:(c + 1) * 128, :], hob[:], [(kho, 0), (kho, 1)], ["hbuf"])
            S.barrier()

        wst = contextlib.ExitStack()
        Wcur = load_w(wst, 0) if "0" in phases else None
        for l in range(n_layers):
            if "0" in phases:
                phase_p0(l, Wcur)
                wst.close()
            if "1" in phases or "2" in phases:
                phase_p12(l)
            if "3" in phases:
                phase_p3(l)
            if "4" in phases:
                phase_p4(l)
            if "5" in phases:
                if "0" in phases and l + 1 < n_layers:
                    wst = contextlib.ExitStack()
                    Wcur = load_w(wst, l + 1)
                phase_p5(l, l == n_layers - 1)
        final_ops[:] = [o for o in final_ops if not o.skipped]
        S.limit = None
        print("n_ops", len(S.ops))
        if not final_ops:
            zo = dma("sync", out_d.ap()[0:128, :], zf[:], ["zf"], ["out"])
            final_ops.append(zo)
        S.emit(nc, final_ops)
    return nc


_PARAM_NAMES = ("meta_tokens", "norm_w", "w_in", "w_out", "pool_w", "pool_scale", "fourier_w", "conv_w", "conv_b",
                "dt_bias", "a_log", "d_skip", "ssd_norm_w", "q_norm_w", "k_norm_w")


def make_in_maps(inputs, cores):
    consts = host_consts()
    shared = {k: np.ascontiguousarray(np.asarray(inputs[k], dtype=np.float32)) for k in _PARAM_NAMES}
    shared.update(consts)
    x = np.asarray(inputs["x"], dtype=np.float32)
    maps = []
    for b in cores:
        m = dict(shared)
        m["x"] = np.ascontiguousarray(x[b])
        maps.append(m)
    return maps


def kernel(**inputs):
    nc = build()
    in_maps = make_in_maps(inputs, list(range(8)))
    res = run_bass_kernel_spmd(nc, in_maps, core_ids=list(range(8)))
    return np.stack([np.asarray(r["out"], dtype=np.float32) for r in res.results], axis=0)
```

```python
import contextlib
import numpy as np
import ml_dtypes
import concourse.bass as bass
import concourse.mybir as mybir
from concourse.bass_utils import run_bass_kernel_spmd

F32 = mybir.dt.float32
BF16 = mybir.dt.bfloat16
AF = mybir.ActivationFunctionType
ALU = mybir.AluOpType
AX = mybir.AxisListType

D = 1024
NTOK = 4096
NMETA = 16
L = NTOK + NMETA
NPAD = 112
TP = 4224
NCH = 33
NT = 11
TS = 384
TW = TP + 16
DEPTH = 4
INC = 4880
EPS = 1e-6
NPC = 3
PCC = 11

ENGS = ("tensor", "vector", "scalar", "gpsimd", "sync")
N_DMA_SEMS = 28
N_SW_SEMS = 8
PSUM_KEYS = frozenset(["pT", "pf0", "pf1", "pz", "pq", "pkv", "ptq", "py", "acc", "pcv0", "pcv1", "ptr", "pseg", "pyd",
                       "pyo", "psn", "Sp0", "Sp1", "Oe0", "Oe1", "Oo0", "Oo1", "po"])


class Op:
    __slots__ = ("eng", "fn", "deps", "is_dma", "needed", "sem", "val", "nobar", "skipped")

    def __init__(self, eng, fn, deps, is_dma):
        self.eng = eng
        self.fn = fn
        self.deps = deps
        self.is_dma = is_dma
        self.needed = False
        self.sem = None
        self.val = 0
        self.nobar = False
        self.skipped = False


class Sched:
    def __init__(self):
        self.ops = []
        self.last_w = {}
        self.readers = {}
        self.last_eng = {}
        self.dmas = []
        self.limit = None

    def _add(self, eng, fn, reads, writes, is_dma, extra=()):
        ex = tuple(k for k in reads if (k[0] if isinstance(k, tuple) else k) in PSUM_KEYS)
        if ex:
            writes = tuple(writes) + tuple(k for k in ex if k not in writes)
        deps = list(extra)
        for k in reads:
            w = self.last_w.get(k)
            if w is not None:
                deps.append(w)
        for k in writes:
            w = self.last_w.get(k)
            if w is not None:
                deps.append(w)
            deps.extend(self.readers.get(k, ()))
        op = Op(eng, fn, deps, is_dma)
        if self.limit is not None and len(self.ops) >= self.limit:
            op.skipped = True
            return op
        self.ops.append(op)
        for k in writes:
            self.last_w[k] = op
            self.readers[k] = []
        for k in reads:
            if k in writes:
                continue
            lst = self.readers.setdefault(k, [])
            if not is_dma:
                lst[:] = [o for o in lst if o.is_dma or o.eng != eng]
            lst.append(op)
        if is_dma:
            self.dmas.append(op)
        else:
            self.last_eng[eng] = op
        return op

    def op(self, eng, fn, reads=(), writes=()):
        return self._add(eng, fn, tuple(reads), tuple(writes), False)

    def dma(self, eng, fn, reads=(), writes=(), nobar=False):
        o = self._add(eng, fn, tuple(reads), tuple(writes), True)
        o.nobar = nobar
        return o

    def barrier(self):
        deps = [o for o in self.last_eng.values()] + [o for o in self.dmas if not o.nobar]
        self.dmas = [o for o in self.dmas if o.nobar]
        for e in ENGS:
            self._add(e, lambda eng: eng.nop(), (), (), False, extra=deps)

    def emit(self, nc, final_wait_ops=()):
        ops = self.ops
        for o in ops:
            for d in o.deps:
                if d.eng == "tensor" and o.eng == "tensor" and not d.is_dma and not o.is_dma:
                    continue
                d.needed = True
        for o in final_wait_ops:
            o.needed = True
        with contextlib.ExitStack() as st:
            esem = {e: st.enter_context(nc.semaphore("s_" + e)) for e in ENGS}
            dsem = [st.enter_context(nc.semaphore("d_%d" % i)) for i in range(N_DMA_SEMS)]
            ecnt = {e: 0 for e in ENGS}
            dcnt = [0] * N_DMA_SEMS
            dnext = 0
            prev_on_dsem = [None] * N_DMA_SEMS
            swnext = 0
            for o in ops:
                if o.is_dma:
                    if o.eng == "gpsimd":
                        i = N_DMA_SEMS - N_SW_SEMS + swnext
                        swnext = (swnext + 1) % N_SW_SEMS
                    else:
                        i = dnext
                        dnext = (dnext + 1) % (N_DMA_SEMS - N_SW_SEMS)
                    dcnt[i] += 16
                    o.sem = ("d", i)
                    o.val = dcnt[i]
                    if prev_on_dsem[i] is not None:
                        o.deps.append(prev_on_dsem[i])
                    prev_on_dsem[i] = o
                elif o.needed:
                    ecnt[o.eng] += 1
                    o.sem = ("e", o.eng)
                    o.val = ecnt[o.eng]
            per = {e: [o for o in ops if o.eng == e] for e in ENGS}
            final = list(final_wait_ops)
            blk = st.enter_context(nc.Block())

            def semh(s):
                return esem[s[1]] if s[0] == "e" else dsem[s[1]]

            def run(e, eng):
                seen = {}
                for o in per[e]:
                    need = {}
                    for d in o.deps:
                        if d.sem is None:
                            continue
                        if (not d.is_dma) and (not o.is_dma) and d.eng == "tensor" and e == "tensor":
                            continue
                        if need.get(d.sem, 0) < d.val:
                            need[d.sem] = d.val
                    for s, v in need.items():
                        if seen.get(s, 0) < v:
                            eng.wait_ge(semh(s), v)
                            seen[s] = v
                    ins = o.fn(eng)
                    if o.is_dma:
                        ins.then_inc(semh(o.sem), 16)
                    elif o.sem is not None:
                        ins.then_inc(semh(o.sem), 1)
                if e == "sync":
                    for o in final:
                        if seen.get(o.sem, 0) < o.val:
                            eng.wait_ge(semh(o.sem), o.val)
                            seen[o.sem] = o.val

            @blk.tensor
            def _(eng):
                run("tensor", eng)

            @blk.vector
            def _(eng):
                run("vector", eng)

            @blk.scalar
            def _(eng):
                run("scalar", eng)

            @blk.gpsimd
            def _(eng):
                run("gpsimd", eng)

            @blk.sync
            def _(eng):
                run("sync", eng)
        return ecnt, dcnt


_CONSTS = None


def host_consts():
    global _CONSTS
    if _CONSTS is not None:
        return _CONSTS
    bf = ml_dtypes.bfloat16
    c = {}
    c["c_ident"] = np.eye(128, dtype=np.float32).astype(bf)
    tphys = np.arange(TP)
    tlog = tphys - NPAD
    row = np.where(tlog >= NMETA, (tlog - NMETA) // 64, 0).astype(np.float64)
    col = np.where(tlog >= NMETA, (tlog - NMETA) % 64, 0).astype(np.float64)
    freqs = (10000.0 ** (-np.arange(0, 32, 2, dtype=np.float32) / 32.0)).astype(np.float32)
    ang = np.concatenate([row[:, None].astype(np.float32) * freqs[None], col[:, None].astype(np.float32) * freqs[None]],
                         axis=-1).astype(np.float32)
    cos2 = np.repeat(np.cos(ang), 2, axis=-1).astype(np.float32)
    sin2 = np.repeat(np.sin(ang), 2, axis=-1).astype(np.float32)
    rope = np.stack([cos2, sin2], 0).reshape(2, NCH, 128, 64).transpose(2, 0, 1, 3)
    c["c_rope"] = np.ascontiguousarray(rope, dtype=np.float32)
    k = np.arange(128)[:, None]
    l_ = np.arange(128)[None, :]
    trif = (k <= l_).astype(np.float32)
    trib = (k >= l_).astype(np.float32)
    ones = np.ones((128, 128), np.float32)
    c["c_ssd_f32"] = np.ascontiguousarray(np.stack([trif, trib, ones], 1), dtype=np.float32)
    t1f = (k > l_).astype(np.float32)
    t1b = (k < l_).astype(np.float32)
    c["c_ssd_bf"] = np.ascontiguousarray(np.stack([trif, trib, t1f, t1b, trif, trib], 1)).astype(bf)
    m = np.arange(512)
    a = 2.0 * np.pi * ((m[:, None] * m[None, :]) % 512) / 512.0
    cc = np.cos(a).reshape(4, 128, 512).transpose(1, 0, 2)
    sc = np.sin(a).reshape(4, 128, 512).transpose(1, 0, 2)
    c["c_dft_c"] = np.ascontiguousarray(np.stack([cc, sc], 1)).astype(bf)
    scale = 1.0 / np.sqrt(float(L) * 512.0)
    tl = (np.arange(TP) - NPAD).astype(np.int64)
    valid = (tl >= 0)
    prod = (np.maximum(tl, 0)[:, None] * np.maximum(tl, 0)[None, :]) % L
    angp = (2.0 * np.pi / L) * prod.astype(np.float64)
    vm = (valid[:, None] & valid[None, :])
    tabs = []
    for fn_, sgn in ((np.cos, 1.0), (np.sin, -1.0)):
        t_ = (fn_(angp) * (sgn * scale) * vm).astype(np.float32).astype(bf)
        t_ = t_.reshape(NPC, PCC, 128, NT, TS)
        t_ = t_.transpose(3, 0, 2, 1, 4)
        tabs.append(np.ascontiguousarray(t_).reshape(NT, NPC, 128, PCC * TS))
    c["c_tab"] = np.ascontiguousarray(np.stack(tabs, 0))
    pinv = np.zeros((128, 4, 2, 8), np.float32)
    for g, w in enumerate((2, 4, 8, 16)):
        for side in range(2):
            for q in range(8):
                t = q if side == 0 else L - 8 + q
                lo = min(max(t - w // 2, 0), L)
                hi = min(max(t - w // 2 + w, 0), L)
                pinv[:, g, side, q] = 1.0 / float(hi - lo)
    c["c_pool"] = pinv
    vmask = np.ones((128, 64), np.float32)
    vmask[:NPAD] = 0.0
    c["c_vmask"] = vmask.astype(bf)
    _CONSTS = c
    return c


def build(n_layers=DEPTH, dbg=False, phases="012345", limit=None):
    nc = bass.Bass("TRN2", target_bir_lowering=False)
    S = Sched()
    S.limit = limit
    uid = [0]

    def din(name, shape, dt):
        return nc.dram_tensor(name, list(shape), dt, kind="ExternalInput")

    x_d = din("x", [NTOK, D], F32)
    meta_d = din("meta_tokens", [NMETA, D], F32)
    normw_d = din("norm_w", [DEPTH, D], F32)
    win_d = din("w_in", [DEPTH, D, INC], F32)
    wout_d = din("w_out", [DEPTH, 2048, D], F32)
    poolw_d = din("pool_w", [DEPTH, 4, 128, 128], F32)
    pools_d = din("pool_scale", [DEPTH, 512], F32)
    fw_d = din("fourier_w", [DEPTH, 512, 512], F32)
    convw_d = din("conv_w", [DEPTH, 4, 1024], F32)
    convb_d = din("conv_b", [DEPTH, 1024], F32)
    dtb_d = din("dt_bias", [DEPTH, 2, 8], F32)
    alog_d = din("a_log", [DEPTH, 2, 8], F32)
    dskip_d = din("d_skip", [DEPTH, 8], F32)
    ssdnw_d = din("ssd_norm_w", [DEPTH, 512], F32)
    qnw_d = din("q_norm_w", [DEPTH, 64], F32)
    knw_d = din("k_norm_w", [DEPTH, 64], F32)
    c_ident_d = din("c_ident", [128, 128], BF16)
    c_rope_d = din("c_rope", [128, 2, NCH, 64], F32)
    c_ssdf_d = din("c_ssd_f32", [128, 3, 128], F32)
    c_ssdb_d = din("c_ssd_bf", [128, 6, 128], BF16)
    c_dftc_d = din("c_dft_c", [128, 2, 4, 512], BF16)
    c_tab_d = din("c_tab", [2, NT, NPC, 128, PCC * TS], BF16)
    c_pool_d = din("c_pool", [128, 4, 2, 8], F32)
    c_vmask_d = din("c_vmask", [128, 64], BF16)
    out_d = nc.dram_tensor("out", [NTOK, D], F32, kind="ExternalOutput")
    skind = "ExternalOutput" if dbg else "Internal"

    def dscr(name, shape, dt):
        return nc.dram_tensor(name, list(shape), dt, kind=skind)

    hbuf_d = dscr("hbuf", [TP, D], F32)
    fm_d = dscr("fm", [3584, TW], BF16)
    qT_d = dscr("qT", [512, TP], BF16)
    kT_d = dscr("kT", [128, TP], BF16)
    vtm_d = dscr("vtm", [TP, 128], BF16)
    sz_d = dscr("sz", [TP, 512], BF16)
    dtr_d = dscr("dtr", [TP, 16], F32)
    yf_d = dscr("yf", [TP, 512], F32)
    mixT_d = dscr("mixT", [2048, TP], BF16)

    def DA(h, off, dims):
        return bass.AP(h, off, [list(d) for d in dims])

    def mm(out, lhsT, rhs, start, stop, r, w):
        return S.op("tensor", lambda e: e.matmul(out, lhsT=lhsT, rhs=rhs, start=start, stop=stop), r, w)

    def act(out, in_, func, r, w, bias=None, scale=None, accum_out=None, eng="scalar"):
        kw = {}
        if bias is not None:
            kw["bias"] = bias
        if scale is not None:
            kw["scale"] = scale
        if accum_out is not None:
            kw["accum_out"] = accum_out
        return S.op("scalar", lambda e: e.activation(out=out, in_=in_, func=func, **kw), r, w)

    def tt(eng, out, in0, in1, op, r, w):
        return S.op(eng, lambda e: e.tensor_tensor(out=out, in0=in0, in1=in1, op=op), r, w)

    def tsc(eng, out, in0, s1, s2, op0, op1, r, w):
        if op1 is None:
            return S.op(eng, lambda e: e.tensor_scalar(out=out, in0=in0, scalar1=s1, scalar2=None, op0=op0), r, w)
        return S.op(eng, lambda e: e.tensor_scalar(out=out, in0=in0, scalar1=s1, scalar2=s2, op0=op0, op1=op1), r, w)

    def stt(out, in0, scalar, in1, op0, op1, r, w):
        return S.op("vector", lambda e: e.scalar_tensor_tensor(out=out, in0=in0, scalar=scalar, in1=in1, op0=op0, op1=op1), r, w)

    def cp(eng, out, in_, r, w):
        if eng == "scalar":
            return S.op("scalar", lambda e: e.activation(out=out, in_=in_, func=AF.Copy), r, w)
        return S.op(eng, lambda e: e.tensor_copy(out=out, in_=in_), r, w)

    def recip(out, in_, r, w):
        return S.op("vector", lambda e: e.reciprocal(out=out, in_=in_), r, w)

    def memset(eng, ap, val, w):
        return S.op(eng, lambda e: e.memset(ap, val), (), w)

    def dma(eng, out, in_, r, w, nobar=False, slow=False):
        if slow:
            return S.dma(eng, lambda e: e.dma_start(out=out, in_=in_, allow_slow_non_contiguous=True), r, w, nobar=nobar)
        return S.dma(eng, lambda e: e.dma_start(out=out, in_=in_), r, w, nobar=nobar)

    final_ops = []

    with contextlib.ExitStack() as top:
        def alloc(st, name, shape, dt):
            uid[0] += 1
            return st.enter_context(nc.sbuf_tensor("%s_%d" % (name, uid[0]), list(shape), dt))

        def palloc(st, name, shape, dt):
            uid[0] += 1
            return st.enter_context(nc.psum_tensor("%s_%d" % (name, uid[0]), list(shape), dt))

        ident = alloc(top, "ident", [128, 128], BF16)
        zero = alloc(top, "zero", [128, 16], BF16)
        dma("sync", ident[:], c_ident_d.ap(), (), ["ident"])
        memset("gpsimd", zero[:], 0.0, ["zero"])
        for r0, nj in ((0, 4), (2048, 8)):
            for c0 in (0, 8 + TP):
                dma("sync", DA(fm_d, r0 * TW + c0, [[TW, 128], [128 * TW, nj], [1, 8]]),
                    zero[:, 0:8].unsqueeze(1).broadcast_to([128, nj, 8]), ["zero"], ["fm_margin"])
        mhalf = alloc(top, "mhalf", [128, 16], F32)
        memset("gpsimd", mhalf[:], -0.5, ["mhalf"])
        zf = alloc(top, "zf", [128, D], F32)
        memset("gpsimd", zf[:], 0.0, ["zf"])
        dma("sync", hbuf_d.ap()[0:NPAD, :], zf[0:NPAD, :], ["zf"], ["hbuf_pad"])
        S.barrier()

        def identr(o, i_, r, w):
            return S.op("tensor", lambda e: e.transpose(o, i_, ident[:]), list(r) + ["ident"], w)

        W_PIECES = ((0, 1024, 0), (1024, 1024, 1024), (2048, 1024, 2048), (4368, 512, 3072),
                    (3072, 512, 3584), (3600, 512, 4096), (4112, 256, 4608), (3584, 16, 4864))

        def load_w(st, l):
            W = alloc(st, "W", [128, 8, INC], BF16)
            for (s0, n, d0) in W_PIECES:
                dma("gpsimd", W[:, :, d0:d0 + n],
                    DA(win_d, l * D * INC + s0, [[INC, 128], [128 * INC, 8], [1, n]]), (), [("W", d0)], nobar=True)
            return W

        def phase_p0(l, W):
            with contextlib.ExitStack() as st:
                pieces = W_PIECES
                wkeys = [("W", p[2]) for p in pieces]
                hc = [alloc(st, "hc", [128, D], F32) for _ in range(3)]
                junk = alloc(st, "junk", [128, D], BF16)
                nwb = alloc(st, "nwb", [128, D], F32)
                hnb = [alloc(st, "hnb", [128, D], BF16) for _ in range(3)]
                hnT = [alloc(st, "hnT", [128, 8, TS], BF16) for _ in range(2)]
                stt_t = [alloc(st, "stat", [128, 48], F32) for _ in range(2)]
                fst = [alloc(st, "fst", [128, 4, TS], BF16) for _ in range(3)]
                zst = [alloc(st, "zst", [128, 3, 512], BF16) for _ in range(2)]
                qTst = [alloc(st, "qTst", [128, 4, TS], BF16) for _ in range(2)]
                kTst = [alloc(st, "kTst", [128, TS], BF16) for _ in range(2)]
                vst = [alloc(st, "vst", [128, 3, 128], BF16) for _ in range(2)]
                dtst = [alloc(st, "dtst", [128, 3, 16], F32) for _ in range(2)]
                sq = alloc(st, "sq", [128, 640], F32)
                qn = alloc(st, "qn", [128, 640], F32)
                qw = alloc(st, "qw", [128, 640], F32)
                m1 = alloc(st, "m1", [128, 640], F32)
                m2 = alloc(st, "m2", [128, 640], F32)
                qr = [alloc(st, "qr", [128, 640], BF16) for _ in range(2)]
                wqk = alloc(st, "wqk", [128, 640], F32)
                rope = alloc(st, "rope", [128, 2, NCH, 64], F32)
                pT = palloc(st, "pT", [128, 1024], BF16)
                pf = [palloc(st, "pf", [128, 512], F32) for _ in range(2)]
                pz = palloc(st, "pz", [128, 512], F32)
                pq = palloc(st, "pq", [128, 512], F32)
                pkv = palloc(st, "pkv", [128, 512], F32)
                ptq = palloc(st, "ptq", [128, 1024], BF16)

                dma("sync", rope[:], c_rope_d.ap(), (), ["rope"])
                dma("sync", nwb[:], DA(normw_d, l * D, [[0, 128], [1, D]]), (), ["nwb"])
                dma("sync", wqk[:, 0:512].rearrange("p (h d) -> p h d", h=8),
                    DA(qnw_d, l * 64, [[0, 128], [0, 8], [1, 64]]), (), ["wqk"])
                dma("sync", wqk[:, 512:640].rearrange("p (h d) -> p h d", h=2),
                    DA(knw_d, l * 64, [[0, 128], [0, 2], [1, 64]]), (), ["wqk"])
                tsc("gpsimd", wqk[:, 0:512], wqk[:, 0:512], 0.125, None, ALU.mult, None, ["wqk"], ["wqk"])

                nfm = [0]

                def norm_load(c):
                    hcb = hc[c % 3]
                    khc = "hc%d" % (c % 3)
                    if l == 0 and c == 0:
                        memset("gpsimd", hcb[:], 0.0, [khc])
                        dma("sync", hcb[NPAD:128, :], meta_d.ap(), (), [khc])
                    elif l == 0:
                        dma("sync", hcb[:], x_d.ap()[(c - 1) * 128:c * 128, :], (), [khc])
                    else:
                        dma("sync", hcb[:], hbuf_d.ap()[c * 128:(c + 1) * 128, :], ["hbuf", "hbuf_pad"], [khc])

                def norm_front(c):
                    hcb = hc[c % 3]
                    khc = "hc%d" % (c % 3)
                    sta = stt_t[c % 2]
                    ks = "stat%d" % (c % 2)
                    act(junk[:], hcb[:], AF.Square, [khc], ["junk", (ks, 0)], accum_out=sta[:, 0:1])
                    tsc("vector", sta[:, 1:2], sta[:, 0:1], 1.0 / D, EPS, ALU.mult, ALU.add, [(ks, 0)], [(ks, 1)])
                    tt("gpsimd", sta[:, 3:4], sta[:, 1:2], mhalf[:, 0:1], ALU.pow, [(ks, 1), "mhalf"], [(ks, 3)])
                    stt(hnb[c % 3][:], hcb[:], sta[:, 3:4], nwb[:], ALU.mult, ALU.mult, [khc, (ks, 3), "nwb"], ["hnb%d" % (c % 3)])

                def norm_back(c):
                    i_, cl = c // 3, c % 3
                    hT = hnT[i_ % 2]
                    kh = "hnT%d" % (i_ % 2)
                    hb = hnb[c % 3]
                    khb = "hnb%d" % (c % 3)
                    for j in range(8):
                        identr(pT[:, j * 128:(j + 1) * 128], hb[:, j * 128:(j + 1) * 128], [khb], ["pT"])
                    cp("scalar", hT[:, :, cl * 128:(cl + 1) * 128], pT[:].rearrange("p (j t) -> p j t", j=8),
                       ["pT"], [(kh, cl)])

                for c in range(3):
                    norm_load(c)
                for c in range(3):
                    norm_front(c)
                for c in range(3):
                    norm_back(c)
                for i in range(NT):
                    t0 = i * TS
                    hT = hnT[i % 2]
                    kh = "hnT%d" % (i % 2)
                    hkeys = [(kh, 0), (kh, 1), (kh, 2)]
                    if i + 1 < NT:
                        for cl in range(3):
                            norm_load(3 * (i + 1) + cl)
                    def fm_seg(seg, i=i, t0=t0, hT=hT, hkeys=hkeys):
                        fs = fst[seg % 3]
                        kfs = "fst%d" % (seg % 3)
                        gate = seg in (1, 3, 6)
                        for j in range(4):
                            p = pf[nfm[0] % 2]
                            kp = "pf%d" % (nfm[0] % 2)
                            col0 = seg * 512 + j * 128
                            for dk in range(8):
                                mm(p[:, 0:TS], W[:, dk, col0:col0 + 128], hT[:, dk, :], dk == 0, dk == 7,
                                   wkeys + hkeys, [kp])
                            if gate:
                                act(fs[:, j, :], p[:, 0:TS], AF.Silu, [kp], [kfs])
                            else:
                                cp("scalar", fs[:, j, :], p[:, 0:TS], [kp], [kfs])
                            nfm[0] += 1
                        dma("sync", DA(fm_d, seg * 512 * TW + 8 + t0, [[TW, 128], [128 * TW, 4], [1, TS]]), fs[:],
                            [kfs, "fm_margin"], [("fm", seg)])
                        if seg in (0, 2, 4) and i + 1 < NT:
                            norm_front(3 * (i + 1) + seg // 2)
                    zs = zst[i % 2]
                    kzs = "zst%d" % (i % 2)
                    qTs = qTst[i % 2]
                    kqs = "qTst%d" % (i % 2)
                    kTs = kTst[i % 2]
                    kks = "kTst%d" % (i % 2)
                    vs = vst[i % 2]
                    kvs = "vst%d" % (i % 2)
                    dts = dtst[i % 2]
                    kds = "dtst%d" % (i % 2)

                    def qtrans(cl, i=i, qTs=qTs, kqs=kqs, kTs=kTs, kks=kks):
                        c = 3 * i + cl
                        lt = slice(cl * 128, (cl + 1) * 128)
                        qrb = qr[c % 2]
                        kqr = "qr%d" % (c % 2)
                        for j in range(5):
                            identr(ptq[:, j * 128:(j + 1) * 128], qrb[:, j * 128:(j + 1) * 128],
                                   [(kqr, 0), (kqr, 1)], ["ptq"])
                        cp("scalar", qTs[:, :, lt], ptq[:, 0:512].rearrange("p (j t) -> p j t", j=4), ["ptq"], [kqs])
                        cp("vector", kTs[:, lt], ptq[:, 512:640], ["ptq"], [kks])

                    for cl in range(3):
                        c = 3 * i + cl
                        lt = slice(cl * 128, (cl + 1) * 128)
                        if cl >= 1:
                            fm_seg(2 * cl - 2)
                            fm_seg(2 * cl - 1)
                        for dk in range(8):
                            mm(pz[:, :], hT[:, dk, lt], W[:, dk, 3584:4096], dk == 0, dk == 7, wkeys + hkeys, ["pz"])
                        act(zs[:, cl, :], pz[:, :], AF.Silu, ["pz"], [kzs])
                        for dk in range(8):
                            mm(pq[:, :], hT[:, dk, lt], W[:, dk, 4096:4608], dk == 0, dk == 7, wkeys + hkeys, ["pq"])
                        for dk in range(8):
                            mm(pkv[:, 0:272], hT[:, dk, lt], W[:, dk, 4608:4880], dk == 0, dk == 7, wkeys + hkeys, ["pkv"])
                        if cl >= 1:
                            qtrans(cl - 1)
                        cp("vector", vs[:, cl, :], pkv[:, 128:256], ["pkv"], [kvs])
                        cp("vector", dts[:, cl, :], pkv[:, 256:272], ["pkv"], [kds])
                        sta = stt_t[c % 2]
                        ks = "stat%d" % (c % 2)
                        act(sq[:, 0:512], pq[:, :], AF.Square, ["pq"], [("sq", 0)])
                        act(sq[:, 512:640], pkv[:, 0:128], AF.Square, ["pkv"], [("sq", 1)])
                        S.op("vector", lambda e, o=sta[:, 8:18], a=sq[:].rearrange("p (h d) -> p h d", h=10):
                             e.tensor_reduce(out=o, in_=a, axis=AX.X, op=ALU.add), [("sq", 0), ("sq", 1)], [(ks, 8)])
                        tsc("vector", sta[:, 18:28], sta[:, 8:18], 1.0 / 64, EPS, ALU.mult, ALU.add, [(ks, 8)], [(ks, 18)])
                        tt("gpsimd", sta[:, 38:48], sta[:, 18:28], mhalf[:, 0:10], ALU.pow, [(ks, 18), "mhalf"], [(ks, 38)])
                        tt("vector", qn[:, 0:512].rearrange("p (h d) -> p h d", h=8),
                           pq[:, :].rearrange("p (h d) -> p h d", h=8),
                           sta[:, 38:46].unsqueeze(2).broadcast_to([128, 8, 64]), ALU.mult, ["pq", (ks, 38)], [("qn", 0)])
                        tt("vector", qn[:, 512:640].rearrange("p (h d) -> p h d", h=2),
                           pkv[:, 0:128].rearrange("p (h d) -> p h d", h=2),
                           sta[:, 46:48].unsqueeze(2).broadcast_to([128, 2, 64]), ALU.mult, ["pkv", (ks, 38)], [("qn", 1)])
                        tt("gpsimd", qw[:], qn[:], wqk[:], ALU.mult, [("qn", 0), ("qn", 1), "wqk"], ["qw"])
                        qw3 = qw[:].rearrange("p (h d) -> p h d", h=10)
                        tt("gpsimd", m1[:].rearrange("p (h d) -> p h d", h=10), qw3,
                           rope[:, 0, c, :].unsqueeze(1).broadcast_to([128, 10, 64]), ALU.mult, ["qw", "rope"], ["m1"])
                        tt("gpsimd", m2[:].rearrange("p (h d) -> p h d", h=10), qw3,
                           rope[:, 1, c, :].unsqueeze(1).broadcast_to([128, 10, 64]), ALU.mult, ["qw", "rope"], ["m2"])
                        qrb = qr[c % 2]
                        kqr = "qr%d" % (c % 2)

                        def pv(tile_, off):
                            return bass.AP(tile_, off, [[640, 128], [2, 320]])
                        tt("vector", pv(qrb, 0), pv(m1, 0), pv(m2, 1), ALU.subtract, ["m1", "m2"], [(kqr, 0)])
                        tt("gpsimd", pv(qrb, 1), pv(m2, 0), pv(m1, 1), ALU.add, ["m1", "m2"], [(kqr, 1)])
                    for seg in (4, 5, 6):
                        fm_seg(seg)
                    if i + 1 < NT:
                        for cl in range(3):
                            norm_back(3 * (i + 1) + cl)
                    qtrans(2)
                    dma("sync", DA(sz_d, t0 * 512, [[512, 128], [128 * 512, 3], [1, 512]]), zs[:], [kzs], ["sz"])
                    dma("sync", DA(qT_d, t0, [[TP, 128], [128 * TP, 4], [1, TS]]), qTs[:], [kqs], ["qT"])
                    dma("sync", DA(kT_d, t0, [[TP, 128], [1, TS]]), kTs[:], [kks], ["kT"])
                    dma("sync", DA(vtm_d, t0 * 128, [[128, 128], [128 * 128, 3], [1, 128]]), vs[:], [kvs], ["vtm"])
                    dma("sync", DA(dtr_d, t0 * 16, [[16, 128], [128 * 16, 3], [1, 16]]), dts[:], [kds], ["dtr"])
            S.barrier()

        def phase_p12(l):
            with contextlib.ExitStack() as st:
                pw = alloc(st, "pw", [128, 4, 128], BF16)
                psc = alloc(st, "psc", [128, 4], F32)
                cinv = alloc(st, "cinv", [128, 4, 2, 8], F32)
                ub = [alloc(st, "ub", [128, 4, 400], BF16) for _ in range(2)]
                sgp = [alloc(st, "sg", [128, 4, TS], BF16) for _ in range(2)]
                T2 = alloc(st, "T2", [128, 4, 400], F32)
                T4 = alloc(st, "T4", [128, 4, 400], F32)
                T8 = alloc(st, "T8", [128, 4, 400], F32)
                T16 = alloc(st, "T16", [128, 4, 400], F32)
                tmp = alloc(st, "ptmp", [128, 4, 8], F32)
                dT = [alloc(st, "dT", [128, 4, TS], BF16) for _ in range(2)]
                ostp = [alloc(st, "ost", [128, 4, TS], BF16) for _ in range(2)]
                Ap = alloc(st, "Ap", [128, NCH, 512], BF16)
                Bp = alloc(st, "Bp", [128, NCH, 512], BF16)
                ring = [alloc(st, "ring", [128, PCC, TS], BF16) for _ in range(4)]
                cs = alloc(st, "dftc", [128, 2, 4, 512], BF16)
                wf = alloc(st, "wf", [128, 4, 512], BF16)
                cw = alloc(st, "cw", [128, 2, 4, 512], BF16)
                uT = [alloc(st, "uT", [128, 4, TS], BF16) for _ in range(2)]
                sg = [alloc(st, "sgf", [128, 4, TS], BF16) for _ in range(2)]
                ost = [alloc(st, "ostf", [128, 4, TS], BF16) for _ in range(2)]
                acc = [palloc(st, "acc", [128, 512], F32) for _ in range(4)]
                py = [palloc(st, "py", [128, 512], F32) for _ in range(4)]
                banks = [(acc[k], ("acc", 0, k)) for k in range(4)] + [(py[g], ("py", g)) for g in range(4)]

                dma("gpsimd", pw[:], DA(poolw_d, l * 4 * 128 * 128, [[128, 128], [128 * 128, 4], [1, 128]]), (), ["pw"])
                dma("sync", psc[:].unsqueeze(2), DA(pools_d, l * 512, [[1, 128], [128, 4], [1, 1]]), (), ["psc"], slow=True)
                dma("sync", cinv[:], c_pool_d.ap(), (), ["cinv"])
                dma("sync", cs[:], c_dftc_d.ap(), (), ["dftc"])
                dma("gpsimd", wf[:], DA(fw_d, l * 512 * 512, [[512, 128], [128 * 512, 4], [1, 512]]), (), ["wf"])
                Ts = (T2, T4, T8, T16)

                def p1_front(i):
                    t0 = i * TS
                    u = ub[i % 2]
                    ku = "ub%d" % (i % 2)
                    d_ = dT[i % 2]
                    kd = "dT%d" % (i % 2)
                    dma("sync", u[:], DA(fm_d, t0, [[TW, 128], [128 * TW, 4], [1, 400]]), [("fm", 0), "fm_margin"], [ku])
                    dma("sync", sgp[i % 2][:], DA(fm_d, 512 * TW + 8 + t0, [[TW, 128], [128 * TW, 4], [1, TS]]), [("fm", 1)],
                        ["sg%d" % (i % 2)])
                    tt("vector", T2[:, :, 0:399], u[:, :, 0:399], u[:, :, 1:400], ALU.add, [ku], ["T2"])
                    tt("gpsimd", T4[:, 1:4, 0:397], T2[:, 1:4, 0:397], T2[:, 1:4, 2:399], ALU.add, ["T2"], ["T4"])
                    tt("vector", T8[:, 2:4, 0:393], T4[:, 2:4, 0:393], T4[:, 2:4, 4:397], ALU.add, ["T4"], ["T8"])
                    tt("gpsimd", T16[:, 3, 0:385], T8[:, 3, 0:385], T8[:, 3, 8:393], ALU.add, ["T8"], ["T16"])
                    for g, w_ in enumerate((2, 4, 8, 16)):
                        half = w_ // 2
                        Pg = Ts[g]
                        kP = ("T2", "T4", "T8", "T16")[g]
                        o8 = 8 - half
                        stt(d_[:, g, :], Pg[:, g, o8:o8 + TS], 1.0 / w_, u[:, g, 8:8 + TS], ALU.mult, ALU.subtract,
                            [kP, ku], [(kd, g)])
                        for side, tile_i, m0 in ((0, 0, NPAD), (1, NT - 1, TS - 8)):
                            if i != tile_i:
                                continue
                            tt("vector", tmp[:, g, :], Pg[:, g, o8 + m0:o8 + m0 + 8], cinv[:, g, side, :], ALU.mult,
                               [kP, "cinv"], [("ptmp", g)])
                            tt("vector", d_[:, g, m0:m0 + 8], tmp[:, g, :], u[:, g, 8 + m0:8 + m0 + 8], ALU.subtract,
                               [("ptmp", g), ku], [(kd, g)])

                def p1_back(i):
                    t0 = i * TS
                    d_ = dT[i % 2]
                    kd = "dT%d" % (i % 2)
                    s_ = sgp[i % 2]
                    ksg = "sg%d" % (i % 2)
                    o_ = ostp[i % 2]
                    ko = "ost%d" % (i % 2)
                    for g in range(4):
                        mm(py[g][:, 0:TS], pw[:, g, :], d_[:, g, :], True, True, ["pw", (kd, g)], [("py", g)])
                    for g in range(4):
                        stt(o_[:, g, :], py[g][:, 0:TS], psc[:, g:g + 1], s_[:, g, :], ALU.mult, ALU.mult,
                            [("py", g), "psc", ksg], [(ko, g)])
                    dma("sync", DA(mixT_d, t0, [[TP, 128], [128 * TP, 4], [1, TS]]), o_[:],
                        [(ko, g) for g in range(4)], [("mixT", 0)])

                n = 0
                for tb in range(2):
                    for ci in range(4):
                        p, kp = banks[n % 8]
                        for mk in range(4):
                            mm(p[:, :], cs[:, tb, mk, ci * 128:(ci + 1) * 128], wf[:, mk, :], mk == 0, mk == 3,
                               ["dftc", "wf"], [kp])
                        cp("scalar" if n % 2 else "vector", cw[:, tb, ci, :], p[:, :], [kp], [("cw", tb, ci)])
                        n += 1
                cwk = [("cw", tb, ci) for tb in range(2) for ci in range(4)]
                for i in range(NT):
                    t0 = i * TS
                    u = uT[i % 2]
                    ku = "uT%d" % (i % 2)
                    dma("sync", u[:], DA(fm_d, 1024 * TW + 8 + t0, [[TW, 128], [128 * TW, 4], [1, TS]]), [("fm", 2)], [ku])
                    for cl in range(3):
                        c = 3 * i + cl
                        for tb, dst in ((0, Ap), (1, Bp)):
                            p, kp = banks[(tb * 3 + cl) % 8]
                            for ck in range(4):
                                mm(p[:, :], u[:, ck, cl * 128:(cl + 1) * 128], cw[:, tb, ck, :], ck == 0, ck == 3,
                                   [ku] + cwk, [kp])
                            cp("scalar" if tb else "vector", dst[:, c, :], p[:, :], [kp], [("ApBp", c)])
                apk = [("ApBp", c) for c in range(NCH)]
                npiece = 0
                for kt in range(NT):
                    t0 = kt * TS
                    s_ = sg[kt % 2]
                    ksg = "sgf%d" % (kt % 2)
                    o_ = ost[kt % 2]
                    ko = "ostf%d" % (kt % 2)
                    dma("sync", s_[:], DA(fm_d, 1536 * TW + 8 + t0, [[TW, 128], [128 * TW, 4], [1, TS]]), [("fm", 3)], [ksg])
                    p1_front(kt)
                    for pc in range(NPC):
                        for tb in range(2):
                            rb = ring[npiece % 4]
                            kr = "ring%d" % (npiece % 4)
                            npiece += 1
                            off = (((tb * NT + kt) * NPC + pc) * 128) * (PCC * TS)
                            dma("sync", rb[:], DA(c_tab_d, off, [[PCC * TS, 128], [TS, PCC], [1, TS]]), (), [kr])
                            src = Bp if tb else Ap
                            for tci in range(PCC):
                                tc_ = pc * PCC + tci
                                first = (pc == 0 and tb == 0 and tci == 0)
                                last = (pc == NPC - 1 and tb == 1 and tci == PCC - 1)
                                for dc in range(4):
                                    mm(acc[dc][:, 0:TS], src[:, tc_, dc * 128:(dc + 1) * 128], rb[:, tci, :], first, last,
                                       apk + [kr], [("acc", 0, dc)])
                    for dc in range(4):
                        tt("vector", o_[:, dc, :], acc[dc][:, 0:TS], s_[:, dc, :], ALU.mult,
                           [("acc", 0, dc), ksg], [(ko, dc)])
                    dma("sync", DA(mixT_d, 512 * TP + t0, [[TP, 128], [128 * TP, 4], [1, TS]]), o_[:],
                        [(ko, dc) for dc in range(4)], [("mixT", 1)])
                    p1_back(kt)
            S.barrier()

        def phase_p3(l):
            with contextlib.ExitStack() as st:
                BT = alloc(st, "BT", [128, 2, TP], BF16)
                CT = alloc(st, "CT", [128, 2, TP], BF16)
                xtm = alloc(st, "xtm", [128, NCH, 512], BF16)
                btm = alloc(st, "btm", [128, NCH, 256], BF16)
                dtr = alloc(st, "dtr", [128, NCH, 16], F32)
                dtt = alloc(st, "dtt", [128, NCH, 16], F32)
                da = alloc(st, "da", [128, NCH, 16], F32)
                tA = alloc(st, "tA", [128, NCH, 16], F32)
                tB = alloc(st, "tB", [128, NCH, 16], F32)
                dtb = alloc(st, "dtb", [128, 16], F32)
                alg = alloc(st, "alg", [128, 16], F32)
                dsk = alloc(st, "dsk", [128, 8], F32)
                snw = alloc(st, "snw", [128, 512], F32)
                cwt = alloc(st, "cwt", [128, 8, 4], F32)
                cbt = alloc(st, "cbt", [128, 8], F32)
                dg = alloc(st, "dg", [128, 8, 4, 128], BF16)
                cf = alloc(st, "cf", [128, 3, 128], F32)
                cb = alloc(st, "cb", [128, 6, 128], BF16)
                xin = [alloc(st, "xin", [128, 8, TS + 3], BF16) for _ in range(2)]
                xTt = [alloc(st, "xTt", [128, 4, TS], BF16) for _ in range(2)]
                H = alloc(st, "H", [128, 512], F32)
                Hb = alloc(st, "Hb", [128, 512], BF16)
                Ht = alloc(st, "Ht", [128, 512], F32)
                cst = alloc(st, "cst", [128, 24], F32)
                E = alloc(st, "E", [128, 24], F32)
                R = alloc(st, "R", [128, 8, 128], BF16)
                cbm = alloc(st, "cbm", [128, 2, 128], BF16)
                eseg = alloc(st, "eseg", [128, 8, 128], BF16)
                MT = alloc(st, "MT", [128, 8, 128], BF16)
                xdt = alloc(st, "xdt", [128, 512], BF16)
                xw = alloc(st, "xw", [128, 512], BF16)
                ytmp = alloc(st, "ytmp", [128, 512], F32)
                yc = [alloc(st, "yc", [128, 512], F32) for _ in range(2)]
                yfl = [alloc(st, "yfl", [128, 512], F32) for _ in range(2)]
                szl = [alloc(st, "szl", [128, 512], BF16) for _ in range(2)]
                yg = alloc(st, "yg", [128, 512], F32)
                yjunk = alloc(st, "yjunk", [128, 512], BF16)
                ynb = alloc(st, "ynb", [128, 512], BF16)
                ystat = alloc(st, "ystat", [128, 4], F32)
                yst = [alloc(st, "yst", [128, 4, TS], BF16) for _ in range(2)]
                pcv = [palloc(st, "pcv", [128, 512], F32) for _ in range(2)]
                ptr = palloc(st, "ptr", [128, 1024], BF16)
                pcb = pcv[0]
                pseg = [palloc(st, "pseg", [128, 512], F32) for _ in range(2)]
                pyd = palloc(st, "pyd", [128, 512], F32)
                pyo = palloc(st, "pyo", [128, 512], F32)
                psn = palloc(st, "psn", [128, 512], F32)
                dgD = alloc(st, "dgD", [128, 8, 128], BF16)
                pyd2 = [pcv[1], pyd]
                kpyd = ["pcv1", "pyd"]
                cbm2 = [alloc(st, "cbm2", [128, 2, 128], BF16) for _ in range(2)]
                cst2 = [alloc(st, "cst2", [128, 24], F32) for _ in range(2)]
                E2 = [alloc(st, "E2", [128, 24], F32) for _ in range(2)]
                MT2 = [alloc(st, "MT2", [128, 8, 128], BF16) for _ in range(2)]
                xdt2 = [alloc(st, "xdt2", [128, 512], BF16) for _ in range(2)]
                xw2 = [alloc(st, "xw2", [128, 512], BF16) for _ in range(2)]
                ytmp2 = alloc(st, "ytmp2", [128, 512], F32)

                dma("sync", cf[:], c_ssdf_d.ap(), (), ["cf"])
                dma("sync", cb[:], c_ssdb_d.ap(), (), ["cb"])
                dma("sync", dtb[:], DA(dtb_d, l * 16, [[0, 128], [1, 16]]), (), ["dtb"])
                dma("sync", alg[:], DA(alog_d, l * 16, [[0, 128], [1, 16]]), (), ["alg"])
                dma("sync", dsk[:], DA(dskip_d, l * 8, [[0, 128], [1, 8]]), (), ["dsk"])
                snwT = alloc(st, "snwT", [128, 4], F32)
                dma("sync", snwT[:].unsqueeze(2), DA(ssdnw_d, l * 512, [[1, 128], [128, 4], [1, 1]]), (), ["snwT"], slow=True)
                for j in range(4):
                    dma("sync", cwt[:, :, j:j + 1], DA(convw_d, (l * 4 + j) * 1024, [[1, 128], [128, 8], [1, 1]]), (), ["cwt"], slow=True)
                dma("sync", cbt[:].unsqueeze(2), DA(convb_d, l * 1024, [[1, 128], [128, 8], [1, 1]]), (), ["cbt"], slow=True)
                for ch in range(8):
                    for j in range(4):
                        tsc("gpsimd", dg[:, ch, j, :], ident[:], cwt[:, ch, j:j + 1], None, ALU.mult, None,
                            ["ident", "cwt"], [("dg", ch)])
                for c3 in range(3):
                    dma("sync", dtr[:, c3 * 11:(c3 + 1) * 11, :],
                        DA(dtr_d, c3 * 11 * 128 * 16, [[16, 128], [128 * 16, 11], [1, 16]]), ["dtr"], ["dtrl"])
                bcd = dtb[:].unsqueeze(1).broadcast_to([128, NCH, 16])
                tt("vector", tA[:], dtr[:], bcd, ALU.add, ["dtrl", "dtb"], ["tA"])
                act(tB[:], tA[:], AF.Abs, ["tA"], ["tB"])
                act(tB[:], tB[:], AF.Exp, ["tB"], ["tB"], scale=-1.0)
                tsc("vector", tB[:], tB[:], 1.0, None, ALU.add, None, ["tB"], ["tB"])
                act(tB[:], tB[:], AF.Ln, ["tB"], ["tB"])
                tsc("vector", tA[:], tA[:], 0.0, None, ALU.max, None, ["tA"], ["tA"])
                tt("vector", dtt[:], tA[:], tB[:], ALU.add, ["tA", "tB"], ["dtt"])
                act(alg[:], alg[:], AF.Exp, ["alg"], ["alg"])
                tsc("vector", alg[:], alg[:], -1.0, None, ALU.mult, None, ["alg"], ["alg"])
                tt("vector", da[:], dtt[:], alg[:].unsqueeze(1).broadcast_to([128, NCH, 16]), ALU.mult, ["dtt", "alg"], ["da"])

                for i in range(NT):
                    t0 = i * TS
                    xi = xin[i % 2]
                    kx = "xin%d" % (i % 2)
                    xt_ = xTt[i % 2]
                    kxt = "xTt%d" % (i % 2)
                    dma("sync", xi[:], DA(fm_d, 2048 * TW + 8 + t0 - 2, [[TW, 128], [128 * TW, 8], [1, TS + 3]]),
                        [("fm", 4), ("fm", 5), "fm_margin"], [kx])
                    for ch in range(8):
                        p = pcv[ch % 2]
                        kp = "pcv%d" % (ch % 2)
                        for j in range(4):
                            mm(p[:, 0:TS], dg[:, ch, j, :], xi[:, ch, j:j + TS], j == 0, j == 3, [("dg", ch), kx], [kp])
                        if ch < 4:
                            dst, kdst = xt_[:, ch, :], (kxt, ch)
                        elif ch < 6:
                            dst, kdst = BT[:, ch - 4, t0:t0 + TS], ("BT", i)
                        else:
                            dst, kdst = CT[:, ch - 6, t0:t0 + TS], ("CT", i)
                        act(dst, p[:, 0:TS], AF.Silu, [kp, "cbt"], [kdst], bias=cbt[:, ch:ch + 1])
                    if i == 0:
                        memset("gpsimd", xt_[:, :, 0:NPAD], 0.0, [(kxt, ch) for ch in range(4)])
                    for cl in range(3):
                        c = 3 * i + cl
                        lt = slice(cl * 128, (cl + 1) * 128)
                        for ch in range(4):
                            identr(ptr[:, ch * 128:(ch + 1) * 128], xt_[:, ch, lt], [(kxt, q) for q in range(4)], ["ptr"])
                        for g in range(2):
                            identr(ptr[:, 512 + g * 128:512 + (g + 1) * 128], BT[:, g, t0 + cl * 128:t0 + (cl + 1) * 128],
                                   [("BT", i)], ["ptr"])
                        cp("vector", xtm[:, c, :], ptr[:, 0:512], ["ptr"], [("xtm", c)])
                        cp("scalar", btm[:, c, :], ptr[:, 512:768], ["ptr"], [("btm", c)])

                for h in range(8):
                    tsc("gpsimd", dgD[:, h, :], ident[:], dsk[:, h:h + 1], None, ALU.mult, None, ["ident", "dsk"], ["dgD"])
                ydx = ytmp2
                for d_ in range(2):
                    memset("gpsimd", H[:], 0.0, ["H"])
                    memset("gpsimd", Hb[:], 0.0, ["Hb"])
                    order = list(range(NCH)) if d_ == 0 else list(range(NCH - 1, -1, -1))

                    def stage_a(n_, d_=d_, order=order):
                        c = order[n_]
                        i = c // 3
                        b2 = n_ % 2
                        cols = slice(c * 128, (c + 1) * 128)
                        dac = da[:, c, d_ * 8:(d_ + 1) * 8]
                        for g in range(2):
                            mm(pcb[:, g * 128:(g + 1) * 128], BT[:, g, cols], CT[:, g, cols], True, True,
                               [("BT", i), ("CT", i)], ["pcv0"])
                        mm(pcb[:, 256:264], cf[:, d_, :], dac, True, True, ["cf", "da"], ["pcv0"])
                        mm(pcb[:, 264:272], cf[:, 2, :], dac, True, True, ["cf", "da"], ["pcv0"])
                        tt("vector", cbm2[b2][:], pcb[:, 0:256].rearrange("p (g s) -> p g s", g=2),
                           cb[:, 4 + d_, :].unsqueeze(1).broadcast_to([128, 2, 128]), ALU.mult, ["pcv0", "cb"], ["cbm%d" % b2])
                        cp("vector", cst2[b2][:, 0:16], pcb[:, 256:272], ["pcv0"], [("cst%d" % b2, 0)])
                        tt("vector", cst2[b2][:, 16:24], cst2[b2][:, 8:16], cst2[b2][:, 0:8], ALU.subtract,
                           [("cst%d" % b2, 0)], [("cst%d" % b2, 1)])
                        act(E2[b2][:], cst2[b2][:], AF.Exp, [("cst%d" % b2, 0), ("cst%d" % b2, 1)], ["E%d" % b2])
                        tt("gpsimd", R[:], cb[:, d_, :].unsqueeze(1).broadcast_to([128, 8, 128]),
                           dac.unsqueeze(2).broadcast_to([128, 8, 128]), ALU.mult, ["cb", "da"], ["R"])
                        tt("gpsimd", xdt2[b2][:].rearrange("p (h d) -> p h d", h=8),
                           xtm[:, c, :].rearrange("p (h d) -> p h d", h=8),
                           dtt[:, c, d_ * 8:(d_ + 1) * 8].unsqueeze(2).broadcast_to([128, 8, 64]), ALU.mult,
                           [("xtm", c), "dtt"], ["xdt%d" % b2])
                        for hh in range(2):
                            mm(pseg[hh][:, :], cb[:, 2 + d_, :], R[:, hh * 4:(hh + 1) * 4, :], True, True, ["cb", "R"],
                               [("pseg", hh)])
                            act(eseg[:, hh * 4:(hh + 1) * 4, :], pseg[hh][:, :].rearrange("p (h l) -> p h l", h=4), AF.Exp,
                                [("pseg", hh)], [("eseg", hh)])
                            tt("vector", MT2[b2][:, hh * 4:(hh + 1) * 4, :], eseg[:, hh * 4:(hh + 1) * 4, :],
                               cbm2[b2][:, hh, :].unsqueeze(1).broadcast_to([128, 4, 128]), ALU.mult,
                               [("eseg", hh), "cbm%d" % b2], [("MT%d" % b2, hh)])
                        tt("gpsimd", xw2[b2][:].rearrange("p (h d) -> p h d", h=8),
                           xdt2[b2][:].rearrange("p (h d) -> p h d", h=8),
                           E2[b2][:, 16:24].unsqueeze(2).broadcast_to([128, 8, 64]), ALU.mult, ["xdt%d" % b2, "E%d" % b2],
                           ["xw%d" % b2])
                        for h in range(8):
                            mm(pyd2[b2][:, h * 64:(h + 1) * 64], MT2[b2][:, h, :], xdt2[b2][:, h * 64:(h + 1) * 64], True, d_ == 0,
                               [("MT%d" % b2, h // 4), "xdt%d" % b2], [kpyd[b2]])
                            if d_ == 1:
                                mm(pyd2[b2][:, h * 64:(h + 1) * 64], dgD[:, h, :], xtm[:, c, h * 64:(h + 1) * 64], False, True,
                                   ["dgD", ("xtm", c)], [kpyd[b2]])

                    def stage_b(n_, d_=d_, order=order):
                        c = order[n_]
                        i = c // 3
                        b2 = n_ % 2
                        cols = slice(c * 128, (c + 1) * 128)
                        Eb = E2[b2]
                        kE = "E%d" % b2
                        for g in range(2):
                            mm(pyo[:, g * 256:(g + 1) * 256], CT[:, g, cols], Hb[:, g * 256:(g + 1) * 256], True, True,
                               [("CT", i), "Hb"], ["pyo"])
                        tt("vector", ytmp[:].rearrange("p (h d) -> p h d", h=8), pyo[:, :].rearrange("p (h d) -> p h d", h=8),
                           Eb[:, 0:8].unsqueeze(2).broadcast_to([128, 8, 64]), ALU.mult, ["pyo", kE], ["ytmp"])
                        ycb = yc[b2]
                        kyc = "yc%d" % b2
                        tt("vector", ycb[:], pyd2[b2][:, :], ytmp[:], ALU.add, [kpyd[b2], "ytmp"], [kyc])
                        for g in range(2):
                            mm(psn[:, g * 256:(g + 1) * 256], btm[:, c, g * 128:(g + 1) * 128], xw2[b2][:, g * 256:(g + 1) * 256],
                               True, True, [("btm", c), "xw%d" % b2], ["psn"])
                        tt("gpsimd", Ht[:].rearrange("p (h d) -> p h d", h=8), H[:].rearrange("p (h d) -> p h d", h=8),
                           Eb[:, 8:16].unsqueeze(2).broadcast_to([128, 8, 64]), ALU.mult, ["H", kE], ["Ht"])
                        tt("vector", H[:], psn[:, :], Ht[:], ALU.add, ["psn", "Ht"], ["H"])
                        cp("scalar", Hb[:], H[:], ["H"], ["Hb"])
                        if d_ == 0:
                            dma("sync", yf_d.ap()[c * 128:(c + 1) * 128, :], ycb[:], [kyc], ["yf"])
                        else:
                            yfb = yfl[b2]
                            kyf = "yfl%d" % b2
                            szb = szl[b2]
                            ksz = "szl%d" % b2
                            dma("sync", yfb[:], yf_d.ap()[c * 128:(c + 1) * 128, :], ["yf"], [kyf])
                            dma("sync", szb[:], sz_d.ap()[c * 128:(c + 1) * 128, :], ["sz"], [ksz])
                            tt("vector", yg[:], ycb[:], yfb[:], ALU.add, [kyc, kyf], ["yg"])
                            tt("gpsimd", yg[:], yg[:], szb[:], ALU.mult, ["yg", ksz], ["yg"])
                            act(yjunk[:], yg[:], AF.Square, ["yg"], ["yjunk", ("ystat", 0)], accum_out=ystat[:, 0:1])
                            tsc("vector", ystat[:, 1:2], ystat[:, 0:1], 1.0 / 512, EPS, ALU.mult, ALU.add, [("ystat", 0)], [("ystat", 1)])
                            tt("gpsimd", ystat[:, 3:4], ystat[:, 1:2], mhalf[:, 0:1], ALU.pow, [("ystat", 1), "mhalf"], [("ystat", 3)])
                            act(ynb[:], yg[:], AF.Copy, ["yg", ("ystat", 3)], ["ynb"], scale=ystat[:, 3:4])
                            ti = c // 3
                            cl = c % 3
                            ys = yst[ti % 2]
                            kys = "yst%d" % (ti % 2)
                            for ch in range(4):
                                identr(ptr[:, ch * 128:(ch + 1) * 128], ynb[:, ch * 128:(ch + 1) * 128], ["ynb"], ["ptr"])
                            for j4 in range(4):
                                act(ys[:, j4, cl * 128:(cl + 1) * 128], ptr[:, j4 * 128:(j4 + 1) * 128], AF.Copy,
                                    ["ptr", "snwT"], [(kys, cl)], scale=snwT[:, j4:j4 + 1])
                            if cl == 0:
                                dma("sync", DA(mixT_d, 1024 * TP + ti * TS, [[TP, 128], [128 * TP, 4], [1, TS]]), ys[:],
                                    [(kys, q) for q in range(3)], [("mixT", 2)])

                    stage_a(0)
                    for n_ in range(NCH):
                        if n_ + 1 < NCH:
                            stage_a(n_ + 1)
                        stage_b(n_)
            S.barrier()

        def phase_p4(l):
            TQ = 512
            qtiles = [(NPAD + k * 456, 456) for k in range(8)] + [(NPAD + 8 * 456, 464)]
            with contextlib.ExitStack() as st:
                KT2 = alloc(st, "KT2", [128, 2, TP], BF16)
                Vp = alloc(st, "Vp", [128, NCH, 2, 192], BF16)
                vmask = alloc(st, "vmask", [128, 64], BF16)
                QT = [alloc(st, "QT", [128, 4, TQ], BF16) for _ in range(2)]
                sg = [alloc(st, "sga", [128, 4, TQ], BF16) for _ in range(2)]
                P = [alloc(st, "P", [128, 2, TQ], BF16) for _ in range(3)]
                rec = alloc(st, "rec", [128, TQ], F32)
                yt = alloc(st, "yt", [128, TQ], F32)
                ost = [alloc(st, "osta", [128, 4, TQ], BF16) for _ in range(2)]
                Sp = [palloc(st, "Sp", [128, 2, 512], F32) for _ in range(2)]
                Oe = [palloc(st, "Oe", [128, 512], F32) for _ in range(2)]
                Oo = [palloc(st, "Oo", [128, 512], F32) for _ in range(2)]
                dma("sync", vmask[:], c_vmask_d.ap(), (), ["vmask"])
                for g in range(2):
                    for hs in range(2):
                        dma("sync", KT2[hs * 64:(hs + 1) * 64, g, :], DA(kT_d, g * 64 * TP, [[TP, 64], [1, TP]]), ["kT"],
                            [("KT2", g, hs)])
                vkeys = [("Vp", c3, g) for c3 in range(3) for g in range(2)]
                memset("gpsimd", Vp[:], 1.0, vkeys)
                for g in range(2):
                    for o0 in (0, 128):
                        cp("gpsimd", Vp[:, 0, g, o0:o0 + 64], vmask[:], ["vmask"], [("Vp", 0, g)])
                for c3 in range(3):
                    for g in range(2):
                        dma("sync", Vp[:, c3 * 11:(c3 + 1) * 11, g, 64:128],
                            DA(vtm_d, c3 * 11 * 128 * 128 + g * 64, [[128, 128], [128 * 128, 11], [1, 64]]), ["vtm"],
                            [("Vp", c3, g)])
                ktk = [("KT2", g, hs) for g in range(2) for hs in range(2)]
                npair = 0
                nS = [0]

                def p4_load(i):
                    t0, tw = qtiles[i]
                    dma("sync", QT[i % 2][:, :, 0:tw], DA(qT_d, t0, [[TP, 128], [128 * TP, 4], [1, tw]]), ["qT"], ["QT%d" % (i % 2)])
                    dma("sync", sg[i % 2][:, :, 0:tw], DA(fm_d, 3072 * TW + 8 + t0, [[TW, 128], [128 * TW, 4], [1, tw]]),
                        [("fm", 6)], ["sga%d" % (i % 2)])

                for i in range(len(qtiles)):
                    t0, tw = qtiles[i]
                    q_ = QT[i % 2]
                    kq = "QT%d" % (i % 2)
                    s_ = sg[i % 2]
                    ksg = "sga%d" % (i % 2)
                    o_ = ost[i % 2]
                    ko = "osta%d" % (i % 2)
                    if i == 0:
                        p4_load(0)
                    if i + 1 < len(qtiles):
                        p4_load(i + 1)
                    for j in range(4):
                        g = j // 2
                        oe = Oe[npair % 2]
                        oo = Oo[npair % 2]
                        koe = "Oe%d" % (npair % 2)
                        koo = "Oo%d" % (npair % 2)
                        npair += 1
                        pend = {}

                        def qk(kc, g=g, j=j, q_=q_, kq=kq, pend=pend, tw=tw):
                            n_ = nS[0]
                            nS[0] += 1
                            sp = Sp[n_ % 2]
                            ksp = "Sp%d" % (n_ % 2)
                            pb = P[n_ % 3]
                            kpb = "P%d" % (n_ % 3)
                            kcols = slice(kc * 128, (kc + 1) * 128)
                            mm(sp[:, 0, 0:tw], KT2[0:64, g, kcols], q_[0:64, j, 0:tw], True, True, ktk + [kq], [ksp])
                            mm(sp[:, 1, 0:tw], KT2[64:128, g, kcols], q_[64:128, j, 0:tw], True, True, ktk + [kq], [ksp])
                            act(pb[:, :, 0:tw], sp[:, :, 0:tw], AF.Exp, [ksp], [kpb])
                            pend[kc] = (pb, kpb)

                        def pvm(kc, g=g, oe=oe, oo=oo, koe=koe, koo=koo, pend=pend, tw=tw):
                            pb, kpb = pend.pop(kc)
                            mm(oe[:, 0:tw], Vp[:, kc, g, 64:192], pb[:, 0, 0:tw], kc == 0, kc == NCH - 1, vkeys + [kpb], [koe])
                            mm(oo[:, 0:tw], Vp[:, kc, g, 0:128], pb[:, 1, 0:tw], kc == 0, kc == NCH - 1, vkeys + [kpb], [koo])

                        qk(0)
                        for kc in range(NCH):
                            if kc + 1 < NCH:
                                qk(kc + 1)
                            pvm(kc)
                        recip(rec[0:64, 0:tw], oe[64:128, 0:tw], [koe], [("rec", 0)])
                        recip(rec[64:128, 0:tw], oo[0:64, 0:tw], [koo], [("rec", 1)])
                        tt("vector", yt[0:64, 0:tw], oe[0:64, 0:tw], rec[0:64, 0:tw], ALU.mult, [koe, ("rec", 0)], [("yt", 0)])
                        tt("vector", yt[64:128, 0:tw], oo[64:128, 0:tw], rec[64:128, 0:tw], ALU.mult, [koo, ("rec", 1)], [("yt", 1)])
                        tt("gpsimd", o_[:, j, 0:tw], yt[:, 0:tw], s_[:, j, 0:tw], ALU.mult, [("yt", 0), ("yt", 1), ksg], [(ko, j)])
                    dma("sync", DA(mixT_d, 1536 * TP + t0, [[TP, 128], [128 * TP, 4], [1, tw]]), o_[:, :, 0:tw],
                        [(ko, j) for j in range(4)], [("mixT", 3)])
            S.barrier()

        def phase_p5(l, last):
            with contextlib.ExitStack() as st:
                wo = alloc(st, "wo", [128, 16, D], BF16)
                mx = [alloc(st, "mx", [128, 16, TS], BF16) for _ in range(2)]
                hc = [alloc(st, "hc5", [128, D], F32) for _ in range(4)]
                ho = [alloc(st, "ho5", [128, D], F32) for _ in range(2)]
                po = [[palloc(st, "po", [128, 512], F32) for _ in range(2)] for _ in range(2)]
                for hh in range(2):
                    dma("gpsimd", wo[:, :, hh * 512:(hh + 1) * 512],
                        DA(wout_d, l * 2048 * D + hh * 512, [[D, 128], [128 * D, 16], [1, 512]]), (), [("wo", hh)])
                def p5_load(i):
                    for q4 in range(4):
                        dma("sync", mx[i % 2][:, q4 * 4:(q4 + 1) * 4, :],
                            DA(mixT_d, q4 * 512 * TP + i * TS, [[TP, 128], [128 * TP, 4], [1, TS]]), [("mixT", q4)], ["mx%d" % (i % 2)])

                def p5_hload(c):
                    if c >= NCH or (last and c == 0):
                        return
                    hcb = hc[c % 4]
                    khc = "hc5%d" % (c % 4)
                    if l == 0 and c == 0:
                        memset("gpsimd", hcb[:], 0.0, [khc])
                        dma("sync", hcb[NPAD:128, :], meta_d.ap(), (), [khc])
                    elif l == 0:
                        dma("sync", hcb[:], x_d.ap()[(c - 1) * 128:c * 128, :], (), [khc])
                    else:
                        dma("sync", hcb[:], hbuf_d.ap()[c * 128:(c + 1) * 128, :], ["hbuf", "hbuf_pad"], [khc])

                p5_load(0)
                p5_hload(0)
                p5_hload(1)
                for i in range(NT):
                    t0 = i * TS
                    m_ = mx[i % 2]
                    km = "mx%d" % (i % 2)
                    if i + 1 < NT:
                        p5_load(i + 1)
                    for cl in range(3):
                        c = 3 * i + cl
                        p5_hload(c + 2)
                        if last and c == 0:
                            continue
                        hcb = hc[c % 4]
                        khc = "hc5%d" % (c % 4)
                        hob = ho[c % 2]
                        kho = "ho5%d" % (c % 2)
                        for hh in range(2):
                            p = po[c % 2][hh]
                            kp = ("po", c % 2, hh)
                            for ec in range(16):
                                mm(p[:, :], m_[:, ec, cl * 128:(cl + 1) * 128], wo[:, ec, hh * 512:(hh + 1) * 512],
                                   ec == 0, ec == 15, [km, ("wo", hh)], [kp])
                            tt("vector", hob[:, hh * 512:(hh + 1) * 512], p[:, :], hcb[:, hh * 512:(hh + 1) * 512], ALU.add,
                               [kp, khc], [(kho, hh)])
                        if last:
                            o = dma("sync", out_d.ap()[(c - 1) * 128:c * 128, :], hob[:], [(kho, 0), (kho, 1)], ["out"])
                            final_ops.append(o)
                        elif c == 0:
                            dma("sync", hbuf_d.ap()[NPAD:128, :], hob[NPAD:128, :], [(kho, 0), (kho, 1)], ["hbuf"])
                        else:
                            dma("sync", hbuf_d.ap()[c * 128:(c + 1) * 128, :], hob[:], [(kho, 0), (kho, 1)], ["hbuf"])
            S.barrier()

        wst = contextlib.ExitStack()
        Wcur = load_w(wst, 0) if "0" in phases else None
        for l in range(n_layers):
            if "0" in phases:
                phase_p0(l, Wcur)
                wst.close()
            if "1" in phases or "2" in phases:
                phase_p12(l)
            if "3" in phases:
                phase_p3(l)
            if "4" in phases:
                phase_p4(l)
            if "5" in phases:
                if "0" in phases and l + 1 < n_layers:
                    wst = contextlib.ExitStack()
                    Wcur = load_w(wst, l + 1)
                phase_p5(l, l == n_layers - 1)
        final_ops[:] = [o for o in final_ops if not o.skipped]
        S.limit = None
        print("n_ops", len(S.ops))
        if not final_ops:
            zo = dma("sync", out_d.ap()[0:128, :], zf[:], ["zf"], ["out"])
            final_ops.append(zo)
        S.emit(nc, final_ops)
    return nc


_PARAM_NAMES = ("meta_tokens", "norm_w", "w_in", "w_out", "pool_w", "pool_scale", "fourier_w", "conv_w", "conv_b",
                "dt_bias", "a_log", "d_skip", "ssd_norm_w", "q_norm_w", "k_norm_w")


def make_in_maps(inputs, cores):
    consts = host_consts()
    shared = {k: np.ascontiguousarray(np.asarray(inputs[k], dtype=np.float32)) for k in _PARAM_NAMES}
    shared.update(consts)
    x = np.asarray(inputs["x"], dtype=np.float32)
    maps = []
    for b in cores:
        m = dict(shared)
        m["x"] = np.ascontiguousarray(x[b])
        maps.append(m)
    return maps


def kernel(**inputs):
    nc = build()
    in_maps = make_in_maps(inputs, list(range(8)))
    res = run_bass_kernel_spmd(nc, in_maps, core_ids=list(range(8)))
    return np.stack([np.asarray(r["out"], dtype=np.float32) for r in res.results], axis=0)
```
